# Optimizing a Trainium2 kernel written in Bass

```python
import math
import jax, jax.numpy as jnp
from jax import lax
import numpy as np

D_MODEL = 1024
BATCH = 8
SEQ = 8192
DEPTH = 2
DEC_BATCH = 16
DEC_SEQ = 64
PAST_LEN = 1024

CHUNK = 64
Q_BLOCK = 128
EPS = 1e-6
NEG_BIG = -1e30

A_HEADS = 8
A_DQK = 64
A_DV = 2 * A_DQK
A_ROT = A_DQK // 4
ROPE_THETA = 500000.0
A_QK_W = A_HEADS * 2 * A_DQK
A_WIDTH = A_HEADS * A_DV

B_HEADS = 8
B_DK = 128
B_DV = 128
B_QK_W = B_HEADS * B_DK
B_WIDTH = B_HEADS * B_DV

C_HEADS = 16
C_DH = 64
C_WIDTH = C_HEADS * C_DH
C_DECAY_RANK = 64
C_A_RANK = 64
C_VRES_RANK = 32
C_GN_EPS = 64e-5
C_SIZES = (C_WIDTH, C_DECAY_RANK, C_WIDTH, C_WIDTH, C_A_RANK)
C_SHIFT_W = 3 * C_WIDTH + C_DECAY_RANK + C_A_RANK

IN_SIZES = (A_QK_W, A_QK_W, A_WIDTH, A_WIDTH,
            B_QK_W, B_QK_W, B_WIDTH, B_WIDTH,
            C_SHIFT_W, C_WIDTH,
            D_MODEL, D_MODEL, D_MODEL)
IN_COLS = 2 * A_QK_W + 2 * A_WIDTH + 2 * B_QK_W + 2 * B_WIDTH + C_SHIFT_W + C_WIDTH + 3 * D_MODEL

kernel_name = "hybrid_stream_diffattn_hgrn2_rwkv7_step"


def rms_norm(x, g):
    xf = x.astype(jnp.float32)
    y = xf * lax.rsqrt(jnp.mean(xf * xf, axis=-1, keepdims=True) + EPS)
    return y.astype(x.dtype) * g


def split_cols(p, sizes):
    idx = [int(s) for s in np.cumsum(sizes)[:-1]]
    return jnp.split(p, idx, axis=-1)


def partial_rope(x, pos):
    half = A_ROT // 2
    inv_freq = ROPE_THETA ** (-(jnp.arange(half, dtype=jnp.float32) * (2.0 / A_ROT)))
    ang = pos.astype(jnp.float32)[:, None] * inv_freq[None, :]
    shp = (pos.shape[0],) + (1,) * (x.ndim - 3) + (half,)
    cos = jnp.cos(ang).reshape(shp)
    sin = jnp.sin(ang).reshape(shp)
    x1 = x[..., :half].astype(jnp.float32)
    x2 = x[..., half:A_ROT].astype(jnp.float32)
    rot = jnp.concatenate([x1 * cos - x2 * sin, x2 * cos + x1 * sin], axis=-1).astype(x.dtype)
    return jnp.concatenate([rot, x[..., A_ROT:]], axis=-1)


def diff_attn_block(q, k, v, q_pos, k_pos, lam):
    s = jnp.einsum("bqhnd,bkhnd->bhnqk", q, k, preferred_element_type=jnp.float32) * (A_DQK ** -0.5)
    visible = (k_pos[None, :] // CHUNK) <= (q_pos[:, None] // CHUNK)
    s = jnp.where(visible, s, NEG_BIG)
    p = jax.nn.softmax(s, axis=-1)
    w = p[:, :, 0] - lam * p[:, :, 1]
    return jnp.einsum("bhqk,bkhd->bqhd", w.astype(v.dtype), v)


def mixer_diff_attn(a_q, a_k, a_v, pos, past_k, past_v, qn_g, kn_g, lam_p, subln_g, l):
    Bsz, T, _ = a_q.shape
    q = partial_rope(rms_norm(a_q.reshape(Bsz, T, A_HEADS, 2, A_DQK), qn_g), pos)
    k = partial_rope(rms_norm(a_k.reshape(Bsz, T, A_HEADS, 2, A_DQK), kn_g), pos)
    v = a_v.reshape(Bsz, T, A_HEADS, A_DV)
    if past_k is None:
        k_all, v_all, k_pos = k, v, pos
    else:
        p_len = past_k.shape[1]
        k_all = jnp.concatenate([past_k.reshape(Bsz, p_len, A_HEADS, 2, A_DQK).astype(k.dtype), k], axis=1)
        v_all = jnp.concatenate([past_v.astype(v.dtype), v], axis=1)
        k_pos = jnp.concatenate([jnp.arange(p_len, dtype=jnp.int32), pos])
    lam_init = 0.8 - 0.6 * math.exp(-0.3 * l)
    lp = lam_p.astype(jnp.float32)
    lam = jnp.exp(jnp.sum(lp[0] * lp[1])) - jnp.exp(jnp.sum(lp[2] * lp[3])) + lam_init
    if T > Q_BLOCK and T % Q_BLOCK == 0:
        nb = T // Q_BLOCK
        qb = q.reshape(Bsz, nb, Q_BLOCK, A_HEADS, 2, A_DQK).transpose(1, 0, 2, 3, 4, 5)
        pb = pos.reshape(nb, Q_BLOCK)
        o = lax.map(lambda blk: diff_attn_block(blk[0], k_all, v_all, blk[1], k_pos, lam), (qb, pb))
        o = o.transpose(1, 0, 2, 3, 4).reshape(Bsz, T, A_HEADS, A_DV)
    else:
        o = diff_attn_block(q, k_all, v_all, pos, k_pos, lam)
    o = rms_norm(o, subln_g) * (1.0 - lam_init)
    return o.reshape(Bsz, T, A_WIDTH), k.reshape(Bsz, T, A_HEADS, 2 * A_DQK), v


def mixer_hgrn2(b_q, b_f, b_i, s0, lb, norm_g):
    f32 = jnp.float32
    Bsz, T, _ = b_q.shape
    z = b_f.astype(f32)
    lb = lb.astype(f32)
    log_f = jnp.log(lb + (1.0 - lb) * jax.nn.sigmoid(z))
    k_in = (1.0 - lb) * jax.nn.sigmoid(-z)
    q = jax.nn.silu(b_q.astype(f32))
    i = b_i.astype(f32)
    pad = (-T) % CHUNK
    nc = (T + pad) // CHUNK

    def blocks(t, d):
        t = jnp.pad(t.reshape(Bsz, T, B_HEADS, d), ((0, 0), (0, pad), (0, 0), (0, 0)))
        return t.reshape(Bsz, nc, CHUNK, B_HEADS, d).transpose(1, 0, 3, 2, 4)

    causal = jnp.tril(jnp.ones((CHUNK, CHUNK), dtype=bool))

    def step(S, inp):
        qc, lfc, kc, ic = inp
        cum = jnp.cumsum(lfc, axis=2)
        rel = cum[:, :, :, None, :] - cum[:, :, None, :, :]
        dec = jnp.where(causal[None, None, :, :, None], jnp.exp(jnp.minimum(rel, 0.0)), 0.0)
        scores = jnp.einsum("bhtk,bhtsk,bhsk->bhts", qc, dec, kc)
        o = (jnp.einsum("bhts,bhsv->bhtv", scores, ic)
             + jnp.einsum("bhtk,bhkv->bhtv", qc * jnp.exp(cum), S))
        tail = jnp.exp(cum[:, :, -1:, :] - cum)
        S = jnp.exp(cum[:, :, -1, :])[..., None] * S + jnp.einsum("bhsk,bhsv->bhkv", kc * tail, ic)
        return S, o

    s_fin, o = lax.scan(step, s0.astype(f32),
                        (blocks(q, B_DK), blocks(log_f, B_DK), blocks(k_in, B_DK), blocks(i, B_DV)))
    o = o.transpose(1, 0, 3, 2, 4).reshape(Bsz, nc * CHUNK, B_HEADS, B_DV)[:, :T]
    o = rms_norm(o, norm_g)
    return o.reshape(Bsz, T, B_WIDTH).astype(b_q.dtype), s_fin


def rwkv7_scan(s0, r, w, k, v, a, b):
    def step(S, inp):
        r_t, w_t, k_t, v_t, a_t, b_t = inp
        sa = jnp.einsum("bhij,bhj->bhi", S, a_t)
        S = S * w_t[:, :, None, :] + sa[..., None] * b_t[:, :, None, :] + v_t[..., None] * k_t[:, :, None, :]
        return S, jnp.einsum("bhij,bhj->bhi", S, r_t)
    xs = tuple(jnp.moveaxis(t, 1, 0) for t in (r, w, k, v, a, b))
    s_fin, y = lax.scan(step, s0, xs)
    return s_fin, jnp.moveaxis(y, 0, 1)


def mixer_rwkv7(c_p, shift_prev, s0, h, v_first, l, P):
    f32 = jnp.float32
    Bsz, T, _ = c_p.shape
    prev = jnp.concatenate([shift_prev[:, None, :].astype(c_p.dtype), c_p[:, :-1]], axis=1)
    cs = c_p + (prev - c_p) * P["c_shift_mu"][l]
    r, w_lo, k, v, a_lo = split_cols(cs.astype(f32), C_SIZES)
    w_log = -jax.nn.softplus(-(P["c_w0"][l] + jnp.tanh(w_lo) @ P["c_w2"][l])) - 0.5
    decay = jnp.exp(-jnp.exp(w_log))
    a = jax.nn.sigmoid(P["c_a0"][l] + a_lo @ P["c_a2"][l])
    if l > 0:
        v_mix = jax.nn.sigmoid(P["c_v0"][l - 1] + (h @ P["c_vres_w1"][l - 1]) @ P["c_vres_w2"][l - 1])
        v = v + (v_first - v) * v_mix.astype(f32)
    heads = lambda t: t.reshape(Bsz, T, C_HEADS, C_DH)
    hp = lambda t: t.reshape(C_HEADS, C_DH)
    r, k, vh, decay, a = heads(r), heads(k), heads(v), heads(decay), heads(a)
    kk = k * hp(P["c_k_k"][l])
    kk = kk / jnp.maximum(jnp.sqrt(jnp.sum(kk * kk, axis=-1, keepdims=True)), 1e-12)
    k = k * (1.0 + (a - 1.0) * hp(P["c_k_a"][l]))
    s_fin, y = rwkv7_scan(s0.astype(f32), r, decay, k, vh, -kk, kk * a)
    mu = jnp.mean(y, axis=-1, keepdims=True)
    var = jnp.mean(jnp.square(y - mu), axis=-1, keepdims=True)
    y = (y - mu) * lax.rsqrt(var + C_GN_EPS) * hp(P["c_ln_w"][l]) + hp(P["c_ln_b"][l])
    y = y + jnp.sum(r * k * P["c_r_k"][l], axis=-1, keepdims=True) * vh
    return y.reshape(Bsz, T, C_WIDTH).astype(h.dtype), s_fin, c_p[:, -1], v


def trunk_layer(l, x, pos, P, lb, past_k, past_v, s_hgrn, s_rwkv, shift_prev, v_first):
    h = rms_norm(x, P["norm_g"][l])
    proj = jnp.einsum("btd,dc->btc", h, P["w_in"][l])
    a_q, a_k, a_v, a_g, b_q, b_f, b_i, b_g, c_p, c_g, m_a, m_b, m_c = split_cols(proj, IN_SIZES)
    o_a, k_rows, v_rows = mixer_diff_attn(a_q, a_k, a_v, pos, past_k, past_v,
                                          P["a_qnorm_g"][l], P["a_knorm_g"][l],
                                          P["a_lambda"][l], P["a_subln_g"][l], l)
    o_b, s_hgrn_new = mixer_hgrn2(b_q, b_f, b_i, s_hgrn, lb[l], P["b_norm_g"][l])
    o_c, s_rwkv_new, shift_new, v_c = mixer_rwkv7(c_p, shift_prev, s_rwkv, h, v_first, l, P)

    def branch(o, gate, w):
        return jnp.einsum("btc,cd->btd", o * jax.nn.silu(gate), w)

    merged = (jax.nn.sigmoid(m_a) * branch(o_a, a_g, P["w_out_a"][l])
              + jax.nn.sigmoid(m_b) * branch(o_b, b_g, P["w_out_b"][l])
              + jax.nn.sigmoid(m_c) * branch(o_c, c_g, P["w_out_c"][l]))
    y = x + jnp.einsum("btd,de->bte", merged, P["w_o"][l])
    return y, (k_rows, v_rows, s_hgrn_new, s_rwkv_new, shift_new), v_c


def run_trunk(x, pos, P, lb, past):
    Bsz = x.shape[0]
    outs = ([], [], [], [], [])
    v_first = None
    for l in range(DEPTH):
        if past is None:
            pk, pv = None, None
            s_h = jnp.zeros((Bsz, B_HEADS, B_DK, B_DV), jnp.float32)
            s_r = jnp.zeros((Bsz, C_HEADS, C_DH, C_DH), jnp.float32)
            s_sh = jnp.zeros((Bsz, C_SHIFT_W), x.dtype)
        else:
            pk, pv, s_h, s_r, s_sh = (t[l] for t in past)
        x, entries, v_c = trunk_layer(l, x, pos, P, lb, pk, pv, s_h, s_r, s_sh, v_first)
        if l == 0:
            v_first = v_c
        for lst, e in zip(outs, entries):
            lst.append(e)
    return x, [jnp.stack(lst) for lst in outs]


def setup_inputs(seed: int = 0) -> dict:
    key = jax.random.key(seed)
    ks = iter(jax.random.split(key, 40))
    nrm = lambda shape, scale: jax.random.normal(next(ks), shape, jnp.float32) * scale
    gain = lambda shape: 1.0 + 0.05 * jax.random.normal(next(ks), shape, jnp.float32)
    unif = lambda shape, lo, hi: jax.random.uniform(next(ks), shape, jnp.float32, lo, hi)
    L = DEPTH
    return {
        "x_prompt": nrm((BATCH, SEQ, D_MODEL), 1.0),
        "x_sample": nrm((DEC_BATCH, DEC_SEQ, D_MODEL), 1.0),
        "cache_attn_k": nrm((L, DEC_BATCH, PAST_LEN, A_HEADS, 2 * A_DQK), 1.0),
        "cache_attn_v": nrm((L, DEC_BATCH, PAST_LEN, A_HEADS, A_DV), 1.0),
        "state_hgrn": nrm((L, DEC_BATCH, B_HEADS, B_DK, B_DV), 0.5),
        "state_rwkv": nrm((L, DEC_BATCH, C_HEADS, C_DH, C_DH), 0.3),
        "state_rwkv_shift": nrm((L, DEC_BATCH, C_SHIFT_W), 1.0),
        "norm_g": gain((L, D_MODEL)),
        "w_in": nrm((L, D_MODEL, IN_COLS), D_MODEL ** -0.5),
        "a_qnorm_g": gain((L, A_DQK)),
        "a_knorm_g": gain((L, A_DQK)),
        "a_lambda": nrm((L, 4, A_DQK), 0.1),
        "a_subln_g": gain((L, A_DV)),
        "b_lower": nrm((L, B_QK_W), 1.0),
        "b_norm_g": gain((L, B_DV)),
        "c_shift_mu": unif((L, C_SHIFT_W), 0.0, 1.0),
        "c_w0": unif((L, C_WIDTH), -4.0, 1.0),
        "c_w2": nrm((L, C_DECAY_RANK, C_WIDTH), 0.1),
        "c_a0": nrm((L, C_WIDTH), 0.5),
        "c_a2": nrm((L, C_A_RANK, C_WIDTH), 0.1),
        "c_k_k": 0.85 + nrm((L, C_WIDTH), 0.05),
        "c_k_a": gain((L, C_WIDTH)),
        "c_r_k": nrm((L, C_HEADS, C_DH), 0.3),
        "c_ln_w": gain((L, C_WIDTH)),
        "c_ln_b": nrm((L, C_WIDTH), 0.02),
        "c_vres_w1": nrm((L - 1, D_MODEL, C_VRES_RANK), D_MODEL ** -0.5),
        "c_vres_w2": nrm((L - 1, C_VRES_RANK, C_WIDTH), 0.1),
        "c_v0": nrm((L - 1, C_WIDTH), 0.5),
        "w_out_a": nrm((L, A_WIDTH, D_MODEL), A_WIDTH ** -0.5),
        "w_out_b": nrm((L, B_WIDTH, D_MODEL), B_WIDTH ** -0.5),
        "w_out_c": nrm((L, C_WIDTH, D_MODEL), C_WIDTH ** -0.5),
        "w_o": nrm((L, D_MODEL, D_MODEL), D_MODEL ** -0.5),
    }


def reference(x_prompt, x_sample, cache_attn_k, cache_attn_v, state_hgrn, state_rwkv, state_rwkv_shift,
              norm_g, w_in, a_qnorm_g, a_knorm_g, a_lambda, a_subln_g, b_lower, b_norm_g,
              c_shift_mu, c_w0, c_w2, c_a0, c_a2, c_k_k, c_k_a, c_r_k, c_ln_w, c_ln_b,
              c_vres_w1, c_vres_w2, c_v0, w_out_a, w_out_b, w_out_c, w_o):
    P = {"norm_g": norm_g, "w_in": w_in, "a_qnorm_g": a_qnorm_g, "a_knorm_g": a_knorm_g,
         "a_lambda": a_lambda, "a_subln_g": a_subln_g, "b_norm_g": b_norm_g,
         "c_shift_mu": c_shift_mu, "c_w0": c_w0, "c_w2": c_w2, "c_a0": c_a0, "c_a2": c_a2,
         "c_k_k": c_k_k, "c_k_a": c_k_a, "c_r_k": c_r_k, "c_ln_w": c_ln_w, "c_ln_b": c_ln_b,
         "c_vres_w1": c_vres_w1, "c_vres_w2": c_vres_w2, "c_v0": c_v0,
         "w_out_a": w_out_a, "w_out_b": w_out_b, "w_out_c": w_out_c, "w_o": w_o}
    sm = jax.nn.softmax(b_lower.astype(jnp.float32), axis=0)
    lb = jnp.cumsum(sm, axis=0) - sm[0:1]
    pos_p = jnp.arange(x_prompt.shape[1], dtype=jnp.int32)
    past_len = cache_attn_k.shape[2]
    pos_s = past_len + jnp.arange(x_sample.shape[1], dtype=jnp.int32)
    y_prompt, (k_p, v_p, hg_p, rw_p, sh_p) = run_trunk(x_prompt, pos_p, P, lb, None)
    y_sample, (k_s, v_s, hg_s, rw_s, sh_s) = run_trunk(
        x_sample, pos_s, P, lb, (cache_attn_k, cache_attn_v, state_hgrn, state_rwkv, state_rwkv_shift))
    return (y_prompt, y_sample, k_p, v_p, hg_p, rw_p, sh_p, k_s, v_s, hg_s, rw_s, sh_s)
```

```python
import math
from contextlib import ExitStack

import numpy as np
import concourse.bass as bass
import concourse.mybir as mybir
from concourse.bass_utils import run_bass_kernel_spmd

F32 = mybir.dt.float32
BF16 = mybir.dt.bfloat16
AF = mybir.ActivationFunctionType
ALU = mybir.AluOpType
AX = mybir.AxisListType

D = 1024
NCOL = 15488
NCX = NCOL + 32
PAST = 1024
TS = 64
EPS = 1e-6
GN_EPS = 64e-5
ROPE_THETA = 500000.0
O_AQ, O_AK, O_AV, O_AG = 0, 1024, 2048, 3072
O_BQ, O_BF, O_BI, O_BG = 4096, 5120, 6144, 7168
O_CP = 8192
O_CG = 11392
O_MA, O_MB, O_MC = 12416, 13440, 14464
O_EXT = 15488
C_R, C_WLO, C_K, C_V, C_ALO = 0, 1024, 1088, 2112, 3136
CW = 3200


class Tk:
    def __init__(self, h, name, dram=False):
        self.h = h
        self.name = name
        self.lw = None
        self.rd = []
        self.ds = {}
        self.dram = dram
        self.tok = {}
        self.rtok = {}

    def __getitem__(self, k):
        return self.h[k]


class Ctx:
    def __init__(self, nc):
        self.nc = nc
        self.es = ExitStack()
        self.eng = {"pe": nc.tensor, "act": nc.scalar, "dve": nc.vector, "pool": nc.gpsimd, "sp": nc.sync}
        self.sem = {}
        self.cnt = {}
        self.waited = {}
        for k in self.eng:
            self.sem[k] = self.es.enter_context(nc.semaphore("es_" + k))
            self.cnt[k] = 0
            self.waited[k] = {}
        self.free_ds = {"hw": [], "sw": []}
        self.nds = 0
        self.uid = 0
        self.ninst = 0

    def get_ds(self, t, q):
        kind = "sw" if q == "pool" else "hw"
        if kind not in t.ds:
            t.ds[kind] = self.new_ds(kind)
        return t.ds[kind]

    def new_ds(self, kind):
        if self.free_ds[kind]:
            return self.free_ds[kind].pop()
        self.nds += 1
        h = self.es.enter_context(self.nc.semaphore("ds%d" % self.nds))
        return [h, 0, "ds%d" % self.nds]

    def sb(self, ph, shape, dt, name):
        self.uid += 1
        h = ph.enter_context(self.nc.sbuf_tensor("%s_%d" % (name, self.uid), list(shape), dt))
        t = Tk(h, name)
        ph.tiles.append(t)
        return t

    def ps(self, ph, shape, dt, name):
        self.uid += 1
        h = ph.enter_context(self.nc.psum_tensor("%s_%d" % (name, self.uid), list(shape), dt))
        t = Tk(h, name)
        ph.tiles.append(t)
        return t

    def phase(self):
        ph = ExitStack()
        ph.tiles = []
        return ph

    def end_phase(self, ph):
        self.barrier(ph.tiles)
        for t in ph.tiles:
            for kind, ds in t.ds.items():
                self.free_ds[kind].append(ds)
            t.ds = {}
        ph.close()

    def barrier(self, tiles):
        deps = []
        for k in self.eng:
            if k != "sp" and self.cnt[k] > 0:
                deps.append(("e", k, self.cnt[k]))
        for t in tiles:
            for ds in t.ds.values():
                if ds[1] > 0:
                    deps.append(("d", ds, ds[1]))
        for k in self.eng:
            for d in deps:
                self._wait(k, d)

    def _wait(self, ename, dep):
        if dep is None:
            return
        kind, obj, val = dep
        if kind == "e":
            if obj == ename and ename in ("pe", "sp"):
                return
            key = "e_" + obj
            semh = self.sem[obj]
        else:
            key = obj[2]
            semh = obj[0]
        w = self.waited[ename]
        if w.get(key, 0) >= val:
            return
        self.eng[ename].wait_ge(semh, val)
        w[key] = val
        self.ninst += 1

    def op(self, ename, fn, R=(), W=()):
        deps = []
        for t in R:
            deps.append(t.lw)
        for t in W:
            deps.append(t.lw)
            deps.extend(t.rd)
        for d in deps:
            self._wait(ename, d)
        ins = fn(self.eng[ename])
        self.cnt[ename] += 1
        ins.then_inc(self.sem[ename], 1)
        self.ninst += 1
        tok = ("e", ename, self.cnt[ename])
        for t in R:
            t.rd.append(tok)
        for t in W:
            t.lw = tok
            t.rd = []
        return ins

    def pe(self, fn, R=(), W=()):
        return self.op("pe", fn, R, W)

    def act(self, fn, R=(), W=()):
        return self.op("act", fn, R, W)

    def dve(self, fn, R=(), W=()):
        return self.op("dve", fn, R, W)

    def pool(self, fn, R=(), W=()):
        return self.op("pool", fn, R, W)

    def load(self, q, out_ap, in_ap, sbt, dr=None, slow=False):
        deps = [sbt.lw] + list(sbt.rd)
        for d in deps:
            self._wait(q, d)
        ds = self.get_ds(sbt, q)
        if slow:
            ins = self.eng[q].dma_start(out=out_ap, in_=in_ap, allow_slow_non_contiguous=True)
        else:
            ins = self.eng[q].dma_start(out=out_ap, in_=in_ap)
        ds[1] += 16
        ins.then_inc(ds[0], 16)
        self.ninst += 1
        sbt.lw = ("d", ds, ds[1])
        sbt.rd = []

    def store(self, q, out_ap, in_ap, sbt, dr=None):
        deps = [sbt.lw]
        for d in deps:
            self._wait(q, d)
        ds = self.get_ds(sbt, q)
        ins = self.eng[q].dma_start(out=out_ap, in_=in_ap)
        ds[1] += 16
        ins.then_inc(ds[0], 16)
        self.ninst += 1
        sbt.rd.append(("d", ds, ds[1]))


class Rot:
    def __init__(self, k, ph, n, shape, dt, name, psum=False):
        mk = k.ps if psum else k.sb
        self.tiles = [mk(ph, shape, dt, "%s%d" % (name, i)) for i in range(n)]
        self.i = 0

    def next(self):
        t = self.tiles[self.i % len(self.tiles)]
        self.i += 1
        return t


def rsqrt(k, out, in_, R, W, scale, bias):
    k.act(lambda e: e.activation(out=out, in_=in_, func=AF.Sqrt, scale=scale, bias=bias), R=R, W=W)
    k.dve(lambda e: e.reciprocal(out=out, in_=out), R=W, W=W)


def bc(ap, shape):
    return ap.to_broadcast(list(shape))


def make_consts():
    s = np.arange(128)[:, None]
    t = np.arange(128)[None, :]
    same = (s // 64) == (t // 64)
    c = {}
    c["ident"] = np.eye(128)
    ut64 = (same & (s <= t)).astype(np.float64)
    mid = 64 * (t // 64) + 31
    a_mid = (same & (s <= mid)).astype(np.float64)
    a_end = same.astype(np.float64)
    c["h_ut"] = ut64
    c["h_d1"] = ut64 - a_mid
    c["h_d2"] = a_end - ut64
    c["h_end"] = a_end
    c["h_mask"] = ut64
    c["r_ut"] = (s <= t).astype(np.float64)
    c["r_uts"] = (s < t).astype(np.float64)
    c["r_low"] = (s > t).astype(np.float64)
    c["r_one"] = np.ones((128, 128))
    names = ["ident", "h_ut", "h_d1", "h_d2", "h_end", "h_mask", "r_ut", "r_uts", "r_low", "r_one"]
    arr = np.concatenate([c[n] for n in names], axis=1).astype(np.float32)
    offs = {n: i * 128 for i, n in enumerate(names)}
    return arr, offs


def rope_tables(pos):
    half = 8
    inv_freq = (np.float32(ROPE_THETA) ** (-(np.arange(half, dtype=np.float32) * np.float32(2.0 / 16)))).astype(np.float32)
    ang = pos.astype(np.float32)[:, None] * inv_freq[None, :]
    return np.concatenate([np.cos(ang), np.sin(ang)], axis=1).astype(np.float32)


class Prog:
    pass


def build(T, nlayers=2, upto=99, debug=False):
    nc = bass.Bass("TRN2", target_bir_lowering=False)
    k = Ctx(nc)
    P = Prog()
    P.nc, P.k, P.T = nc, k, T
    Ttot = T + 2 * TS
    P.Ttot = Ttot
    carr, coff = make_consts()
    P.coff = coff

    def din(name, shape, dt=F32):
        return nc.dram_tensor(name, list(shape), dt, kind="ExternalInput").ap()

    def dout(name, shape, dt=F32):
        return Tk(nc.dram_tensor(name, list(shape), dt, kind="ExternalOutput").ap(), name, dram=True)

    def dscr(name, shape, dt=F32):
        kind = "ExternalOutput" if (debug and name in debug) else "Internal"
        return Tk(nc.dram_tensor(name, list(shape), dt, kind=kind).ap(), name, dram=True)

    I = {}
    I["x_p"] = din("x_p", [T, D])
    I["x_s"] = din("x_s", [2, TS, D])
    I["ck"] = din("ck", [2, 2, PAST, D])
    I["cv"] = din("cv", [2, 2, PAST, D])
    I["sth"] = din("sth", [2, 2, 8, 128, 128])
    I["str"] = din("str", [2, 2, 16, 64, 64])
    I["stsh"] = din("stsh", [2, 2, CW])
    I["norm_g"] = din("norm_g", [2, D])
    I["w_in"] = din("w_in", [2, D, NCOL])
    I["a_qnorm_g"] = din("a_qnorm_g", [2, 64])
    I["a_knorm_g"] = din("a_knorm_g", [2, 64])
    I["a_lambda"] = din("a_lambda", [2, 256])
    I["a_subln_g"] = din("a_subln_g", [2, 128])
    I["b_lower"] = din("b_lower", [2, 1024])
    I["b_norm_g"] = din("b_norm_g", [2, 128])
    I["c_shift_mu"] = din("c_shift_mu", [2, CW])
    I["c_w0"] = din("c_w0", [2, 1024])
    I["c_w2"] = din("c_w2", [2, 64, 1024])
    I["c_a0"] = din("c_a0", [2, 1024])
    I["c_a2"] = din("c_a2", [2, 64, 1024])
    I["c_k_k"] = din("c_k_k", [2, 1024])
    I["c_k_a"] = din("c_k_a", [2, 1024])
    I["c_r_k"] = din("c_r_k", [2, 1024])
    I["c_ln_w"] = din("c_ln_w", [2, 1024])
    I["c_ln_b"] = din("c_ln_b", [2, 1024])
    I["c_vres_w1"] = din("c_vres_w1", [1, D, 32])
    I["c_vres_w2"] = din("c_vres_w2", [1, 32, 1024])
    I["c_v0"] = din("c_v0", [1, 1024])
    I["w_out_a"] = din("w_out_a", [2, D, D])
    I["w_out_b"] = din("w_out_b", [2, D, D])
    I["w_out_c"] = din("w_out_c", [2, D, D])
    I["w_o"] = din("w_o", [2, D, D])
    I["consts"] = din("consts", list(carr.shape))
    I["rope_p"] = din("rope_p", [T, 16])
    I["rope_s"] = din("rope_s", [TS, 16])
    P.I = I

    O = {}
    O["y_p"] = dout("y_p", [T, D])
    O["y_s"] = dout("y_s", [2, TS, D])
    O["k_p"] = dout("k_p", [2, T, D])
    O["v_p"] = dout("v_p", [2, T, D])
    O["hg_p"] = dout("hg_p", [2, 8, 128, 128])
    O["rw_p"] = dout("rw_p", [2, 16, 64, 64])
    O["sh_p"] = dout("sh_p", [2, CW])
    O["k_s"] = dout("k_s", [2, 2, TS, D])
    O["v_s"] = dout("v_s", [2, 2, TS, D])
    O["hg_s"] = dout("hg_s", [2, 2, 8, 128, 128])
    O["rw_s"] = dout("rw_s", [2, 2, 16, 64, 64])
    O["sh_s"] = dout("sh_s", [2, 2, CW])
    P.O = O

    seqs = []
    seqs.append(dict(name="p", T=T, g0=0, n=128, past=0, b=None))
    seqs.append(dict(name="s0", T=TS, g0=T, n=TS, past=PAST, b=0))
    seqs.append(dict(name="s1", T=TS, g0=T + TS, n=TS, past=PAST, b=1))
    tiles = []
    for s in seqs:
        s["tiles"] = []
        for t0 in range(0, s["T"], s["n"]):
            tl = dict(seq=s, t0=t0, n=s["n"], g=s["g0"] + t0, idx=len(tiles))
            tiles.append(tl)
            s["tiles"].append(tl)
    P.seqs, P.tiles = seqs, tiles
    NTL = len(tiles)

    S = {}
    S["hT"] = dscr("hT", [NTL, 128, 8, 128], BF16)
    PJ = [(0, 4096, "pjA"), (4096, 8192, "pjB"), (8192, 12416, "pjC"), (12416, NCX, "pjM")]
    for lo, hi, nm in PJ:
        S[nm] = dscr(nm, [Ttot, hi - lo])

    def pj(r0, r1, c0, c1):
        for lo, hi, nm in PJ:
            if lo <= c0 and c1 <= hi:
                return S[nm][r0:r1, c0 - lo:c1 - lo]
        raise ValueError((c0, c1))
    P.pj = pj
    S["xmid"] = dscr("xmid", [Ttot, D])
    S["oa"] = dscr("oa", [Ttot, D])
    S["ob"] = dscr("ob", [Ttot, D])
    S["oc"] = dscr("oc", [Ttot, D])
    S["vf"] = dscr("vf", [Ttot, D])
    for s in seqs:
        TK = s["past"] + s["T"]
        s["TK"] = TK
        s["qT"] = dscr("qT_" + s["name"], [8, 128, s["T"]], BF16)
        s["kT"] = dscr("kT_" + s["name"], [8, 128, TK], BF16)
        s["vb"] = dscr("vb_" + s["name"], [TK, 8, 129], BF16)
    P.S = S

    def xsrc(l, tl):
        s = tl["seq"]
        if l == 0:
            if s["b"] is None:
                return I["x_p"][tl["t0"]:tl["t0"] + tl["n"], :], None
            return I["x_s"][s["b"], tl["t0"]:tl["t0"] + tl["n"], :], None
        return S["xmid"][tl["g"]:tl["g"] + tl["n"], :], S["xmid"]
    P.xsrc = xsrc

    gph = k.phase()
    P.gph = gph
    cst = k.sb(gph, [128, carr.shape[1]], F32, "cst")
    k.load("sp", cst[:], I["consts"][:, :], cst)
    identb = k.sb(gph, [128, 128], BF16, "identb")
    k.dve(lambda e: e.tensor_copy(out=identb[:], in_=cst[:, coff["ident"]:coff["ident"] + 128]), R=[cst], W=[identb])
    P.cst, P.identb = cst, identb

    def C(name):
        return cst[:, coff[name]:coff[name] + 128]
    P.C = C

    for l in range(nlayers):
        if upto >= 0:
            phase0(P, l)
        if upto >= 1:
            phase1(P, l)
        if upto >= 2:
            phase2(P, l)
        if upto >= 3:
            phase3(P, l)
        if upto >= 4:
            phase4(P, l)
        if upto >= 5:
            phase5(P, l)
        if upto >= 6:
            phase6(P, l)

    k.end_phase(gph)
    return P


def phase0(P, l):
    k, I, S = P.k, P.I, P.S
    ph = k.phase()
    xr = Rot(k, ph, 3, [128, D], F32, "p0x")
    jr = Rot(k, ph, 2, [128, D], BF16, "p0j")
    hr = Rot(k, ph, 2, [128, D], BF16, "p0h")
    sr = Rot(k, ph, 4, [128, 2], F32, "p0s")
    tr = Rot(k, ph, 2, [128, 8, 128], BF16, "p0t")
    pr = Rot(k, ph, 2, [128, 8, 128], BF16, "p0p", psum=True)
    for tl in P.tiles:
        n = tl["n"]
        src, dr = P.xsrc(l, tl)
        x = xr.next()
        k.load("sp", x[:n, :], src, x, dr)
        st = sr.next()
        j = jr.next()
        k.act(lambda e: e.activation(out=j[:n, :], in_=x[:n, :], func=AF.Square, accum_out=st[:n, 0:1]), R=[x], W=[j, st])
        rsqrt(k, st[:n, 1:2], st[:n, 0:1], [st], [st], 1.0 / D, EPS)
        h = hr.next()
        k.dve(lambda e: e.tensor_scalar(out=h[:n, :], in0=x[:n, :], scalar1=st[:n, 1:2], scalar2=None,
                                        op0=ALU.mult), R=[x, st], W=[h])
        pt = pr.next()
        for kk in range(8):
            k.pe(lambda e: e.transpose(out=pt[:, kk, :n], in_=h[:n, kk * 128:(kk + 1) * 128], identity=P.identb[:n, :n]),
                 R=[h, P.identb], W=[pt])
        ht = tr.next()
        k.act(lambda e: e.activation(out=ht[:, :, :n], in_=pt[:, :, :n], func=AF.Copy), R=[pt], W=[ht])
        k.store("pool", S["hT"][tl["idx"], :, :, :n], ht[:, :, :n], ht, S["hT"])
    k.end_phase(ph)


def phase1(P, l):
    k, I, S = P.k, P.I, P.S
    ph = k.phase()
    gcol = k.sb(ph, [128, 8], F32, "p1g")
    k.load("sp", gcol[:], I["norm_g"][l].rearrange("(k p) -> p k", p=128), gcol, slow=True)
    wf = Rot(k, ph, 2, [128, 8, 1024], F32, "p1wf")
    wb = Rot(k, ph, 2, [128, 8, 1024], BF16, "p1wb")
    hr = Rot(k, ph, 3, [128, 8, 128], BF16, "p1h")
    orr = Rot(k, ph, 3, [128, 1024], F32, "p1o")
    pr = Rot(k, ph, 2, [128, 1024], F32, "p1p", psum=True)
    groups = [(c0, 1024) for c0 in range(0, 11264, 1024)] + [(11264, 128)] + [(c0, 1024) for c0 in range(11392, NCOL, 1024)]
    if l == 1:
        groups.append((O_EXT, 32))
    ev = 0
    for (c0, cw) in groups:
        w32 = wf.next()
        if c0 == O_EXT:
            src = I["c_vres_w1"][0].rearrange("(k p) c -> p k c", p=128)
        else:
            src = I["w_in"][l][:, c0:c0 + cw].rearrange("(k p) c -> p k c", p=128)
        k.load("sp", w32[:, :, :cw], src, w32)
        w = wb.next()
        k.dve(lambda e: e.tensor_tensor(out=w[:, :, :cw], in0=w32[:, :, :cw],
                                        in1=bc(gcol[:, :].unsqueeze(2), [128, 8, cw]), op=ALU.mult),
              R=[w32, gcol], W=[w])
        for tl in P.tiles:
            n = tl["n"]
            h = hr.next()
            k.load("sp", h[:, :, :n], S["hT"][tl["idx"], :, :, :n], h, S["hT"])
            pt = pr.next()
            for n0 in range(0, cw, 512):
                nw = min(512, cw - n0)
                for kk in range(8):
                    k.pe(lambda e: e.matmul(pt[:n, n0:n0 + nw], lhsT=h[:, kk, :n], rhs=w[:, kk, n0:n0 + nw],
                                            start=(kk == 0), stop=(kk == 7)), R=[h, w], W=[pt])
            o = orr.next()
            if ev % 2 == 0:
                k.act(lambda e: e.activation(out=o[:n, :cw], in_=pt[:n, :cw], func=AF.Copy), R=[pt], W=[o])
            else:
                k.dve(lambda e: e.tensor_copy(out=o[:n, :cw], in_=pt[:n, :cw]), R=[pt], W=[o])
            ev += 1
            k.store("pool", P.pj(tl["g"], tl["g"] + n, c0, c0 + cw), o[:n, :cw], o)
    k.end_phase(ph)


def bcast_row(k, ph, ap1d, width, name, q="sp"):
    t = k.sb(ph, [128, width], F32, name)
    k.load(q, t[:], ap1d.partition_broadcast(128), t)
    return t


def phase2(P, l):
    k, I, S, O = P.k, P.I, P.S, P.O
    ph = k.phase()
    gq = bcast_row(k, ph, I["a_qnorm_g"][l], 64, "p2gq")
    gk = bcast_row(k, ph, I["a_knorm_g"][l], 64, "p2gk")
    xr = Rot(k, ph, 4, [128, D], F32, "p2x")
    tmp = Rot(k, ph, 2, [128, D], F32, "p2tmp")
    xn = Rot(k, ph, 3, [128, D], F32, "p2xn")
    ssr = Rot(k, ph, 4, [128, 16], F32, "p2ss")
    csr = Rot(k, ph, 2, [128, 16], F32, "p2cs")
    rtr = Rot(k, ph, 2, [128, 4, 16, 8], F32, "p2rt")
    xbr = Rot(k, ph, 3, [128, D], BF16, "p2xb")
    vbr = Rot(k, ph, 2, [128, 8, 129], BF16, "p2vb")
    tTr = Rot(k, ph, 3, [128, 8, 128], BF16, "p2tT")
    ptr = Rot(k, ph, 3, [128, 8, 128], BF16, "p2pt", psum=True)
    for vt in vbr.tiles:
        k.dve(lambda e: e.memset(vt[:, :, 128:129], 1.0), W=[vt])

    def transpose_store(xb, n, dst_ap):
        pt = ptr.next()
        for h in range(8):
            k.pe(lambda e: e.transpose(out=pt[:, h, :n], in_=xb[:n, h * 128:(h + 1) * 128], identity=P.identb[:n, :n]),
                 R=[xb, P.identb], W=[pt])
        tT = tTr.next()
        k.act(lambda e: e.activation(out=tT[:, :, :n], in_=pt[:, :, :n], func=AF.Copy), R=[pt], W=[tT])
        k.store("pool", dst_ap, tT[:, :, :n], tT)

    def v_store(v, n, dst_rows):
        vb = vbr.next()
        k.act(lambda e: e.activation(out=vb[:n, :, 0:128], in_=v[:n, :].rearrange("p (h d) -> p h d", h=8), func=AF.Copy),
              R=[v], W=[vb])
        k.store("pool", dst_rows, vb[:n, :, :], vb)

    def normrope(x, n, g, cs):
        t = tmp.next()
        k.act(lambda e: e.activation(out=t[:n, :], in_=x[:n, :], func=AF.Square), R=[x], W=[t])
        ss = ssr.next()
        k.dve(lambda e: e.tensor_reduce(out=ss[:n, :], in_=t[:n, :].rearrange("p (s d) -> p s d", s=16), axis=AX.X, op=ALU.add),
              R=[t], W=[ss])
        rsqrt(k, ss[:n, :], ss[:n, :], [ss], [ss], 1.0 / 64, EPS)
        y = xn.next()
        y3 = y[:n, :].rearrange("p (s d) -> p s d", s=16)
        x3 = x[:n, :].rearrange("p (s d) -> p s d", s=16)
        k.dve(lambda e: e.tensor_tensor(out=y3, in0=x3, in1=bc(ss[:n, :].unsqueeze(2), [n, 16, 64]), op=ALU.mult),
              R=[x, ss], W=[y])
        k.pool(lambda e: e.tensor_tensor(out=y3, in0=y3, in1=bc(g[:n, :].unsqueeze(1), [n, 16, 64]), op=ALU.mult),
               R=[y, g], W=[y])
        rt = rtr.next()
        cosb = bc(cs[:n, 0:8].unsqueeze(1), [n, 16, 8])
        sinb = bc(cs[:n, 8:16].unsqueeze(1), [n, 16, 8])
        x1 = y3[:, :, 0:8]
        x2 = y3[:, :, 8:16]
        k.dve(lambda e: e.tensor_tensor(out=rt[:n, 0], in0=x1, in1=cosb, op=ALU.mult), R=[y, cs], W=[rt])
        k.dve(lambda e: e.tensor_tensor(out=rt[:n, 1], in0=x2, in1=sinb, op=ALU.mult), R=[y, cs], W=[rt])
        k.dve(lambda e: e.tensor_tensor(out=rt[:n, 2], in0=x2, in1=cosb, op=ALU.mult), R=[y, cs], W=[rt])
        k.dve(lambda e: e.tensor_tensor(out=rt[:n, 3], in0=x1, in1=sinb, op=ALU.mult), R=[y, cs], W=[rt])
        k.dve(lambda e: e.tensor_tensor(out=x1, in0=rt[:n, 0], in1=rt[:n, 1], op=ALU.subtract), R=[rt], W=[y])
        k.dve(lambda e: e.tensor_tensor(out=x2, in0=rt[:n, 2], in1=rt[:n, 3], op=ALU.add), R=[rt], W=[y])
        return y

    for s in P.seqs:
        b = s["b"]
        for j in range(s["past"] // 128):
            x = xr.next()
            k.load("sp", x[:, :], I["ck"][l, b, j * 128:(j + 1) * 128, :], x)
            xb = xbr.next()
            k.act(lambda e: e.activation(out=xb[:, :], in_=x[:, :], func=AF.Copy), R=[x], W=[xb])
            transpose_store(xb, 128, s["kT"][:, :, j * 128:(j + 1) * 128].rearrange("h p t -> p h t"))
            v = xr.next()
            k.load("sp", v[:, :], I["cv"][l, b, j * 128:(j + 1) * 128, :], v)
            v_store(v, 128, s["vb"][j * 128:(j + 1) * 128, :, :])
        for tl in s["tiles"]:
            n, t0, g = tl["n"], tl["t0"], tl["g"]
            cs = csr.next()
            rsrc = I["rope_p"][t0:t0 + n, :] if b is None else I["rope_s"][t0:t0 + n, :]
            k.load("sp", cs[:n, :], rsrc, cs)
            x = xr.next()
            k.load("sp", x[:n, :], P.pj(g, g + n, O_AQ, O_AQ + D), x)
            y = normrope(x, n, gq, cs)
            xb = xbr.next()
            k.act(lambda e: e.activation(out=xb[:n, :], in_=y[:n, :], func=AF.Copy), R=[y], W=[xb])
            transpose_store(xb, n, s["qT"][:, :, t0:t0 + n].rearrange("h p t -> p h t"))
            x = xr.next()
            k.load("sp", x[:n, :], P.pj(g, g + n, O_AK, O_AK + D), x)
            y = normrope(x, n, gk, cs)
            kdst = O["k_p"][l, t0:t0 + n, :] if b is None else O["k_s"][l, b, t0:t0 + n, :]
            k.store("pool", kdst, y[:n, :], y)
            xb = xbr.next()
            k.act(lambda e: e.activation(out=xb[:n, :], in_=y[:n, :], func=AF.Copy), R=[y], W=[xb])
            p0 = s["past"]
            transpose_store(xb, n, s["kT"][:, :, p0 + t0:p0 + t0 + n].rearrange("h p t -> p h t"))
            v = xr.next()
            k.load("sp", v[:n, :], P.pj(g, g + n, O_AV, O_AV + D), v)
            vdst = O["v_p"][l, t0:t0 + n, :] if b is None else O["v_s"][l, b, t0:t0 + n, :]
            k.store("pool", vdst, v[:n, :], v)
            v_store(v, n, s["vb"][p0 + t0:p0 + t0 + n, :, :])
    k.end_phase(ph)


def phase3(P, l):
    k, I, S, O = P.k, P.I, P.S, P.O
    ph = k.phase()
    lam_init = 0.8 - 0.6 * math.exp(-0.3 * l)
    lamt = bcast_row(k, ph, I["a_lambda"][l], 256, "p3lam")
    lw = k.sb(ph, [128, 2, 64], F32, "p3lw")
    l4 = lamt[:, :].rearrange("p (a b d) -> p a b d", a=2, b=2)
    k.dve(lambda e: e.tensor_tensor(out=lw[:, :, :], in0=l4[:, :, 0, :], in1=l4[:, :, 1, :], op=ALU.mult), R=[lamt], W=[lw])
    lc = k.sb(ph, [128, 4], F32, "p3lc")
    k.dve(lambda e: e.tensor_reduce(out=lc[:, 0:2], in_=lw[:, :, :], axis=AX.X, op=ALU.add), R=[lw], W=[lc])
    k.act(lambda e: e.activation(out=lc[:, 0:2], in_=lc[:, 0:2], func=AF.Exp), R=[lc], W=[lc])
    k.dve(lambda e: e.tensor_tensor(out=lc[:, 2:3], in0=lc[:, 0:1], in1=lc[:, 1:2], op=ALU.subtract), R=[lc], W=[lc])
    k.dve(lambda e: e.tensor_scalar(out=lc[:, 3:4], in0=lc[:, 2:3], scalar1=lam_init, scalar2=None, op0=ALU.add), R=[lc], W=[lc])
    gs = bcast_row(k, ph, I["a_subln_g"][l], 128, "p3gs")
    k.dve(lambda e: e.tensor_scalar(out=gs[:, :], in0=gs[:, :], scalar1=1.0 - lam_init, scalar2=None, op0=ALU.mult), R=[gs], W=[gs])

    TKmax = max(s["TK"] for s in P.seqs)
    ntkmax = (TKmax + 127) // 128
    ktr = Rot(k, ph, 2, [128, TKmax], BF16, "p3kt")
    vtr = Rot(k, ph, 2, [128, ntkmax, 129], BF16, "p3vt")
    qtr = Rot(k, ph, 2, [128, 512], BF16, "p3qt")
    psr = Rot(k, ph, 2, [128, 2, 512], F32, "p3ps", psum=True)
    acc = k.ps(ph, [128, 8, 256], F32, "p3acc")
    ptr = Rot(k, ph, 3, [128, 2, 512], BF16, "p3pt")
    accr = Rot(k, ph, 2, [128, 8, 129], F32, "p3accs")
    rrr = Rot(k, ph, 4, [128, 8], F32, "p3rr")
    tr_ = Rot(k, ph, 2, [128, 128], F32, "p3t")
    orr = Rot(k, ph, 2, [128, 128], F32, "p3o")
    ofr = Rot(k, ph, 3, [128, 128], F32, "p3of")

    for s in P.seqs:
        TK, Tq = s["TK"], s["T"]
        ntk = (TK + 127) // 128
        prompt = s["b"] is None
        qw = min(512, Tq)
        for h in range(8):
            kt = ktr.next()
            k.load("sp", kt[:, :TK], s["kT"][h, :, :], kt)
            vt = vtr.next()
            nfull = TK // 128
            k.load("sp", vt[:, :nfull, :], s["vb"][0:nfull * 128, h, :].rearrange("(j p) d -> p j d", p=128), vt)
            if TK % 128:
                rem = TK % 128
                k.load("sp", vt[:rem, nfull, :], s["vb"][nfull * 128:TK, h, :], vt)
            for q0 in range(0, Tq, qw):
                qt = qtr.next()
                k.load("sp", qt[:, :qw], s["qT"][h, :, q0:q0 + qw], qt)
                nqt = (qw + 127) // 128
                jq0 = q0 // 128
                jlast = (jq0 + nqt - 1) if prompt else (ntk - 1)
                def s_mm(j):
                    nk = min(128, TK - j * 128)
                    ps = psr.next()
                    for m in range(2):
                        k.pe(lambda e: e.matmul(ps[:nk, m, :qw], lhsT=kt[m * 64:(m + 1) * 64, j * 128:j * 128 + nk],
                                                rhs=qt[m * 64:(m + 1) * 64, :qw], start=True, stop=True),
                             R=[kt, qt], W=[ps])
                    return ps

                ps_next = s_mm(0)
                for j in range(jlast + 1):
                    nk = min(128, TK - j * 128)
                    ps = ps_next
                    if j < jlast:
                        ps_next = s_mm(j + 1)
                    pt = ptr.next()
                    k.act(lambda e: e.activation(out=pt[:nk, :, :qw], in_=ps[:nk, :, :qw], func=AF.Exp, scale=0.125),
                          R=[ps], W=[pt])
                    if prompt and j >= jq0:
                        i = j - jq0
                        k.pool(lambda e: e.memset(pt[64:128, :, i * 128:i * 128 + 64], 0.0), W=[pt])
                    for m in range(2):
                        for i in range(nqt):
                            nq = min(128, qw - i * 128)
                            last = (jq0 + i) if prompt else (ntk - 1)
                            if j > last:
                                continue
                            k.pe(lambda e: e.matmul(acc[:nq, m * 4 + i, 0:129], lhsT=pt[:nk, m, i * 128:i * 128 + nq],
                                                    rhs=vt[:nk, j, :], start=(j == 0 and i % 2 == 0), stop=(j == last),
                                                    skip_group_check=True),
                                 R=[pt, vt], W=[acc])
                nqmax = min(128, qw)
                accs = accr.next()
                for m in range(2):
                    k.dve(lambda e: e.tensor_copy(out=accs[:nqmax, m * 4:m * 4 + nqt, :], in_=acc[:nqmax, m * 4:m * 4 + nqt, 0:129]),
                          R=[acc], W=[accs])
                for i in range(nqt):
                    nq = min(128, qw - i * 128)
                    rr = rrr.next()
                    k.dve(lambda e: e.reciprocal(out=rr[:nq, 0:1], in_=accs[:nq, i, 128:129]), R=[accs], W=[rr])
                    k.dve(lambda e: e.reciprocal(out=rr[:nq, 1:2], in_=accs[:nq, 4 + i, 128:129]), R=[accs], W=[rr])
                    k.dve(lambda e: e.tensor_tensor(out=rr[:nq, 2:3], in0=rr[:nq, 1:2], in1=lc[:nq, 3:4], op=ALU.mult),
                          R=[rr, lc], W=[rr])
                    t = tr_.next()
                    k.dve(lambda e: e.tensor_scalar(out=t[:nq, :], in0=accs[:nq, 4 + i, 0:128], scalar1=rr[:nq, 2:3],
                                                    scalar2=None, op0=ALU.mult), R=[accs, rr], W=[t])
                    o = orr.next()
                    k.dve(lambda e: e.scalar_tensor_tensor(out=o[:nq, :], in0=accs[:nq, i, 0:128], scalar=rr[:nq, 0:1],
                                                           in1=t[:nq, :], op0=ALU.mult, op1=ALU.subtract),
                          R=[accs, rr, t], W=[o])
                    k.pool(lambda e: e.tensor_tensor(out=t[:nq, :], in0=o[:nq, :], in1=o[:nq, :], op=ALU.mult), R=[o], W=[t])
                    k.dve(lambda e: e.tensor_reduce(out=rr[:nq, 3:4], in_=t[:nq, :], axis=AX.X, op=ALU.add), R=[t], W=[rr])
                    k.act(lambda e: e.activation(out=rr[:nq, 4:5], in_=rr[:nq, 3:4], func=AF.Ln, scale=1.0 / 128, bias=EPS),
                          R=[rr], W=[rr])
                    k.act(lambda e: e.activation(out=rr[:nq, 5:6], in_=rr[:nq, 4:5], func=AF.Exp, scale=-0.5), R=[rr], W=[rr])
                    of = ofr.next()
                    k.dve(lambda e: e.scalar_tensor_tensor(out=of[:nq, :], in0=o[:nq, :], scalar=rr[:nq, 5:6],
                                                           in1=gs[:nq, :], op0=ALU.mult, op1=ALU.mult),
                          R=[o, rr, gs], W=[of])
                    g = s["g0"] + q0 + i * 128
                    k.store("pool", S["oa"][g:g + nq, h * 128:(h + 1) * 128], of[:nq, :], of)
    k.end_phase(ph)


def phase4(P, l):
    k, I, S, O, C = P.k, P.I, P.S, P.O, P.C
    ph = k.phase()
    lbr = k.sb(ph, [128, D], F32, "p4lb")
    oml = k.sb(ph, [128, D], F32, "p4oml")
    if l == 0:
        k.dve(lambda e: e.memset(lbr[:, :], 0.0), W=[lbr])
        k.dve(lambda e: e.memset(oml[:, :], 1.0), W=[oml])
    else:
        k.load("sp", lbr[:, :], I["b_lower"][1].partition_broadcast(128), lbr)
        k.load("sp", oml[:, :], I["b_lower"][0].partition_broadcast(128), oml)
        k.dve(lambda e: e.tensor_tensor(out=lbr[:, :], in0=lbr[:, :], in1=oml[:, :], op=ALU.subtract), R=[lbr, oml], W=[lbr])
        k.act(lambda e: e.activation(out=lbr[:, :], in_=lbr[:, :], func=AF.Sigmoid), R=[lbr], W=[lbr])
        k.dve(lambda e: e.tensor_scalar(out=oml[:, :], in0=lbr[:, :], scalar1=-1.0, scalar2=1.0, op0=ALU.mult, op1=ALU.add),
              R=[lbr], W=[oml])
    gn = bcast_row(k, ph, I["b_norm_g"][l], 128, "p4gn")
    ldr = Rot(k, ph, 6, [128, D], F32, "p4ld")
    f32r = Rot(k, ph, 8, [128, D], F32, "p4f")
    er = Rot(k, ph, 3, [128, D], F32, "p4e")
    b16r = Rot(k, ph, 10, [128, D], BF16, "p4b")
    tTr = Rot(k, ph, 4, [128, 8, 128], BF16, "p4tT")
    qpr = Rot(k, ph, 2, [128, 8, 2, 128], BF16, "p4qp")
    for t in qpr.tiles:
        k.dve(lambda e: e.memset(t[:, :, :, :], 0.0), W=[t])
    scmr = Rot(k, ph, 2, [128, 8, 128], BF16, "p4scm")
    dcr = Rot(k, ph, 2, [128, 8, 2], F32, "p4dc")
    ssr = Rot(k, ph, 2, [128, 8], F32, "p4ss")
    St = k.sb(ph, [128, 8, 128], F32, "p4S")
    Sb = k.sb(ph, [128, 8, 128], BF16, "p4Sb")
    pc = k.ps(ph, [128, D], F32, "p4pc")
    ptp = k.ps(ph, [128, 8, 128], BF16, "p4pt")
    psc = k.ps(ph, [128, 4, 128], F32, "p4psc")
    po = k.ps(ph, [128, 8, 128], F32, "p4po")
    pS = k.ps(ph, [128, 4, 128], F32, "p4pS")
    pd = k.ps(ph, [128, 8, 2], F32, "p4pd")
    ioff = P.coff["ident"]

    def cum_mm(name, logf, n):
        for hf in range(2):
            k.pe(lambda e: e.matmul(pc[:n, hf * 512:(hf + 1) * 512], lhsT=C(name)[:n, :n], rhs=logf[:n, hf * 512:(hf + 1) * 512],
                                    start=True, stop=True), R=[P.cst, logf], W=[pc])

    def transp(src, n):
        for h in range(8):
            k.pe(lambda e: e.transpose(out=ptp[:, h, :n], in_=src[:n, h * 128:(h + 1) * 128], identity=P.identb[:n, :n]),
                 R=[src, P.identb], W=[ptp])

    for s in P.seqs:
        b = s["b"]
        if b is None:
            k.dve(lambda e: e.memset(St[:, :, :], 0.0), W=[St])
        else:
            k.load("sp", St[:, :, :], I["sth"][l, b].rearrange("h k v -> k h v"), St)
        k.act(lambda e: e.activation(out=Sb[:, :, :], in_=St[:, :, :], func=AF.Copy), R=[St], W=[Sb])
        for tl in s["tiles"]:
            n, g = tl["n"], tl["g"]
            nch = n // 64
            bq, bf_, bi = ldr.next(), ldr.next(), ldr.next()
            k.load("sp", bq[:n, :], P.pj(g, g + n, O_BQ, O_BQ + D), bq)
            k.load("sp", bf_[:n, :], P.pj(g, g + n, O_BF, O_BF + D), bf_)
            k.load("sp", bi[:n, :], P.pj(g, g + n, O_BI, O_BI + D), bi)
            sg, t1, f, kin, logf, q = (f32r.next() for _ in range(6))
            k.act(lambda e: e.activation(out=sg[:n, :], in_=bf_[:n, :], func=AF.Sigmoid), R=[bf_], W=[sg])
            k.dve(lambda e: e.tensor_tensor(out=t1[:n, :], in0=sg[:n, :], in1=oml[:n, :], op=ALU.mult), R=[sg, oml], W=[t1])
            k.pool(lambda e: e.tensor_tensor(out=f[:n, :], in0=t1[:n, :], in1=lbr[:n, :], op=ALU.add), R=[t1, lbr], W=[f])
            k.dve(lambda e: e.tensor_tensor(out=kin[:n, :], in0=oml[:n, :], in1=t1[:n, :], op=ALU.subtract), R=[t1, oml], W=[kin])
            k.act(lambda e: e.activation(out=logf[:n, :], in_=f[:n, :], func=AF.Ln), R=[f], W=[logf])
            k.act(lambda e: e.activation(out=q[:n, :], in_=bq[:n, :], func=AF.Silu), R=[bq], W=[q])
            qt_, qh, kh, kt_, ib = (b16r.next() for _ in range(5))
            k.pool(lambda e: e.tensor_copy(out=ib[:n, :], in_=bi[:n, :]), R=[bi], W=[ib])
            cum_mm("h_ut", logf, n)
            e1 = er.next()
            k.act(lambda e: e.activation(out=e1[:n, :], in_=pc[:n, :], func=AF.Exp), R=[pc], W=[e1])
            k.dve(lambda e: e.tensor_tensor(out=qt_[:n, :], in0=q[:n, :], in1=e1[:n, :], op=ALU.mult), R=[q, e1], W=[qt_])
            cum_mm("h_d1", logf, n)
            e2, e3 = er.next(), er.next()
            k.act(lambda e: e.activation(out=e2[:n, :], in_=pc[:n, :], func=AF.Exp), R=[pc], W=[e2])
            k.act(lambda e: e.activation(out=e3[:n, :], in_=pc[:n, :], func=AF.Exp, scale=-1.0), R=[pc], W=[e3])
            k.dve(lambda e: e.tensor_tensor(out=qh[:n, :], in0=q[:n, :], in1=e2[:n, :], op=ALU.mult), R=[q, e2], W=[qh])
            k.pool(lambda e: e.tensor_tensor(out=kh[:n, :], in0=kin[:n, :], in1=e3[:n, :], op=ALU.mult), R=[kin, e3], W=[kh])
            cum_mm("h_d2", logf, n)
            e4 = er.next()
            k.act(lambda e: e.activation(out=e4[:n, :], in_=pc[:n, :], func=AF.Exp), R=[pc], W=[e4])
            k.dve(lambda e: e.tensor_tensor(out=kt_[:n, :], in0=kin[:n, :], in1=e4[:n, :], op=ALU.mult), R=[kin, e4], W=[kt_])
            cum_mm("h_end", logf, n)
            e5 = er.next()
            k.act(lambda e: e.activation(out=e5[:n, :], in_=pc[:n, :], func=AF.Exp), R=[pc], W=[e5])
            for h in range(8):
                k.pe(lambda e: e.matmul(pd[:, h, :nch], lhsT=e5[:n, h * 128:(h + 1) * 128],
                                        rhs=P.cst[:n, ioff:ioff + 64 * nch:64], start=True, stop=True),
                     R=[e5, P.cst], W=[pd])
            dC = dcr.next()
            k.dve(lambda e: e.tensor_copy(out=dC[:, :, :nch], in_=pd[:, :, :nch]), R=[pd], W=[dC])
            transp(qh, n)
            qhT = tTr.next()
            k.act(lambda e: e.activation(out=qhT[:, :, :n], in_=ptp[:, :, :n], func=AF.Copy), R=[ptp], W=[qhT])
            transp(kh, n)
            khT = tTr.next()
            k.dve(lambda e: e.tensor_copy(out=khT[:, :, :n], in_=ptp[:, :, :n]), R=[ptp], W=[khT])
            transp(qt_, n)
            qp = qpr.next()
            k.act(lambda e: e.activation(out=qp[:, :, 0, 0:64], in_=ptp[:, :, 0:64], func=AF.Copy), R=[ptp], W=[qp])
            if nch == 2:
                k.dve(lambda e: e.tensor_copy(out=qp[:, :, 1, 64:128], in_=ptp[:, :, 64:128]), R=[ptp], W=[qp])
            scm = scmr.next()
            k.dve(lambda e: e.memset(po[:, :, :], 0.0), W=[po])
            for hg in range(2):
                for hh in range(4):
                    h = hg * 4 + hh
                    k.pe(lambda e: e.matmul(psc[:n, hh, :n], lhsT=khT[:, h, :n], rhs=qhT[:, h, :n], start=True, stop=True),
                         R=[khT, qhT], W=[psc])
                k.dve(lambda e: e.tensor_tensor(out=scm[:n, hg * 4:hg * 4 + 4, :n], in0=psc[:n, :, :n],
                                                in1=bc(C("h_mask")[:n, :n].unsqueeze(1), [n, 4, n]), op=ALU.mult),
                      R=[psc, P.cst], W=[scm])
            for h in range(8):
                hs = slice(h * 128, (h + 1) * 128)
                k.pe(lambda e: e.matmul(po[:n, h, :], lhsT=scm[:n, h, :n], rhs=ib[:n, hs], start=False, stop=False,
                                        skip_group_check=True), R=[scm, ib], W=[po])
                k.pe(lambda e: e.matmul(po[:n, h, :], lhsT=qp[:, h, 0, :n], rhs=Sb[:, h, :], start=False, stop=(nch == 1),
                                        skip_group_check=True), R=[qp, Sb], W=[po])
            for c in range(nch):
                rows = slice(c * 64, (c + 1) * 64)
                for hg in range(2):
                    for hh in range(4):
                        h = hg * 4 + hh
                        hs = slice(h * 128, (h + 1) * 128)
                        k.pe(lambda e: e.matmul(pS[:, hh, :], lhsT=kt_[rows, hs], rhs=ib[rows, hs], start=True, stop=True),
                             R=[kt_, ib], W=[pS])
                    hsl = slice(hg * 4, hg * 4 + 4)
                    k.dve(lambda e: e.tensor_tensor(out=St[:, hsl, :], in0=St[:, hsl, :],
                                                    in1=bc(dC[:, hsl, c:c + 1], [128, 4, 128]), op=ALU.mult),
                          R=[St, dC], W=[St])
                    k.dve(lambda e: e.tensor_tensor(out=St[:, hsl, :], in0=St[:, hsl, :], in1=pS[:, :, :], op=ALU.add),
                          R=[St, pS], W=[St])
                    k.act(lambda e: e.activation(out=Sb[:, hsl, :], in_=St[:, hsl, :], func=AF.Copy), R=[St], W=[Sb])
                if c == 0 and nch == 2:
                    for h in range(8):
                        k.pe(lambda e: e.matmul(po[:n, h, :], lhsT=qp[:, h, 1, :n], rhs=Sb[:, h, :], start=False, stop=True,
                                                skip_group_check=True), R=[qp, Sb], W=[po])
            sq = f32r.next()
            k.act(lambda e: e.activation(out=sq[:n, :], in_=po[:n, :, :].rearrange("p h d -> p (h d)"), func=AF.Square),
                  R=[po], W=[sq])
            ss = ssr.next()
            k.dve(lambda e: e.tensor_reduce(out=ss[:n, :], in_=sq[:n, :].rearrange("p (h d) -> p h d", h=8), axis=AX.X, op=ALU.add),
                  R=[sq], W=[ss])
            rsqrt(k, ss[:n, :], ss[:n, :], [ss], [ss], 1.0 / 128, EPS)
            ob = f32r.next()
            ob3 = ob[:n, :].rearrange("p (h d) -> p h d", h=8)
            k.dve(lambda e: e.tensor_tensor(out=ob3, in0=po[:n, :, :], in1=bc(ss[:n, :].unsqueeze(2), [n, 8, 128]), op=ALU.mult),
                  R=[po, ss], W=[ob])
            k.pool(lambda e: e.tensor_tensor(out=ob3, in0=ob3, in1=bc(gn[:n, :].unsqueeze(1), [n, 8, 128]), op=ALU.mult),
                   R=[ob, gn], W=[ob])
            k.store("pool", S["ob"][g:g + n, :], ob[:n, :], ob)
        hdst = O["hg_p"][l] if b is None else O["hg_s"][l, b]
        k.store("pool", hdst.rearrange("h k v -> k h v"), St[:, :, :], St)
    k.end_phase(ph)


import os
P5CUT = int(os.environ.get("P5CUT", "0"))
P5NOFIN = int(os.environ.get("P5NOFIN", "0"))
P5STEPS = int(os.environ.get("P5STEPS", "-1"))
P5SUB = int(os.environ.get("P5SUB", "9"))
P5EXP = int(os.environ.get("P5EXP", "0"))


def phase5(P, l):
    k, I, S, O, C = P.k, P.I, P.S, P.O, P.C
    ph = k.phase()
    NEG_E = -math.exp(-0.5)
    mu = bcast_row(k, ph, I["c_shift_mu"][l], CW, "p5mu")
    w0 = bcast_row(k, ph, I["c_w0"][l], D, "p5w0")
    a0 = bcast_row(k, ph, I["c_a0"][l], D, "p5a0")
    kkr = bcast_row(k, ph, I["c_k_k"][l], D, "p5kk")
    kar = bcast_row(k, ph, I["c_k_a"][l], D, "p5ka")
    rkr = bcast_row(k, ph, I["c_r_k"][l], D, "p5rk")
    lnw = bcast_row(k, ph, I["c_ln_w"][l], D, "p5lnw")
    lnb = bcast_row(k, ph, I["c_ln_b"][l], D, "p5lnb")
    if l == 1:
        v0 = bcast_row(k, ph, I["c_v0"][0], D, "p5v0")
    tmpr = Rot(k, ph, 4, [128, D], F32, "p5tmp")
    stg = tmpr.tiles[0]
    w2b = k.sb(ph, [64, D], BF16, "p5w2")
    a2b = k.sb(ph, [64, D], BF16, "p5a2")
    k.load("sp", stg[:64, :], I["c_w2"][l], stg)
    k.dve(lambda e: e.tensor_copy(out=w2b[:, :], in_=stg[:64, :]), R=[stg], W=[w2b])
    k.load("sp", stg[:64, :], I["c_a2"][l], stg)
    k.dve(lambda e: e.tensor_copy(out=a2b[:, :], in_=stg[:64, :]), R=[stg], W=[a2b])
    if l == 1:
        v2b = k.sb(ph, [32, D], BF16, "p5v2w")
        k.load("sp", stg[:32, :], I["c_vres_w2"][0], stg)
        k.dve(lambda e: e.tensor_copy(out=v2b[:, :], in_=stg[:32, :]), R=[stg], W=[v2b])
    mk = k.sb(ph, [128, 4, 128], F32, "p5mk")
    for i_, nm in enumerate(["r_uts", "r_ut", "r_uts", "r_ut"]):
        k.dve(lambda e: e.tensor_copy(out=mk[:, i_, :], in_=C(nm)), R=[P.cst], W=[mk])

    mz = k.sb(ph, [128, 4, 128], F32, "p5mz")
    i4 = k.sb(ph, [128, 4, 128], BF16, "p5i4")
    for i_ in range(4):
        k.dve(lambda e: e.tensor_copy(out=mz[:, i_, :], in_=C("r_low")), R=[P.cst], W=[mz])
        k.dve(lambda e: e.tensor_copy(out=i4[:, i_, :], in_=P.identb[:, :]), R=[P.identb], W=[i4])
    cp = k.sb(ph, [128, CW], F32, "p5cp")
    cs = k.sb(ph, [128, CW], F32, "p5cs")
    hv = k.sb(ph, [128, 32], F32, "p5hv")
    vft = k.sb(ph, [128, D], F32, "p5vf")
    k2 = k.sb(ph, [128, D], F32, "p5k2")
    v2 = k.sb(ph, [128, D], F32, "p5v2")
    asig = k.sb(ph, [128, D], F32, "p5as")
    kk = k.sb(ph, [128, D], F32, "p5kkt")
    bs = k.sb(ph, [128, D], F32, "p5bs")
    ld = k.sb(ph, [128, D], F32, "p5ld")
    yt = k.sb(ph, [128, D], F32, "p5y")
    yn = k.sb(ph, [128, D], F32, "p5yn")
    s16 = Rot(k, ph, 6, [128, 16], F32, "p5s16")
    smb = k.sb(ph, [128, 3, 64], BF16, "p5smb")
    smT = k.sb(ph, [64, 3, 128], BF16, "p5smT")
    rt_, at_, bt_, kt_, bb_, kb_, vb_ = (k.sb(ph, [128, D], BF16, "p5b%d" % i_) for i_ in range(7))
    arT = k.sb(ph, [128, 8, 2, 128], BF16, "p5arT")
    bT = k.sb(ph, [128, 8, 128], BF16, "p5bT")
    kT = k.sb(ph, [128, 8, 128], BF16, "p5kT")
    AM = k.sb(ph, [128, 16, 4, 128], BF16, "p5AM")
    Qt = [k.sb(ph, [128, 4, 128], BF16, "p5Q%d" % i_) for i_ in range(4)]
    yzr = Rot(k, ph, 18, [128, 4, 128], BF16, "p5yz")
    R1 = k.sb(ph, [128, 16, 64], BF16, "p5R1")
    Ub = k.sb(ph, [128, 16, 64], BF16, "p5Ub")
    H = k.sb(ph, [128, 8, 64], F32, "p5H")
    Hb = k.sb(ph, [128, 8, 128], BF16, "p5Hb")
    k.dve(lambda e: e.memset(Hb[:, :, :], 0.0), W=[Hb])

    def refresh_hb():
        for e_ in range(2):
            rows = slice(e_ * 64, (e_ + 1) * 64)
            k.act(lambda e: e.activation(out=Hb[rows, :, e_ * 64:(e_ + 1) * 64], in_=H[rows, :, :], func=AF.Copy), R=[H], W=[Hb])
    wc = k.sb(ph, [128, 8], F32, "p5wc")
    B01 = k.ps(ph, [128, D], F32, "p5B01")
    Bt = k.ps(ph, [128, 8, 128], BF16, "p5Bt")
    gbank = Rot(k, ph, 5, [128, 512], F32, "p5g", psum=True)
    if os.environ.get("KDEBUG"):
        print("phase5 sbuf bytes remaining", P.nc.sbuf_bytes_remaining)

    def v4(bank):
        return bank[:, :].rearrange("p (a b) -> p a b", a=4)

    def v8(bank):
        return bank[:, :].rearrange("p (a b) -> p a b", a=8)

    def small_mm(col, wts, kdim, n, bias, out, func):
        for hf in range(2):
            k.pe(lambda e: e.matmul(B01[:n, hf * 512:(hf + 1) * 512], lhsT=smT[0:kdim, col, :n],
                                    rhs=wts[0:kdim, hf * 512:(hf + 1) * 512], start=True, stop=True),
                 R=[smT, wts], W=[B01])
        t = tmpr.next()
        k.dve(lambda e: e.tensor_tensor(out=t[:n, :], in0=B01[:n, :], in1=bias[:n, :], op=ALU.add), R=[B01, bias], W=[t])
        k.act(lambda e: e.activation(out=out[:n, :], in_=t[:n, :], func=func), R=[t], W=[out])

    def cum_exp(name, n, outs):
        for hf in range(2):
            k.pe(lambda e: e.matmul(B01[:n, hf * 512:(hf + 1) * 512], lhsT=C(name)[:n, :n], rhs=ld[:n, hf * 512:(hf + 1) * 512],
                                    start=True, stop=True), R=[P.cst, ld], W=[B01])
        for (t, sc) in outs:
            k.act(lambda e: e.activation(out=t[:n, :], in_=B01[:n, :], func=AF.Exp, scale=sc), R=[B01], W=[t])

    def transp8(src, n, dst_ap, dst_t, eng):
        for p in range(8):
            k.pe(lambda e: e.transpose(out=Bt[:, p, :n], in_=src[:n, p * 128:(p + 1) * 128], identity=P.identb[:n, :n]),
                 R=[src, P.identb], W=[Bt])
        if eng == "act":
            k.act(lambda e: e.activation(out=dst_ap, in_=Bt[:, :, :n], func=AF.Copy), R=[Bt], W=[dst_t])
        else:
            k.dve(lambda e: e.tensor_copy(out=dst_ap, in_=Bt[:, :, :n]), R=[Bt], W=[dst_t])

    for s in P.seqs:
        b = s["b"]
        if b is None:
            k.dve(lambda e: e.memset(H[:, :, :], 0.0), W=[H])
        else:
            Sld_t = tmpr.next()
            Sld = Sld_t[0:64, :].rearrange("p (a b) -> p a b", a=16)
            k.load("sp", Sld, I["str"][l, b].rearrange("h i j -> i h j"), Sld_t)
            for half in range(2):
                g_ = gbank.next()
                g4 = g_[:, :].rearrange("p (a b) -> p a b", a=4)
                for pp in range(4):
                    p = half * 4 + pp
                    k.pe(lambda e: e.transpose(out=g4[:, pp, 0:64], in_=Sld_t[0:64, 2 * p * 64:(2 * p + 2) * 64],
                                               identity=C("ident")[:64, :64]), R=[Sld_t, P.cst], W=[g_])
                k.dve(lambda e: e.tensor_copy(out=H[:, half * 4:half * 4 + 4, :], in_=g4[:, :, 0:64]), R=[g_], W=[H])
        refresh_hb()
        ntl = len(s["tiles"])
        for ti, tl in enumerate(s["tiles"]):
            n, g, t0 = tl["n"], tl["g"], tl["t0"]
            k.load("sp", cp[:n, :], P.pj(g, g + n, O_CP, O_CP + CW), cp)
            if t0 == 0:
                if b is None:
                    k.dve(lambda e: e.memset(cs[0:1, :], 0.0), W=[cs])
                else:
                    k.load("sp", cs[0:1, :], I["stsh"][l, b:b + 1, :], cs)
                k.load("sp", cs[1:n, :], P.pj(g, g + n - 1, O_CP, O_CP + CW), cs)
            else:
                k.load("sp", cs[:n, :], P.pj(g - 1, g + n - 1, O_CP, O_CP + CW), cs)
            if ti == ntl - 1:
                sdst = O["sh_p"][l:l + 1, :] if b is None else O["sh_s"][l, b:b + 1, :]
                k.store("pool", sdst, cp[n - 1:n, :], cp)
            k.pool(lambda e: e.tensor_tensor(out=cs[:n, :], in0=cs[:n, :], in1=cp[:n, :], op=ALU.subtract), R=[cs, cp], W=[cs])
            k.dve(lambda e: e.tensor_tensor(out=cs[:n, :], in0=cs[:n, :], in1=mu[:n, :], op=ALU.mult), R=[cs, mu], W=[cs])
            k.pool(lambda e: e.tensor_tensor(out=cs[:n, :], in0=cs[:n, :], in1=cp[:n, :], op=ALU.add), R=[cs, cp], W=[cs])
            r_ = cs[:n, C_R:C_R + D]
            kx = cs[:n, C_K:C_K + D]
            vx = cs[:n, C_V:C_V + D]
            if P5CUT and P5CUT <= 1:
                continue
            k.act(lambda e: e.activation(out=smb[:n, 0, :], in_=cs[:n, C_WLO:C_WLO + 64], func=AF.Tanh), R=[cs], W=[smb])
            k.act(lambda e: e.activation(out=smb[:n, 1, :], in_=cs[:n, C_ALO:C_ALO + 64], func=AF.Copy), R=[cs], W=[smb])
            if l == 1:
                k.load("sp", hv[:n, :], P.pj(g, g + n, O_EXT, O_EXT + 32), hv)
                k.act(lambda e: e.activation(out=smb[:n, 2, 0:32], in_=hv[:n, :], func=AF.Copy), R=[hv], W=[smb])
            for c_ in range(3 if l == 1 else 2):
                kd = 32 if c_ == 2 else 64
                k.pe(lambda e: e.transpose(out=Bt[0:kd, c_, :n], in_=smb[:n, c_, 0:kd], identity=P.identb[:n, :n]),
                     R=[smb, P.identb], W=[Bt])
            k.dve(lambda e: e.tensor_copy(out=smT[:, 0:2, :n], in_=Bt[0:64, 0:2, :n]), R=[Bt], W=[smT])
            if l == 1:
                k.dve(lambda e: e.tensor_copy(out=smT[0:32, 2, :n], in_=Bt[0:32, 2, :n]), R=[Bt], W=[smT])
            sgw = tmpr.next()
            small_mm(0, w2b, 64, n, w0, sgw, AF.Sigmoid)
            k.dve(lambda e: e.tensor_scalar(out=ld[:n, :], in0=sgw[:n, :], scalar1=NEG_E, scalar2=None, op0=ALU.mult),
                  R=[sgw], W=[ld])
            small_mm(1, a2b, 64, n, a0, asig, AF.Sigmoid)
            if l == 1:
                vmix = tmpr.next()
                small_mm(2, v2b, 32, n, v0, vmix, AF.Sigmoid)
                k.load("sp", vft[:n, :], S["vf"][g:g + n, :], vft)
                k.pool(lambda e: e.tensor_tensor(out=vft[:n, :], in0=vft[:n, :], in1=vx, op=ALU.subtract), R=[vft, cs], W=[vft])
                k.dve(lambda e: e.tensor_tensor(out=vft[:n, :], in0=vft[:n, :], in1=vmix[:n, :], op=ALU.mult), R=[vft, vmix], W=[vft])
                k.pool(lambda e: e.tensor_tensor(out=v2[:n, :], in0=vft[:n, :], in1=vx, op=ALU.add), R=[vft, cs], W=[v2])
            else:
                k.pool(lambda e: e.tensor_copy(out=v2[:n, :], in_=vx), R=[cs], W=[v2])
                k.store("pool", S["vf"][g:g + n, :], v2[:n, :], v2)
            if P5CUT and P5CUT <= 2:
                continue
            k.dve(lambda e: e.tensor_tensor(out=kk[:n, :], in0=kx, in1=kkr[:n, :], op=ALU.mult), R=[cs, kkr], W=[kk])
            t = tmpr.next()
            k.pool(lambda e: e.tensor_tensor(out=t[:n, :], in0=kk[:n, :], in1=kk[:n, :], op=ALU.mult), R=[kk], W=[t])
            sk = s16.next()
            k.dve(lambda e: e.tensor_reduce(out=sk[:n, :], in_=t[:n, :].rearrange("p (h d) -> p h d", h=16), axis=AX.X, op=ALU.add),
                  R=[t], W=[sk])
            k.act(lambda e: e.activation(out=sk[:n, :], in_=sk[:n, :], func=AF.Sqrt), R=[sk], W=[sk])
            k.dve(lambda e: e.tensor_scalar(out=sk[:n, :], in0=sk[:n, :], scalar1=1e-12, scalar2=None, op0=ALU.max), R=[sk], W=[sk])
            k.dve(lambda e: e.reciprocal(out=sk[:n, :], in_=sk[:n, :]), R=[sk], W=[sk])
            kk3 = kk[:n, :].rearrange("p (h d) -> p h d", h=16)
            k.dve(lambda e: e.tensor_tensor(out=kk3, in0=kk3, in1=bc(sk[:n, :].unsqueeze(2), [n, 16, 64]), op=ALU.mult),
                  R=[kk, sk], W=[kk])
            t = tmpr.next()
            k.dve(lambda e: e.scalar_tensor_tensor(out=t[:n, :], in0=asig[:n, :], scalar=-1.0, in1=kar[:n, :],
                                                   op0=ALU.add, op1=ALU.mult), R=[asig, kar], W=[t])
            k.pool(lambda e: e.tensor_tensor(out=t[:n, :], in0=t[:n, :], in1=kx, op=ALU.mult), R=[t, cs], W=[t])
            k.pool(lambda e: e.tensor_tensor(out=k2[:n, :], in0=t[:n, :], in1=kx, op=ALU.add), R=[t, cs], W=[k2])
            k.pool(lambda e: e.tensor_tensor(out=bs[:n, :], in0=kk[:n, :], in1=asig[:n, :], op=ALU.mult), R=[kk, asig], W=[bs])
            if P5CUT and P5CUT <= 3:
                continue
            t = tmpr.next()
            k.pool(lambda e: e.tensor_tensor(out=t[:n, :], in0=r_, in1=k2[:n, :], op=ALU.mult), R=[cs, k2], W=[t])
            k.dve(lambda e: e.tensor_tensor(out=t[:n, :], in0=t[:n, :], in1=rkr[:n, :], op=ALU.mult), R=[t, rkr], W=[t])
            s3 = s16.next()
            k.dve(lambda e: e.tensor_reduce(out=s3[:n, :], in_=t[:n, :].rearrange("p (h d) -> p h d", h=16), axis=AX.X, op=ALU.add),
                  R=[t], W=[s3])
            k.dve(lambda e: e.tensor_tensor(out=yn[:n, :].rearrange("p (h d) -> p h d", h=16),
                                            in0=v2[:n, :].rearrange("p (h d) -> p h d", h=16),
                                            in1=bc(s3[:n, :].unsqueeze(2), [n, 16, 64]), op=ALU.mult), R=[v2, s3], W=[yn])
            k.pool(lambda e: e.tensor_copy(out=vb_[:n, :], in_=v2[:n, :]), R=[v2], W=[vb_])
            ep, en = tmpr.next(), tmpr.next()
            cum_exp("r_ut", n, [(ep, 1.0), (en, -1.0)])
            k.dve(lambda e: e.tensor_tensor(out=rt_[:n, :], in0=r_, in1=ep[:n, :], op=ALU.mult), R=[cs, ep], W=[rt_])
            k.dve(lambda e: e.tensor_tensor(out=bt_[:n, :], in0=bs[:n, :], in1=en[:n, :], op=ALU.mult), R=[bs, en], W=[bt_])
            k.pool(lambda e: e.tensor_tensor(out=kt_[:n, :], in0=k2[:n, :], in1=en[:n, :], op=ALU.mult), R=[k2, en], W=[kt_])
            epa = tmpr.next()
            cum_exp("r_uts", n, [(epa, 1.0)])
            k.dve(lambda e: e.scalar_tensor_tensor(out=at_[:n, :], in0=kk[:n, :], scalar=-1.0, in1=epa[:n, :],
                                                   op0=ALU.mult, op1=ALU.mult), R=[kk, epa], W=[at_])
            eend = tmpr.next()
            cum_exp("r_low", n, [(eend, 1.0)])
            k.dve(lambda e: e.tensor_tensor(out=bb_[:n, :], in0=bs[:n, :], in1=eend[:n, :], op=ALU.mult), R=[bs, eend], W=[bb_])
            k.pool(lambda e: e.tensor_tensor(out=kb_[:n, :], in0=k2[:n, :], in1=eend[:n, :], op=ALU.mult), R=[k2, eend], W=[kb_])
            gw = gbank.next()
            for p in range(8):
                k.pe(lambda e: e.matmul(gw[:, p:p + 1], lhsT=ld[:n, p * 128:(p + 1) * 128], rhs=C("r_one")[:n, 0:1],
                                        start=True, stop=True), R=[ld, P.cst], W=[gw])
            k.act(lambda e: e.activation(out=wc[:, :], in_=gw[:, 0:8], func=AF.Exp), R=[gw], W=[wc])
            if P5CUT and P5CUT <= 4:
                continue
            transp8(at_, n, arT[:, :, 0, :n], arT, "act")
            transp8(rt_, n, arT[:, :, 1, :n], arT, "dve")
            transp8(bt_, n, bT[:, :, :n], bT, "act")
            transp8(kt_, n, kT[:, :, :n], kT, "dve")
            if P5CUT and P5CUT <= 5:
                continue
            Ycur, Zcur = [None] * 4, [None] * 4
            for hg in range(4):
                for hh in range(4):
                    hd = hg * 4 + hh
                    p, base = hd // 2, (hd % 2) * 64
                    bs_ = slice(base, base + 64)
                    gm = gbank.next()
                    m4 = v4(gm)
                    if n == 128:
                        k.pe(lambda e: e.matmul(m4[:n, 0:2, :n], lhsT=bT[bs_, p, :n], rhs=arT[bs_, p, :, :n], start=True, stop=True),
                             R=[bT, arT], W=[gm])
                        k.pe(lambda e: e.matmul(m4[:n, 2:4, :n], lhsT=kT[bs_, p, :n], rhs=arT[bs_, p, :, :n], start=True, stop=True),
                             R=[kT, arT], W=[gm])
                    else:
                        for w_ in range(2):
                            k.pe(lambda e: e.matmul(m4[:n, w_, :n], lhsT=bT[bs_, p, :n], rhs=arT[bs_, p, w_, :n], start=True, stop=True),
                                 R=[bT, arT], W=[gm])
                            k.pe(lambda e: e.matmul(m4[:n, 2 + w_, :n], lhsT=kT[bs_, p, :n], rhs=arT[bs_, p, w_, :n], start=True, stop=True),
                                 R=[kT, arT], W=[gm])
                    k.dve(lambda e: e.tensor_tensor(out=AM[:n, hd, :, :n], in0=m4[:n, :, :n], in1=mk[:n, :, :n], op=ALU.mult),
                          R=[gm, mk], W=[AM])
            for hg in range(4):
                hsl = slice(hg * 4, hg * 4 + 4)
                for hh in range(4):
                    hd = hg * 4 + hh
                    k.pe(lambda e: e.transpose(out=Bt[:n, hg * 4 + hh - (hg // 2) * 8, :n], in_=AM[:n, hd, 0, :n], identity=P.identb[:n, :n]),
                         R=[AM, P.identb], W=[Bt])
                if hg % 2 == 1:
                    for h2 in range(2):
                        hgg = hg - 1 + h2
                        Z = yzr.next()
                        k.act(lambda e: e.activation(out=Z[:n, :, :n], in_=Bt[:n, h2 * 4:h2 * 4 + 4, :n], func=AF.Copy), R=[Bt], W=[Z])
                        Zcur[hgg] = Z
                k.dve(lambda e: e.tensor_tensor(out=Qt[hg][:n, :, :n], in0=AM[:n, hsl, 0, :n], in1=i4[:n, :, :n], op=ALU.add),
                      R=[AM, i4], W=[Qt[hg]])
            nsteps = 6 if n == 128 else 5
            for step in range(1, nsteps + 1):
                gys, gzs = [None] * 4, [None] * 4
                for hg in range(4):
                    Y, Z = Ycur[hg], Zcur[hg]
                    gy = gbank.next() if step < nsteps else None
                    gzz = gbank.next()
                    for hh in range(4):
                        hd = hg * 4 + hh
                        ysrc = AM[:n, hd, 0, :n] if Y is None else Y[:n, hh, :n]
                        ytk = AM if Y is None else Y
                        if gy is not None:
                            k.pe(lambda e: e.matmul(v4(gy)[:n, hh, :n], lhsT=Z[:n, hh, :n], rhs=ysrc, start=True, stop=True),
                                 R=[Z, ytk], W=[gy])
                        k.pe(lambda e: e.matmul(v4(gzz)[:n, hh, :n], lhsT=ysrc, rhs=Z[:n, hh, :n], start=True, stop=True),
                             R=[Z, ytk], W=[gzz])
                    Zn = yzr.next()
                    k.act(lambda e: e.activation(out=Zn[:n, :, :n], in_=v4(gzz)[:n, :, :n], func=AF.Copy), R=[gzz], W=[Zn])
                    if gy is not None:
                        Yn = yzr.next()
                        k.act(lambda e: e.activation(out=Yn[:n, :, :n], in_=v4(gy)[:n, :, :n], func=AF.Copy), R=[gy], W=[Yn])
                    else:
                        Yn = None
                    Ycur[hg], Zcur[hg] = Yn, Zn
                for hg in range(4):
                    hsl = slice(hg * 4, hg * 4 + 4)
                    Zn = Zcur[hg]
                    gq = gbank.next()
                    for hh in range(4):
                        hd = hg * 4 + hh
                        k.pe(lambda e: e.matmul(v4(gq)[:n, hh, :n], lhsT=Zn[:n, hh, :n], rhs=Qt[hg][:n, hh, :n], start=True, stop=False),
                             R=[Zn, Qt[hg]], W=[gq])
                        k.pe(lambda e: e.matmul(v4(gq)[:n, hh, :n], lhsT=P.identb[:n, :n], rhs=Qt[hg][:n, hh, :n], start=False, stop=True),
                             R=[P.identb, Qt[hg]], W=[gq])
                    k.act(lambda e: e.activation(out=Qt[hg][:n, :, :n], in_=v4(gq)[:n, :, :n], func=AF.Copy), R=[gq], W=[Qt[hg]])
            if P5CUT and P5CUT <= 6:
                continue
            for half in range(2):
                g1 = gbank.next()
                for pp in range(4):
                    p = half * 4 + pp
                    k.pe(lambda e: e.matmul(v4(g1)[:n, pp, :], lhsT=arT[:, p, 0, :n], rhs=Hb[:, p, :], start=True, stop=False),
                         R=[arT, Hb], W=[g1])
                    for e_ in range(2):
                        hd = 2 * p + e_
                        k.pe(lambda e: e.matmul(v8(g1)[:n, 2 * pp + e_, :], lhsT=AM[:n, hd, 2, :n], rhs=vb_[:n, hd * 64:(hd + 1) * 64],
                                                start=False, stop=(e_ == 1)), R=[AM, vb_], W=[g1])
                k.act(lambda e: e.activation(out=R1[:n, half * 8:half * 8 + 8, :], in_=v8(g1)[:n, :, :], func=AF.Copy), R=[g1], W=[R1])
            if P5CUT == 65:
                continue
            for half in range(2):
                g2 = gbank.next()
                for h8 in range(8):
                    hd = half * 8 + h8
                    k.pe(lambda e: e.matmul(v8(g2)[:n, h8, :], lhsT=Qt[hd // 4][:n, hd % 4, :n], rhs=R1[:n, hd, :], start=True, stop=True),
                         R=[Qt[hd // 4], R1], W=[g2])
                k.act(lambda e: e.activation(out=Ub[:n, half * 8:half * 8 + 8, :], in_=v8(g2)[:n, :, :], func=AF.Copy), R=[g2], W=[Ub])
            if P5CUT and P5CUT <= 7:
                continue
            for half in range(2):
                g3 = gbank.next()
                for pp in range(4):
                    p = half * 4 + pp
                    k.pe(lambda e: e.matmul(v4(g3)[:n, pp, :], lhsT=arT[:, p, 1, :n], rhs=Hb[:, p, :], start=True, stop=False),
                         R=[arT, Hb], W=[g3])
                    for e_ in range(2):
                        hd = 2 * p + e_
                        k.pe(lambda e: e.matmul(v8(g3)[:n, 2 * pp + e_, :], lhsT=AM[:n, hd, 1, :n], rhs=Ub[:n, hd, :], start=False, stop=False),
                             R=[AM, Ub], W=[g3])
                        k.pe(lambda e: e.matmul(v8(g3)[:n, 2 * pp + e_, :], lhsT=AM[:n, hd, 3, :n], rhs=vb_[:n, hd * 64:(hd + 1) * 64],
                                                start=False, stop=(e_ == 1)), R=[AM, vb_], W=[g3])
                k.act(lambda e: e.activation(out=yt[:n, half * 512:(half + 1) * 512], in_=g3[:n, :], func=AF.Copy), R=[g3], W=[yt])
            if P5CUT and P5CUT <= 8:
                continue
            for half in range(2):
                g4_ = gbank.next()
                for pp in range(4):
                    p = half * 4 + pp
                    ps_ = slice(p * 128, (p + 1) * 128)
                    k.pe(lambda e: e.matmul(v4(g4_)[:, pp, :], lhsT=bb_[:n, ps_], rhs=Ub[:n, 2 * p:2 * p + 2, :].rearrange("p a b -> p (a b)"),
                                            start=True, stop=False), R=[bb_, Ub], W=[g4_])
                    k.pe(lambda e: e.matmul(v4(g4_)[:, pp, :], lhsT=kb_[:n, ps_], rhs=vb_[:n, ps_], start=False, stop=True),
                         R=[kb_, vb_], W=[g4_])
                hs_ = slice(half * 4, half * 4 + 4)
                hst = tmpr.next()
                hst4 = hst[:, 0:512].rearrange("p (a b) -> p a b", a=4)
                k.act(lambda e: e.activation(out=hst[:, 0:512], in_=g4_[:, :], func=AF.Copy), R=[g4_], W=[hst])
                for e_ in range(2):
                    rows = slice(e_ * 64, (e_ + 1) * 64)
                    k.dve(lambda e: e.tensor_tensor(out=H[rows, hs_, :], in0=H[rows, hs_, :],
                                                    in1=bc(wc[rows, hs_].unsqueeze(2), [64, 4, 64]), op=ALU.mult),
                          R=[H, wc], W=[H])
                    k.dve(lambda e: e.tensor_tensor(out=H[rows, hs_, :], in0=H[rows, hs_, :],
                                                    in1=hst4[rows, :, e_ * 64:(e_ + 1) * 64], op=ALU.add),
                          R=[H, hst], W=[H])
            refresh_hb()
            if P5CUT and P5CUT <= 9:
                continue
            y3 = yt[:n, :].rearrange("p (h d) -> p h d", h=16)
            s1 = s16.next()
            k.dve(lambda e: e.tensor_reduce(out=s1[:n, :], in_=y3, axis=AX.X, op=ALU.add), R=[yt], W=[s1])
            k.dve(lambda e: e.tensor_scalar(out=s1[:n, :], in0=s1[:n, :], scalar1=1.0 / 64, scalar2=None, op0=ALU.mult), R=[s1], W=[s1])
            k.dve(lambda e: e.tensor_tensor(out=y3, in0=y3, in1=bc(s1[:n, :].unsqueeze(2), [n, 16, 64]), op=ALU.subtract),
                  R=[yt, s1], W=[yt])
            t = tmpr.next()
            k.pool(lambda e: e.tensor_tensor(out=t[:n, :], in0=yt[:n, :], in1=yt[:n, :], op=ALU.mult), R=[yt], W=[t])
            s2 = s16.next()
            k.dve(lambda e: e.tensor_reduce(out=s2[:n, :], in_=t[:n, :].rearrange("p (h d) -> p h d", h=16), axis=AX.X, op=ALU.add),
                  R=[t], W=[s2])
            rsqrt(k, s2[:n, :], s2[:n, :], [s2], [s2], 1.0 / 64, GN_EPS)
            tn = tmpr.next()
            tn3 = tn[:n, :].rearrange("p (h d) -> p h d", h=16)
            k.dve(lambda e: e.tensor_tensor(out=tn3, in0=y3, in1=bc(s2[:n, :].unsqueeze(2), [n, 16, 64]), op=ALU.mult),
                  R=[yt, s2], W=[tn])
            k.pool(lambda e: e.tensor_tensor(out=tn[:n, :], in0=tn[:n, :], in1=lnw[:n, :], op=ALU.mult), R=[tn, lnw], W=[tn])
            k.dve(lambda e: e.tensor_tensor(out=tn[:n, :], in0=tn[:n, :], in1=lnb[:n, :], op=ALU.add), R=[tn, lnb], W=[tn])
            k.pool(lambda e: e.tensor_tensor(out=yn[:n, :], in0=yn[:n, :], in1=tn[:n, :], op=ALU.add), R=[yn, tn], W=[yn])
            k.store("pool", S["oc"][g:g + n, :], yn[:n, :], yn)
        rwo_t = tmpr.next()
        rwo = rwo_t[0:64, :].rearrange("p (a b) -> p a b", a=8)
        for half in range(0 if P5NOFIN else 2):
            g_ = gbank.next()
            g4 = v4(g_)
            for pp in range(4):
                p = half * 4 + pp
                k.pe(lambda e: e.transpose(out=g4[0:64, pp, :], in_=H[:, p, :], identity=C("ident")), R=[H, P.cst], W=[g_])
            k.dve(lambda e: e.tensor_copy(out=rwo[:, half * 4:half * 4 + 4, :], in_=g4[0:64, :, :]), R=[g_], W=[rwo_t])
        rdst = O["rw_p"][l] if b is None else O["rw_s"][l, b]
        k.store("pool", rdst.rearrange("(p e) i j -> i p e j", e=2), rwo.rearrange("i p (e j) -> i p e j", e=2), rwo_t)
    k.end_phase(ph)


def phase6(P, l):
    k, I, S, O = P.k, P.I, P.S, P.O
    ph = k.phase()
    stg = Rot(k, ph, 2, [128, 2, D], F32, "p6stg")
    W = {}
    for nm in ["w_out_a", "w_out_b", "w_out_c", "w_o"]:
        wt = k.sb(ph, [128, 8, D], BF16, "p6" + nm)
        src = I[nm][l].rearrange("(k p) c -> p k c", p=128)
        for c4 in range(4):
            st = stg.next()
            k.load("sp", st[:, :, :], src[:, 2 * c4:2 * c4 + 2, :], st)
            k.dve(lambda e: e.tensor_copy(out=wt[:, 2 * c4:2 * c4 + 2, :], in_=st[:, :, :]), R=[st], W=[wt])
        W[nm] = wt
    ldr = Rot(k, ph, 12, [128, D], F32, "p6ld")
    tmpr = Rot(k, ph, 4, [128, D], F32, "p6tmp")
    mrg = Rot(k, ph, 2, [128, D], F32, "p6mrg")
    ogr = Rot(k, ph, 2, [128, D], BF16, "p6og")
    tTr = Rot(k, ph, 2, [128, 8, 128], BF16, "p6tT")
    yr = Rot(k, ph, 2, [128, D], F32, "p6y")
    pbr = Rot(k, ph, 2, [128, D], F32, "p6pb", psum=True)
    ptr = Rot(k, ph, 2, [128, 8, 128], BF16, "p6pt", psum=True)

    def proj_mm(srcb, n, wt):
        pt = ptr.next()
        for kk in range(8):
            k.pe(lambda e: e.transpose(out=pt[:, kk, :n], in_=srcb[:n, kk * 128:(kk + 1) * 128], identity=P.identb[:n, :n]),
                 R=[srcb, P.identb], W=[pt])
        tT = tTr.next()
        k.act(lambda e: e.activation(out=tT[:, :, :n], in_=pt[:, :, :n], func=AF.Copy), R=[pt], W=[tT])
        pb = pbr.next()
        for hf in range(2):
            for kk in range(8):
                k.pe(lambda e: e.matmul(pb[:n, hf * 512:(hf + 1) * 512], lhsT=tT[:, kk, :n], rhs=wt[:, kk, hf * 512:(hf + 1) * 512],
                                        start=(kk == 0), stop=(kk == 7)), R=[tT, wt], W=[pb])
        return pb

    for tl in P.tiles:
        n, g, t0 = tl["n"], tl["g"], tl["t0"]
        s = tl["seq"]
        b = s["b"]
        merged = mrg.next()
        for mi, (osrc, gcol, mcol, wn) in enumerate([("oa", O_AG, O_MA, "w_out_a"), ("ob", O_BG, O_MB, "w_out_b"),
                                                     ("oc", O_CG, O_MC, "w_out_c")]):
            o_, g_, m_ = ldr.next(), ldr.next(), ldr.next()
            k.load("sp", o_[:n, :], S[osrc][g:g + n, :], o_)
            k.load("sp", g_[:n, :], P.pj(g, g + n, gcol, gcol + D), g_)
            k.load("sp", m_[:n, :], P.pj(g, g + n, mcol, mcol + D), m_)
            sg = tmpr.next()
            k.act(lambda e: e.activation(out=sg[:n, :], in_=g_[:n, :], func=AF.Silu), R=[g_], W=[sg])
            og = ogr.next()
            k.dve(lambda e: e.tensor_tensor(out=og[:n, :], in0=o_[:n, :], in1=sg[:n, :], op=ALU.mult), R=[o_, sg], W=[og])
            pb = proj_mm(og, n, W[wn])
            sm = tmpr.next()
            k.act(lambda e: e.activation(out=sm[:n, :], in_=m_[:n, :], func=AF.Sigmoid), R=[m_], W=[sm])
            if mi == 0:
                k.dve(lambda e: e.tensor_tensor(out=merged[:n, :], in0=pb[:n, :], in1=sm[:n, :], op=ALU.mult), R=[pb, sm], W=[merged])
            else:
                k.dve(lambda e: e.tensor_tensor(out=sm[:n, :], in0=pb[:n, :], in1=sm[:n, :], op=ALU.mult), R=[pb, sm], W=[sm])
                k.pool(lambda e: e.tensor_tensor(out=merged[:n, :], in0=merged[:n, :], in1=sm[:n, :], op=ALU.add),
                       R=[merged, sm], W=[merged])
        mb = ogr.next()
        k.act(lambda e: e.activation(out=mb[:n, :], in_=merged[:n, :], func=AF.Copy), R=[merged], W=[mb])
        py = proj_mm(mb, n, W["w_o"])
        x = ldr.next()
        src, _ = P.xsrc(l, tl)
        k.load("sp", x[:n, :], src, x)
        y = yr.next()
        k.dve(lambda e: e.tensor_tensor(out=y[:n, :], in0=py[:n, :], in1=x[:n, :], op=ALU.add), R=[py, x], W=[y])
        if l == 0:
            dst = S["xmid"][g:g + n, :]
        elif b is None:
            dst = O["y_p"][t0:t0 + n, :]
        else:
            dst = O["y_s"][b, t0:t0 + n, :]
        k.store("pool", dst, y[:n, :], y)
    k.end_phase(ph)


_CACHE = {}
NCORES = 8


def _get_prog(T):
    if T not in _CACHE:
        _CACHE[T] = build(T)
    return _CACHE[T]


def kernel(x_prompt, x_sample, cache_attn_k, cache_attn_v, state_hgrn, state_rwkv, state_rwkv_shift,
           norm_g, w_in, a_qnorm_g, a_knorm_g, a_lambda, a_subln_g, b_lower, b_norm_g,
           c_shift_mu, c_w0, c_w2, c_a0, c_a2, c_k_k, c_k_a, c_r_k, c_ln_w, c_ln_b,
           c_vres_w1, c_vres_w2, c_v0, w_out_a, w_out_b, w_out_c, w_o):
    f = lambda a: np.ascontiguousarray(np.asarray(a, dtype=np.float32))
    x_prompt = f(x_prompt)
    B, T, _ = x_prompt.shape
    assert B == NCORES
    P = _get_prog(T)
    carr, _ = make_consts()
    shared = {
        "norm_g": f(norm_g), "w_in": f(w_in), "a_qnorm_g": f(a_qnorm_g), "a_knorm_g": f(a_knorm_g),
        "a_lambda": f(a_lambda).reshape(2, 256), "a_subln_g": f(a_subln_g), "b_lower": f(b_lower), "b_norm_g": f(b_norm_g),
        "c_shift_mu": f(c_shift_mu), "c_w0": f(c_w0), "c_w2": f(c_w2), "c_a0": f(c_a0), "c_a2": f(c_a2),
        "c_k_k": f(c_k_k), "c_k_a": f(c_k_a), "c_r_k": f(c_r_k).reshape(2, 1024), "c_ln_w": f(c_ln_w), "c_ln_b": f(c_ln_b),
        "c_vres_w1": f(c_vres_w1), "c_vres_w2": f(c_vres_w2), "c_v0": f(c_v0),
        "w_out_a": f(w_out_a), "w_out_b": f(w_out_b), "w_out_c": f(w_out_c), "w_o": f(w_o),
        "consts": carr,
        "rope_p": rope_tables(np.arange(T)), "rope_s": rope_tables(PAST + np.arange(TS)),
    }
    x_sample = f(x_sample)
    ck, cv = f(cache_attn_k), f(cache_attn_v)
    sth, str_, stsh = f(state_hgrn), f(state_rwkv), f(state_rwkv_shift)
    in_maps = []
    for c in range(NCORES):
        m = dict(shared)
        sl = slice(2 * c, 2 * c + 2)
        m["x_p"] = x_prompt[c]
        m["x_s"] = x_sample[sl]
        m["ck"] = np.ascontiguousarray(ck[:, sl]).reshape(2, 2, PAST, D)
        m["cv"] = np.ascontiguousarray(cv[:, sl]).reshape(2, 2, PAST, D)
        m["sth"] = np.ascontiguousarray(sth[:, sl])
        m["str"] = np.ascontiguousarray(str_[:, sl])
        m["stsh"] = np.ascontiguousarray(stsh[:, sl])
        in_maps.append(m)
    res = run_bass_kernel_spmd(P.nc, in_maps, core_ids=list(range(NCORES)))
    R = res.results
    NB = 2 * NCORES
    y_p = np.stack([R[c]["y_p"] for c in range(NCORES)], 0)
    y_s = np.concatenate([R[c]["y_s"] for c in range(NCORES)], 0)
    k_p = np.stack([R[c]["k_p"].reshape(2, T, 8, 128) for c in range(NCORES)], 1)
    v_p = np.stack([R[c]["v_p"].reshape(2, T, 8, 128) for c in range(NCORES)], 1)
    hg_p = np.stack([R[c]["hg_p"] for c in range(NCORES)], 1)
    rw_p = np.stack([R[c]["rw_p"] for c in range(NCORES)], 1)
    sh_p = np.stack([R[c]["sh_p"] for c in range(NCORES)], 1)
    k_s = np.concatenate([R[c]["k_s"].reshape(2, 2, TS, 8, 128) for c in range(NCORES)], 1)
    v_s = np.concatenate([R[c]["v_s"].reshape(2, 2, TS, 8, 128) for c in range(NCORES)], 1)
    hg_s = np.concatenate([R[c]["hg_s"] for c in range(NCORES)], 1)
    rw_s = np.concatenate([R[c]["rw_s"] for c in range(NCORES)], 1)
    sh_s = np.concatenate([R[c]["sh_s"] for c in range(NCORES)], 1)
    outs = (y_p, y_s, k_p, v_p, hg_p, rw_p, sh_p, k_s, v_s, hg_s, rw_s, sh_s)
    return tuple(np.ascontiguousarray(o, dtype=np.float32) for o in outs)
```

```python
import math
from contextlib import ExitStack

import numpy as np
import concourse.bass as bass
import concourse.mybir as mybir
from concourse.bass_utils import run_bass_kernel_spmd

F32 = mybir.dt.float32
BF16 = mybir.dt.bfloat16
AF = mybir.ActivationFunctionType
ALU = mybir.AluOpType
AX = mybir.AxisListType

D = 1024
NCOL = 15488
NCX = NCOL + 32
PAST = 1024
TS = 64
EPS = 1e-6
GN_EPS = 64e-5
ROPE_THETA = 500000.0
O_AQ, O_AK, O_AV, O_AG = 0, 1024, 2048, 3072
O_BQ, O_BF, O_BI, O_BG = 4096, 5120, 6144, 7168
O_CP = 8192
O_CG = 11392
O_MA, O_MB, O_MC = 12416, 13440, 14464
O_EXT = 15488
C_R, C_WLO, C_K, C_V, C_ALO = 0, 1024, 1088, 2112, 3136
CW = 3200


class Tk:
    def __init__(self, h, name, dram=False):
        self.h = h
        self.name = name
        self.lw = None
        self.rd = []
        self.ds = {}
        self.dram = dram
        self.tok = {}
        self.rtok = {}

    def __getitem__(self, k):
        return self.h[k]


class Ctx:
    def __init__(self, nc):
        self.nc = nc
        self.es = ExitStack()
        self.eng = {"pe": nc.tensor, "act": nc.scalar, "dve": nc.vector, "pool": nc.gpsimd, "sp": nc.sync}
        self.sem = {}
        self.cnt = {}
        self.waited = {}
        for k in self.eng:
            self.sem[k] = self.es.enter_context(nc.semaphore("es_" + k))
            self.cnt[k] = 0
            self.waited[k] = {}
        self.free_ds = {"hw": [], "sw": []}
        self.nds = 0
        self.uid = 0
        self.ninst = 0

    def get_ds(self, t, q):
        kind = "sw" if q == "pool" else "hw"
        if kind not in t.ds:
            t.ds[kind] = self.new_ds(kind)
        return t.ds[kind]

    def new_ds(self, kind):
        if self.free_ds[kind]:
            return self.free_ds[kind].pop()
        self.nds += 1
        h = self.es.enter_context(self.nc.semaphore("ds%d" % self.nds))
        return [h, 0, "ds%d" % self.nds]

    def sb(self, ph, shape, dt, name):
        self.uid += 1
        h = ph.enter_context(self.nc.sbuf_tensor("%s_%d" % (name, self.uid), list(shape), dt))
        t = Tk(h, name)
        ph.tiles.append(t)
        return t

    def ps(self, ph, shape, dt, name):
        self.uid += 1
        h = ph.enter_context(self.nc.psum_tensor("%s_%d" % (name, self.uid), list(shape), dt))
        t = Tk(h, name)
        ph.tiles.append(t)
        return t

    def phase(self):
        ph = ExitStack()
        ph.tiles = []
        return ph

    def end_phase(self, ph):
        self.barrier(ph.tiles)
        for t in ph.tiles:
            for kind, ds in t.ds.items():
                self.free_ds[kind].append(ds)
            t.ds = {}
        ph.close()

    def barrier(self, tiles):
        deps = []
        for k in self.eng:
            if k != "sp" and self.cnt[k] > 0:
                deps.append(("e", k, self.cnt[k]))
        for t in tiles:
            for ds in t.ds.values():
                if ds[1] > 0:
                    deps.append(("d", ds, ds[1]))
        for k in self.eng:
            for d in deps:
                self._wait(k, d)

    def _wait(self, ename, dep):
        if dep is None:
            return
        kind, obj, val = dep
        if kind == "e":
            if obj == ename and ename in ("pe", "sp"):
                return
            key = "e_" + obj
            semh = self.sem[obj]
        else:
            key = obj[2]
            semh = obj[0]
        w = self.waited[ename]
        if w.get(key, 0) >= val:
            return
        self.eng[ename].wait_ge(semh, val)
        w[key] = val
        self.ninst += 1

    def op(self, ename, fn, R=(), W=()):
        deps = []
        for t in R:
            deps.append(t.lw)
        for t in W:
            deps.append(t.lw)
            deps.extend(t.rd)
        for d in deps:
            self._wait(ename, d)
        ins = fn(self.eng[ename])
        self.cnt[ename] += 1
        ins.then_inc(self.sem[ename], 1)
        self.ninst += 1
        tok = ("e", ename, self.cnt[ename])
        for t in R:
            t.rd.append(tok)
        for t in W:
            t.lw = tok
            t.rd = []
        return ins

    def pe(self, fn, R=(), W=()):
        return self.op("pe", fn, R, W)

    def act(self, fn, R=(), W=()):
        return self.op("act", fn, R, W)

    def dve(self, fn, R=(), W=()):
        return self.op("dve", fn, R, W)

    def pool(self, fn, R=(), W=()):
        return self.op("pool", fn, R, W)

    def load(self, q, out_ap, in_ap, sbt, dr=None, slow=False):
        deps = [sbt.lw] + list(sbt.rd)
        for d in deps:
            self._wait(q, d)
        ds = self.get_ds(sbt, q)
        if slow:
            ins = self.eng[q].dma_start(out=out_ap, in_=in_ap, allow_slow_non_contiguous=True)
        else:
            ins = self.eng[q].dma_start(out=out_ap, in_=in_ap)
        ds[1] += 16
        ins.then_inc(ds[0], 16)
        self.ninst += 1
        sbt.lw = ("d", ds, ds[1])
        sbt.rd = []

    def store(self, q, out_ap, in_ap, sbt, dr=None):
        deps = [sbt.lw]
        for d in deps:
            self._wait(q, d)
        ds = self.get_ds(sbt, q)
        ins = self.eng[q].dma_start(out=out_ap, in_=in_ap)
        ds[1] += 16
        ins.then_inc(ds[0], 16)
        self.ninst += 1
        sbt.rd.append(("d", ds, ds[1]))


class Rot:
    def __init__(self, k, ph, n, shape, dt, name, psum=False):
        mk = k.ps if psum else k.sb
        self.tiles = [mk(ph, shape, dt, "%s%d" % (name, i)) for i in range(n)]
        self.i = 0

    def next(self):
        t = self.tiles[self.i % len(self.tiles)]
        self.i += 1
        return t


def rsqrt(k, out, in_, R, W, scale, bias):
    k.act(lambda e: e.activation(out=out, in_=in_, func=AF.Sqrt, scale=scale, bias=bias), R=R, W=W)
    k.dve(lambda e: e.reciprocal(out=out, in_=out), R=W, W=W)


def bc(ap, shape):
    return ap.to_broadcast(list(shape))


def make_consts():
    s = np.arange(128)[:, None]
    t = np.arange(128)[None, :]
    same = (s // 64) == (t // 64)
    c = {}
    c["ident"] = np.eye(128)
    ut64 = (same & (s <= t)).astype(np.float64)
    mid = 64 * (t // 64) + 31
    a_mid = (same & (s <= mid)).astype(np.float64)
    a_end = same.astype(np.float64)
    c["h_ut"] = ut64
    c["h_d1"] = ut64 - a_mid
    c["h_d2"] = a_end - ut64
    c["h_end"] = a_end
    c["h_mask"] = ut64
    c["r_ut"] = (s <= t).astype(np.float64)
    c["r_uts"] = (s < t).astype(np.float64)
    c["r_low"] = (s > t).astype(np.float64)
    c["r_one"] = np.ones((128, 128))
    names = ["ident", "h_ut", "h_d1", "h_d2", "h_end", "h_mask", "r_ut", "r_uts", "r_low", "r_one"]
    arr = np.concatenate([c[n] for n in names], axis=1).astype(np.float32)
    offs = {n: i * 128 for i, n in enumerate(names)}
    return arr, offs


def rope_tables(pos):
    half = 8
    inv_freq = (np.float32(ROPE_THETA) ** (-(np.arange(half, dtype=np.float32) * np.float32(2.0 / 16)))).astype(np.float32)
    ang = pos.astype(np.float32)[:, None] * inv_freq[None, :]
    return np.concatenate([np.cos(ang), np.sin(ang)], axis=1).astype(np.float32)


class Prog:
    pass


def build(T, nlayers=2, upto=99, debug=False):
    nc = bass.Bass("TRN2", target_bir_lowering=False)
    k = Ctx(nc)
    P = Prog()
    P.nc, P.k, P.T = nc, k, T
    Ttot = T + 2 * TS
    P.Ttot = Ttot
    carr, coff = make_consts()
    P.coff = coff

    def din(name, shape, dt=F32):
        return nc.dram_tensor(name, list(shape), dt, kind="ExternalInput").ap()

    def dout(name, shape, dt=F32):
        return Tk(nc.dram_tensor(name, list(shape), dt, kind="ExternalOutput").ap(), name, dram=True)

    def dscr(name, shape, dt=F32):
        kind = "ExternalOutput" if (debug and name in debug) else "Internal"
        return Tk(nc.dram_tensor(name, list(shape), dt, kind=kind).ap(), name, dram=True)

    I = {}
    I["x_p"] = din("x_p", [T, D])
    I["x_s"] = din("x_s", [2, TS, D])
    I["ck"] = din("ck", [2, 2, PAST, D])
    I["cv"] = din("cv", [2, 2, PAST, D])
    I["sth"] = din("sth", [2, 2, 8, 128, 128])
    I["str"] = din("str", [2, 2, 16, 64, 64])
    I["stsh"] = din("stsh", [2, 2, CW])
    I["norm_g"] = din("norm_g", [2, D])
    I["w_in"] = din("w_in", [2, D, NCOL])
    I["a_qnorm_g"] = din("a_qnorm_g", [2, 64])
    I["a_knorm_g"] = din("a_knorm_g", [2, 64])
    I["a_lambda"] = din("a_lambda", [2, 256])
    I["a_subln_g"] = din("a_subln_g", [2, 128])
    I["b_lower"] = din("b_lower", [2, 1024])
    I["b_norm_g"] = din("b_norm_g", [2, 128])
    I["c_shift_mu"] = din("c_shift_mu", [2, CW])
    I["c_w0"] = din("c_w0", [2, 1024])
    I["c_w2"] = din("c_w2", [2, 64, 1024])
    I["c_a0"] = din("c_a0", [2, 1024])
    I["c_a2"] = din("c_a2", [2, 64, 1024])
    I["c_k_k"] = din("c_k_k", [2, 1024])
    I["c_k_a"] = din("c_k_a", [2, 1024])
    I["c_r_k"] = din("c_r_k", [2, 1024])
    I["c_ln_w"] = din("c_ln_w", [2, 1024])
    I["c_ln_b"] = din("c_ln_b", [2, 1024])
    I["c_vres_w1"] = din("c_vres_w1", [1, D, 32])
    I["c_vres_w2"] = din("c_vres_w2", [1, 32, 1024])
    I["c_v0"] = din("c_v0", [1, 1024])
    I["w_out_a"] = din("w_out_a", [2, D, D])
    I["w_out_b"] = din("w_out_b", [2, D, D])
    I["w_out_c"] = din("w_out_c", [2, D, D])
    I["w_o"] = din("w_o", [2, D, D])
    I["consts"] = din("consts", list(carr.shape))
    I["rope_p"] = din("rope_p", [T, 16])
    I["rope_s"] = din("rope_s", [TS, 16])
    P.I = I

    O = {}
    O["y_p"] = dout("y_p", [T, D])
    O["y_s"] = dout("y_s", [2, TS, D])
    O["k_p"] = dout("k_p", [2, T, D])
    O["v_p"] = dout("v_p", [2, T, D])
    O["hg_p"] = dout("hg_p", [2, 8, 128, 128])
    O["rw_p"] = dout("rw_p", [2, 16, 64, 64])
    O["sh_p"] = dout("sh_p", [2, CW])
    O["k_s"] = dout("k_s", [2, 2, TS, D])
    O["v_s"] = dout("v_s", [2, 2, TS, D])
    O["hg_s"] = dout("hg_s", [2, 2, 8, 128, 128])
    O["rw_s"] = dout("rw_s", [2, 2, 16, 64, 64])
    O["sh_s"] = dout("sh_s", [2, 2, CW])
    P.O = O

    seqs = []
    seqs.append(dict(name="p", T=T, g0=0, n=128, past=0, b=None))
    seqs.append(dict(name="s0", T=TS, g0=T, n=TS, past=PAST, b=0))
    seqs.append(dict(name="s1", T=TS, g0=T + TS, n=TS, past=PAST, b=1))
    tiles = []
    for s in seqs:
        s["tiles"] = []
        for t0 in range(0, s["T"], s["n"]):
            tl = dict(seq=s, t0=t0, n=s["n"], g=s["g0"] + t0, idx=len(tiles))
            tiles.append(tl)
            s["tiles"].append(tl)
    P.seqs, P.tiles = seqs, tiles
    NTL = len(tiles)

    S = {}
    S["hT"] = dscr("hT", [NTL, 128, 8, 128], BF16)
    PJ = [(0, 4096, "pjA"), (4096, 8192, "pjB"), (8192, 12416, "pjC"), (12416, NCX, "pjM")]
    for lo, hi, nm in PJ:
        S[nm] = dscr(nm, [Ttot, hi - lo])

    def pj(r0, r1, c0, c1):
        for lo, hi, nm in PJ:
            if lo <= c0 and c1 <= hi:
                return S[nm][r0:r1, c0 - lo:c1 - lo]
        raise ValueError((c0, c1))
    P.pj = pj
    S["xmid"] = dscr("xmid", [Ttot, D])
    S["oa"] = dscr("oa", [Ttot, D])
    S["ob"] = dscr("ob", [Ttot, D])
    S["oc"] = dscr("oc", [Ttot, D])
    S["vf"] = dscr("vf", [Ttot, D])
    for s in seqs:
        TK = s["past"] + s["T"]
        s["TK"] = TK
        s["qT"] = dscr("qT_" + s["name"], [8, 128, s["T"]], BF16)
        s["kT"] = dscr("kT_" + s["name"], [8, 128, TK], BF16)
        s["vb"] = dscr("vb_" + s["name"], [TK, 8, 129], BF16)
    P.S = S

    def xsrc(l, tl):
        s = tl["seq"]
        if l == 0:
            if s["b"] is None:
                return I["x_p"][tl["t0"]:tl["t0"] + tl["n"], :], None
            return I["x_s"][s["b"], tl["t0"]:tl["t0"] + tl["n"], :], None
        return S["xmid"][tl["g"]:tl["g"] + tl["n"], :], S["xmid"]
    P.xsrc = xsrc

    gph = k.phase()
    P.gph = gph
    cst = k.sb(gph, [128, carr.shape[1]], F32, "cst")
    k.load("sp", cst[:], I["consts"][:, :], cst)
    identb = k.sb(gph, [128, 128], BF16, "identb")
    k.dve(lambda e: e.tensor_copy(out=identb[:], in_=cst[:, coff["ident"]:coff["ident"] + 128]), R=[cst], W=[identb])
    P.cst, P.identb = cst, identb

    def C(name):
        return cst[:, coff[name]:coff[name] + 128]
    P.C = C

    for l in range(nlayers):
        if upto >= 0:
            phase0(P, l)
        if upto >= 1:
            phase1(P, l)
        if upto >= 2:
            phase2(P, l)
        if upto >= 3:
            phase3(P, l)
        if upto >= 4:
            phase4(P, l)
        if upto >= 5:
            phase5(P, l)
        if upto >= 6:
            phase6(P, l)

    k.end_phase(gph)
    return P


def phase0(P, l):
    k, I, S = P.k, P.I, P.S
    ph = k.phase()
    xr = Rot(k, ph, 3, [128, D], F32, "p0x")
    jr = Rot(k, ph, 2, [128, D], BF16, "p0j")
    hr = Rot(k, ph, 2, [128, D], BF16, "p0h")
    sr = Rot(k, ph, 4, [128, 2], F32, "p0s")
    tr = Rot(k, ph, 2, [128, 8, 128], BF16, "p0t")
    pr = Rot(k, ph, 2, [128, 8, 128], BF16, "p0p", psum=True)
    for tl in P.tiles:
        n = tl["n"]
        src, dr = P.xsrc(l, tl)
        x = xr.next()
        k.load("sp", x[:n, :], src, x, dr)
        st = sr.next()
        j = jr.next()
        k.act(lambda e: e.activation(out=j[:n, :], in_=x[:n, :], func=AF.Square, accum_out=st[:n, 0:1]), R=[x], W=[j, st])
        rsqrt(k, st[:n, 1:2], st[:n, 0:1], [st], [st], 1.0 / D, EPS)
        h = hr.next()
        k.dve(lambda e: e.tensor_scalar(out=h[:n, :], in0=x[:n, :], scalar1=st[:n, 1:2], scalar2=None,
                                        op0=ALU.mult), R=[x, st], W=[h])
        pt = pr.next()
        for kk in range(8):
            k.pe(lambda e: e.transpose(out=pt[:, kk, :n], in_=h[:n, kk * 128:(kk + 1) * 128], identity=P.identb[:n, :n]),
                 R=[h, P.identb], W=[pt])
        ht = tr.next()
        k.act(lambda e: e.activation(out=ht[:, :, :n], in_=pt[:, :, :n], func=AF.Copy), R=[pt], W=[ht])
        k.store("pool", S["hT"][tl["idx"], :, :, :n], ht[:, :, :n], ht, S["hT"])
    k.end_phase(ph)


def phase1(P, l):
    k, I, S = P.k, P.I, P.S
    ph = k.phase()
    gcol = k.sb(ph, [128, 8], F32, "p1g")
    k.load("sp", gcol[:], I["norm_g"][l].rearrange("(k p) -> p k", p=128), gcol, slow=True)
    wf = Rot(k, ph, 2, [128, 8, 1024], F32, "p1wf")
    wb = Rot(k, ph, 2, [128, 8, 1024], BF16, "p1wb")
    hr = Rot(k, ph, 3, [128, 8, 128], BF16, "p1h")
    orr = Rot(k, ph, 3, [128, 1024], F32, "p1o")
    pr = Rot(k, ph, 2, [128, 1024], F32, "p1p", psum=True)
    groups = [(c0, 1024) for c0 in range(0, 11264, 1024)] + [(11264, 128)] + [(c0, 1024) for c0 in range(11392, NCOL, 1024)]
    if l == 1:
        groups.append((O_EXT, 32))
    ev = 0
    for (c0, cw) in groups:
        w32 = wf.next()
        if c0 == O_EXT:
            src = I["c_vres_w1"][0].rearrange("(k p) c -> p k c", p=128)
        else:
            src = I["w_in"][l][:, c0:c0 + cw].rearrange("(k p) c -> p k c", p=128)
        k.load("sp", w32[:, :, :cw], src, w32)
        w = wb.next()
        k.dve(lambda e: e.tensor_tensor(out=w[:, :, :cw], in0=w32[:, :, :cw],
                                        in1=bc(gcol[:, :].unsqueeze(2), [128, 8, cw]), op=ALU.mult),
              R=[w32, gcol], W=[w])
        for tl in P.tiles:
            n = tl["n"]
            h = hr.next()
            k.load("sp", h[:, :, :n], S["hT"][tl["idx"], :, :, :n], h, S["hT"])
            pt = pr.next()
            for n0 in range(0, cw, 512):
                nw = min(512, cw - n0)
                for kk in range(8):
                    k.pe(lambda e: e.matmul(pt[:n, n0:n0 + nw], lhsT=h[:, kk, :n], rhs=w[:, kk, n0:n0 + nw],
                                            start=(kk == 0), stop=(kk == 7)), R=[h, w], W=[pt])
            o = orr.next()
            if ev % 2 == 0:
                k.act(lambda e: e.activation(out=o[:n, :cw], in_=pt[:n, :cw], func=AF.Copy), R=[pt], W=[o])
            else:
                k.dve(lambda e: e.tensor_copy(out=o[:n, :cw], in_=pt[:n, :cw]), R=[pt], W=[o])
            ev += 1
            k.store("pool", P.pj(tl["g"], tl["g"] + n, c0, c0 + cw), o[:n, :cw], o)
    k.end_phase(ph)


def bcast_row(k, ph, ap1d, width, name, q="sp"):
    t = k.sb(ph, [128, width], F32, name)
    k.load(q, t[:], ap1d.partition_broadcast(128), t)
    return t


def phase2(P, l):
    k, I, S, O = P.k, P.I, P.S, P.O
    ph = k.phase()
    gq = bcast_row(k, ph, I["a_qnorm_g"][l], 64, "p2gq")
    gk = bcast_row(k, ph, I["a_knorm_g"][l], 64, "p2gk")
    xr = Rot(k, ph, 4, [128, D], F32, "p2x")
    tmp = Rot(k, ph, 2, [128, D], F32, "p2tmp")
    xn = Rot(k, ph, 3, [128, D], F32, "p2xn")
    ssr = Rot(k, ph, 4, [128, 16], F32, "p2ss")
    csr = Rot(k, ph, 2, [128, 16], F32, "p2cs")
    rtr = Rot(k, ph, 2, [128, 4, 16, 8], F32, "p2rt")
    xbr = Rot(k, ph, 3, [128, D], BF16, "p2xb")
    vbr = Rot(k, ph, 2, [128, 8, 129], BF16, "p2vb")
    tTr = Rot(k, ph, 3, [128, 8, 128], BF16, "p2tT")
    ptr = Rot(k, ph, 3, [128, 8, 128], BF16, "p2pt", psum=True)
    for vt in vbr.tiles:
        k.dve(lambda e: e.memset(vt[:, :, 128:129], 1.0), W=[vt])

    def transpose_store(xb, n, dst_ap):
        pt = ptr.next()
        for h in range(8):
            k.pe(lambda e: e.transpose(out=pt[:, h, :n], in_=xb[:n, h * 128:(h + 1) * 128], identity=P.identb[:n, :n]),
                 R=[xb, P.identb], W=[pt])
        tT = tTr.next()
        k.act(lambda e: e.activation(out=tT[:, :, :n], in_=pt[:, :, :n], func=AF.Copy), R=[pt], W=[tT])
        k.store("pool", dst_ap, tT[:, :, :n], tT)

    def v_store(v, n, dst_rows):
        vb = vbr.next()
        k.act(lambda e: e.activation(out=vb[:n, :, 0:128], in_=v[:n, :].rearrange("p (h d) -> p h d", h=8), func=AF.Copy),
              R=[v], W=[vb])
        k.store("pool", dst_rows, vb[:n, :, :], vb)

    def normrope(x, n, g, cs):
        t = tmp.next()
        k.act(lambda e: e.activation(out=t[:n, :], in_=x[:n, :], func=AF.Square), R=[x], W=[t])
        ss = ssr.next()
        k.dve(lambda e: e.tensor_reduce(out=ss[:n, :], in_=t[:n, :].rearrange("p (s d) -> p s d", s=16), axis=AX.X, op=ALU.add),
              R=[t], W=[ss])
        rsqrt(k, ss[:n, :], ss[:n, :], [ss], [ss], 1.0 / 64, EPS)
        y = xn.next()
        y3 = y[:n, :].rearrange("p (s d) -> p s d", s=16)
        x3 = x[:n, :].rearrange("p (s d) -> p s d", s=16)
        k.dve(lambda e: e.tensor_tensor(out=y3, in0=x3, in1=bc(ss[:n, :].unsqueeze(2), [n, 16, 64]), op=ALU.mult),
              R=[x, ss], W=[y])
        k.pool(lambda e: e.tensor_tensor(out=y3, in0=y3, in1=bc(g[:n, :].unsqueeze(1), [n, 16, 64]), op=ALU.mult),
               R=[y, g], W=[y])
        rt = rtr.next()
        cosb = bc(cs[:n, 0:8].unsqueeze(1), [n, 16, 8])
        sinb = bc(cs[:n, 8:16].unsqueeze(1), [n, 16, 8])
        x1 = y3[:, :, 0:8]
        x2 = y3[:, :, 8:16]
        k.dve(lambda e: e.tensor_tensor(out=rt[:n, 0], in0=x1, in1=cosb, op=ALU.mult), R=[y, cs], W=[rt])
        k.dve(lambda e: e.tensor_tensor(out=rt[:n, 1], in0=x2, in1=sinb, op=ALU.mult), R=[y, cs], W=[rt])
        k.dve(lambda e: e.tensor_tensor(out=rt[:n, 2], in0=x2, in1=cosb, op=ALU.mult), R=[y, cs], W=[rt])
        k.dve(lambda e: e.tensor_tensor(out=rt[:n, 3], in0=x1, in1=sinb, op=ALU.mult), R=[y, cs], W=[rt])
        k.dve(lambda e: e.tensor_tensor(out=x1, in0=rt[:n, 0], in1=rt[:n, 1], op=ALU.subtract), R=[rt], W=[y])
        k.dve(lambda e: e.tensor_tensor(out=x2, in0=rt[:n, 2], in1=rt[:n, 3], op=ALU.add), R=[rt], W=[y])
        return y

    for s in P.seqs:
        b = s["b"]
        for j in range(s["past"] // 128):
            x = xr.next()
            k.load("sp", x[:, :], I["ck"][l, b, j * 128:(j + 1) * 128, :], x)
            xb = xbr.next()
            k.act(lambda e: e.activation(out=xb[:, :], in_=x[:, :], func=AF.Copy), R=[x], W=[xb])
            transpose_store(xb, 128, s["kT"][:, :, j * 128:(j + 1) * 128].rearrange("h p t -> p h t"))
            v = xr.next()
            k.load("sp", v[:, :], I["cv"][l, b, j * 128:(j + 1) * 128, :], v)
            v_store(v, 128, s["vb"][j * 128:(j + 1) * 128, :, :])
        for tl in s["tiles"]:
            n, t0, g = tl["n"], tl["t0"], tl["g"]
            cs = csr.next()
            rsrc = I["rope_p"][t0:t0 + n, :] if b is None else I["rope_s"][t0:t0 + n, :]
            k.load("sp", cs[:n, :], rsrc, cs)
            x = xr.next()
            k.load("sp", x[:n, :], P.pj(g, g + n, O_AQ, O_AQ + D), x)
            y = normrope(x, n, gq, cs)
            xb = xbr.next()
            k.act(lambda e: e.activation(out=xb[:n, :], in_=y[:n, :], func=AF.Copy), R=[y], W=[xb])
            transpose_store(xb, n, s["qT"][:, :, t0:t0 + n].rearrange("h p t -> p h t"))
            x = xr.next()
            k.load("sp", x[:n, :], P.pj(g, g + n, O_AK, O_AK + D), x)
            y = normrope(x, n, gk, cs)
            kdst = O["k_p"][l, t0:t0 + n, :] if b is None else O["k_s"][l, b, t0:t0 + n, :]
            k.store("pool", kdst, y[:n, :], y)
            xb = xbr.next()
            k.act(lambda e: e.activation(out=xb[:n, :], in_=y[:n, :], func=AF.Copy), R=[y], W=[xb])
            p0 = s["past"]
            transpose_store(xb, n, s["kT"][:, :, p0 + t0:p0 + t0 + n].rearrange("h p t -> p h t"))
            v = xr.next()
            k.load("sp", v[:n, :], P.pj(g, g + n, O_AV, O_AV + D), v)
            vdst = O["v_p"][l, t0:t0 + n, :] if b is None else O["v_s"][l, b, t0:t0 + n, :]
            k.store("pool", vdst, v[:n, :], v)
            v_store(v, n, s["vb"][p0 + t0:p0 + t0 + n, :, :])
    k.end_phase(ph)


def phase3(P, l):
    k, I, S, O = P.k, P.I, P.S, P.O
    ph = k.phase()
    lam_init = 0.8 - 0.6 * math.exp(-0.3 * l)
    lamt = bcast_row(k, ph, I["a_lambda"][l], 256, "p3lam")
    lw = k.sb(ph, [128, 2, 64], F32, "p3lw")
    l4 = lamt[:, :].rearrange("p (a b d) -> p a b d", a=2, b=2)
    k.dve(lambda e: e.tensor_tensor(out=lw[:, :, :], in0=l4[:, :, 0, :], in1=l4[:, :, 1, :], op=ALU.mult), R=[lamt], W=[lw])
    lc = k.sb(ph, [128, 4], F32, "p3lc")
    k.dve(lambda e: e.tensor_reduce(out=lc[:, 0:2], in_=lw[:, :, :], axis=AX.X, op=ALU.add), R=[lw], W=[lc])
    k.act(lambda e: e.activation(out=lc[:, 0:2], in_=lc[:, 0:2], func=AF.Exp), R=[lc], W=[lc])
    k.dve(lambda e: e.tensor_tensor(out=lc[:, 2:3], in0=lc[:, 0:1], in1=lc[:, 1:2], op=ALU.subtract), R=[lc], W=[lc])
    k.dve(lambda e: e.tensor_scalar(out=lc[:, 3:4], in0=lc[:, 2:3], scalar1=lam_init, scalar2=None, op0=ALU.add), R=[lc], W=[lc])
    gs = bcast_row(k, ph, I["a_subln_g"][l], 128, "p3gs")
    k.dve(lambda e: e.tensor_scalar(out=gs[:, :], in0=gs[:, :], scalar1=1.0 - lam_init, scalar2=None, op0=ALU.mult), R=[gs], W=[gs])

    TKmax = max(s["TK"] for s in P.seqs)
    ntkmax = (TKmax + 127) // 128
    ktr = Rot(k, ph, 2, [128, TKmax], BF16, "p3kt")
    vtr = Rot(k, ph, 2, [128, ntkmax, 129], BF16, "p3vt")
    qtr = Rot(k, ph, 2, [128, 512], BF16, "p3qt")
    psr = Rot(k, ph, 2, [128, 2, 512], F32, "p3ps", psum=True)
    acc = k.ps(ph, [128, 8, 256], F32, "p3acc")
    ptr = Rot(k, ph, 3, [128, 2, 512], BF16, "p3pt")
    accr = Rot(k, ph, 2, [128, 8, 129], F32, "p3accs")
    rrr = Rot(k, ph, 4, [128, 8], F32, "p3rr")
    tr_ = Rot(k, ph, 2, [128, 128], F32, "p3t")
    orr = Rot(k, ph, 2, [128, 128], F32, "p3o")
    ofr = Rot(k, ph, 3, [128, 128], F32, "p3of")

    for s in P.seqs:
        TK, Tq = s["TK"], s["T"]
        ntk = (TK + 127) // 128
        prompt = s["b"] is None
        qw = min(512, Tq)
        for h in range(8):
            kt = ktr.next()
            k.load("sp", kt[:, :TK], s["kT"][h, :, :], kt)
            vt = vtr.next()
            nfull = TK // 128
            k.load("sp", vt[:, :nfull, :], s["vb"][0:nfull * 128, h, :].rearrange("(j p) d -> p j d", p=128), vt)
            if TK % 128:
                rem = TK % 128
                k.load("sp", vt[:rem, nfull, :], s["vb"][nfull * 128:TK, h, :], vt)
            for q0 in range(0, Tq, qw):
                qt = qtr.next()
                k.load("sp", qt[:, :qw], s["qT"][h, :, q0:q0 + qw], qt)
                nqt = (qw + 127) // 128
                jq0 = q0 // 128
                jlast = (jq0 + nqt - 1) if prompt else (ntk - 1)
                def s_mm(j):
                    nk = min(128, TK - j * 128)
                    ps = psr.next()
                    for m in range(2):
                        k.pe(lambda e: e.matmul(ps[:nk, m, :qw], lhsT=kt[m * 64:(m + 1) * 64, j * 128:j * 128 + nk],
                                                rhs=qt[m * 64:(m + 1) * 64, :qw], start=True, stop=True),
                             R=[kt, qt], W=[ps])
                    return ps

                ps_next = s_mm(0)
                for j in range(jlast + 1):
                    nk = min(128, TK - j * 128)
                    ps = ps_next
                    if j < jlast:
                        ps_next = s_mm(j + 1)
                    pt = ptr.next()
                    k.act(lambda e: e.activation(out=pt[:nk, :, :qw], in_=ps[:nk, :, :qw], func=AF.Exp, scale=0.125),
                          R=[ps], W=[pt])
                    if prompt and j >= jq0:
                        i = j - jq0
                        k.pool(lambda e: e.memset(pt[64:128, :, i * 128:i * 128 + 64], 0.0), W=[pt])
                    for m in range(2):
                        for i in range(nqt):
                            nq = min(128, qw - i * 128)
                            last = (jq0 + i) if prompt else (ntk - 1)
                            if j > last:
                                continue
                            k.pe(lambda e: e.matmul(acc[:nq, m * 4 + i, 0:129], lhsT=pt[:nk, m, i * 128:i * 128 + nq],
                                                    rhs=vt[:nk, j, :], start=(j == 0 and i % 2 == 0), stop=(j == last),
                                                    skip_group_check=True),
                                 R=[pt, vt], W=[acc])
                nqmax = min(128, qw)
                accs = accr.next()
                for m in range(2):
                    k.dve(lambda e: e.tensor_copy(out=accs[:nqmax, m * 4:m * 4 + nqt, :], in_=acc[:nqmax, m * 4:m * 4 + nqt, 0:129]),
                          R=[acc], W=[accs])
                for i in range(nqt):
                    nq = min(128, qw - i * 128)
                    rr = rrr.next()
                    k.dve(lambda e: e.reciprocal(out=rr[:nq, 0:1], in_=accs[:nq, i, 128:129]), R=[accs], W=[rr])
                    k.dve(lambda e: e.reciprocal(out=rr[:nq, 1:2], in_=accs[:nq, 4 + i, 128:129]), R=[accs], W=[rr])
                    k.dve(lambda e: e.tensor_tensor(out=rr[:nq, 2:3], in0=rr[:nq, 1:2], in1=lc[:nq, 3:4], op=ALU.mult),
                          R=[rr, lc], W=[rr])
                    t = tr_.next()
                    k.dve(lambda e: e.tensor_scalar(out=t[:nq, :], in0=accs[:nq, 4 + i, 0:128], scalar1=rr[:nq, 2:3],
                                                    scalar2=None, op0=ALU.mult), R=[accs, rr], W=[t])
                    o = orr.next()
                    k.dve(lambda e: e.scalar_tensor_tensor(out=o[:nq, :], in0=accs[:nq, i, 0:128], scalar=rr[:nq, 0:1],
                                                           in1=t[:nq, :], op0=ALU.mult, op1=ALU.subtract),
                          R=[accs, rr, t], W=[o])
                    k.pool(lambda e: e.tensor_tensor(out=t[:nq, :], in0=o[:nq, :], in1=o[:nq, :], op=ALU.mult), R=[o], W=[t])
                    k.dve(lambda e: e.tensor_reduce(out=rr[:nq, 3:4], in_=t[:nq, :], axis=AX.X, op=ALU.add), R=[t], W=[rr])
                    k.act(lambda e: e.activation(out=rr[:nq, 4:5], in_=rr[:nq, 3:4], func=AF.Ln, scale=1.0 / 128, bias=EPS),
                          R=[rr], W=[rr])
                    k.act(lambda e: e.activation(out=rr[:nq, 5:6], in_=rr[:nq, 4:5], func=AF.Exp, scale=-0.5), R=[rr], W=[rr])
                    of = ofr.next()
                    k.dve(lambda e: e.scalar_tensor_tensor(out=of[:nq, :], in0=o[:nq, :], scalar=rr[:nq, 5:6],
                                                           in1=gs[:nq, :], op0=ALU.mult, op1=ALU.mult),
                          R=[o, rr, gs], W=[of])
                    g = s["g0"] + q0 + i * 128
                    k.store("pool", S["oa"][g:g + nq, h * 128:(h + 1) * 128], of[:nq, :], of)
    k.end_phase(ph)


def phase4(P, l):
    k, I, S, O, C = P.k, P.I, P.S, P.O, P.C
    ph = k.phase()
    lbr = k.sb(ph, [128, D], F32, "p4lb")
    oml = k.sb(ph, [128, D], F32, "p4oml")
    if l == 0:
        k.dve(lambda e: e.memset(lbr[:, :], 0.0), W=[lbr])
        k.dve(lambda e: e.memset(oml[:, :], 1.0), W=[oml])
    else:
        k.load("sp", lbr[:, :], I["b_lower"][1].partition_broadcast(128), lbr)
        k.load("sp", oml[:, :], I["b_lower"][0].partition_broadcast(128), oml)
        k.dve(lambda e: e.tensor_tensor(out=lbr[:, :], in0=lbr[:, :], in1=oml[:, :], op=ALU.subtract), R=[lbr, oml], W=[lbr])
        k.act(lambda e: e.activation(out=lbr[:, :], in_=lbr[:, :], func=AF.Sigmoid), R=[lbr], W=[lbr])
        k.dve(lambda e: e.tensor_scalar(out=oml[:, :], in0=lbr[:, :], scalar1=-1.0, scalar2=1.0, op0=ALU.mult, op1=ALU.add),
              R=[lbr], W=[oml])
    gn = bcast_row(k, ph, I["b_norm_g"][l], 128, "p4gn")
    ldr = Rot(k, ph, 6, [128, D], F32, "p4ld")
    f32r = Rot(k, ph, 8, [128, D], F32, "p4f")
    er = Rot(k, ph, 3, [128, D], F32, "p4e")
    b16r = Rot(k, ph, 10, [128, D], BF16, "p4b")
    tTr = Rot(k, ph, 4, [128, 8, 128], BF16, "p4tT")
    qpr = Rot(k, ph, 2, [128, 8, 2, 128], BF16, "p4qp")
    for t in qpr.tiles:
        k.dve(lambda e: e.memset(t[:, :, :, :], 0.0), W=[t])
    scmr = Rot(k, ph, 2, [128, 8, 128], BF16, "p4scm")
    dcr = Rot(k, ph, 2, [128, 8, 2], F32, "p4dc")
    ssr = Rot(k, ph, 2, [128, 8], F32, "p4ss")
    St = k.sb(ph, [128, 8, 128], F32, "p4S")
    Sb = k.sb(ph, [128, 8, 128], BF16, "p4Sb")
    pc = k.ps(ph, [128, D], F32, "p4pc")
    ptp = k.ps(ph, [128, 8, 128], BF16, "p4pt")
    psc = k.ps(ph, [128, 4, 128], F32, "p4psc")
    po = k.ps(ph, [128, 8, 128], F32, "p4po")
    pS = k.ps(ph, [128, 4, 128], F32, "p4pS")
    pd = k.ps(ph, [128, 8, 2], F32, "p4pd")
    ioff = P.coff["ident"]

    def cum_mm(name, logf, n):
        for hf in range(2):
            k.pe(lambda e: e.matmul(pc[:n, hf * 512:(hf + 1) * 512], lhsT=C(name)[:n, :n], rhs=logf[:n, hf * 512:(hf + 1) * 512],
                                    start=True, stop=True), R=[P.cst, logf], W=[pc])

    def transp(src, n):
        for h in range(8):
            k.pe(lambda e: e.transpose(out=ptp[:, h, :n], in_=src[:n, h * 128:(h + 1) * 128], identity=P.identb[:n, :n]),
                 R=[src, P.identb], W=[ptp])

    for s in P.seqs:
        b = s["b"]
        if b is None:
            k.dve(lambda e: e.memset(St[:, :, :], 0.0), W=[St])
        else:
            k.load("sp", St[:, :, :], I["sth"][l, b].rearrange("h k v -> k h v"), St)
        k.act(lambda e: e.activation(out=Sb[:, :, :], in_=St[:, :, :], func=AF.Copy), R=[St], W=[Sb])
        for tl in s["tiles"]:
            n, g = tl["n"], tl["g"]
            nch = n // 64
            bq, bf_, bi = ldr.next(), ldr.next(), ldr.next()
            k.load("sp", bq[:n, :], P.pj(g, g + n, O_BQ, O_BQ + D), bq)
            k.load("sp", bf_[:n, :], P.pj(g, g + n, O_BF, O_BF + D), bf_)
            k.load("sp", bi[:n, :], P.pj(g, g + n, O_BI, O_BI + D), bi)
            sg, t1, f, kin, logf, q = (f32r.next() for _ in range(6))
            k.act(lambda e: e.activation(out=sg[:n, :], in_=bf_[:n, :], func=AF.Sigmoid), R=[bf_], W=[sg])
            k.dve(lambda e: e.tensor_tensor(out=t1[:n, :], in0=sg[:n, :], in1=oml[:n, :], op=ALU.mult), R=[sg, oml], W=[t1])
            k.pool(lambda e: e.tensor_tensor(out=f[:n, :], in0=t1[:n, :], in1=lbr[:n, :], op=ALU.add), R=[t1, lbr], W=[f])
            k.dve(lambda e: e.tensor_tensor(out=kin[:n, :], in0=oml[:n, :], in1=t1[:n, :], op=ALU.subtract), R=[t1, oml], W=[kin])
            k.act(lambda e: e.activation(out=logf[:n, :], in_=f[:n, :], func=AF.Ln), R=[f], W=[logf])
            k.act(lambda e: e.activation(out=q[:n, :], in_=bq[:n, :], func=AF.Silu), R=[bq], W=[q])
            qt_, qh, kh, kt_, ib = (b16r.next() for _ in range(5))
            k.pool(lambda e: e.tensor_copy(out=ib[:n, :], in_=bi[:n, :]), R=[bi], W=[ib])
            cum_mm("h_ut", logf, n)
            e1 = er.next()
            k.act(lambda e: e.activation(out=e1[:n, :], in_=pc[:n, :], func=AF.Exp), R=[pc], W=[e1])
            k.dve(lambda e: e.tensor_tensor(out=qt_[:n, :], in0=q[:n, :], in1=e1[:n, :], op=ALU.mult), R=[q, e1], W=[qt_])
            cum_mm("h_d1", logf, n)
            e2, e3 = er.next(), er.next()
            k.act(lambda e: e.activation(out=e2[:n, :], in_=pc[:n, :], func=AF.Exp), R=[pc], W=[e2])
            k.act(lambda e: e.activation(out=e3[:n, :], in_=pc[:n, :], func=AF.Exp, scale=-1.0), R=[pc], W=[e3])
            k.dve(lambda e: e.tensor_tensor(out=qh[:n, :], in0=q[:n, :], in1=e2[:n, :], op=ALU.mult), R=[q, e2], W=[qh])
            k.pool(lambda e: e.tensor_tensor(out=kh[:n, :], in0=kin[:n, :], in1=e3[:n, :], op=ALU.mult), R=[kin, e3], W=[kh])
            cum_mm("h_d2", logf, n)
            e4 = er.next()
            k.act(lambda e: e.activation(out=e4[:n, :], in_=pc[:n, :], func=AF.Exp), R=[pc], W=[e4])
            k.dve(lambda e: e.tensor_tensor(out=kt_[:n, :], in0=kin[:n, :], in1=e4[:n, :], op=ALU.mult), R=[kin, e4], W=[kt_])
            cum_mm("h_end", logf, n)
            e5 = er.next()
            k.act(lambda e: e.activation(out=e5[:n, :], in_=pc[:n, :], func=AF.Exp), R=[pc], W=[e5])
            for h in range(8):
                k.pe(lambda e: e.matmul(pd[:, h, :nch], lhsT=e5[:n, h * 128:(h + 1) * 128],
                                        rhs=P.cst[:n, ioff:ioff + 64 * nch:64], start=True, stop=True),
                     R=[e5, P.cst], W=[pd])
            dC = dcr.next()
            k.dve(lambda e: e.tensor_copy(out=dC[:, :, :nch], in_=pd[:, :, :nch]), R=[pd], W=[dC])
            transp(qh, n)
            qhT = tTr.next()
            k.act(lambda e: e.activation(out=qhT[:, :, :n], in_=ptp[:, :, :n], func=AF.Copy), R=[ptp], W=[qhT])
            transp(kh, n)
            khT = tTr.next()
            k.dve(lambda e: e.tensor_copy(out=khT[:, :, :n], in_=ptp[:, :, :n]), R=[ptp], W=[khT])
            transp(qt_, n)
            qp = qpr.next()
            k.act(lambda e: e.activation(out=qp[:, :, 0, 0:64], in_=ptp[:, :, 0:64], func=AF.Copy), R=[ptp], W=[qp])
            if nch == 2:
                k.dve(lambda e: e.tensor_copy(out=qp[:, :, 1, 64:128], in_=ptp[:, :, 64:128]), R=[ptp], W=[qp])
            scm = scmr.next()
            k.dve(lambda e: e.memset(po[:, :, :], 0.0), W=[po])
            for hg in range(2):
                for hh in range(4):
                    h = hg * 4 + hh
                    k.pe(lambda e: e.matmul(psc[:n, hh, :n], lhsT=khT[:, h, :n], rhs=qhT[:, h, :n], start=True, stop=True),
                         R=[khT, qhT], W=[psc])
                k.dve(lambda e: e.tensor_tensor(out=scm[:n, hg * 4:hg * 4 + 4, :n], in0=psc[:n, :, :n],
                                                in1=bc(C("h_mask")[:n, :n].unsqueeze(1), [n, 4, n]), op=ALU.mult),
                      R=[psc, P.cst], W=[scm])
            for h in range(8):
                hs = slice(h * 128, (h + 1) * 128)
                k.pe(lambda e: e.matmul(po[:n, h, :], lhsT=scm[:n, h, :n], rhs=ib[:n, hs], start=False, stop=False,
                                        skip_group_check=True), R=[scm, ib], W=[po])
                k.pe(lambda e: e.matmul(po[:n, h, :], lhsT=qp[:, h, 0, :n], rhs=Sb[:, h, :], start=False, stop=(nch == 1),
                                        skip_group_check=True), R=[qp, Sb], W=[po])
            for c in range(nch):
                rows = slice(c * 64, (c + 1) * 64)
                for hg in range(2):
                    for hh in range(4):
                        h = hg * 4 + hh
                        hs = slice(h * 128, (h + 1) * 128)
                        k.pe(lambda e: e.matmul(pS[:, hh, :], lhsT=kt_[rows, hs], rhs=ib[rows, hs], start=True, stop=True),
                             R=[kt_, ib], W=[pS])
                    hsl = slice(hg * 4, hg * 4 + 4)
                    k.dve(lambda e: e.tensor_tensor(out=St[:, hsl, :], in0=St[:, hsl, :],
                                                    in1=bc(dC[:, hsl, c:c + 1], [128, 4, 128]), op=ALU.mult),
                          R=[St, dC], W=[St])
                    k.dve(lambda e: e.tensor_tensor(out=St[:, hsl, :], in0=St[:, hsl, :], in1=pS[:, :, :], op=ALU.add),
                          R=[St, pS], W=[St])
                    k.act(lambda e: e.activation(out=Sb[:, hsl, :], in_=St[:, hsl, :], func=AF.Copy), R=[St], W=[Sb])
                if c == 0 and nch == 2:
                    for h in range(8):
                        k.pe(lambda e: e.matmul(po[:n, h, :], lhsT=qp[:, h, 1, :n], rhs=Sb[:, h, :], start=False, stop=True,
                                                skip_group_check=True), R=[qp, Sb], W=[po])
            sq = f32r.next()
            k.act(lambda e: e.activation(out=sq[:n, :], in_=po[:n, :, :].rearrange("p h d -> p (h d)"), func=AF.Square),
                  R=[po], W=[sq])
            ss = ssr.next()
            k.dve(lambda e: e.tensor_reduce(out=ss[:n, :], in_=sq[:n, :].rearrange("p (h d) -> p h d", h=8), axis=AX.X, op=ALU.add),
                  R=[sq], W=[ss])
            rsqrt(k, ss[:n, :], ss[:n, :], [ss], [ss], 1.0 / 128, EPS)
            ob = f32r.next()
            ob3 = ob[:n, :].rearrange("p (h d) -> p h d", h=8)
            k.dve(lambda e: e.tensor_tensor(out=ob3, in0=po[:n, :, :], in1=bc(ss[:n, :].unsqueeze(2), [n, 8, 128]), op=ALU.mult),
                  R=[po, ss], W=[ob])
            k.pool(lambda e: e.tensor_tensor(out=ob3, in0=ob3, in1=bc(gn[:n, :].unsqueeze(1), [n, 8, 128]), op=ALU.mult),
                   R=[ob, gn], W=[ob])
            k.store("pool", S["ob"][g:g + n, :], ob[:n, :], ob)
        hdst = O["hg_p"][l] if b is None else O["hg_s"][l, b]
        k.store("pool", hdst.rearrange("h k v -> k h v"), St[:, :, :], St)
    k.end_phase(ph)


import os
P5CUT = int(os.environ.get("P5CUT", "0"))
P5NOFIN = int(os.environ.get("P5NOFIN", "0"))
P5STEPS = int(os.environ.get("P5STEPS", "-1"))
P5SUB = int(os.environ.get("P5SUB", "9"))
P5EXP = int(os.environ.get("P5EXP", "0"))


def phase5(P, l):
    k, I, S, O, C = P.k, P.I, P.S, P.O, P.C
    ph = k.phase()
    NEG_E = -math.exp(-0.5)
    mu = bcast_row(k, ph, I["c_shift_mu"][l], CW, "p5mu")
    w0 = bcast_row(k, ph, I["c_w0"][l], D, "p5w0")
    a0 = bcast_row(k, ph, I["c_a0"][l], D, "p5a0")
    kkr = bcast_row(k, ph, I["c_k_k"][l], D, "p5kk")
    kar = bcast_row(k, ph, I["c_k_a"][l], D, "p5ka")
    rkr = bcast_row(k, ph, I["c_r_k"][l], D, "p5rk")
    lnw = bcast_row(k, ph, I["c_ln_w"][l], D, "p5lnw")
    lnb = bcast_row(k, ph, I["c_ln_b"][l], D, "p5lnb")
    if l == 1:
        v0 = bcast_row(k, ph, I["c_v0"][0], D, "p5v0")
    tmpr = Rot(k, ph, 4, [128, D], F32, "p5tmp")
    stg = tmpr.tiles[0]
    w2b = k.sb(ph, [64, D], BF16, "p5w2")
    a2b = k.sb(ph, [64, D], BF16, "p5a2")
    k.load("sp", stg[:64, :], I["c_w2"][l], stg)
    k.dve(lambda e: e.tensor_copy(out=w2b[:, :], in_=stg[:64, :]), R=[stg], W=[w2b])
    k.load("sp", stg[:64, :], I["c_a2"][l], stg)
    k.dve(lambda e: e.tensor_copy(out=a2b[:, :], in_=stg[:64, :]), R=[stg], W=[a2b])
    if l == 1:
        v2b = k.sb(ph, [32, D], BF16, "p5v2w")
        k.load("sp", stg[:32, :], I["c_vres_w2"][0], stg)
        k.dve(lambda e: e.tensor_copy(out=v2b[:, :], in_=stg[:32, :]), R=[stg], W=[v2b])
    mk = k.sb(ph, [128, 4, 128], F32, "p5mk")
    for i_, nm in enumerate(["r_uts", "r_ut", "r_uts", "r_ut"]):
        k.dve(lambda e: e.tensor_copy(out=mk[:, i_, :], in_=C(nm)), R=[P.cst], W=[mk])

    mz = k.sb(ph, [128, 4, 128], F32, "p5mz")
    i4 = k.sb(ph, [128, 4, 128], BF16, "p5i4")
    for i_ in range(4):
        k.dve(lambda e: e.tensor_copy(out=mz[:, i_, :], in_=C("r_low")), R=[P.cst], W=[mz])
        k.dve(lambda e: e.tensor_copy(out=i4[:, i_, :], in_=P.identb[:, :]), R=[P.identb], W=[i4])
    cp = k.sb(ph, [128, CW], F32, "p5cp")
    cs = k.sb(ph, [128, CW], F32, "p5cs")
    hv = k.sb(ph, [128, 32], F32, "p5hv")
    vft = k.sb(ph, [128, D], F32, "p5vf")
    k2 = k.sb(ph, [128, D], F32, "p5k2")
    v2 = k.sb(ph, [128, D], F32, "p5v2")
    asig = k.sb(ph, [128, D], F32, "p5as")
    kk = k.sb(ph, [128, D], F32, "p5kkt")
    bs = k.sb(ph, [128, D], F32, "p5bs")
    ld = k.sb(ph, [128, D], F32, "p5ld")
    yt = k.sb(ph, [128, D], F32, "p5y")
    yn = k.sb(ph, [128, D], F32, "p5yn")
    s16 = Rot(k, ph, 6, [128, 16], F32, "p5s16")
    smb = k.sb(ph, [128, 3, 64], BF16, "p5smb")
    smT = k.sb(ph, [64, 3, 128], BF16, "p5smT")
    rt_, at_, bt_, kt_, bb_, kb_, vb_ = (k.sb(ph, [128, D], BF16, "p5b%d" % i_) for i_ in range(7))
    arT = k.sb(ph, [128, 8, 2, 128], BF16, "p5arT")
    bT = k.sb(ph, [128, 8, 128], BF16, "p5bT")
    kT = k.sb(ph, [128, 8, 128], BF16, "p5kT")
    AM = k.sb(ph, [128, 16, 4, 128], BF16, "p5AM")
    Qt = [k.sb(ph, [128, 4, 128], BF16, "p5Q%d" % i_) for i_ in range(4)]
    yzr = Rot(k, ph, 18, [128, 4, 128], BF16, "p5yz")
    R1 = k.sb(ph, [128, 16, 64], BF16, "p5R1")
    Ub = k.sb(ph, [128, 16, 64], BF16, "p5Ub")
    H = k.sb(ph, [128, 8, 64], F32, "p5H")
    Hb = k.sb(ph, [128, 8, 128], BF16, "p5Hb")
    k.dve(lambda e: e.memset(Hb[:, :, :], 0.0), W=[Hb])

    def refresh_hb():
        for e_ in range(2):
            rows = slice(e_ * 64, (e_ + 1) * 64)
            k.act(lambda e: e.activation(out=Hb[rows, :, e_ * 64:(e_ + 1) * 64], in_=H[rows, :, :], func=AF.Copy), R=[H], W=[Hb])
    wc = k.sb(ph, [128, 8], F32, "p5wc")
    B01 = k.ps(ph, [128, D], F32, "p5B01")
    Bt = k.ps(ph, [128, 8, 128], BF16, "p5Bt")
    gbank = Rot(k, ph, 5, [128, 512], F32, "p5g", psum=True)
    if os.environ.get("KDEBUG"):
        print("phase5 sbuf bytes remaining", P.nc.sbuf_bytes_remaining)

    def v4(bank):
        return bank[:, :].rearrange("p (a b) -> p a b", a=4)

    def v8(bank):
        return bank[:, :].rearrange("p (a b) -> p a b", a=8)

    def small_mm(col, wts, kdim, n, bias, out, func):
        for hf in range(2):
            k.pe(lambda e: e.matmul(B01[:n, hf * 512:(hf + 1) * 512], lhsT=smT[0:kdim, col, :n],
                                    rhs=wts[0:kdim, hf * 512:(hf + 1) * 512], start=True, stop=True),
                 R=[smT, wts], W=[B01])
        t = tmpr.next()
        k.dve(lambda e: e.tensor_tensor(out=t[:n, :], in0=B01[:n, :], in1=bias[:n, :], op=ALU.add), R=[B01, bias], W=[t])
        k.act(lambda e: e.activation(out=out[:n, :], in_=t[:n, :], func=func), R=[t], W=[out])

    def cum_exp(name, n, outs):
        for hf in range(2):
            k.pe(lambda e: e.matmul(B01[:n, hf * 512:(hf + 1) * 512], lhsT=C(name)[:n, :n], rhs=ld[:n, hf * 512:(hf + 1) * 512],
                                    start=True, stop=True), R=[P.cst, ld], W=[B01])
        for (t, sc) in outs:
            k.act(lambda e: e.activation(out=t[:n, :], in_=B01[:n, :], func=AF.Exp, scale=sc), R=[B01], W=[t])

    def transp8(src, n, dst_ap, dst_t, eng):
        for p in range(8):
            k.pe(lambda e: e.transpose(out=Bt[:, p, :n], in_=src[:n, p * 128:(p + 1) * 128], identity=P.identb[:n, :n]),
                 R=[src, P.identb], W=[Bt])
        if eng == "act":
            k.act(lambda e: e.activation(out=dst_ap, in_=Bt[:, :, :n], func=AF.Copy), R=[Bt], W=[dst_t])
        else:
            k.dve(lambda e: e.tensor_copy(out=dst_ap, in_=Bt[:, :, :n]), R=[Bt], W=[dst_t])

    for s in P.seqs:
        b = s["b"]
        if b is None:
            k.dve(lambda e: e.memset(H[:, :, :], 0.0), W=[H])
        else:
            Sld_t = tmpr.next()
            Sld = Sld_t[0:64, :].rearrange("p (a b) -> p a b", a=16)
            k.load("sp", Sld, I["str"][l, b].rearrange("h i j -> i h j"), Sld_t)
            for half in range(2):
                g_ = gbank.next()
                g4 = g_[:, :].rearrange("p (a b) -> p a b", a=4)
                for pp in range(4):
                    p = half * 4 + pp
                    k.pe(lambda e: e.transpose(out=g4[:, pp, 0:64], in_=Sld_t[0:64, 2 * p * 64:(2 * p + 2) * 64],
                                               identity=C("ident")[:64, :64]), R=[Sld_t, P.cst], W=[g_])
                k.dve(lambda e: e.tensor_copy(out=H[:, half * 4:half * 4 + 4, :], in_=g4[:, :, 0:64]), R=[g_], W=[H])
        refresh_hb()
        ntl = len(s["tiles"])
        for ti, tl in enumerate(s["tiles"]):
            n, g, t0 = tl["n"], tl["g"], tl["t0"]
            k.load("sp", cp[:n, :], P.pj(g, g + n, O_CP, O_CP + CW), cp)
            if t0 == 0:
                if b is None:
                    k.dve(lambda e: e.memset(cs[0:1, :], 0.0), W=[cs])
                else:
                    k.load("sp", cs[0:1, :], I["stsh"][l, b:b + 1, :], cs)
                k.load("sp", cs[1:n, :], P.pj(g, g + n - 1, O_CP, O_CP + CW), cs)
            else:
                k.load("sp", cs[:n, :], P.pj(g - 1, g + n - 1, O_CP, O_CP + CW), cs)
            if ti == ntl - 1:
                sdst = O["sh_p"][l:l + 1, :] if b is None else O["sh_s"][l, b:b + 1, :]
                k.store("pool", sdst, cp[n - 1:n, :], cp)
            k.pool(lambda e: e.tensor_tensor(out=cs[:n, :], in0=cs[:n, :], in1=cp[:n, :], op=ALU.subtract), R=[cs, cp], W=[cs])
            k.dve(lambda e: e.tensor_tensor(out=cs[:n, :], in0=cs[:n, :], in1=mu[:n, :], op=ALU.mult), R=[cs, mu], W=[cs])
            k.pool(lambda e: e.tensor_tensor(out=cs[:n, :], in0=cs[:n, :], in1=cp[:n, :], op=ALU.add), R=[cs, cp], W=[cs])
            r_ = cs[:n, C_R:C_R + D]
            kx = cs[:n, C_K:C_K + D]
            vx = cs[:n, C_V:C_V + D]
            if P5CUT and P5CUT <= 1:
                continue
            k.act(lambda e: e.activation(out=smb[:n, 0, :], in_=cs[:n, C_WLO:C_WLO + 64], func=AF.Tanh), R=[cs], W=[smb])
            k.act(lambda e: e.activation(out=smb[:n, 1, :], in_=cs[:n, C_ALO:C_ALO + 64], func=AF.Copy), R=[cs], W=[smb])
            if l == 1:
                k.load("sp", hv[:n, :], P.pj(g, g + n, O_EXT, O_EXT + 32), hv)
                k.act(lambda e: e.activation(out=smb[:n, 2, 0:32], in_=hv[:n, :], func=AF.Copy), R=[hv], W=[smb])
            for c_ in range(3 if l == 1 else 2):
                kd = 32 if c_ == 2 else 64
                k.pe(lambda e: e.transpose(out=Bt[0:kd, c_, :n], in_=smb[:n, c_, 0:kd], identity=P.identb[:n, :n]),
                     R=[smb, P.identb], W=[Bt])
            k.dve(lambda e: e.tensor_copy(out=smT[:, 0:2, :n], in_=Bt[0:64, 0:2, :n]), R=[Bt], W=[smT])
            if l == 1:
                k.dve(lambda e: e.tensor_copy(out=smT[0:32, 2, :n], in_=Bt[0:32, 2, :n]), R=[Bt], W=[smT])
            sgw = tmpr.next()
            small_mm(0, w2b, 64, n, w0, sgw, AF.Sigmoid)
            k.dve(lambda e: e.tensor_scalar(out=ld[:n, :], in0=sgw[:n, :], scalar1=NEG_E, scalar2=None, op0=ALU.mult),
                  R=[sgw], W=[ld])
            small_mm(1, a2b, 64, n, a0, asig, AF.Sigmoid)
            if l == 1:
                vmix = tmpr.next()
                small_mm(2, v2b, 32, n, v0, vmix, AF.Sigmoid)
                k.load("sp", vft[:n, :], S["vf"][g:g + n, :], vft)
                k.pool(lambda e: e.tensor_tensor(out=vft[:n, :], in0=vft[:n, :], in1=vx, op=ALU.subtract), R=[vft, cs], W=[vft])
                k.dve(lambda e: e.tensor_tensor(out=vft[:n, :], in0=vft[:n, :], in1=vmix[:n, :], op=ALU.mult), R=[vft, vmix], W=[vft])
                k.pool(lambda e: e.tensor_tensor(out=v2[:n, :], in0=vft[:n, :], in1=vx, op=ALU.add), R=[vft, cs], W=[v2])
            else:
                k.pool(lambda e: e.tensor_copy(out=v2[:n, :], in_=vx), R=[cs], W=[v2])
                k.store("pool", S["vf"][g:g + n, :], v2[:n, :], v2)
            if P5CUT and P5CUT <= 2:
                continue
            k.dve(lambda e: e.tensor_tensor(out=kk[:n, :], in0=kx, in1=kkr[:n, :], op=ALU.mult), R=[cs, kkr], W=[kk])
            t = tmpr.next()
            k.pool(lambda e: e.tensor_tensor(out=t[:n, :], in0=kk[:n, :], in1=kk[:n, :], op=ALU.mult), R=[kk], W=[t])
            sk = s16.next()
            k.dve(lambda e: e.tensor_reduce(out=sk[:n, :], in_=t[:n, :].rearrange("p (h d) -> p h d", h=16), axis=AX.X, op=ALU.add),
                  R=[t], W=[sk])
            k.act(lambda e: e.activation(out=sk[:n, :], in_=sk[:n, :], func=AF.Sqrt), R=[sk], W=[sk])
            k.dve(lambda e: e.tensor_scalar(out=sk[:n, :], in0=sk[:n, :], scalar1=1e-12, scalar2=None, op0=ALU.max), R=[sk], W=[sk])
            k.dve(lambda e: e.reciprocal(out=sk[:n, :], in_=sk[:n, :]), R=[sk], W=[sk])
            kk3 = kk[:n, :].rearrange("p (h d) -> p h d", h=16)
            k.dve(lambda e: e.tensor_tensor(out=kk3, in0=kk3, in1=bc(sk[:n, :].unsqueeze(2), [n, 16, 64]), op=ALU.mult),
                  R=[kk, sk], W=[kk])
            t = tmpr.next()
            k.dve(lambda e: e.scalar_tensor_tensor(out=t[:n, :], in0=asig[:n, :], scalar=-1.0, in1=kar[:n, :],
                                                   op0=ALU.add, op1=ALU.mult), R=[asig, kar], W=[t])
            k.pool(lambda e: e.tensor_tensor(out=t[:n, :], in0=t[:n, :], in1=kx, op=ALU.mult), R=[t, cs], W=[t])
            k.pool(lambda e: e.tensor_tensor(out=k2[:n, :], in0=t[:n, :], in1=kx, op=ALU.add), R=[t, cs], W=[k2])
            k.pool(lambda e: e.tensor_tensor(out=bs[:n, :], in0=kk[:n, :], in1=asig[:n, :], op=ALU.mult), R=[kk, asig], W=[bs])
            if P5CUT and P5CUT <= 3:
                continue
            t = tmpr.next()
            k.pool(lambda e: e.tensor_tensor(out=t[:n, :], in0=r_, in1=k2[:n, :], op=ALU.mult), R=[cs, k2], W=[t])
            k.dve(lambda e: e.tensor_tensor(out=t[:n, :], in0=t[:n, :], in1=rkr[:n, :], op=ALU.mult), R=[t, rkr], W=[t])
            s3 = s16.next()
            k.dve(lambda e: e.tensor_reduce(out=s3[:n, :], in_=t[:n, :].rearrange("p (h d) -> p h d", h=16), axis=AX.X, op=ALU.add),
                  R=[t], W=[s3])
            k.dve(lambda e: e.tensor_tensor(out=yn[:n, :].rearrange("p (h d) -> p h d", h=16),
                                            in0=v2[:n, :].rearrange("p (h d) -> p h d", h=16),
                                            in1=bc(s3[:n, :].unsqueeze(2), [n, 16, 64]), op=ALU.mult), R=[v2, s3], W=[yn])
            k.pool(lambda e: e.tensor_copy(out=vb_[:n, :], in_=v2[:n, :]), R=[v2], W=[vb_])
            ep, en = tmpr.next(), tmpr.next()
            cum_exp("r_ut", n, [(ep, 1.0), (en, -1.0)])
            k.dve(lambda e: e.tensor_tensor(out=rt_[:n, :], in0=r_, in1=ep[:n, :], op=ALU.mult), R=[cs, ep], W=[rt_])
            k.dve(lambda e: e.tensor_tensor(out=bt_[:n, :], in0=bs[:n, :], in1=en[:n, :], op=ALU.mult), R=[bs, en], W=[bt_])
            k.pool(lambda e: e.tensor_tensor(out=kt_[:n, :], in0=k2[:n, :], in1=en[:n, :], op=ALU.mult), R=[k2, en], W=[kt_])
            epa = tmpr.next()
            cum_exp("r_uts", n, [(epa, 1.0)])
            k.dve(lambda e: e.scalar_tensor_tensor(out=at_[:n, :], in0=kk[:n, :], scalar=-1.0, in1=epa[:n, :],
                                                   op0=ALU.mult, op1=ALU.mult), R=[kk, epa], W=[at_])
            eend = tmpr.next()
            cum_exp("r_low", n, [(eend, 1.0)])
            k.dve(lambda e: e.tensor_tensor(out=bb_[:n, :], in0=bs[:n, :], in1=eend[:n, :], op=ALU.mult), R=[bs, eend], W=[bb_])
            k.pool(lambda e: e.tensor_tensor(out=kb_[:n, :], in0=k2[:n, :], in1=eend[:n, :], op=ALU.mult), R=[k2, eend], W=[kb_])
            gw = gbank.next()
            for p in range(8):
                k.pe(lambda e: e.matmul(gw[:, p:p + 1], lhsT=ld[:n, p * 128:(p + 1) * 128], rhs=C("r_one")[:n, 0:1],
                                        start=True, stop=True), R=[ld, P.cst], W=[gw])
            k.act(lambda e: e.activation(out=wc[:, :], in_=gw[:, 0:8], func=AF.Exp), R=[gw], W=[wc])
            if P5CUT and P5CUT <= 4:
                continue
            transp8(at_, n, arT[:, :, 0, :n], arT, "act")
            transp8(rt_, n, arT[:, :, 1, :n], arT, "dve")
            transp8(bt_, n, bT[:, :, :n], bT, "act")
            transp8(kt_, n, kT[:, :, :n], kT, "dve")
            if P5CUT and P5CUT <= 5:
                continue
            Ycur, Zcur = [None] * 4, [None] * 4
            for hg in range(4):
                for hh in range(4):
                    hd = hg * 4 + hh
                    p, base = hd // 2, (hd % 2) * 64
                    bs_ = slice(base, base + 64)
                    gm = gbank.next()
                    m4 = v4(gm)
                    if n == 128:
                        k.pe(lambda e: e.matmul(m4[:n, 0:2, :n], lhsT=bT[bs_, p, :n], rhs=arT[bs_, p, :, :n], start=True, stop=True),
                             R=[bT, arT], W=[gm])
                        k.pe(lambda e: e.matmul(m4[:n, 2:4, :n], lhsT=kT[bs_, p, :n], rhs=arT[bs_, p, :, :n], start=True, stop=True),
                             R=[kT, arT], W=[gm])
                    else:
                        for w_ in range(2):
                            k.pe(lambda e: e.matmul(m4[:n, w_, :n], lhsT=bT[bs_, p, :n], rhs=arT[bs_, p, w_, :n], start=True, stop=True),
                                 R=[bT, arT], W=[gm])
                            k.pe(lambda e: e.matmul(m4[:n, 2 + w_, :n], lhsT=kT[bs_, p, :n], rhs=arT[bs_, p, w_, :n], start=True, stop=True),
                                 R=[kT, arT], W=[gm])
                    k.dve(lambda e: e.tensor_tensor(out=AM[:n, hd, :, :n], in0=m4[:n, :, :n], in1=mk[:n, :, :n], op=ALU.mult),
                          R=[gm, mk], W=[AM])
            for hg in range(4):
                hsl = slice(hg * 4, hg * 4 + 4)
                for hh in range(4):
                    hd = hg * 4 + hh
                    k.pe(lambda e: e.transpose(out=Bt[:n, hg * 4 + hh - (hg // 2) * 8, :n], in_=AM[:n, hd, 0, :n], identity=P.identb[:n, :n]),
                         R=[AM, P.identb], W=[Bt])
                if hg % 2 == 1:
                    for h2 in range(2):
                        hgg = hg - 1 + h2
                        Z = yzr.next()
                        k.act(lambda e: e.activation(out=Z[:n, :, :n], in_=Bt[:n, h2 * 4:h2 * 4 + 4, :n], func=AF.Copy), R=[Bt], W=[Z])
                        Zcur[hgg] = Z
                k.dve(lambda e: e.tensor_tensor(out=Qt[hg][:n, :, :n], in0=AM[:n, hsl, 0, :n], in1=i4[:n, :, :n], op=ALU.add),
                      R=[AM, i4], W=[Qt[hg]])
            nsteps = 6 if n == 128 else 5
            for step in range(1, nsteps + 1):
                gys, gzs = [None] * 4, [None] * 4
                for hg in range(4):
                    Y, Z = Ycur[hg], Zcur[hg]
                    gy = gbank.next() if step < nsteps else None
                    gzz = gbank.next()
                    for hh in range(4):
                        hd = hg * 4 + hh
                        ysrc = AM[:n, hd, 0, :n] if Y is None else Y[:n, hh, :n]
                        ytk = AM if Y is None else Y
                        if gy is not None:
                            k.pe(lambda e: e.matmul(v4(gy)[:n, hh, :n], lhsT=Z[:n, hh, :n], rhs=ysrc, start=True, stop=True),
                                 R=[Z, ytk], W=[gy])
                        k.pe(lambda e: e.matmul(v4(gzz)[:n, hh, :n], lhsT=ysrc, rhs=Z[:n, hh, :n], start=True, stop=True),
                             R=[Z, ytk], W=[gzz])
                    Zn = yzr.next()
                    k.act(lambda e: e.activation(out=Zn[:n, :, :n], in_=v4(gzz)[:n, :, :n], func=AF.Copy), R=[gzz], W=[Zn])
                    if gy is not None:
                        Yn = yzr.next()
                        k.act(lambda e: e.activation(out=Yn[:n, :, :n], in_=v4(gy)[:n, :, :n], func=AF.Copy), R=[gy], W=[Yn])
                    else:
                        Yn = None
                    Ycur[hg], Zcur[hg] = Yn, Zn
                for hg in range(4):
                    hsl = slice(hg * 4, hg * 4 + 4)
                    Zn = Zcur[hg]
                    gq = gbank.next()
                    ZI = yzr.next()
                    k.dve(lambda e: e.tensor_tensor(out=ZI[:n, :, :n], in0=Zn[:n, :, :n], in1=i4[:n, :, :n], op=ALU.add),
                          R=[Zn, i4], W=[ZI])
                    for hh in range(4):
                        hd = hg * 4 + hh
                        k.pe(lambda e: e.matmul(v4(gq)[:n, hh, :n], lhsT=ZI[:n, hh, :n], rhs=Qt[hg][:n, hh, :n], start=True, stop=True),
                             R=[ZI, Qt[hg]], W=[gq])
                    k.act(lambda e: e.activation(out=Qt[hg][:n, :, :n], in_=v4(gq)[:n, :, :n], func=AF.Copy), R=[gq], W=[Qt[hg]])
            if P5CUT and P5CUT <= 6:
                continue
            for half in range(2):
                g1 = gbank.next()
                for pp in range(4):
                    p = half * 4 + pp
                    k.pe(lambda e: e.matmul(v4(g1)[:n, pp, :], lhsT=arT[:, p, 0, :n], rhs=Hb[:, p, :], start=True, stop=False),
                         R=[arT, Hb], W=[g1])
                    for e_ in range(2):
                        hd = 2 * p + e_
                        k.pe(lambda e: e.matmul(v8(g1)[:n, 2 * pp + e_, :], lhsT=AM[:n, hd, 2, :n], rhs=vb_[:n, hd * 64:(hd + 1) * 64],
                                                start=False, stop=(e_ == 1)), R=[AM, vb_], W=[g1])
                k.act(lambda e: e.activation(out=R1[:n, half * 8:half * 8 + 8, :], in_=v8(g1)[:n, :, :], func=AF.Copy), R=[g1], W=[R1])
            if P5CUT == 65:
                continue
            for half in range(2):
                g2 = gbank.next()
                for h8 in range(8):
                    hd = half * 8 + h8
                    k.pe(lambda e: e.matmul(v8(g2)[:n, h8, :], lhsT=Qt[hd // 4][:n, hd % 4, :n], rhs=R1[:n, hd, :], start=True, stop=True),
                         R=[Qt[hd // 4], R1], W=[g2])
                k.act(lambda e: e.activation(out=Ub[:n, half * 8:half * 8 + 8, :], in_=v8(g2)[:n, :, :], func=AF.Copy), R=[g2], W=[Ub])
            if P5CUT and P5CUT <= 7:
                continue
            for half in range(2):
                g3 = gbank.next()
                for pp in range(4):
                    p = half * 4 + pp
                    k.pe(lambda e: e.matmul(v4(g3)[:n, pp, :], lhsT=arT[:, p, 1, :n], rhs=Hb[:, p, :], start=True, stop=False),
                         R=[arT, Hb], W=[g3])
                    for e_ in range(2):
                        hd = 2 * p + e_
                        k.pe(lambda e: e.matmul(v8(g3)[:n, 2 * pp + e_, :], lhsT=AM[:n, hd, 1, :n], rhs=Ub[:n, hd, :], start=False, stop=False),
                             R=[AM, Ub], W=[g3])
                        k.pe(lambda e: e.matmul(v8(g3)[:n, 2 * pp + e_, :], lhsT=AM[:n, hd, 3, :n], rhs=vb_[:n, hd * 64:(hd + 1) * 64],
                                                start=False, stop=(e_ == 1)), R=[AM, vb_], W=[g3])
                k.act(lambda e: e.activation(out=yt[:n, half * 512:(half + 1) * 512], in_=g3[:n, :], func=AF.Copy), R=[g3], W=[yt])
            if P5CUT and P5CUT <= 8:
                continue
            for half in range(2):
                g4_ = gbank.next()
                for pp in range(4):
                    p = half * 4 + pp
                    ps_ = slice(p * 128, (p + 1) * 128)
                    k.pe(lambda e: e.matmul(v4(g4_)[:, pp, :], lhsT=bb_[:n, ps_], rhs=Ub[:n, 2 * p:2 * p + 2, :].rearrange("p a b -> p (a b)"),
                                            start=True, stop=False), R=[bb_, Ub], W=[g4_])
                    k.pe(lambda e: e.matmul(v4(g4_)[:, pp, :], lhsT=kb_[:n, ps_], rhs=vb_[:n, ps_], start=False, stop=True),
                         R=[kb_, vb_], W=[g4_])
                hs_ = slice(half * 4, half * 4 + 4)
                hst = tmpr.next()
                hst4 = hst[:, 0:512].rearrange("p (a b) -> p a b", a=4)
                k.act(lambda e: e.activation(out=hst[:, 0:512], in_=g4_[:, :], func=AF.Copy), R=[g4_], W=[hst])
                for e_ in range(2):
                    rows = slice(e_ * 64, (e_ + 1) * 64)
                    k.dve(lambda e: e.tensor_tensor(out=H[rows, hs_, :], in0=H[rows, hs_, :],
                                                    in1=bc(wc[rows, hs_].unsqueeze(2), [64, 4, 64]), op=ALU.mult),
                          R=[H, wc], W=[H])
                    k.dve(lambda e: e.tensor_tensor(out=H[rows, hs_, :], in0=H[rows, hs_, :],
                                                    in1=hst4[rows, :, e_ * 64:(e_ + 1) * 64], op=ALU.add),
                          R=[H, hst], W=[H])
            refresh_hb()
            if P5CUT and P5CUT <= 9:
                continue
            y3 = yt[:n, :].rearrange("p (h d) -> p h d", h=16)
            s1 = s16.next()
            k.dve(lambda e: e.tensor_reduce(out=s1[:n, :], in_=y3, axis=AX.X, op=ALU.add), R=[yt], W=[s1])
            k.dve(lambda e: e.tensor_scalar(out=s1[:n, :], in0=s1[:n, :], scalar1=1.0 / 64, scalar2=None, op0=ALU.mult), R=[s1], W=[s1])
            k.dve(lambda e: e.tensor_tensor(out=y3, in0=y3, in1=bc(s1[:n, :].unsqueeze(2), [n, 16, 64]), op=ALU.subtract),
                  R=[yt, s1], W=[yt])
            t = tmpr.next()
            k.pool(lambda e: e.tensor_tensor(out=t[:n, :], in0=yt[:n, :], in1=yt[:n, :], op=ALU.mult), R=[yt], W=[t])
            s2 = s16.next()
            k.dve(lambda e: e.tensor_reduce(out=s2[:n, :], in_=t[:n, :].rearrange("p (h d) -> p h d", h=16), axis=AX.X, op=ALU.add),
                  R=[t], W=[s2])
            rsqrt(k, s2[:n, :], s2[:n, :], [s2], [s2], 1.0 / 64, GN_EPS)
            tn = tmpr.next()
            tn3 = tn[:n, :].rearrange("p (h d) -> p h d", h=16)
            k.dve(lambda e: e.tensor_tensor(out=tn3, in0=y3, in1=bc(s2[:n, :].unsqueeze(2), [n, 16, 64]), op=ALU.mult),
                  R=[yt, s2], W=[tn])
            k.pool(lambda e: e.tensor_tensor(out=tn[:n, :], in0=tn[:n, :], in1=lnw[:n, :], op=ALU.mult), R=[tn, lnw], W=[tn])
            k.dve(lambda e: e.tensor_tensor(out=tn[:n, :], in0=tn[:n, :], in1=lnb[:n, :], op=ALU.add), R=[tn, lnb], W=[tn])
            k.pool(lambda e: e.tensor_tensor(out=yn[:n, :], in0=yn[:n, :], in1=tn[:n, :], op=ALU.add), R=[yn, tn], W=[yn])
            k.store("pool", S["oc"][g:g + n, :], yn[:n, :], yn)
        rwo_t = tmpr.next()
        rwo = rwo_t[0:64, :].rearrange("p (a b) -> p a b", a=8)
        for half in range(0 if P5NOFIN else 2):
            g_ = gbank.next()
            g4 = v4(g_)
            for pp in range(4):
                p = half * 4 + pp
                k.pe(lambda e: e.transpose(out=g4[0:64, pp, :], in_=H[:, p, :], identity=C("ident")), R=[H, P.cst], W=[g_])
            k.dve(lambda e: e.tensor_copy(out=rwo[:, half * 4:half * 4 + 4, :], in_=g4[0:64, :, :]), R=[g_], W=[rwo_t])
        rdst = O["rw_p"][l] if b is None else O["rw_s"][l, b]
        k.store("pool", rdst.rearrange("(p e) i j -> i p e j", e=2), rwo.rearrange("i p (e j) -> i p e j", e=2), rwo_t)
    k.end_phase(ph)


def phase6(P, l):
    k, I, S, O = P.k, P.I, P.S, P.O
    ph = k.phase()
    stg = Rot(k, ph, 2, [128, 2, D], F32, "p6stg")
    W = {}
    for nm in ["w_out_a", "w_out_b", "w_out_c", "w_o"]:
        wt = k.sb(ph, [128, 8, D], BF16, "p6" + nm)
        src = I[nm][l].rearrange("(k p) c -> p k c", p=128)
        for c4 in range(4):
            st = stg.next()
            k.load("sp", st[:, :, :], src[:, 2 * c4:2 * c4 + 2, :], st)
            k.dve(lambda e: e.tensor_copy(out=wt[:, 2 * c4:2 * c4 + 2, :], in_=st[:, :, :]), R=[st], W=[wt])
        W[nm] = wt
    ldr = Rot(k, ph, 12, [128, D], F32, "p6ld")
    tmpr = Rot(k, ph, 4, [128, D], F32, "p6tmp")
    mrg = Rot(k, ph, 2, [128, D], F32, "p6mrg")
    ogr = Rot(k, ph, 2, [128, D], BF16, "p6og")
    tTr = Rot(k, ph, 2, [128, 8, 128], BF16, "p6tT")
    yr = Rot(k, ph, 2, [128, D], F32, "p6y")
    pbr = Rot(k, ph, 2, [128, D], F32, "p6pb", psum=True)
    ptr = Rot(k, ph, 2, [128, 8, 128], BF16, "p6pt", psum=True)

    def proj_mm(srcb, n, wt):
        pt = ptr.next()
        for kk in range(8):
            k.pe(lambda e: e.transpose(out=pt[:, kk, :n], in_=srcb[:n, kk * 128:(kk + 1) * 128], identity=P.identb[:n, :n]),
                 R=[srcb, P.identb], W=[pt])
        tT = tTr.next()
        k.act(lambda e: e.activation(out=tT[:, :, :n], in_=pt[:, :, :n], func=AF.Copy), R=[pt], W=[tT])
        pb = pbr.next()
        for hf in range(2):
            for kk in range(8):
                k.pe(lambda e: e.matmul(pb[:n, hf * 512:(hf + 1) * 512], lhsT=tT[:, kk, :n], rhs=wt[:, kk, hf * 512:(hf + 1) * 512],
                                        start=(kk == 0), stop=(kk == 7)), R=[tT, wt], W=[pb])
        return pb

    for tl in P.tiles:
        n, g, t0 = tl["n"], tl["g"], tl["t0"]
        s = tl["seq"]
        b = s["b"]
        merged = mrg.next()
        for mi, (osrc, gcol, mcol, wn) in enumerate([("oa", O_AG, O_MA, "w_out_a"), ("ob", O_BG, O_MB, "w_out_b"),
                                                     ("oc", O_CG, O_MC, "w_out_c")]):
            o_, g_, m_ = ldr.next(), ldr.next(), ldr.next()
            k.load("sp", o_[:n, :], S[osrc][g:g + n, :], o_)
            k.load("sp", g_[:n, :], P.pj(g, g + n, gcol, gcol + D), g_)
            k.load("sp", m_[:n, :], P.pj(g, g + n, mcol, mcol + D), m_)
            sg = tmpr.next()
            k.act(lambda e: e.activation(out=sg[:n, :], in_=g_[:n, :], func=AF.Silu), R=[g_], W=[sg])
            og = ogr.next()
            k.dve(lambda e: e.tensor_tensor(out=og[:n, :], in0=o_[:n, :], in1=sg[:n, :], op=ALU.mult), R=[o_, sg], W=[og])
            pb = proj_mm(og, n, W[wn])
            sm = tmpr.next()
            k.act(lambda e: e.activation(out=sm[:n, :], in_=m_[:n, :], func=AF.Sigmoid), R=[m_], W=[sm])
            if mi == 0:
                k.dve(lambda e: e.tensor_tensor(out=merged[:n, :], in0=pb[:n, :], in1=sm[:n, :], op=ALU.mult), R=[pb, sm], W=[merged])
            else:
                k.dve(lambda e: e.tensor_tensor(out=sm[:n, :], in0=pb[:n, :], in1=sm[:n, :], op=ALU.mult), R=[pb, sm], W=[sm])
                k.pool(lambda e: e.tensor_tensor(out=merged[:n, :], in0=merged[:n, :], in1=sm[:n, :], op=ALU.add),
                       R=[merged, sm], W=[merged])
        mb = ogr.next()
        k.act(lambda e: e.activation(out=mb[:n, :], in_=merged[:n, :], func=AF.Copy), R=[merged], W=[mb])
        py = proj_mm(mb, n, W["w_o"])
        x = ldr.next()
        src, _ = P.xsrc(l, tl)
        k.load("sp", x[:n, :], src, x)
        y = yr.next()
        k.dve(lambda e: e.tensor_tensor(out=y[:n, :], in0=py[:n, :], in1=x[:n, :], op=ALU.add), R=[py, x], W=[y])
        if l == 0:
            dst = S["xmid"][g:g + n, :]
        elif b is None:
            dst = O["y_p"][t0:t0 + n, :]
        else:
            dst = O["y_s"][b, t0:t0 + n, :]
        k.store("pool", dst, y[:n, :], y)
    k.end_phase(ph)


_CACHE = {}
NCORES = 8


def _get_prog(T):
    if T not in _CACHE:
        _CACHE[T] = build(T)
    return _CACHE[T]


def kernel(x_prompt, x_sample, cache_attn_k, cache_attn_v, state_hgrn, state_rwkv, state_rwkv_shift,
           norm_g, w_in, a_qnorm_g, a_knorm_g, a_lambda, a_subln_g, b_lower, b_norm_g,
           c_shift_mu, c_w0, c_w2, c_a0, c_a2, c_k_k, c_k_a, c_r_k, c_ln_w, c_ln_b,
           c_vres_w1, c_vres_w2, c_v0, w_out_a, w_out_b, w_out_c, w_o):
    f = lambda a: np.ascontiguousarray(np.asarray(a, dtype=np.float32))
    x_prompt = f(x_prompt)
    B, T, _ = x_prompt.shape
    assert B == NCORES
    P = _get_prog(T)
    carr, _ = make_consts()
    shared = {
        "norm_g": f(norm_g), "w_in": f(w_in), "a_qnorm_g": f(a_qnorm_g), "a_knorm_g": f(a_knorm_g),
        "a_lambda": f(a_lambda).reshape(2, 256), "a_subln_g": f(a_subln_g), "b_lower": f(b_lower), "b_norm_g": f(b_norm_g),
        "c_shift_mu": f(c_shift_mu), "c_w0": f(c_w0), "c_w2": f(c_w2), "c_a0": f(c_a0), "c_a2": f(c_a2),
        "c_k_k": f(c_k_k), "c_k_a": f(c_k_a), "c_r_k": f(c_r_k).reshape(2, 1024), "c_ln_w": f(c_ln_w), "c_ln_b": f(c_ln_b),
        "c_vres_w1": f(c_vres_w1), "c_vres_w2": f(c_vres_w2), "c_v0": f(c_v0),
        "w_out_a": f(w_out_a), "w_out_b": f(w_out_b), "w_out_c": f(w_out_c), "w_o": f(w_o),
        "consts": carr,
        "rope_p": rope_tables(np.arange(T)), "rope_s": rope_tables(PAST + np.arange(TS)),
    }
    x_sample = f(x_sample)
    ck, cv = f(cache_attn_k), f(cache_attn_v)
    sth, str_, stsh = f(state_hgrn), f(state_rwkv), f(state_rwkv_shift)
    in_maps = []
    for c in range(NCORES):
        m = dict(shared)
        sl = slice(2 * c, 2 * c + 2)
        m["x_p"] = x_prompt[c]
        m["x_s"] = x_sample[sl]
        m["ck"] = np.ascontiguousarray(ck[:, sl]).reshape(2, 2, PAST, D)
        m["cv"] = np.ascontiguousarray(cv[:, sl]).reshape(2, 2, PAST, D)
        m["sth"] = np.ascontiguousarray(sth[:, sl])
        m["str"] = np.ascontiguousarray(str_[:, sl])
        m["stsh"] = np.ascontiguousarray(stsh[:, sl])
        in_maps.append(m)
    res = run_bass_kernel_spmd(P.nc, in_maps, core_ids=list(range(NCORES)))
    R = res.results
    NB = 2 * NCORES
    y_p = np.stack([R[c]["y_p"] for c in range(NCORES)], 0)
    y_s = np.concatenate([R[c]["y_s"] for c in range(NCORES)], 0)
    k_p = np.stack([R[c]["k_p"].reshape(2, T, 8, 128) for c in range(NCORES)], 1)
    v_p = np.stack([R[c]["v_p"].reshape(2, T, 8, 128) for c in range(NCORES)], 1)
    hg_p = np.stack([R[c]["hg_p"] for c in range(NCORES)], 1)
    rw_p = np.stack([R[c]["rw_p"] for c in range(NCORES)], 1)
    sh_p = np.stack([R[c]["sh_p"] for c in range(NCORES)], 1)
    k_s = np.concatenate([R[c]["k_s"].reshape(2, 2, TS, 8, 128) for c in range(NCORES)], 1)
    v_s = np.concatenate([R[c]["v_s"].reshape(2, 2, TS, 8, 128) for c in range(NCORES)], 1)
    hg_s = np.concatenate([R[c]["hg_s"] for c in range(NCORES)], 1)
    rw_s = np.concatenate([R[c]["rw_s"] for c in range(NCORES)], 1)
    sh_s = np.concatenate([R[c]["sh_s"] for c in range(NCORES)], 1)
    outs = (y_p, y_s, k_p, v_p, hg_p, rw_p, sh_p, k_s, v_s, hg_s, rw_s, sh_s)
    return tuple(np.ascontiguousarray(o, dtype=np.float32) for o in outs)
```

```python
import math
from contextlib import ExitStack

import numpy as np
import concourse.bass as bass
import concourse.mybir as mybir
from concourse.bass_utils import run_bass_kernel_spmd

F32 = mybir.dt.float32
BF16 = mybir.dt.bfloat16
AF = mybir.ActivationFunctionType
ALU = mybir.AluOpType
AX = mybir.AxisListType

D = 1024
NCOL = 15488
NCX = NCOL + 32
PAST = 1024
TS = 64
EPS = 1e-6
GN_EPS = 64e-5
ROPE_THETA = 500000.0
O_AQ, O_AK, O_AV, O_AG = 0, 1024, 2048, 3072
O_BQ, O_BF, O_BI, O_BG = 4096, 5120, 6144, 7168
O_CP = 8192
O_CG = 11392
O_MA, O_MB, O_MC = 12416, 13440, 14464
O_EXT = 15488
C_R, C_WLO, C_K, C_V, C_ALO = 0, 1024, 1088, 2112, 3136
CW = 3200


class Tk:
    def __init__(self, h, name, dram=False):
        self.h = h
        self.name = name
        self.lw = None
        self.rd = []
        self.ds = {}
        self.dram = dram
        self.tok = {}
        self.rtok = {}

    def __getitem__(self, k):
        return self.h[k]


class Ctx:
    def __init__(self, nc):
        self.nc = nc
        self.es = ExitStack()
        self.eng = {"pe": nc.tensor, "act": nc.scalar, "dve": nc.vector, "pool": nc.gpsimd, "sp": nc.sync}
        self.sem = {}
        self.cnt = {}
        self.waited = {}
        for k in self.eng:
            self.sem[k] = self.es.enter_context(nc.semaphore("es_" + k))
            self.cnt[k] = 0
            self.waited[k] = {}
        self.free_ds = {"hw": [], "sw": []}
        self.nds = 0
        self.uid = 0
        self.ninst = 0

    def get_ds(self, t, q):
        kind = "sw" if q == "pool" else "hw"
        if kind not in t.ds:
            t.ds[kind] = self.new_ds(kind)
        return t.ds[kind]

    def new_ds(self, kind):
        if self.free_ds[kind]:
            return self.free_ds[kind].pop()
        self.nds += 1
        h = self.es.enter_context(self.nc.semaphore("ds%d" % self.nds))
        return [h, 0, "ds%d" % self.nds]

    def sb(self, ph, shape, dt, name):
        self.uid += 1
        h = ph.enter_context(self.nc.sbuf_tensor("%s_%d" % (name, self.uid), list(shape), dt))
        t = Tk(h, name)
        ph.tiles.append(t)
        return t

    def ps(self, ph, shape, dt, name):
        self.uid += 1
        h = ph.enter_context(self.nc.psum_tensor("%s_%d" % (name, self.uid), list(shape), dt))
        t = Tk(h, name)
        ph.tiles.append(t)
        return t

    def phase(self):
        ph = ExitStack()
        ph.tiles = []
        return ph

    def end_phase(self, ph):
        self.barrier(ph.tiles)
        for t in ph.tiles:
            for kind, ds in t.ds.items():
                self.free_ds[kind].append(ds)
            t.ds = {}
        ph.close()

    def barrier(self, tiles):
        deps = []
        for k in self.eng:
            if k != "sp" and self.cnt[k] > 0:
                deps.append(("e", k, self.cnt[k]))
        for t in tiles:
            for ds in t.ds.values():
                if ds[1] > 0:
                    deps.append(("d", ds, ds[1]))
        for k in self.eng:
            for d in deps:
                self._wait(k, d)

    def _wait(self, ename, dep):
        if dep is None:
            return
        kind, obj, val = dep
        if kind == "e":
            if obj == ename and ename in ("pe", "sp"):
                return
            key = "e_" + obj
            semh = self.sem[obj]
        else:
            key = obj[2]
            semh = obj[0]
        w = self.waited[ename]
        if w.get(key, 0) >= val:
            return
        self.eng[ename].wait_ge(semh, val)
        w[key] = val
        self.ninst += 1

    def op(self, ename, fn, R=(), W=()):
        deps = []
        for t in R:
            deps.append(t.lw)
        for t in W:
            deps.append(t.lw)
            deps.extend(t.rd)
        for d in deps:
            self._wait(ename, d)
        ins = fn(self.eng[ename])
        self.cnt[ename] += 1
        ins.then_inc(self.sem[ename], 1)
        self.ninst += 1
        tok = ("e", ename, self.cnt[ename])
        for t in R:
            t.rd.append(tok)
        for t in W:
            t.lw = tok
            t.rd = []
        return ins

    def pe(self, fn, R=(), W=()):
        return self.op("pe", fn, R, W)

    def act(self, fn, R=(), W=()):
        return self.op("act", fn, R, W)

    def dve(self, fn, R=(), W=()):
        return self.op("dve", fn, R, W)

    def pool(self, fn, R=(), W=()):
        return self.op("pool", fn, R, W)

    def load(self, q, out_ap, in_ap, sbt, dr=None, slow=False):
        deps = [sbt.lw] + list(sbt.rd)
        for d in deps:
            self._wait(q, d)
        ds = self.get_ds(sbt, q)
        if slow:
            ins = self.eng[q].dma_start(out=out_ap, in_=in_ap, allow_slow_non_contiguous=True)
        else:
            ins = self.eng[q].dma_start(out=out_ap, in_=in_ap)
        ds[1] += 16
        ins.then_inc(ds[0], 16)
        self.ninst += 1
        sbt.lw = ("d", ds, ds[1])
        sbt.rd = []

    def store(self, q, out_ap, in_ap, sbt, dr=None):
        deps = [sbt.lw]
        for d in deps:
            self._wait(q, d)
        ds = self.get_ds(sbt, q)
        ins = self.eng[q].dma_start(out=out_ap, in_=in_ap)
        ds[1] += 16
        ins.then_inc(ds[0], 16)
        self.ninst += 1
        sbt.rd.append(("d", ds, ds[1]))


class Rot:
    def __init__(self, k, ph, n, shape, dt, name, psum=False):
        mk = k.ps if psum else k.sb
        self.tiles = [mk(ph, shape, dt, "%s%d" % (name, i)) for i in range(n)]
        self.i = 0

    def next(self):
        t = self.tiles[self.i % len(self.tiles)]
        self.i += 1
        return t


def rsqrt(k, out, in_, R, W, scale, bias):
    k.act(lambda e: e.activation(out=out, in_=in_, func=AF.Sqrt, scale=scale, bias=bias), R=R, W=W)
    k.dve(lambda e: e.reciprocal(out=out, in_=out), R=W, W=W)


def bc(ap, shape):
    return ap.to_broadcast(list(shape))


def make_consts():
    s = np.arange(128)[:, None]
    t = np.arange(128)[None, :]
    same = (s // 64) == (t // 64)
    c = {}
    c["ident"] = np.eye(128)
    ut64 = (same & (s <= t)).astype(np.float64)
    mid = 64 * (t // 64) + 31
    a_mid = (same & (s <= mid)).astype(np.float64)
    a_end = same.astype(np.float64)
    c["h_ut"] = ut64
    c["h_d1"] = ut64 - a_mid
    c["h_d2"] = a_end - ut64
    c["h_end"] = a_end
    c["h_mask"] = ut64
    c["r_ut"] = (s <= t).astype(np.float64)
    c["r_uts"] = (s < t).astype(np.float64)
    c["r_low"] = (s > t).astype(np.float64)
    c["r_one"] = np.ones((128, 128))
    names = ["ident", "h_ut", "h_d1", "h_d2", "h_end", "h_mask", "r_ut", "r_uts", "r_low", "r_one"]
    arr = np.concatenate([c[n] for n in names], axis=1).astype(np.float32)
    offs = {n: i * 128 for i, n in enumerate(names)}
    return arr, offs


def rope_tables(pos):
    half = 8
    inv_freq = (np.float32(ROPE_THETA) ** (-(np.arange(half, dtype=np.float32) * np.float32(2.0 / 16)))).astype(np.float32)
    ang = pos.astype(np.float32)[:, None] * inv_freq[None, :]
    return np.concatenate([np.cos(ang), np.sin(ang)], axis=1).astype(np.float32)


class Prog:
    pass


def build(T, nlayers=2, upto=99, debug=False):
    nc = bass.Bass("TRN2", target_bir_lowering=False)
    k = Ctx(nc)
    P = Prog()
    P.nc, P.k, P.T = nc, k, T
    Ttot = T + 2 * TS
    P.Ttot = Ttot
    carr, coff = make_consts()
    P.coff = coff

    def din(name, shape, dt=F32):
        return nc.dram_tensor(name, list(shape), dt, kind="ExternalInput").ap()

    def dout(name, shape, dt=F32):
        return Tk(nc.dram_tensor(name, list(shape), dt, kind="ExternalOutput").ap(), name, dram=True)

    def dscr(name, shape, dt=F32):
        kind = "ExternalOutput" if (debug and name in debug) else "Internal"
        return Tk(nc.dram_tensor(name, list(shape), dt, kind=kind).ap(), name, dram=True)

    I = {}
    I["x_p"] = din("x_p", [T, D])
    I["x_s"] = din("x_s", [2, TS, D])
    I["ck"] = din("ck", [2, 2, PAST, D])
    I["cv"] = din("cv", [2, 2, PAST, D])
    I["sth"] = din("sth", [2, 2, 8, 128, 128])
    I["str"] = din("str", [2, 2, 16, 64, 64])
    I["stsh"] = din("stsh", [2, 2, CW])
    I["norm_g"] = din("norm_g", [2, D])
    I["w_in"] = din("w_in", [2, D, NCOL])
    I["a_qnorm_g"] = din("a_qnorm_g", [2, 64])
    I["a_knorm_g"] = din("a_knorm_g", [2, 64])
    I["a_lambda"] = din("a_lambda", [2, 256])
    I["a_subln_g"] = din("a_subln_g", [2, 128])
    I["b_lower"] = din("b_lower", [2, 1024])
    I["b_norm_g"] = din("b_norm_g", [2, 128])
    I["c_shift_mu"] = din("c_shift_mu", [2, CW])
    I["c_w0"] = din("c_w0", [2, 1024])
    I["c_w2"] = din("c_w2", [2, 64, 1024])
    I["c_a0"] = din("c_a0", [2, 1024])
    I["c_a2"] = din("c_a2", [2, 64, 1024])
    I["c_k_k"] = din("c_k_k", [2, 1024])
    I["c_k_a"] = din("c_k_a", [2, 1024])
    I["c_r_k"] = din("c_r_k", [2, 1024])
    I["c_ln_w"] = din("c_ln_w", [2, 1024])
    I["c_ln_b"] = din("c_ln_b", [2, 1024])
    I["c_vres_w1"] = din("c_vres_w1", [1, D, 32])
    I["c_vres_w2"] = din("c_vres_w2", [1, 32, 1024])
    I["c_v0"] = din("c_v0", [1, 1024])
    I["w_out_a"] = din("w_out_a", [2, D, D])
    I["w_out_b"] = din("w_out_b", [2, D, D])
    I["w_out_c"] = din("w_out_c", [2, D, D])
    I["w_o"] = din("w_o", [2, D, D])
    I["consts"] = din("consts", list(carr.shape))
    I["rope_p"] = din("rope_p", [T, 16])
    I["rope_s"] = din("rope_s", [TS, 16])
    P.I = I

    O = {}
    O["y_p"] = dout("y_p", [T, D])
    O["y_s"] = dout("y_s", [2, TS, D])
    O["k_p"] = dout("k_p", [2, T, D])
    O["v_p"] = dout("v_p", [2, T, D])
    O["hg_p"] = dout("hg_p", [2, 8, 128, 128])
    O["rw_p"] = dout("rw_p", [2, 16, 64, 64])
    O["sh_p"] = dout("sh_p", [2, CW])
    O["k_s"] = dout("k_s", [2, 2, TS, D])
    O["v_s"] = dout("v_s", [2, 2, TS, D])
    O["hg_s"] = dout("hg_s", [2, 2, 8, 128, 128])
    O["rw_s"] = dout("rw_s", [2, 2, 16, 64, 64])
    O["sh_s"] = dout("sh_s", [2, 2, CW])
    P.O = O

    seqs = []
    seqs.append(dict(name="p", T=T, g0=0, n=128, past=0, b=None))
    seqs.append(dict(name="s0", T=TS, g0=T, n=TS, past=PAST, b=0))
    seqs.append(dict(name="s1", T=TS, g0=T + TS, n=TS, past=PAST, b=1))
    tiles = []
    for s in seqs:
        s["tiles"] = []
        for t0 in range(0, s["T"], s["n"]):
            tl = dict(seq=s, t0=t0, n=s["n"], g=s["g0"] + t0, idx=len(tiles))
            tiles.append(tl)
            s["tiles"].append(tl)
    P.seqs, P.tiles = seqs, tiles
    NTL = len(tiles)

    S = {}
    S["hT"] = dscr("hT", [NTL, 128, 8, 128], BF16)
    PJ = [(0, 4096, "pjA"), (4096, 8192, "pjB"), (8192, 12416, "pjC"), (12416, NCX, "pjM")]
    for lo, hi, nm in PJ:
        S[nm] = dscr(nm, [Ttot, hi - lo])

    def pj(r0, r1, c0, c1):
        for lo, hi, nm in PJ:
            if lo <= c0 and c1 <= hi:
                return S[nm][r0:r1, c0 - lo:c1 - lo]
        raise ValueError((c0, c1))
    P.pj = pj
    S["xmid"] = dscr("xmid", [Ttot, D])
    S["oa"] = dscr("oa", [Ttot, D])
    S["ob"] = dscr("ob", [Ttot, D])
    S["oc"] = dscr("oc", [Ttot, D])
    S["vf"] = dscr("vf", [Ttot, D])
    for s in seqs:
        TK = s["past"] + s["T"]
        s["TK"] = TK
        s["qT"] = dscr("qT_" + s["name"], [8, 128, s["T"]], BF16)
        s["kT"] = dscr("kT_" + s["name"], [8, 128, TK], BF16)
        s["vb"] = dscr("vb_" + s["name"], [TK, 8, 129], BF16)
    P.S = S

    def xsrc(l, tl):
        s = tl["seq"]
        if l == 0:
            if s["b"] is None:
                return I["x_p"][tl["t0"]:tl["t0"] + tl["n"], :], None
            return I["x_s"][s["b"], tl["t0"]:tl["t0"] + tl["n"], :], None
        return S["xmid"][tl["g"]:tl["g"] + tl["n"], :], S["xmid"]
    P.xsrc = xsrc

    gph = k.phase()
    P.gph = gph
    cst = k.sb(gph, [128, carr.shape[1]], F32, "cst")
    k.load("sp", cst[:], I["consts"][:, :], cst)
    identb = k.sb(gph, [128, 128], BF16, "identb")
    k.dve(lambda e: e.tensor_copy(out=identb[:], in_=cst[:, coff["ident"]:coff["ident"] + 128]), R=[cst], W=[identb])
    P.cst, P.identb = cst, identb

    def C(name):
        return cst[:, coff[name]:coff[name] + 128]
    P.C = C

    for l in range(nlayers):
        if upto >= 0:
            phase0(P, l)
        if upto >= 1:
            phase1(P, l)
        if upto >= 2:
            phase2(P, l)
        if upto >= 3:
            phase3(P, l)
        if upto >= 4:
            phase4(P, l)
        if upto >= 5:
            phase5(P, l)
        if upto >= 6:
            phase6(P, l)

    k.end_phase(gph)
    return P


def phase0(P, l):
    k, I, S = P.k, P.I, P.S
    ph = k.phase()
    xr = Rot(k, ph, 3, [128, D], F32, "p0x")
    jr = Rot(k, ph, 2, [128, D], BF16, "p0j")
    hr = Rot(k, ph, 2, [128, D], BF16, "p0h")
    sr = Rot(k, ph, 4, [128, 2], F32, "p0s")
    tr = Rot(k, ph, 2, [128, 8, 128], BF16, "p0t")
    pr = Rot(k, ph, 2, [128, 8, 128], BF16, "p0p", psum=True)
    for tl in P.tiles:
        n = tl["n"]
        src, dr = P.xsrc(l, tl)
        x = xr.next()
        k.load("sp", x[:n, :], src, x, dr)
        st = sr.next()
        j = jr.next()
        k.act(lambda e: e.activation(out=j[:n, :], in_=x[:n, :], func=AF.Square, accum_out=st[:n, 0:1]), R=[x], W=[j, st])
        rsqrt(k, st[:n, 1:2], st[:n, 0:1], [st], [st], 1.0 / D, EPS)
        h = hr.next()
        k.dve(lambda e: e.tensor_scalar(out=h[:n, :], in0=x[:n, :], scalar1=st[:n, 1:2], scalar2=None,
                                        op0=ALU.mult), R=[x, st], W=[h])
        pt = pr.next()
        for kk in range(8):
            k.pe(lambda e: e.transpose(out=pt[:, kk, :n], in_=h[:n, kk * 128:(kk + 1) * 128], identity=P.identb[:n, :n]),
                 R=[h, P.identb], W=[pt])
        ht = tr.next()
        k.act(lambda e: e.activation(out=ht[:, :, :n], in_=pt[:, :, :n], func=AF.Copy), R=[pt], W=[ht])
        k.store("pool", S["hT"][tl["idx"], :, :, :n], ht[:, :, :n], ht, S["hT"])
    k.end_phase(ph)


def phase1(P, l):
    k, I, S = P.k, P.I, P.S
    ph = k.phase()
    gcol = k.sb(ph, [128, 8], F32, "p1g")
    k.load("sp", gcol[:], I["norm_g"][l].rearrange("(k p) -> p k", p=128), gcol, slow=True)
    wf = Rot(k, ph, 2, [128, 8, 1024], F32, "p1wf")
    wb = Rot(k, ph, 2, [128, 8, 1024], BF16, "p1wb")
    hr = Rot(k, ph, 3, [128, 8, 128], BF16, "p1h")
    orr = Rot(k, ph, 3, [128, 1024], F32, "p1o")
    pr = Rot(k, ph, 2, [128, 1024], F32, "p1p", psum=True)
    groups = [(c0, 1024) for c0 in range(0, 11264, 1024)] + [(11264, 128)] + [(c0, 1024) for c0 in range(11392, NCOL, 1024)]
    if l == 1:
        groups.append((O_EXT, 32))
    ev = 0
    for (c0, cw) in groups:
        w32 = wf.next()
        if c0 == O_EXT:
            src = I["c_vres_w1"][0].rearrange("(k p) c -> p k c", p=128)
        else:
            src = I["w_in"][l][:, c0:c0 + cw].rearrange("(k p) c -> p k c", p=128)
        k.load("sp", w32[:, :, :cw], src, w32)
        w = wb.next()
        k.dve(lambda e: e.tensor_tensor(out=w[:, :, :cw], in0=w32[:, :, :cw],
                                        in1=bc(gcol[:, :].unsqueeze(2), [128, 8, cw]), op=ALU.mult),
              R=[w32, gcol], W=[w])
        for tl in P.tiles:
            n = tl["n"]
            h = hr.next()
            k.load("sp", h[:, :, :n], S["hT"][tl["idx"], :, :, :n], h, S["hT"])
            pt = pr.next()
            for n0 in range(0, cw, 512):
                nw = min(512, cw - n0)
                for kk in range(8):
                    k.pe(lambda e: e.matmul(pt[:n, n0:n0 + nw], lhsT=h[:, kk, :n], rhs=w[:, kk, n0:n0 + nw],
                                            start=(kk == 0), stop=(kk == 7)), R=[h, w], W=[pt])
            o = orr.next()
            if ev % 2 == 0:
                k.act(lambda e: e.activation(out=o[:n, :cw], in_=pt[:n, :cw], func=AF.Copy), R=[pt], W=[o])
            else:
                k.dve(lambda e: e.tensor_copy(out=o[:n, :cw], in_=pt[:n, :cw]), R=[pt], W=[o])
            ev += 1
            k.store("pool", P.pj(tl["g"], tl["g"] + n, c0, c0 + cw), o[:n, :cw], o)
    k.end_phase(ph)


def bcast_row(k, ph, ap1d, width, name, q="sp"):
    t = k.sb(ph, [128, width], F32, name)
    k.load(q, t[:], ap1d.partition_broadcast(128), t)
    return t


def phase2(P, l):
    k, I, S, O = P.k, P.I, P.S, P.O
    ph = k.phase()
    gq = bcast_row(k, ph, I["a_qnorm_g"][l], 64, "p2gq")
    gk = bcast_row(k, ph, I["a_knorm_g"][l], 64, "p2gk")
    xr = Rot(k, ph, 4, [128, D], F32, "p2x")
    tmp = Rot(k, ph, 2, [128, D], F32, "p2tmp")
    xn = Rot(k, ph, 3, [128, D], F32, "p2xn")
    ssr = Rot(k, ph, 4, [128, 16], F32, "p2ss")
    csr = Rot(k, ph, 2, [128, 16], F32, "p2cs")
    rtr = Rot(k, ph, 2, [128, 4, 16, 8], F32, "p2rt")
    xbr = Rot(k, ph, 3, [128, D], BF16, "p2xb")
    vbr = Rot(k, ph, 2, [128, 8, 129], BF16, "p2vb")
    tTr = Rot(k, ph, 3, [128, 8, 128], BF16, "p2tT")
    ptr = Rot(k, ph, 3, [128, 8, 128], BF16, "p2pt", psum=True)
    for vt in vbr.tiles:
        k.dve(lambda e: e.memset(vt[:, :, 128:129], 1.0), W=[vt])

    def transpose_store(xb, n, dst_ap):
        pt = ptr.next()
        for h in range(8):
            k.pe(lambda e: e.transpose(out=pt[:, h, :n], in_=xb[:n, h * 128:(h + 1) * 128], identity=P.identb[:n, :n]),
                 R=[xb, P.identb], W=[pt])
        tT = tTr.next()
        k.act(lambda e: e.activation(out=tT[:, :, :n], in_=pt[:, :, :n], func=AF.Copy), R=[pt], W=[tT])
        k.store("pool", dst_ap, tT[:, :, :n], tT)

    def v_store(v, n, dst_rows):
        vb = vbr.next()
        k.act(lambda e: e.activation(out=vb[:n, :, 0:128], in_=v[:n, :].rearrange("p (h d) -> p h d", h=8), func=AF.Copy),
              R=[v], W=[vb])
        k.store("pool", dst_rows, vb[:n, :, :], vb)

    def normrope(x, n, g, cs):
        t = tmp.next()
        k.act(lambda e: e.activation(out=t[:n, :], in_=x[:n, :], func=AF.Square), R=[x], W=[t])
        ss = ssr.next()
        k.dve(lambda e: e.tensor_reduce(out=ss[:n, :], in_=t[:n, :].rearrange("p (s d) -> p s d", s=16), axis=AX.X, op=ALU.add),
              R=[t], W=[ss])
        rsqrt(k, ss[:n, :], ss[:n, :], [ss], [ss], 1.0 / 64, EPS)
        y = xn.next()
        y3 = y[:n, :].rearrange("p (s d) -> p s d", s=16)
        x3 = x[:n, :].rearrange("p (s d) -> p s d", s=16)
        k.dve(lambda e: e.tensor_tensor(out=y3, in0=x3, in1=bc(ss[:n, :].unsqueeze(2), [n, 16, 64]), op=ALU.mult),
              R=[x, ss], W=[y])
        k.pool(lambda e: e.tensor_tensor(out=y3, in0=y3, in1=bc(g[:n, :].unsqueeze(1), [n, 16, 64]), op=ALU.mult),
               R=[y, g], W=[y])
        rt = rtr.next()
        cosb = bc(cs[:n, 0:8].unsqueeze(1), [n, 16, 8])
        sinb = bc(cs[:n, 8:16].unsqueeze(1), [n, 16, 8])
        x1 = y3[:, :, 0:8]
        x2 = y3[:, :, 8:16]
        k.dve(lambda e: e.tensor_tensor(out=rt[:n, 0], in0=x1, in1=cosb, op=ALU.mult), R=[y, cs], W=[rt])
        k.dve(lambda e: e.tensor_tensor(out=rt[:n, 1], in0=x2, in1=sinb, op=ALU.mult), R=[y, cs], W=[rt])
        k.dve(lambda e: e.tensor_tensor(out=rt[:n, 2], in0=x2, in1=cosb, op=ALU.mult), R=[y, cs], W=[rt])
        k.dve(lambda e: e.tensor_tensor(out=rt[:n, 3], in0=x1, in1=sinb, op=ALU.mult), R=[y, cs], W=[rt])
        k.dve(lambda e: e.tensor_tensor(out=x1, in0=rt[:n, 0], in1=rt[:n, 1], op=ALU.subtract), R=[rt], W=[y])
        k.dve(lambda e: e.tensor_tensor(out=x2, in0=rt[:n, 2], in1=rt[:n, 3], op=ALU.add), R=[rt], W=[y])
        return y

    for s in P.seqs:
        b = s["b"]
        for j in range(s["past"] // 128):
            x = xr.next()
            k.load("sp", x[:, :], I["ck"][l, b, j * 128:(j + 1) * 128, :], x)
            xb = xbr.next()
            k.act(lambda e: e.activation(out=xb[:, :], in_=x[:, :], func=AF.Copy), R=[x], W=[xb])
            transpose_store(xb, 128, s["kT"][:, :, j * 128:(j + 1) * 128].rearrange("h p t -> p h t"))
            v = xr.next()
            k.load("sp", v[:, :], I["cv"][l, b, j * 128:(j + 1) * 128, :], v)
            v_store(v, 128, s["vb"][j * 128:(j + 1) * 128, :, :])
        for tl in s["tiles"]:
            n, t0, g = tl["n"], tl["t0"], tl["g"]
            cs = csr.next()
            rsrc = I["rope_p"][t0:t0 + n, :] if b is None else I["rope_s"][t0:t0 + n, :]
            k.load("sp", cs[:n, :], rsrc, cs)
            x = xr.next()
            k.load("sp", x[:n, :], P.pj(g, g + n, O_AQ, O_AQ + D), x)
            y = normrope(x, n, gq, cs)
            xb = xbr.next()
            k.act(lambda e: e.activation(out=xb[:n, :], in_=y[:n, :], func=AF.Copy), R=[y], W=[xb])
            transpose_store(xb, n, s["qT"][:, :, t0:t0 + n].rearrange("h p t -> p h t"))
            x = xr.next()
            k.load("sp", x[:n, :], P.pj(g, g + n, O_AK, O_AK + D), x)
            y = normrope(x, n, gk, cs)
            kdst = O["k_p"][l, t0:t0 + n, :] if b is None else O["k_s"][l, b, t0:t0 + n, :]
            k.store("pool", kdst, y[:n, :], y)
            xb = xbr.next()
            k.act(lambda e: e.activation(out=xb[:n, :], in_=y[:n, :], func=AF.Copy), R=[y], W=[xb])
            p0 = s["past"]
            transpose_store(xb, n, s["kT"][:, :, p0 + t0:p0 + t0 + n].rearrange("h p t -> p h t"))
            v = xr.next()
            k.load("sp", v[:n, :], P.pj(g, g + n, O_AV, O_AV + D), v)
            vdst = O["v_p"][l, t0:t0 + n, :] if b is None else O["v_s"][l, b, t0:t0 + n, :]
            k.store("pool", vdst, v[:n, :], v)
            v_store(v, n, s["vb"][p0 + t0:p0 + t0 + n, :, :])
    k.end_phase(ph)


def phase3(P, l):
    k, I, S, O = P.k, P.I, P.S, P.O
    ph = k.phase()
    lam_init = 0.8 - 0.6 * math.exp(-0.3 * l)
    lamt = bcast_row(k, ph, I["a_lambda"][l], 256, "p3lam")
    lw = k.sb(ph, [128, 2, 64], F32, "p3lw")
    l4 = lamt[:, :].rearrange("p (a b d) -> p a b d", a=2, b=2)
    k.dve(lambda e: e.tensor_tensor(out=lw[:, :, :], in0=l4[:, :, 0, :], in1=l4[:, :, 1, :], op=ALU.mult), R=[lamt], W=[lw])
    lc = k.sb(ph, [128, 4], F32, "p3lc")
    k.dve(lambda e: e.tensor_reduce(out=lc[:, 0:2], in_=lw[:, :, :], axis=AX.X, op=ALU.add), R=[lw], W=[lc])
    k.act(lambda e: e.activation(out=lc[:, 0:2], in_=lc[:, 0:2], func=AF.Exp), R=[lc], W=[lc])
    k.dve(lambda e: e.tensor_tensor(out=lc[:, 2:3], in0=lc[:, 0:1], in1=lc[:, 1:2], op=ALU.subtract), R=[lc], W=[lc])
    k.dve(lambda e: e.tensor_scalar(out=lc[:, 3:4], in0=lc[:, 2:3], scalar1=lam_init, scalar2=None, op0=ALU.add), R=[lc], W=[lc])
    gs = bcast_row(k, ph, I["a_subln_g"][l], 128, "p3gs")
    k.dve(lambda e: e.tensor_scalar(out=gs[:, :], in0=gs[:, :], scalar1=1.0 - lam_init, scalar2=None, op0=ALU.mult), R=[gs], W=[gs])

    TKmax = max(s["TK"] for s in P.seqs)
    ntkmax = (TKmax + 127) // 128
    ktr = Rot(k, ph, 2, [128, TKmax], BF16, "p3kt")
    vtr = Rot(k, ph, 2, [128, ntkmax, 129], BF16, "p3vt")
    qtr = Rot(k, ph, 2, [128, 512], BF16, "p3qt")
    psr = Rot(k, ph, 2, [128, 2, 512], F32, "p3ps", psum=True)
    acc = k.ps(ph, [128, 8, 256], F32, "p3acc")
    ptr = Rot(k, ph, 3, [128, 2, 512], BF16, "p3pt")
    accr = Rot(k, ph, 2, [128, 8, 129], F32, "p3accs")
    rrr = Rot(k, ph, 4, [128, 8], F32, "p3rr")
    tr_ = Rot(k, ph, 2, [128, 128], F32, "p3t")
    orr = Rot(k, ph, 2, [128, 128], F32, "p3o")
    ofr = Rot(k, ph, 3, [128, 128], F32, "p3of")

    for s in P.seqs:
        TK, Tq = s["TK"], s["T"]
        ntk = (TK + 127) // 128
        prompt = s["b"] is None
        qw = min(512, Tq)
        for h in range(8):
            kt = ktr.next()
            k.load("sp", kt[:, :TK], s["kT"][h, :, :], kt)
            vt = vtr.next()
            nfull = TK // 128
            k.load("sp", vt[:, :nfull, :], s["vb"][0:nfull * 128, h, :].rearrange("(j p) d -> p j d", p=128), vt)
            if TK % 128:
                rem = TK % 128
                k.load("sp", vt[:rem, nfull, :], s["vb"][nfull * 128:TK, h, :], vt)
            for q0 in range(0, Tq, qw):
                qt = qtr.next()
                k.load("sp", qt[:, :qw], s["qT"][h, :, q0:q0 + qw], qt)
                nqt = (qw + 127) // 128
                jq0 = q0 // 128
                jlast = (jq0 + nqt - 1) if prompt else (ntk - 1)
                def s_mm(j):
                    nk = min(128, TK - j * 128)
                    ps = psr.next()
                    for m in range(2):
                        k.pe(lambda e: e.matmul(ps[:nk, m, :qw], lhsT=kt[m * 64:(m + 1) * 64, j * 128:j * 128 + nk],
                                                rhs=qt[m * 64:(m + 1) * 64, :qw], start=True, stop=True),
                             R=[kt, qt], W=[ps])
                    return ps

                ps_next = s_mm(0)
                for j in range(jlast + 1):
                    nk = min(128, TK - j * 128)
                    ps = ps_next
                    if j < jlast:
                        ps_next = s_mm(j + 1)
                    pt = ptr.next()
                    k.act(lambda e: e.activation(out=pt[:nk, :, :qw], in_=ps[:nk, :, :qw], func=AF.Exp, scale=0.125),
                          R=[ps], W=[pt])
                    if prompt and j >= jq0:
                        i = j - jq0
                        k.pool(lambda e: e.memset(pt[64:128, :, i * 128:i * 128 + 64], 0.0), W=[pt])
                    for m in range(2):
                        for i in range(nqt):
                            nq = min(128, qw - i * 128)
                            last = (jq0 + i) if prompt else (ntk - 1)
                            if j > last:
                                continue
                            k.pe(lambda e: e.matmul(acc[:nq, m * 4 + i, 0:129], lhsT=pt[:nk, m, i * 128:i * 128 + nq],
                                                    rhs=vt[:nk, j, :], start=(j == 0 and i % 2 == 0), stop=(j == last),
                                                    skip_group_check=True),
                                 R=[pt, vt], W=[acc])
                nqmax = min(128, qw)
                accs = accr.next()
                for m in range(2):
                    k.dve(lambda e: e.tensor_copy(out=accs[:nqmax, m * 4:m * 4 + nqt, :], in_=acc[:nqmax, m * 4:m * 4 + nqt, 0:129]),
                          R=[acc], W=[accs])
                for i in range(nqt):
                    nq = min(128, qw - i * 128)
                    rr = rrr.next()
                    k.dve(lambda e: e.reciprocal(out=rr[:nq, 0:1], in_=accs[:nq, i, 128:129]), R=[accs], W=[rr])
                    k.dve(lambda e: e.reciprocal(out=rr[:nq, 1:2], in_=accs[:nq, 4 + i, 128:129]), R=[accs], W=[rr])
                    k.dve(lambda e: e.tensor_tensor(out=rr[:nq, 2:3], in0=rr[:nq, 1:2], in1=lc[:nq, 3:4], op=ALU.mult),
                          R=[rr, lc], W=[rr])
                    t = tr_.next()
                    k.dve(lambda e: e.tensor_scalar(out=t[:nq, :], in0=accs[:nq, 4 + i, 0:128], scalar1=rr[:nq, 2:3],
                                                    scalar2=None, op0=ALU.mult), R=[accs, rr], W=[t])
                    o = orr.next()
                    k.dve(lambda e: e.scalar_tensor_tensor(out=o[:nq, :], in0=accs[:nq, i, 0:128], scalar=rr[:nq, 0:1],
                                                           in1=t[:nq, :], op0=ALU.mult, op1=ALU.subtract),
                          R=[accs, rr, t], W=[o])
                    k.pool(lambda e: e.tensor_tensor(out=t[:nq, :], in0=o[:nq, :], in1=o[:nq, :], op=ALU.mult), R=[o], W=[t])
                    k.dve(lambda e: e.tensor_reduce(out=rr[:nq, 3:4], in_=t[:nq, :], axis=AX.X, op=ALU.add), R=[t], W=[rr])
                    k.act(lambda e: e.activation(out=rr[:nq, 4:5], in_=rr[:nq, 3:4], func=AF.Ln, scale=1.0 / 128, bias=EPS),
                          R=[rr], W=[rr])
                    k.act(lambda e: e.activation(out=rr[:nq, 5:6], in_=rr[:nq, 4:5], func=AF.Exp, scale=-0.5), R=[rr], W=[rr])
                    of = ofr.next()
                    k.dve(lambda e: e.scalar_tensor_tensor(out=of[:nq, :], in0=o[:nq, :], scalar=rr[:nq, 5:6],
                                                           in1=gs[:nq, :], op0=ALU.mult, op1=ALU.mult),
                          R=[o, rr, gs], W=[of])
                    g = s["g0"] + q0 + i * 128
                    k.store("pool", S["oa"][g:g + nq, h * 128:(h + 1) * 128], of[:nq, :], of)
    k.end_phase(ph)


def phase4(P, l):
    k, I, S, O, C = P.k, P.I, P.S, P.O, P.C
    ph = k.phase()
    lbr = k.sb(ph, [128, D], F32, "p4lb")
    oml = k.sb(ph, [128, D], F32, "p4oml")
    if l == 0:
        k.dve(lambda e: e.memset(lbr[:, :], 0.0), W=[lbr])
        k.dve(lambda e: e.memset(oml[:, :], 1.0), W=[oml])
    else:
        k.load("sp", lbr[:, :], I["b_lower"][1].partition_broadcast(128), lbr)
        k.load("sp", oml[:, :], I["b_lower"][0].partition_broadcast(128), oml)
        k.dve(lambda e: e.tensor_tensor(out=lbr[:, :], in0=lbr[:, :], in1=oml[:, :], op=ALU.subtract), R=[lbr, oml], W=[lbr])
        k.act(lambda e: e.activation(out=lbr[:, :], in_=lbr[:, :], func=AF.Sigmoid), R=[lbr], W=[lbr])
        k.dve(lambda e: e.tensor_scalar(out=oml[:, :], in0=lbr[:, :], scalar1=-1.0, scalar2=1.0, op0=ALU.mult, op1=ALU.add),
              R=[lbr], W=[oml])
    gn = bcast_row(k, ph, I["b_norm_g"][l], 128, "p4gn")
    ldr = Rot(k, ph, 6, [128, D], F32, "p4ld")
    f32r = Rot(k, ph, 8, [128, D], F32, "p4f")
    er = Rot(k, ph, 3, [128, D], F32, "p4e")
    b16r = Rot(k, ph, 10, [128, D], BF16, "p4b")
    tTr = Rot(k, ph, 4, [128, 8, 128], BF16, "p4tT")
    qpr = Rot(k, ph, 2, [128, 8, 2, 128], BF16, "p4qp")
    for t in qpr.tiles:
        k.dve(lambda e: e.memset(t[:, :, :, :], 0.0), W=[t])
    scmr = Rot(k, ph, 2, [128, 8, 128], BF16, "p4scm")
    dcr = Rot(k, ph, 2, [128, 8, 2], F32, "p4dc")
    ssr = Rot(k, ph, 2, [128, 8], F32, "p4ss")
    St = k.sb(ph, [128, 8, 128], F32, "p4S")
    Sb = k.sb(ph, [128, 8, 128], BF16, "p4Sb")
    pc = k.ps(ph, [128, D], F32, "p4pc")
    ptp = k.ps(ph, [128, 8, 128], BF16, "p4pt")
    psc = k.ps(ph, [128, 4, 128], F32, "p4psc")
    po = k.ps(ph, [128, 8, 128], F32, "p4po")
    pS = k.ps(ph, [128, 4, 128], F32, "p4pS")
    pd = k.ps(ph, [128, 8, 2], F32, "p4pd")
    ioff = P.coff["ident"]

    def cum_mm(name, logf, n):
        for hf in range(2):
            k.pe(lambda e: e.matmul(pc[:n, hf * 512:(hf + 1) * 512], lhsT=C(name)[:n, :n], rhs=logf[:n, hf * 512:(hf + 1) * 512],
                                    start=True, stop=True), R=[P.cst, logf], W=[pc])

    def transp(src, n):
        for h in range(8):
            k.pe(lambda e: e.transpose(out=ptp[:, h, :n], in_=src[:n, h * 128:(h + 1) * 128], identity=P.identb[:n, :n]),
                 R=[src, P.identb], W=[ptp])

    for s in P.seqs:
        b = s["b"]
        if b is None:
            k.dve(lambda e: e.memset(St[:, :, :], 0.0), W=[St])
        else:
            k.load("sp", St[:, :, :], I["sth"][l, b].rearrange("h k v -> k h v"), St)
        k.act(lambda e: e.activation(out=Sb[:, :, :], in_=St[:, :, :], func=AF.Copy), R=[St], W=[Sb])
        for tl in s["tiles"]:
            n, g = tl["n"], tl["g"]
            nch = n // 64
            bq, bf_, bi = ldr.next(), ldr.next(), ldr.next()
            k.load("sp", bq[:n, :], P.pj(g, g + n, O_BQ, O_BQ + D), bq)
            k.load("sp", bf_[:n, :], P.pj(g, g + n, O_BF, O_BF + D), bf_)
            k.load("sp", bi[:n, :], P.pj(g, g + n, O_BI, O_BI + D), bi)
            sg, t1, f, kin, logf, q = (f32r.next() for _ in range(6))
            k.act(lambda e: e.activation(out=sg[:n, :], in_=bf_[:n, :], func=AF.Sigmoid), R=[bf_], W=[sg])
            k.dve(lambda e: e.tensor_tensor(out=t1[:n, :], in0=sg[:n, :], in1=oml[:n, :], op=ALU.mult), R=[sg, oml], W=[t1])
            k.pool(lambda e: e.tensor_tensor(out=f[:n, :], in0=t1[:n, :], in1=lbr[:n, :], op=ALU.add), R=[t1, lbr], W=[f])
            k.dve(lambda e: e.tensor_tensor(out=kin[:n, :], in0=oml[:n, :], in1=t1[:n, :], op=ALU.subtract), R=[t1, oml], W=[kin])
            k.act(lambda e: e.activation(out=logf[:n, :], in_=f[:n, :], func=AF.Ln), R=[f], W=[logf])
            k.act(lambda e: e.activation(out=q[:n, :], in_=bq[:n, :], func=AF.Silu), R=[bq], W=[q])
            qt_, qh, kh, kt_, ib = (b16r.next() for _ in range(5))
            k.pool(lambda e: e.tensor_copy(out=ib[:n, :], in_=bi[:n, :]), R=[bi], W=[ib])
            cum_mm("h_ut", logf, n)
            e1 = er.next()
            k.act(lambda e: e.activation(out=e1[:n, :], in_=pc[:n, :], func=AF.Exp), R=[pc], W=[e1])
            k.dve(lambda e: e.tensor_tensor(out=qt_[:n, :], in0=q[:n, :], in1=e1[:n, :], op=ALU.mult), R=[q, e1], W=[qt_])
            cum_mm("h_d1", logf, n)
            e2, e3 = er.next(), er.next()
            k.act(lambda e: e.activation(out=e2[:n, :], in_=pc[:n, :], func=AF.Exp), R=[pc], W=[e2])
            k.act(lambda e: e.activation(out=e3[:n, :], in_=pc[:n, :], func=AF.Exp, scale=-1.0), R=[pc], W=[e3])
            k.dve(lambda e: e.tensor_tensor(out=qh[:n, :], in0=q[:n, :], in1=e2[:n, :], op=ALU.mult), R=[q, e2], W=[qh])
            k.pool(lambda e: e.tensor_tensor(out=kh[:n, :], in0=kin[:n, :], in1=e3[:n, :], op=ALU.mult), R=[kin, e3], W=[kh])
            cum_mm("h_d2", logf, n)
            e4 = er.next()
            k.act(lambda e: e.activation(out=e4[:n, :], in_=pc[:n, :], func=AF.Exp), R=[pc], W=[e4])
            k.dve(lambda e: e.tensor_tensor(out=kt_[:n, :], in0=kin[:n, :], in1=e4[:n, :], op=ALU.mult), R=[kin, e4], W=[kt_])
            cum_mm("h_end", logf, n)
            e5 = er.next()
            k.act(lambda e: e.activation(out=e5[:n, :], in_=pc[:n, :], func=AF.Exp), R=[pc], W=[e5])
            for h in range(8):
                k.pe(lambda e: e.matmul(pd[:, h, :nch], lhsT=e5[:n, h * 128:(h + 1) * 128],
                                        rhs=P.cst[:n, ioff:ioff + 64 * nch:64], start=True, stop=True),
                     R=[e5, P.cst], W=[pd])
            dC = dcr.next()
            k.dve(lambda e: e.tensor_copy(out=dC[:, :, :nch], in_=pd[:, :, :nch]), R=[pd], W=[dC])
            transp(qh, n)
            qhT = tTr.next()
            k.act(lambda e: e.activation(out=qhT[:, :, :n], in_=ptp[:, :, :n], func=AF.Copy), R=[ptp], W=[qhT])
            transp(kh, n)
            khT = tTr.next()
            k.dve(lambda e: e.tensor_copy(out=khT[:, :, :n], in_=ptp[:, :, :n]), R=[ptp], W=[khT])
            transp(qt_, n)
            qp = qpr.next()
            k.act(lambda e: e.activation(out=qp[:, :, 0, 0:64], in_=ptp[:, :, 0:64], func=AF.Copy), R=[ptp], W=[qp])
            if nch == 2:
                k.dve(lambda e: e.tensor_copy(out=qp[:, :, 1, 64:128], in_=ptp[:, :, 64:128]), R=[ptp], W=[qp])
            scm = scmr.next()
            k.dve(lambda e: e.memset(po[:, :, :], 0.0), W=[po])
            for hg in range(2):
                for hh in range(4):
                    h = hg * 4 + hh
                    k.pe(lambda e: e.matmul(psc[:n, hh, :n], lhsT=khT[:, h, :n], rhs=qhT[:, h, :n], start=True, stop=True),
                         R=[khT, qhT], W=[psc])
                k.dve(lambda e: e.tensor_tensor(out=scm[:n, hg * 4:hg * 4 + 4, :n], in0=psc[:n, :, :n],
                                                in1=bc(C("h_mask")[:n, :n].unsqueeze(1), [n, 4, n]), op=ALU.mult),
                      R=[psc, P.cst], W=[scm])
            for h in range(8):
                hs = slice(h * 128, (h + 1) * 128)
                k.pe(lambda e: e.matmul(po[:n, h, :], lhsT=scm[:n, h, :n], rhs=ib[:n, hs], start=False, stop=False,
                                        skip_group_check=True), R=[scm, ib], W=[po])
                k.pe(lambda e: e.matmul(po[:n, h, :], lhsT=qp[:, h, 0, :n], rhs=Sb[:, h, :], start=False, stop=(nch == 1),
                                        skip_group_check=True), R=[qp, Sb], W=[po])
            for c in range(nch):
                rows = slice(c * 64, (c + 1) * 64)
                for hg in range(2):
                    for hh in range(4):
                        h = hg * 4 + hh
                        hs = slice(h * 128, (h + 1) * 128)
                        k.pe(lambda e: e.matmul(pS[:, hh, :], lhsT=kt_[rows, hs], rhs=ib[rows, hs], start=True, stop=True),
                             R=[kt_, ib], W=[pS])
                    hsl = slice(hg * 4, hg * 4 + 4)
                    k.dve(lambda e: e.tensor_tensor(out=St[:, hsl, :], in0=St[:, hsl, :],
                                                    in1=bc(dC[:, hsl, c:c + 1], [128, 4, 128]), op=ALU.mult),
                          R=[St, dC], W=[St])
                    k.dve(lambda e: e.tensor_tensor(out=St[:, hsl, :], in0=St[:, hsl, :], in1=pS[:, :, :], op=ALU.add),
                          R=[St, pS], W=[St])
                    k.act(lambda e: e.activation(out=Sb[:, hsl, :], in_=St[:, hsl, :], func=AF.Copy), R=[St], W=[Sb])
                if c == 0 and nch == 2:
                    for h in range(8):
                        k.pe(lambda e: e.matmul(po[:n, h, :], lhsT=qp[:, h, 1, :n], rhs=Sb[:, h, :], start=False, stop=True,
                                                skip_group_check=True), R=[qp, Sb], W=[po])
            sq = f32r.next()
            k.act(lambda e: e.activation(out=sq[:n, :], in_=po[:n, :, :].rearrange("p h d -> p (h d)"), func=AF.Square),
                  R=[po], W=[sq])
            ss = ssr.next()
            k.dve(lambda e: e.tensor_reduce(out=ss[:n, :], in_=sq[:n, :].rearrange("p (h d) -> p h d", h=8), axis=AX.X, op=ALU.add),
                  R=[sq], W=[ss])
            rsqrt(k, ss[:n, :], ss[:n, :], [ss], [ss], 1.0 / 128, EPS)
            ob = f32r.next()
            ob3 = ob[:n, :].rearrange("p (h d) -> p h d", h=8)
            k.dve(lambda e: e.tensor_tensor(out=ob3, in0=po[:n, :, :], in1=bc(ss[:n, :].unsqueeze(2), [n, 8, 128]), op=ALU.mult),
                  R=[po, ss], W=[ob])
            k.pool(lambda e: e.tensor_tensor(out=ob3, in0=ob3, in1=bc(gn[:n, :].unsqueeze(1), [n, 8, 128]), op=ALU.mult),
                   R=[ob, gn], W=[ob])
            k.store("pool", S["ob"][g:g + n, :], ob[:n, :], ob)
        hdst = O["hg_p"][l] if b is None else O["hg_s"][l, b]
        k.store("pool", hdst.rearrange("h k v -> k h v"), St[:, :, :], St)
    k.end_phase(ph)


import os
P5CUT = int(os.environ.get("P5CUT", "0"))
P5NOFIN = int(os.environ.get("P5NOFIN", "0"))
P5STEPS = int(os.environ.get("P5STEPS", "-1"))
P5SUB = int(os.environ.get("P5SUB", "9"))
P5EXP = int(os.environ.get("P5EXP", "0"))


def phase5(P, l):
    k, I, S, O, C = P.k, P.I, P.S, P.O, P.C
    ph = k.phase()
    NEG_E = -math.exp(-0.5)
    mu = bcast_row(k, ph, I["c_shift_mu"][l], CW, "p5mu")
    w0 = bcast_row(k, ph, I["c_w0"][l], D, "p5w0")
    a0 = bcast_row(k, ph, I["c_a0"][l], D, "p5a0")
    kkr = bcast_row(k, ph, I["c_k_k"][l], D, "p5kk")
    kar = bcast_row(k, ph, I["c_k_a"][l], D, "p5ka")
    rkr = bcast_row(k, ph, I["c_r_k"][l], D, "p5rk")
    lnw = bcast_row(k, ph, I["c_ln_w"][l], D, "p5lnw")
    lnb = bcast_row(k, ph, I["c_ln_b"][l], D, "p5lnb")
    if l == 1:
        v0 = bcast_row(k, ph, I["c_v0"][0], D, "p5v0")
    tmpr = Rot(k, ph, 4, [128, D], F32, "p5tmp")
    stg = tmpr.tiles[0]
    w2b = k.sb(ph, [64, D], BF16, "p5w2")
    a2b = k.sb(ph, [64, D], BF16, "p5a2")
    k.load("sp", stg[:64, :], I["c_w2"][l], stg)
    k.dve(lambda e: e.tensor_copy(out=w2b[:, :], in_=stg[:64, :]), R=[stg], W=[w2b])
    k.load("sp", stg[:64, :], I["c_a2"][l], stg)
    k.dve(lambda e: e.tensor_copy(out=a2b[:, :], in_=stg[:64, :]), R=[stg], W=[a2b])
    if l == 1:
        v2b = k.sb(ph, [32, D], BF16, "p5v2w")
        k.load("sp", stg[:32, :], I["c_vres_w2"][0], stg)
        k.dve(lambda e: e.tensor_copy(out=v2b[:, :], in_=stg[:32, :]), R=[stg], W=[v2b])
    mk = k.sb(ph, [128, 4, 128], F32, "p5mk")
    for i_, nm in enumerate(["r_uts", "r_ut", "r_uts", "r_ut"]):
        k.dve(lambda e: e.tensor_copy(out=mk[:, i_, :], in_=C(nm)), R=[P.cst], W=[mk])

    mz = k.sb(ph, [128, 4, 128], F32, "p5mz")
    i4 = k.sb(ph, [128, 4, 128], BF16, "p5i4")
    for i_ in range(4):
        k.dve(lambda e: e.tensor_copy(out=mz[:, i_, :], in_=C("r_low")), R=[P.cst], W=[mz])
        k.dve(lambda e: e.tensor_copy(out=i4[:, i_, :], in_=P.identb[:, :]), R=[P.identb], W=[i4])
    cp = k.sb(ph, [128, CW], F32, "p5cp")
    cs = k.sb(ph, [128, CW], F32, "p5cs")
    hv = k.sb(ph, [128, 32], F32, "p5hv")
    vft = k.sb(ph, [128, D], F32, "p5vf")
    k2 = k.sb(ph, [128, D], F32, "p5k2")
    v2 = k.sb(ph, [128, D], F32, "p5v2")
    asig = k.sb(ph, [128, D], F32, "p5as")
    kk = k.sb(ph, [128, D], F32, "p5kkt")
    bs = k.sb(ph, [128, D], F32, "p5bs")
    ld = k.sb(ph, [128, D], F32, "p5ld")
    yt = k.sb(ph, [128, D], F32, "p5y")
    yn = k.sb(ph, [128, D], F32, "p5yn")
    s16 = Rot(k, ph, 6, [128, 16], F32, "p5s16")
    smb = k.sb(ph, [128, 3, 64], BF16, "p5smb")
    smT = k.sb(ph, [64, 3, 128], BF16, "p5smT")
    rt_, at_, bt_, kt_, bb_, kb_, vb_ = (k.sb(ph, [128, D], BF16, "p5b%d" % i_) for i_ in range(7))
    arT = k.sb(ph, [128, 8, 2, 128], BF16, "p5arT")
    bT = k.sb(ph, [128, 8, 128], BF16, "p5bT")
    kT = k.sb(ph, [128, 8, 128], BF16, "p5kT")
    AM = k.sb(ph, [128, 16, 4, 128], BF16, "p5AM")
    Qt = [k.sb(ph, [128, 4, 128], BF16, "p5Q%d" % i_) for i_ in range(4)]
    yzr = Rot(k, ph, 18, [128, 4, 128], BF16, "p5yz")
    R1 = k.sb(ph, [128, 16, 64], BF16, "p5R1")
    Ub = k.sb(ph, [128, 16, 64], BF16, "p5Ub")
    H = k.sb(ph, [128, 8, 64], F32, "p5H")
    Hb = k.sb(ph, [128, 8, 128], BF16, "p5Hb")
    k.dve(lambda e: e.memset(Hb[:, :, :], 0.0), W=[Hb])

    def refresh_hb():
        for e_ in range(2):
            rows = slice(e_ * 64, (e_ + 1) * 64)
            k.act(lambda e: e.activation(out=Hb[rows, :, e_ * 64:(e_ + 1) * 64], in_=H[rows, :, :], func=AF.Copy), R=[H], W=[Hb])
    wc = k.sb(ph, [128, 8], F32, "p5wc")
    B01 = k.ps(ph, [128, D], F32, "p5B01")
    Bt = k.ps(ph, [128, 8, 128], BF16, "p5Bt")
    gbank = Rot(k, ph, 5, [128, 512], F32, "p5g", psum=True)
    if os.environ.get("KDEBUG"):
        print("phase5 sbuf bytes remaining", P.nc.sbuf_bytes_remaining)

    def v4(bank):
        return bank[:, :].rearrange("p (a b) -> p a b", a=4)

    def v8(bank):
        return bank[:, :].rearrange("p (a b) -> p a b", a=8)

    def small_mm(col, wts, kdim, n, bias, out, func):
        for hf in range(2):
            k.pe(lambda e: e.matmul(B01[:n, hf * 512:(hf + 1) * 512], lhsT=smT[0:kdim, col, :n],
                                    rhs=wts[0:kdim, hf * 512:(hf + 1) * 512], start=True, stop=True),
                 R=[smT, wts], W=[B01])
        t = tmpr.next()
        k.dve(lambda e: e.tensor_tensor(out=t[:n, :], in0=B01[:n, :], in1=bias[:n, :], op=ALU.add), R=[B01, bias], W=[t])
        k.act(lambda e: e.activation(out=out[:n, :], in_=t[:n, :], func=func), R=[t], W=[out])

    def cum_exp(name, n, outs):
        for hf in range(2):
            k.pe(lambda e: e.matmul(B01[:n, hf * 512:(hf + 1) * 512], lhsT=C(name)[:n, :n], rhs=ld[:n, hf * 512:(hf + 1) * 512],
                                    start=True, stop=True), R=[P.cst, ld], W=[B01])
        for (t, sc) in outs:
            k.act(lambda e: e.activation(out=t[:n, :], in_=B01[:n, :], func=AF.Exp, scale=sc), R=[B01], W=[t])

    def transp8(src, n, dst_ap, dst_t, eng):
        for p in range(8):
            k.pe(lambda e: e.transpose(out=Bt[:, p, :n], in_=src[:n, p * 128:(p + 1) * 128], identity=P.identb[:n, :n]),
                 R=[src, P.identb], W=[Bt])
        if eng == "act":
            k.act(lambda e: e.activation(out=dst_ap, in_=Bt[:, :, :n], func=AF.Copy), R=[Bt], W=[dst_t])
        else:
            k.dve(lambda e: e.tensor_copy(out=dst_ap, in_=Bt[:, :, :n]), R=[Bt], W=[dst_t])

    for s in P.seqs:
        b = s["b"]
        if b is None:
            k.dve(lambda e: e.memset(H[:, :, :], 0.0), W=[H])
        else:
            Sld_t = tmpr.next()
            Sld = Sld_t[0:64, :].rearrange("p (a b) -> p a b", a=16)
            k.load("sp", Sld, I["str"][l, b].rearrange("h i j -> i h j"), Sld_t)
            for half in range(2):
                g_ = gbank.next()
                g4 = g_[:, :].rearrange("p (a b) -> p a b", a=4)
                for pp in range(4):
                    p = half * 4 + pp
                    k.pe(lambda e: e.transpose(out=g4[:, pp, 0:64], in_=Sld_t[0:64, 2 * p * 64:(2 * p + 2) * 64],
                                               identity=C("ident")[:64, :64]), R=[Sld_t, P.cst], W=[g_])
                k.dve(lambda e: e.tensor_copy(out=H[:, half * 4:half * 4 + 4, :], in_=g4[:, :, 0:64]), R=[g_], W=[H])
        refresh_hb()
        ntl = len(s["tiles"])
        for ti, tl in enumerate(s["tiles"]):
            n, g, t0 = tl["n"], tl["g"], tl["t0"]
            k.load("sp", cp[:n, :], P.pj(g, g + n, O_CP, O_CP + CW), cp)
            if t0 == 0:
                if b is None:
                    k.dve(lambda e: e.memset(cs[0:1, :], 0.0), W=[cs])
                else:
                    k.load("sp", cs[0:1, :], I["stsh"][l, b:b + 1, :], cs)
                k.load("sp", cs[1:n, :], P.pj(g, g + n - 1, O_CP, O_CP + CW), cs)
            else:
                k.load("sp", cs[:n, :], P.pj(g - 1, g + n - 1, O_CP, O_CP + CW), cs)
            if ti == ntl - 1:
                sdst = O["sh_p"][l:l + 1, :] if b is None else O["sh_s"][l, b:b + 1, :]
                k.store("pool", sdst, cp[n - 1:n, :], cp)
            k.pool(lambda e: e.tensor_tensor(out=cs[:n, :], in0=cs[:n, :], in1=cp[:n, :], op=ALU.subtract), R=[cs, cp], W=[cs])
            k.dve(lambda e: e.tensor_tensor(out=cs[:n, :], in0=cs[:n, :], in1=mu[:n, :], op=ALU.mult), R=[cs, mu], W=[cs])
            k.pool(lambda e: e.tensor_tensor(out=cs[:n, :], in0=cs[:n, :], in1=cp[:n, :], op=ALU.add), R=[cs, cp], W=[cs])
            r_ = cs[:n, C_R:C_R + D]
            kx = cs[:n, C_K:C_K + D]
            vx = cs[:n, C_V:C_V + D]
            if P5CUT and P5CUT <= 1:
                continue
            k.act(lambda e: e.activation(out=smb[:n, 0, :], in_=cs[:n, C_WLO:C_WLO + 64], func=AF.Tanh), R=[cs], W=[smb])
            k.act(lambda e: e.activation(out=smb[:n, 1, :], in_=cs[:n, C_ALO:C_ALO + 64], func=AF.Copy), R=[cs], W=[smb])
            if l == 1:
                k.load("sp", hv[:n, :], P.pj(g, g + n, O_EXT, O_EXT + 32), hv)
                k.act(lambda e: e.activation(out=smb[:n, 2, 0:32], in_=hv[:n, :], func=AF.Copy), R=[hv], W=[smb])
            for c_ in range(3 if l == 1 else 2):
                kd = 32 if c_ == 2 else 64
                k.pe(lambda e: e.transpose(out=Bt[0:kd, c_, :n], in_=smb[:n, c_, 0:kd], identity=P.identb[:n, :n]),
                     R=[smb, P.identb], W=[Bt])
            k.dve(lambda e: e.tensor_copy(out=smT[:, 0:2, :n], in_=Bt[0:64, 0:2, :n]), R=[Bt], W=[smT])
            if l == 1:
                k.dve(lambda e: e.tensor_copy(out=smT[0:32, 2, :n], in_=Bt[0:32, 2, :n]), R=[Bt], W=[smT])
            sgw = tmpr.next()
            small_mm(0, w2b, 64, n, w0, sgw, AF.Sigmoid)
            k.dve(lambda e: e.tensor_scalar(out=ld[:n, :], in0=sgw[:n, :], scalar1=NEG_E, scalar2=None, op0=ALU.mult),
                  R=[sgw], W=[ld])
            small_mm(1, a2b, 64, n, a0, asig, AF.Sigmoid)
            if l == 1:
                vmix = tmpr.next()
                small_mm(2, v2b, 32, n, v0, vmix, AF.Sigmoid)
                k.load("sp", vft[:n, :], S["vf"][g:g + n, :], vft)
                k.pool(lambda e: e.tensor_tensor(out=vft[:n, :], in0=vft[:n, :], in1=vx, op=ALU.subtract), R=[vft, cs], W=[vft])
                k.dve(lambda e: e.tensor_tensor(out=vft[:n, :], in0=vft[:n, :], in1=vmix[:n, :], op=ALU.mult), R=[vft, vmix], W=[vft])
                k.pool(lambda e: e.tensor_tensor(out=v2[:n, :], in0=vft[:n, :], in1=vx, op=ALU.add), R=[vft, cs], W=[v2])
            else:
                k.pool(lambda e: e.tensor_copy(out=v2[:n, :], in_=vx), R=[cs], W=[v2])
                k.store("pool", S["vf"][g:g + n, :], v2[:n, :], v2)
            if P5CUT and P5CUT <= 2:
                continue
            k.dve(lambda e: e.tensor_tensor(out=kk[:n, :], in0=kx, in1=kkr[:n, :], op=ALU.mult), R=[cs, kkr], W=[kk])
            t = tmpr.next()
            k.pool(lambda e: e.tensor_tensor(out=t[:n, :], in0=kk[:n, :], in1=kk[:n, :], op=ALU.mult), R=[kk], W=[t])
            sk = s16.next()
            k.dve(lambda e: e.tensor_reduce(out=sk[:n, :], in_=t[:n, :].rearrange("p (h d) -> p h d", h=16), axis=AX.X, op=ALU.add),
                  R=[t], W=[sk])
            k.act(lambda e: e.activation(out=sk[:n, :], in_=sk[:n, :], func=AF.Sqrt), R=[sk], W=[sk])
            k.dve(lambda e: e.tensor_scalar(out=sk[:n, :], in0=sk[:n, :], scalar1=1e-12, scalar2=None, op0=ALU.max), R=[sk], W=[sk])
            k.dve(lambda e: e.reciprocal(out=sk[:n, :], in_=sk[:n, :]), R=[sk], W=[sk])
            kk3 = kk[:n, :].rearrange("p (h d) -> p h d", h=16)
            k.dve(lambda e: e.tensor_tensor(out=kk3, in0=kk3, in1=bc(sk[:n, :].unsqueeze(2), [n, 16, 64]), op=ALU.mult),
                  R=[kk, sk], W=[kk])
            t = tmpr.next()
            k.dve(lambda e: e.scalar_tensor_tensor(out=t[:n, :], in0=asig[:n, :], scalar=-1.0, in1=kar[:n, :],
                                                   op0=ALU.add, op1=ALU.mult), R=[asig, kar], W=[t])
            k.pool(lambda e: e.tensor_tensor(out=t[:n, :], in0=t[:n, :], in1=kx, op=ALU.mult), R=[t, cs], W=[t])
            k.pool(lambda e: e.tensor_tensor(out=k2[:n, :], in0=t[:n, :], in1=kx, op=ALU.add), R=[t, cs], W=[k2])
            k.pool(lambda e: e.tensor_tensor(out=bs[:n, :], in0=kk[:n, :], in1=asig[:n, :], op=ALU.mult), R=[kk, asig], W=[bs])
            if P5CUT and P5CUT <= 3:
                continue
            t = tmpr.next()
            k.pool(lambda e: e.tensor_tensor(out=t[:n, :], in0=r_, in1=k2[:n, :], op=ALU.mult), R=[cs, k2], W=[t])
            k.dve(lambda e: e.tensor_tensor(out=t[:n, :], in0=t[:n, :], in1=rkr[:n, :], op=ALU.mult), R=[t, rkr], W=[t])
            s3 = s16.next()
            k.dve(lambda e: e.tensor_reduce(out=s3[:n, :], in_=t[:n, :].rearrange("p (h d) -> p h d", h=16), axis=AX.X, op=ALU.add),
                  R=[t], W=[s3])
            k.dve(lambda e: e.tensor_tensor(out=yn[:n, :].rearrange("p (h d) -> p h d", h=16),
                                            in0=v2[:n, :].rearrange("p (h d) -> p h d", h=16),
                                            in1=bc(s3[:n, :].unsqueeze(2), [n, 16, 64]), op=ALU.mult), R=[v2, s3], W=[yn])
            k.pool(lambda e: e.tensor_copy(out=vb_[:n, :], in_=v2[:n, :]), R=[v2], W=[vb_])
            ep, en = tmpr.next(), tmpr.next()
            cum_exp("r_ut", n, [(ep, 1.0), (en, -1.0)])
            k.dve(lambda e: e.tensor_tensor(out=rt_[:n, :], in0=r_, in1=ep[:n, :], op=ALU.mult), R=[cs, ep], W=[rt_])
            k.dve(lambda e: e.tensor_tensor(out=bt_[:n, :], in0=bs[:n, :], in1=en[:n, :], op=ALU.mult), R=[bs, en], W=[bt_])
            k.pool(lambda e: e.tensor_tensor(out=kt_[:n, :], in0=k2[:n, :], in1=en[:n, :], op=ALU.mult), R=[k2, en], W=[kt_])
            epa = tmpr.next()
            cum_exp("r_uts", n, [(epa, 1.0)])
            k.dve(lambda e: e.scalar_tensor_tensor(out=at_[:n, :], in0=kk[:n, :], scalar=-1.0, in1=epa[:n, :],
                                                   op0=ALU.mult, op1=ALU.mult), R=[kk, epa], W=[at_])
            eend = tmpr.next()
            cum_exp("r_low", n, [(eend, 1.0)])
            k.dve(lambda e: e.tensor_tensor(out=bb_[:n, :], in0=bs[:n, :], in1=eend[:n, :], op=ALU.mult), R=[bs, eend], W=[bb_])
            k.pool(lambda e: e.tensor_tensor(out=kb_[:n, :], in0=k2[:n, :], in1=eend[:n, :], op=ALU.mult), R=[k2, eend], W=[kb_])
            gw = gbank.next()
            for p in range(8):
                k.pe(lambda e: e.matmul(gw[:, p:p + 1], lhsT=ld[:n, p * 128:(p + 1) * 128], rhs=C("r_one")[:n, 0:1],
                                        start=True, stop=True), R=[ld, P.cst], W=[gw])
            k.act(lambda e: e.activation(out=wc[:, :], in_=gw[:, 0:8], func=AF.Exp), R=[gw], W=[wc])
            if P5CUT and P5CUT <= 4:
                continue
            transp8(at_, n, arT[:, :, 0, :n], arT, "act")
            transp8(rt_, n, arT[:, :, 1, :n], arT, "dve")
            transp8(bt_, n, bT[:, :, :n], bT, "act")
            transp8(kt_, n, kT[:, :, :n], kT, "dve")
            if P5CUT and P5CUT <= 5:
                continue
            Ycur, Zcur, ZIcur = [None] * 4, [None] * 4, [None] * 4
            for hg in range(4):
                for hh in range(4):
                    hd = hg * 4 + hh
                    p, base = hd // 2, (hd % 2) * 64
                    bs_ = slice(base, base + 64)
                    gm = gbank.next()
                    m4 = v4(gm)
                    if n == 128:
                        k.pe(lambda e: e.matmul(m4[:n, 0:2, :n], lhsT=bT[bs_, p, :n], rhs=arT[bs_, p, :, :n], start=True, stop=True),
                             R=[bT, arT], W=[gm])
                        k.pe(lambda e: e.matmul(m4[:n, 2:4, :n], lhsT=kT[bs_, p, :n], rhs=arT[bs_, p, :, :n], start=True, stop=True),
                             R=[kT, arT], W=[gm])
                    else:
                        for w_ in range(2):
                            k.pe(lambda e: e.matmul(m4[:n, w_, :n], lhsT=bT[bs_, p, :n], rhs=arT[bs_, p, w_, :n], start=True, stop=True),
                                 R=[bT, arT], W=[gm])
                            k.pe(lambda e: e.matmul(m4[:n, 2 + w_, :n], lhsT=kT[bs_, p, :n], rhs=arT[bs_, p, w_, :n], start=True, stop=True),
                                 R=[kT, arT], W=[gm])
                    k.dve(lambda e: e.tensor_tensor(out=AM[:n, hd, :, :n], in0=m4[:n, :, :n], in1=mk[:n, :, :n], op=ALU.mult),
                          R=[gm, mk], W=[AM])
            for hg in range(4):
                hsl = slice(hg * 4, hg * 4 + 4)
                for hh in range(4):
                    hd = hg * 4 + hh
                    k.pe(lambda e: e.transpose(out=Bt[:n, hg * 4 + hh - (hg // 2) * 8, :n], in_=AM[:n, hd, 0, :n], identity=P.identb[:n, :n]),
                         R=[AM, P.identb], W=[Bt])
                if hg % 2 == 1:
                    for h2 in range(2):
                        hgg = hg - 1 + h2
                        Z = yzr.next()
                        k.act(lambda e: e.activation(out=Z[:n, :, :n], in_=Bt[:n, h2 * 4:h2 * 4 + 4, :n], func=AF.Copy), R=[Bt], W=[Z])
                        Zcur[hgg] = Z
                k.dve(lambda e: e.tensor_tensor(out=Qt[hg][:n, :, :n], in0=AM[:n, hsl, 0, :n], in1=i4[:n, :, :n], op=ALU.add),
                      R=[AM, i4], W=[Qt[hg]])
            nsteps = 6 if n == 128 else 5
            for step in range(1, nsteps + 1):
                gys, gzs = [None] * 4, [None] * 4
                for hg in range(4):
                    Y, Z = Ycur[hg], Zcur[hg]
                    gy = gbank.next() if step < nsteps else None
                    gzz = gbank.next()
                    for hh in range(4):
                        hd = hg * 4 + hh
                        ysrc = AM[:n, hd, 0, :n] if Y is None else Y[:n, hh, :n]
                        ytk = AM if Y is None else Y
                        if gy is not None:
                            k.pe(lambda e: e.matmul(v4(gy)[:n, hh, :n], lhsT=Z[:n, hh, :n], rhs=ysrc, start=True, stop=True),
                                 R=[Z, ytk], W=[gy])
                        k.pe(lambda e: e.matmul(v4(gzz)[:n, hh, :n], lhsT=ysrc, rhs=Z[:n, hh, :n], start=True, stop=True),
                             R=[Z, ytk], W=[gzz])
                    Zn = yzr.next()
                    k.dve(lambda e: e.tensor_copy(out=Zn[:n, :, :n], in_=v4(gzz)[:n, :, :n]), R=[gzz], W=[Zn])
                    ZI = yzr.next()
                    k.dve(lambda e: e.tensor_tensor(out=ZI[:n, :, :n], in0=v4(gzz)[:n, :, :n], in1=i4[:n, :, :n], op=ALU.add),
                          R=[gzz, i4], W=[ZI])
                    ZIcur[hg] = ZI
                    if gy is not None:
                        Yn = yzr.next()
                        k.act(lambda e: e.activation(out=Yn[:n, :, :n], in_=v4(gy)[:n, :, :n], func=AF.Copy), R=[gy], W=[Yn])
                    else:
                        Yn = None
                    Ycur[hg], Zcur[hg] = Yn, Zn
                for hg in range(4):
                    hsl = slice(hg * 4, hg * 4 + 4)
                    Zn = Zcur[hg]
                    gq = gbank.next()
                    ZI = ZIcur[hg]
                    for hh in range(4):
                        hd = hg * 4 + hh
                        k.pe(lambda e: e.matmul(v4(gq)[:n, hh, :n], lhsT=ZI[:n, hh, :n], rhs=Qt[hg][:n, hh, :n], start=True, stop=True),
                             R=[ZI, Qt[hg]], W=[gq])
                    k.act(lambda e: e.activation(out=Qt[hg][:n, :, :n], in_=v4(gq)[:n, :, :n], func=AF.Copy), R=[gq], W=[Qt[hg]])
            if P5CUT and P5CUT <= 6:
                continue
            for half in range(2):
                g1 = gbank.next()
                for pp in range(4):
                    p = half * 4 + pp
                    k.pe(lambda e: e.matmul(v4(g1)[:n, pp, :], lhsT=arT[:, p, 0, :n], rhs=Hb[:, p, :], start=True, stop=False),
                         R=[arT, Hb], W=[g1])
                    for e_ in range(2):
                        hd = 2 * p + e_
                        k.pe(lambda e: e.matmul(v8(g1)[:n, 2 * pp + e_, :], lhsT=AM[:n, hd, 2, :n], rhs=vb_[:n, hd * 64:(hd + 1) * 64],
                                                start=False, stop=(e_ == 1)), R=[AM, vb_], W=[g1])
                k.act(lambda e: e.activation(out=R1[:n, half * 8:half * 8 + 8, :], in_=v8(g1)[:n, :, :], func=AF.Copy), R=[g1], W=[R1])
            if P5CUT == 65:
                continue
            for half in range(2):
                g2 = gbank.next()
                for h8 in range(8):
                    hd = half * 8 + h8
                    k.pe(lambda e: e.matmul(v8(g2)[:n, h8, :], lhsT=Qt[hd // 4][:n, hd % 4, :n], rhs=R1[:n, hd, :], start=True, stop=True),
                         R=[Qt[hd // 4], R1], W=[g2])
                k.dve(lambda e: e.tensor_copy(out=Ub[:n, half * 8:half * 8 + 8, :], in_=v8(g2)[:n, :, :]), R=[g2], W=[Ub])
            if P5CUT and P5CUT <= 7:
                continue
            for half in range(2):
                g3 = gbank.next()
                for pp in range(4):
                    p = half * 4 + pp
                    k.pe(lambda e: e.matmul(v4(g3)[:n, pp, :], lhsT=arT[:, p, 1, :n], rhs=Hb[:, p, :], start=True, stop=False),
                         R=[arT, Hb], W=[g3])
                    for e_ in range(2):
                        hd = 2 * p + e_
                        k.pe(lambda e: e.matmul(v8(g3)[:n, 2 * pp + e_, :], lhsT=AM[:n, hd, 1, :n], rhs=Ub[:n, hd, :], start=False, stop=False),
                             R=[AM, Ub], W=[g3])
                        k.pe(lambda e: e.matmul(v8(g3)[:n, 2 * pp + e_, :], lhsT=AM[:n, hd, 3, :n], rhs=vb_[:n, hd * 64:(hd + 1) * 64],
                                                start=False, stop=(e_ == 1)), R=[AM, vb_], W=[g3])
                k.act(lambda e: e.activation(out=yt[:n, half * 512:(half + 1) * 512], in_=g3[:n, :], func=AF.Copy), R=[g3], W=[yt])
            if P5CUT and P5CUT <= 8:
                continue
            for half in range(2):
                g4_ = gbank.next()
                for pp in range(4):
                    p = half * 4 + pp
                    ps_ = slice(p * 128, (p + 1) * 128)
                    k.pe(lambda e: e.matmul(v4(g4_)[:, pp, :], lhsT=bb_[:n, ps_], rhs=Ub[:n, 2 * p:2 * p + 2, :].rearrange("p a b -> p (a b)"),
                                            start=True, stop=False), R=[bb_, Ub], W=[g4_])
                    k.pe(lambda e: e.matmul(v4(g4_)[:, pp, :], lhsT=kb_[:n, ps_], rhs=vb_[:n, ps_], start=False, stop=True),
                         R=[kb_, vb_], W=[g4_])
                hs_ = slice(half * 4, half * 4 + 4)
                hst = tmpr.next()
                hst4 = hst[:, 0:512].rearrange("p (a b) -> p a b", a=4)
                k.act(lambda e: e.activation(out=hst[:, 0:512], in_=g4_[:, :], func=AF.Copy), R=[g4_], W=[hst])
                for e_ in range(2):
                    rows = slice(e_ * 64, (e_ + 1) * 64)
                    k.dve(lambda e: e.tensor_tensor(out=H[rows, hs_, :], in0=H[rows, hs_, :],
                                                    in1=bc(wc[rows, hs_].unsqueeze(2), [64, 4, 64]), op=ALU.mult),
                          R=[H, wc], W=[H])
                    k.dve(lambda e: e.tensor_tensor(out=H[rows, hs_, :], in0=H[rows, hs_, :],
                                                    in1=hst4[rows, :, e_ * 64:(e_ + 1) * 64], op=ALU.add),
                          R=[H, hst], W=[H])
            refresh_hb()
            if P5CUT and P5CUT <= 9:
                continue
            y3 = yt[:n, :].rearrange("p (h d) -> p h d", h=16)
            s1 = s16.next()
            k.dve(lambda e: e.tensor_reduce(out=s1[:n, :], in_=y3, axis=AX.X, op=ALU.add), R=[yt], W=[s1])
            k.dve(lambda e: e.tensor_scalar(out=s1[:n, :], in0=s1[:n, :], scalar1=1.0 / 64, scalar2=None, op0=ALU.mult), R=[s1], W=[s1])
            k.dve(lambda e: e.tensor_tensor(out=y3, in0=y3, in1=bc(s1[:n, :].unsqueeze(2), [n, 16, 64]), op=ALU.subtract),
                  R=[yt, s1], W=[yt])
            t = tmpr.next()
            k.pool(lambda e: e.tensor_tensor(out=t[:n, :], in0=yt[:n, :], in1=yt[:n, :], op=ALU.mult), R=[yt], W=[t])
            s2 = s16.next()
            k.dve(lambda e: e.tensor_reduce(out=s2[:n, :], in_=t[:n, :].rearrange("p (h d) -> p h d", h=16), axis=AX.X, op=ALU.add),
                  R=[t], W=[s2])
            rsqrt(k, s2[:n, :], s2[:n, :], [s2], [s2], 1.0 / 64, GN_EPS)
            tn = tmpr.next()
            tn3 = tn[:n, :].rearrange("p (h d) -> p h d", h=16)
            k.dve(lambda e: e.tensor_tensor(out=tn3, in0=y3, in1=bc(s2[:n, :].unsqueeze(2), [n, 16, 64]), op=ALU.mult),
                  R=[yt, s2], W=[tn])
            k.pool(lambda e: e.tensor_tensor(out=tn[:n, :], in0=tn[:n, :], in1=lnw[:n, :], op=ALU.mult), R=[tn, lnw], W=[tn])
            k.dve(lambda e: e.tensor_tensor(out=tn[:n, :], in0=tn[:n, :], in1=lnb[:n, :], op=ALU.add), R=[tn, lnb], W=[tn])
            k.pool(lambda e: e.tensor_tensor(out=yn[:n, :], in0=yn[:n, :], in1=tn[:n, :], op=ALU.add), R=[yn, tn], W=[yn])
            k.store("pool", S["oc"][g:g + n, :], yn[:n, :], yn)
        rwo_t = tmpr.next()
        rwo = rwo_t[0:64, :].rearrange("p (a b) -> p a b", a=8)
        for half in range(0 if P5NOFIN else 2):
            g_ = gbank.next()
            g4 = v4(g_)
            for pp in range(4):
                p = half * 4 + pp
                k.pe(lambda e: e.transpose(out=g4[0:64, pp, :], in_=H[:, p, :], identity=C("ident")), R=[H, P.cst], W=[g_])
            k.dve(lambda e: e.tensor_copy(out=rwo[:, half * 4:half * 4 + 4, :], in_=g4[0:64, :, :]), R=[g_], W=[rwo_t])
        rdst = O["rw_p"][l] if b is None else O["rw_s"][l, b]
        k.store("pool", rdst.rearrange("(p e) i j -> i p e j", e=2), rwo.rearrange("i p (e j) -> i p e j", e=2), rwo_t)
    k.end_phase(ph)


def phase6(P, l):
    k, I, S, O = P.k, P.I, P.S, P.O
    ph = k.phase()
    stg = Rot(k, ph, 2, [128, 2, D], F32, "p6stg")
    W = {}
    for nm in ["w_out_a", "w_out_b", "w_out_c", "w_o"]:
        wt = k.sb(ph, [128, 8, D], BF16, "p6" + nm)
        src = I[nm][l].rearrange("(k p) c -> p k c", p=128)
        for c4 in range(4):
            st = stg.next()
            k.load("sp", st[:, :, :], src[:, 2 * c4:2 * c4 + 2, :], st)
            k.dve(lambda e: e.tensor_copy(out=wt[:, 2 * c4:2 * c4 + 2, :], in_=st[:, :, :]), R=[st], W=[wt])
        W[nm] = wt
    ldr = Rot(k, ph, 12, [128, D], F32, "p6ld")
    tmpr = Rot(k, ph, 4, [128, D], F32, "p6tmp")
    mrg = Rot(k, ph, 2, [128, D], F32, "p6mrg")
    ogr = Rot(k, ph, 2, [128, D], BF16, "p6og")
    tTr = Rot(k, ph, 2, [128, 8, 128], BF16, "p6tT")
    yr = Rot(k, ph, 2, [128, D], F32, "p6y")
    pbr = Rot(k, ph, 2, [128, D], F32, "p6pb", psum=True)
    ptr = Rot(k, ph, 2, [128, 8, 128], BF16, "p6pt", psum=True)

    def proj_mm(srcb, n, wt):
        pt = ptr.next()
        for kk in range(8):
            k.pe(lambda e: e.transpose(out=pt[:, kk, :n], in_=srcb[:n, kk * 128:(kk + 1) * 128], identity=P.identb[:n, :n]),
                 R=[srcb, P.identb], W=[pt])
        tT = tTr.next()
        k.act(lambda e: e.activation(out=tT[:, :, :n], in_=pt[:, :, :n], func=AF.Copy), R=[pt], W=[tT])
        pb = pbr.next()
        for hf in range(2):
            for kk in range(8):
                k.pe(lambda e: e.matmul(pb[:n, hf * 512:(hf + 1) * 512], lhsT=tT[:, kk, :n], rhs=wt[:, kk, hf * 512:(hf + 1) * 512],
                                        start=(kk == 0), stop=(kk == 7)), R=[tT, wt], W=[pb])
        return pb

    for tl in P.tiles:
        n, g, t0 = tl["n"], tl["g"], tl["t0"]
        s = tl["seq"]
        b = s["b"]
        merged = mrg.next()
        for mi, (osrc, gcol, mcol, wn) in enumerate([("oa", O_AG, O_MA, "w_out_a"), ("ob", O_BG, O_MB, "w_out_b"),
                                                     ("oc", O_CG, O_MC, "w_out_c")]):
            o_, g_, m_ = ldr.next(), ldr.next(), ldr.next()
            k.load("sp", o_[:n, :], S[osrc][g:g + n, :], o_)
            k.load("sp", g_[:n, :], P.pj(g, g + n, gcol, gcol + D), g_)
            k.load("sp", m_[:n, :], P.pj(g, g + n, mcol, mcol + D), m_)
            sg = tmpr.next()
            k.act(lambda e: e.activation(out=sg[:n, :], in_=g_[:n, :], func=AF.Silu), R=[g_], W=[sg])
            og = ogr.next()
            k.dve(lambda e: e.tensor_tensor(out=og[:n, :], in0=o_[:n, :], in1=sg[:n, :], op=ALU.mult), R=[o_, sg], W=[og])
            pb = proj_mm(og, n, W[wn])
            sm = tmpr.next()
            k.act(lambda e: e.activation(out=sm[:n, :], in_=m_[:n, :], func=AF.Sigmoid), R=[m_], W=[sm])
            if mi == 0:
                k.dve(lambda e: e.tensor_tensor(out=merged[:n, :], in0=pb[:n, :], in1=sm[:n, :], op=ALU.mult), R=[pb, sm], W=[merged])
            else:
                k.dve(lambda e: e.tensor_tensor(out=sm[:n, :], in0=pb[:n, :], in1=sm[:n, :], op=ALU.mult), R=[pb, sm], W=[sm])
                k.pool(lambda e: e.tensor_tensor(out=merged[:n, :], in0=merged[:n, :], in1=sm[:n, :], op=ALU.add),
                       R=[merged, sm], W=[merged])
        mb = ogr.next()
        k.act(lambda e: e.activation(out=mb[:n, :], in_=merged[:n, :], func=AF.Copy), R=[merged], W=[mb])
        py = proj_mm(mb, n, W["w_o"])
        x = ldr.next()
        src, _ = P.xsrc(l, tl)
        k.load("sp", x[:n, :], src, x)
        y = yr.next()
        k.dve(lambda e: e.tensor_tensor(out=y[:n, :], in0=py[:n, :], in1=x[:n, :], op=ALU.add), R=[py, x], W=[y])
        if l == 0:
            dst = S["xmid"][g:g + n, :]
        elif b is None:
            dst = O["y_p"][t0:t0 + n, :]
        else:
            dst = O["y_s"][b, t0:t0 + n, :]
        k.store("pool", dst, y[:n, :], y)
    k.end_phase(ph)


_CACHE = {}
NCORES = 8


def _get_prog(T):
    if T not in _CACHE:
        _CACHE[T] = build(T)
    return _CACHE[T]


def kernel(x_prompt, x_sample, cache_attn_k, cache_attn_v, state_hgrn, state_rwkv, state_rwkv_shift,
           norm_g, w_in, a_qnorm_g, a_knorm_g, a_lambda, a_subln_g, b_lower, b_norm_g,
           c_shift_mu, c_w0, c_w2, c_a0, c_a2, c_k_k, c_k_a, c_r_k, c_ln_w, c_ln_b,
           c_vres_w1, c_vres_w2, c_v0, w_out_a, w_out_b, w_out_c, w_o):
    f = lambda a: np.ascontiguousarray(np.asarray(a, dtype=np.float32))
    x_prompt = f(x_prompt)
    B, T, _ = x_prompt.shape
    assert B == NCORES
    P = _get_prog(T)
    carr, _ = make_consts()
    shared = {
        "norm_g": f(norm_g), "w_in": f(w_in), "a_qnorm_g": f(a_qnorm_g), "a_knorm_g": f(a_knorm_g),
        "a_lambda": f(a_lambda).reshape(2, 256), "a_subln_g": f(a_subln_g), "b_lower": f(b_lower), "b_norm_g": f(b_norm_g),
        "c_shift_mu": f(c_shift_mu), "c_w0": f(c_w0), "c_w2": f(c_w2), "c_a0": f(c_a0), "c_a2": f(c_a2),
        "c_k_k": f(c_k_k), "c_k_a": f(c_k_a), "c_r_k": f(c_r_k).reshape(2, 1024), "c_ln_w": f(c_ln_w), "c_ln_b": f(c_ln_b),
        "c_vres_w1": f(c_vres_w1), "c_vres_w2": f(c_vres_w2), "c_v0": f(c_v0),
        "w_out_a": f(w_out_a), "w_out_b": f(w_out_b), "w_out_c": f(w_out_c), "w_o": f(w_o),
        "consts": carr,
        "rope_p": rope_tables(np.arange(T)), "rope_s": rope_tables(PAST + np.arange(TS)),
    }
    x_sample = f(x_sample)
    ck, cv = f(cache_attn_k), f(cache_attn_v)
    sth, str_, stsh = f(state_hgrn), f(state_rwkv), f(state_rwkv_shift)
    in_maps = []
    for c in range(NCORES):
        m = dict(shared)
        sl = slice(2 * c, 2 * c + 2)
        m["x_p"] = x_prompt[c]
        m["x_s"] = x_sample[sl]
        m["ck"] = np.ascontiguousarray(ck[:, sl]).reshape(2, 2, PAST, D)
        m["cv"] = np.ascontiguousarray(cv[:, sl]).reshape(2, 2, PAST, D)
        m["sth"] = np.ascontiguousarray(sth[:, sl])
        m["str"] = np.ascontiguousarray(str_[:, sl])
        m["stsh"] = np.ascontiguousarray(stsh[:, sl])
        in_maps.append(m)
    res = run_bass_kernel_spmd(P.nc, in_maps, core_ids=list(range(NCORES)))
    R = res.results
    NB = 2 * NCORES
    y_p = np.stack([R[c]["y_p"] for c in range(NCORES)], 0)
    y_s = np.concatenate([R[c]["y_s"] for c in range(NCORES)], 0)
    k_p = np.stack([R[c]["k_p"].reshape(2, T, 8, 128) for c in range(NCORES)], 1)
    v_p = np.stack([R[c]["v_p"].reshape(2, T, 8, 128) for c in range(NCORES)], 1)
    hg_p = np.stack([R[c]["hg_p"] for c in range(NCORES)], 1)
    rw_p = np.stack([R[c]["rw_p"] for c in range(NCORES)], 1)
    sh_p = np.stack([R[c]["sh_p"] for c in range(NCORES)], 1)
    k_s = np.concatenate([R[c]["k_s"].reshape(2, 2, TS, 8, 128) for c in range(NCORES)], 1)
    v_s = np.concatenate([R[c]["v_s"].reshape(2, 2, TS, 8, 128) for c in range(NCORES)], 1)
    hg_s = np.concatenate([R[c]["hg_s"] for c in range(NCORES)], 1)
    rw_s = np.concatenate([R[c]["rw_s"] for c in range(NCORES)], 1)
    sh_s = np.concatenate([R[c]["sh_s"] for c in range(NCORES)], 1)
    outs = (y_p, y_s, k_p, v_p, hg_p, rw_p, sh_p, k_s, v_s, hg_s, rw_s, sh_s)
    return tuple(np.ascontiguousarray(o, dtype=np.float32) for o in outs)
```

```python
import math
from contextlib import ExitStack

import numpy as np
import concourse.bass as bass
import concourse.mybir as mybir
from concourse.bass_utils import run_bass_kernel_spmd

F32 = mybir.dt.float32
BF16 = mybir.dt.bfloat16
AF = mybir.ActivationFunctionType
ALU = mybir.AluOpType
AX = mybir.AxisListType

D = 1024
NCOL = 15488
NCX = NCOL + 32
PAST = 1024
TS = 64
EPS = 1e-6
GN_EPS = 64e-5
ROPE_THETA = 500000.0
O_AQ, O_AK, O_AV, O_AG = 0, 1024, 2048, 3072
O_BQ, O_BF, O_BI, O_BG = 4096, 5120, 6144, 7168
O_CP = 8192
O_CG = 11392
O_MA, O_MB, O_MC = 12416, 13440, 14464
O_EXT = 15488
C_R, C_WLO, C_K, C_V, C_ALO = 0, 1024, 1088, 2112, 3136
CW = 3200


class Tk:
    def __init__(self, h, name, dram=False):
        self.h = h
        self.name = name
        self.lw = None
        self.rd = []
        self.ds = {}
        self.dram = dram
        self.tok = {}
        self.rtok = {}

    def __getitem__(self, k):
        return self.h[k]


class Ctx:
    def __init__(self, nc):
        self.nc = nc
        self.es = ExitStack()
        self.eng = {"pe": nc.tensor, "act": nc.scalar, "dve": nc.vector, "pool": nc.gpsimd, "sp": nc.sync}
        self.sem = {}
        self.cnt = {}
        self.waited = {}
        for k in self.eng:
            self.sem[k] = self.es.enter_context(nc.semaphore("es_" + k))
            self.cnt[k] = 0
            self.waited[k] = {}
        self.free_ds = {"hw": [], "sw": []}
        self.nds = 0
        self.uid = 0
        self.ninst = 0

    def get_ds(self, t, q):
        kind = "sw" if q == "pool" else "hw"
        if kind not in t.ds:
            t.ds[kind] = self.new_ds(kind)
        return t.ds[kind]

    def new_ds(self, kind):
        if self.free_ds[kind]:
            return self.free_ds[kind].pop()
        self.nds += 1
        h = self.es.enter_context(self.nc.semaphore("ds%d" % self.nds))
        return [h, 0, "ds%d" % self.nds]

    def sb(self, ph, shape, dt, name):
        self.uid += 1
        h = ph.enter_context(self.nc.sbuf_tensor("%s_%d" % (name, self.uid), list(shape), dt))
        t = Tk(h, name)
        ph.tiles.append(t)
        return t

    def ps(self, ph, shape, dt, name):
        self.uid += 1
        h = ph.enter_context(self.nc.psum_tensor("%s_%d" % (name, self.uid), list(shape), dt))
        t = Tk(h, name)
        ph.tiles.append(t)
        return t

    def phase(self):
        ph = ExitStack()
        ph.tiles = []
        return ph

    def end_phase(self, ph):
        self.barrier(ph.tiles)
        for t in ph.tiles:
            for kind, ds in t.ds.items():
                self.free_ds[kind].append(ds)
            t.ds = {}
        ph.close()

    def barrier(self, tiles):
        deps = []
        for k in self.eng:
            if k != "sp" and self.cnt[k] > 0:
                deps.append(("e", k, self.cnt[k]))
        for t in tiles:
            for ds in t.ds.values():
                if ds[1] > 0:
                    deps.append(("d", ds, ds[1]))
        for k in self.eng:
            for d in deps:
                self._wait(k, d)

    def _wait(self, ename, dep):
        if dep is None:
            return
        kind, obj, val = dep
        if kind == "e":
            if obj == ename and ename in ("pe", "sp"):
                return
            key = "e_" + obj
            semh = self.sem[obj]
        else:
            key = obj[2]
            semh = obj[0]
        w = self.waited[ename]
        if w.get(key, 0) >= val:
            return
        self.eng[ename].wait_ge(semh, val)
        w[key] = val
        self.ninst += 1

    def op(self, ename, fn, R=(), W=()):
        deps = []
        for t in R:
            deps.append(t.lw)
        for t in W:
            deps.append(t.lw)
            deps.extend(t.rd)
        for d in deps:
            self._wait(ename, d)
        ins = fn(self.eng[ename])
        self.cnt[ename] += 1
        ins.then_inc(self.sem[ename], 1)
        self.ninst += 1
        tok = ("e", ename, self.cnt[ename])
        for t in R:
            t.rd.append(tok)
        for t in W:
            t.lw = tok
            t.rd = []
        return ins

    def pe(self, fn, R=(), W=()):
        return self.op("pe", fn, R, W)

    def act(self, fn, R=(), W=()):
        return self.op("act", fn, R, W)

    def dve(self, fn, R=(), W=()):
        return self.op("dve", fn, R, W)

    def pool(self, fn, R=(), W=()):
        return self.op("pool", fn, R, W)

    def load(self, q, out_ap, in_ap, sbt, dr=None, slow=False):
        deps = [sbt.lw] + list(sbt.rd)
        for d in deps:
            self._wait(q, d)
        ds = self.get_ds(sbt, q)
        if slow:
            ins = self.eng[q].dma_start(out=out_ap, in_=in_ap, allow_slow_non_contiguous=True)
        else:
            ins = self.eng[q].dma_start(out=out_ap, in_=in_ap)
        ds[1] += 16
        ins.then_inc(ds[0], 16)
        self.ninst += 1
        sbt.lw = ("d", ds, ds[1])
        sbt.rd = []

    def store(self, q, out_ap, in_ap, sbt, dr=None):
        deps = [sbt.lw]
        for d in deps:
            self._wait(q, d)
        ds = self.get_ds(sbt, q)
        ins = self.eng[q].dma_start(out=out_ap, in_=in_ap)
        ds[1] += 16
        ins.then_inc(ds[0], 16)
        self.ninst += 1
        sbt.rd.append(("d", ds, ds[1]))


class Rot:
    def __init__(self, k, ph, n, shape, dt, name, psum=False):
        mk = k.ps if psum else k.sb
        self.tiles = [mk(ph, shape, dt, "%s%d" % (name, i)) for i in range(n)]
        self.i = 0

    def next(self):
        t = self.tiles[self.i % len(self.tiles)]
        self.i += 1
        return t


def rsqrt(k, out, in_, R, W, scale, bias):
    k.act(lambda e: e.activation(out=out, in_=in_, func=AF.Sqrt, scale=scale, bias=bias), R=R, W=W)
    k.dve(lambda e: e.reciprocal(out=out, in_=out), R=W, W=W)


def bc(ap, shape):
    return ap.to_broadcast(list(shape))


def make_consts():
    s = np.arange(128)[:, None]
    t = np.arange(128)[None, :]
    same = (s // 64) == (t // 64)
    c = {}
    c["ident"] = np.eye(128)
    ut64 = (same & (s <= t)).astype(np.float64)
    mid = 64 * (t // 64) + 31
    a_mid = (same & (s <= mid)).astype(np.float64)
    a_end = same.astype(np.float64)
    c["h_ut"] = ut64
    c["h_d1"] = ut64 - a_mid
    c["h_d2"] = a_end - ut64
    c["h_end"] = a_end
    c["h_mask"] = ut64
    c["r_ut"] = (s <= t).astype(np.float64)
    c["r_uts"] = (s < t).astype(np.float64)
    c["r_low"] = (s > t).astype(np.float64)
    c["r_one"] = np.ones((128, 128))
    names = ["ident", "h_ut", "h_d1", "h_d2", "h_end", "h_mask", "r_ut", "r_uts", "r_low", "r_one"]
    arr = np.concatenate([c[n] for n in names], axis=1).astype(np.float32)
    offs = {n: i * 128 for i, n in enumerate(names)}
    return arr, offs


def rope_tables(pos):
    half = 8
    inv_freq = (np.float32(ROPE_THETA) ** (-(np.arange(half, dtype=np.float32) * np.float32(2.0 / 16)))).astype(np.float32)
    ang = pos.astype(np.float32)[:, None] * inv_freq[None, :]
    return np.concatenate([np.cos(ang), np.sin(ang)], axis=1).astype(np.float32)


class Prog:
    pass


def build(T, nlayers=2, upto=99, debug=False):
    nc = bass.Bass("TRN2", target_bir_lowering=False)
    k = Ctx(nc)
    P = Prog()
    P.nc, P.k, P.T = nc, k, T
    Ttot = T + 2 * TS
    P.Ttot = Ttot
    carr, coff = make_consts()
    P.coff = coff

    def din(name, shape, dt=F32):
        return nc.dram_tensor(name, list(shape), dt, kind="ExternalInput").ap()

    def dout(name, shape, dt=F32):
        return Tk(nc.dram_tensor(name, list(shape), dt, kind="ExternalOutput").ap(), name, dram=True)

    def dscr(name, shape, dt=F32):
        kind = "ExternalOutput" if (debug and name in debug) else "Internal"
        return Tk(nc.dram_tensor(name, list(shape), dt, kind=kind).ap(), name, dram=True)

    I = {}
    I["x_p"] = din("x_p", [T, D])
    I["x_s"] = din("x_s", [2, TS, D])
    I["ck"] = din("ck", [2, 2, PAST, D])
    I["cv"] = din("cv", [2, 2, PAST, D])
    I["sth"] = din("sth", [2, 2, 8, 128, 128])
    I["str"] = din("str", [2, 2, 16, 64, 64])
    I["stsh"] = din("stsh", [2, 2, CW])
    I["norm_g"] = din("norm_g", [2, D])
    I["w_in"] = din("w_in", [2, D, NCOL])
    I["a_qnorm_g"] = din("a_qnorm_g", [2, 64])
    I["a_knorm_g"] = din("a_knorm_g", [2, 64])
    I["a_lambda"] = din("a_lambda", [2, 256])
    I["a_subln_g"] = din("a_subln_g", [2, 128])
    I["b_lower"] = din("b_lower", [2, 1024])
    I["b_norm_g"] = din("b_norm_g", [2, 128])
    I["c_shift_mu"] = din("c_shift_mu", [2, CW])
    I["c_w0"] = din("c_w0", [2, 1024])
    I["c_w2"] = din("c_w2", [2, 64, 1024])
    I["c_a0"] = din("c_a0", [2, 1024])
    I["c_a2"] = din("c_a2", [2, 64, 1024])
    I["c_k_k"] = din("c_k_k", [2, 1024])
    I["c_k_a"] = din("c_k_a", [2, 1024])
    I["c_r_k"] = din("c_r_k", [2, 1024])
    I["c_ln_w"] = din("c_ln_w", [2, 1024])
    I["c_ln_b"] = din("c_ln_b", [2, 1024])
    I["c_vres_w1"] = din("c_vres_w1", [1, D, 32])
    I["c_vres_w2"] = din("c_vres_w2", [1, 32, 1024])
    I["c_v0"] = din("c_v0", [1, 1024])
    I["w_out_a"] = din("w_out_a", [2, D, D])
    I["w_out_b"] = din("w_out_b", [2, D, D])
    I["w_out_c"] = din("w_out_c", [2, D, D])
    I["w_o"] = din("w_o", [2, D, D])
    I["consts"] = din("consts", list(carr.shape))
    I["rope_p"] = din("rope_p", [T, 16])
    I["rope_s"] = din("rope_s", [TS, 16])
    P.I = I

    O = {}
    O["y_p"] = dout("y_p", [T, D])
    O["y_s"] = dout("y_s", [2, TS, D])
    O["k_p"] = dout("k_p", [2, T, D])
    O["v_p"] = dout("v_p", [2, T, D])
    O["hg_p"] = dout("hg_p", [2, 8, 128, 128])
    O["rw_p"] = dout("rw_p", [2, 16, 64, 64])
    O["sh_p"] = dout("sh_p", [2, CW])
    O["k_s"] = dout("k_s", [2, 2, TS, D])
    O["v_s"] = dout("v_s", [2, 2, TS, D])
    O["hg_s"] = dout("hg_s", [2, 2, 8, 128, 128])
    O["rw_s"] = dout("rw_s", [2, 2, 16, 64, 64])
    O["sh_s"] = dout("sh_s", [2, 2, CW])
    P.O = O

    seqs = []
    seqs.append(dict(name="p", T=T, g0=0, n=128, past=0, b=None))
    seqs.append(dict(name="s0", T=TS, g0=T, n=TS, past=PAST, b=0))
    seqs.append(dict(name="s1", T=TS, g0=T + TS, n=TS, past=PAST, b=1))
    tiles = []
    for s in seqs:
        s["tiles"] = []
        for t0 in range(0, s["T"], s["n"]):
            tl = dict(seq=s, t0=t0, n=s["n"], g=s["g0"] + t0, idx=len(tiles))
            tiles.append(tl)
            s["tiles"].append(tl)
    P.seqs, P.tiles = seqs, tiles
    NTL = len(tiles)

    S = {}
    S["hT"] = dscr("hT", [NTL, 128, 8, 128], BF16)
    PJ = [(0, 4096, "pjA"), (4096, 8192, "pjB"), (8192, 12416, "pjC"), (12416, NCX, "pjM")]
    for lo, hi, nm in PJ:
        S[nm] = dscr(nm, [Ttot, hi - lo])

    def pj(r0, r1, c0, c1):
        for lo, hi, nm in PJ:
            if lo <= c0 and c1 <= hi:
                return S[nm][r0:r1, c0 - lo:c1 - lo]
        raise ValueError((c0, c1))
    P.pj = pj
    S["xmid"] = dscr("xmid", [Ttot, D])
    S["oa"] = dscr("oa", [Ttot, D])
    S["ob"] = dscr("ob", [Ttot, D])
    S["oc"] = dscr("oc", [Ttot, D])
    S["vf"] = dscr("vf", [Ttot, D])
    for s in seqs:
        TK = s["past"] + s["T"]
        s["TK"] = TK
        s["qT"] = dscr("qT_" + s["name"], [8, 128, s["T"]], BF16)
        s["kT"] = dscr("kT_" + s["name"], [8, 128, TK], BF16)
        s["vb"] = dscr("vb_" + s["name"], [TK, 8, 129], BF16)
    P.S = S

    def xsrc(l, tl):
        s = tl["seq"]
        if l == 0:
            if s["b"] is None:
                return I["x_p"][tl["t0"]:tl["t0"] + tl["n"], :], None
            return I["x_s"][s["b"], tl["t0"]:tl["t0"] + tl["n"], :], None
        return S["xmid"][tl["g"]:tl["g"] + tl["n"], :], S["xmid"]
    P.xsrc = xsrc

    gph = k.phase()
    P.gph = gph
    cst = k.sb(gph, [128, carr.shape[1]], F32, "cst")
    k.load("sp", cst[:], I["consts"][:, :], cst)
    identb = k.sb(gph, [128, 128], BF16, "identb")
    k.dve(lambda e: e.tensor_copy(out=identb[:], in_=cst[:, coff["ident"]:coff["ident"] + 128]), R=[cst], W=[identb])
    P.cst, P.identb = cst, identb

    def C(name):
        return cst[:, coff[name]:coff[name] + 128]
    P.C = C

    for l in range(nlayers):
        if upto >= 0:
            phase0(P, l)
        if upto >= 1:
            phase1(P, l)
        if upto >= 2:
            phase2(P, l)
        if upto >= 3:
            phase3(P, l)
        if upto >= 4:
            phase4(P, l)
        if upto >= 5:
            phase5(P, l)
        if upto >= 6:
            phase6(P, l)

    k.end_phase(gph)
    return P


def phase0(P, l):
    k, I, S = P.k, P.I, P.S
    ph = k.phase()
    xr = Rot(k, ph, 3, [128, D], F32, "p0x")
    jr = Rot(k, ph, 2, [128, D], BF16, "p0j")
    hr = Rot(k, ph, 2, [128, D], BF16, "p0h")
    sr = Rot(k, ph, 4, [128, 2], F32, "p0s")
    tr = Rot(k, ph, 2, [128, 8, 128], BF16, "p0t")
    pr = Rot(k, ph, 2, [128, 8, 128], BF16, "p0p", psum=True)
    for tl in P.tiles:
        n = tl["n"]
        src, dr = P.xsrc(l, tl)
        x = xr.next()
        k.load("sp", x[:n, :], src, x, dr)
        st = sr.next()
        j = jr.next()
        k.act(lambda e: e.activation(out=j[:n, :], in_=x[:n, :], func=AF.Square, accum_out=st[:n, 0:1]), R=[x], W=[j, st])
        rsqrt(k, st[:n, 1:2], st[:n, 0:1], [st], [st], 1.0 / D, EPS)
        h = hr.next()
        k.dve(lambda e: e.tensor_scalar(out=h[:n, :], in0=x[:n, :], scalar1=st[:n, 1:2], scalar2=None,
                                        op0=ALU.mult), R=[x, st], W=[h])
        pt = pr.next()
        for kk in range(8):
            k.pe(lambda e: e.transpose(out=pt[:, kk, :n], in_=h[:n, kk * 128:(kk + 1) * 128], identity=P.identb[:n, :n]),
                 R=[h, P.identb], W=[pt])
        ht = tr.next()
        k.act(lambda e: e.activation(out=ht[:, :, :n], in_=pt[:, :, :n], func=AF.Copy), R=[pt], W=[ht])
        k.store("pool", S["hT"][tl["idx"], :, :, :n], ht[:, :, :n], ht, S["hT"])
    k.end_phase(ph)


def phase1(P, l):
    k, I, S = P.k, P.I, P.S
    ph = k.phase()
    gcol = k.sb(ph, [128, 8], F32, "p1g")
    k.load("sp", gcol[:], I["norm_g"][l].rearrange("(k p) -> p k", p=128), gcol, slow=True)
    wf = Rot(k, ph, 2, [128, 8, 1024], F32, "p1wf")
    wb = Rot(k, ph, 2, [128, 8, 1024], BF16, "p1wb")
    hr = Rot(k, ph, 3, [128, 8, 128], BF16, "p1h")
    orr = Rot(k, ph, 3, [128, 1024], F32, "p1o")
    pr = Rot(k, ph, 2, [128, 1024], F32, "p1p", psum=True)
    groups = [(c0, 1024) for c0 in range(0, 11264, 1024)] + [(11264, 128)] + [(c0, 1024) for c0 in range(11392, NCOL, 1024)]
    if l == 1:
        groups.append((O_EXT, 32))
    ev = 0
    for (c0, cw) in groups:
        w32 = wf.next()
        if c0 == O_EXT:
            src = I["c_vres_w1"][0].rearrange("(k p) c -> p k c", p=128)
        else:
            src = I["w_in"][l][:, c0:c0 + cw].rearrange("(k p) c -> p k c", p=128)
        k.load("sp", w32[:, :, :cw], src, w32)
        w = wb.next()
        k.dve(lambda e: e.tensor_tensor(out=w[:, :, :cw], in0=w32[:, :, :cw],
                                        in1=bc(gcol[:, :].unsqueeze(2), [128, 8, cw]), op=ALU.mult),
              R=[w32, gcol], W=[w])
        for tl in P.tiles:
            n = tl["n"]
            h = hr.next()
            k.load("sp", h[:, :, :n], S["hT"][tl["idx"], :, :, :n], h, S["hT"])
            pt = pr.next()
            for n0 in range(0, cw, 512):
                nw = min(512, cw - n0)
                for kk in range(8):
                    k.pe(lambda e: e.matmul(pt[:n, n0:n0 + nw], lhsT=h[:, kk, :n], rhs=w[:, kk, n0:n0 + nw],
                                            start=(kk == 0), stop=(kk == 7)), R=[h, w], W=[pt])
            o = orr.next()
            if ev % 2 == 0:
                k.act(lambda e: e.activation(out=o[:n, :cw], in_=pt[:n, :cw], func=AF.Copy), R=[pt], W=[o])
            else:
                k.dve(lambda e: e.tensor_copy(out=o[:n, :cw], in_=pt[:n, :cw]), R=[pt], W=[o])
            ev += 1
            k.store("pool", P.pj(tl["g"], tl["g"] + n, c0, c0 + cw), o[:n, :cw], o)
    k.end_phase(ph)


def bcast_row(k, ph, ap1d, width, name, q="sp"):
    t = k.sb(ph, [128, width], F32, name)
    k.load(q, t[:], ap1d.partition_broadcast(128), t)
    return t


def phase2(P, l):
    k, I, S, O = P.k, P.I, P.S, P.O
    ph = k.phase()
    gq = bcast_row(k, ph, I["a_qnorm_g"][l], 64, "p2gq")
    gk = bcast_row(k, ph, I["a_knorm_g"][l], 64, "p2gk")
    xr = Rot(k, ph, 4, [128, D], F32, "p2x")
    tmp = Rot(k, ph, 2, [128, D], F32, "p2tmp")
    xn = Rot(k, ph, 3, [128, D], F32, "p2xn")
    ssr = Rot(k, ph, 4, [128, 16], F32, "p2ss")
    csr = Rot(k, ph, 2, [128, 16], F32, "p2cs")
    rtr = Rot(k, ph, 2, [128, 4, 16, 8], F32, "p2rt")
    xbr = Rot(k, ph, 3, [128, D], BF16, "p2xb")
    vbr = Rot(k, ph, 2, [128, 8, 129], BF16, "p2vb")
    tTr = Rot(k, ph, 3, [128, 8, 128], BF16, "p2tT")
    ptr = Rot(k, ph, 3, [128, 8, 128], BF16, "p2pt", psum=True)
    for vt in vbr.tiles:
        k.dve(lambda e: e.memset(vt[:, :, 128:129], 1.0), W=[vt])

    def transpose_store(xb, n, dst_ap):
        pt = ptr.next()
        for h in range(8):
            k.pe(lambda e: e.transpose(out=pt[:, h, :n], in_=xb[:n, h * 128:(h + 1) * 128], identity=P.identb[:n, :n]),
                 R=[xb, P.identb], W=[pt])
        tT = tTr.next()
        k.act(lambda e: e.activation(out=tT[:, :, :n], in_=pt[:, :, :n], func=AF.Copy), R=[pt], W=[tT])
        k.store("pool", dst_ap, tT[:, :, :n], tT)

    def v_store(v, n, dst_rows):
        vb = vbr.next()
        k.act(lambda e: e.activation(out=vb[:n, :, 0:128], in_=v[:n, :].rearrange("p (h d) -> p h d", h=8), func=AF.Copy),
              R=[v], W=[vb])
        k.store("pool", dst_rows, vb[:n, :, :], vb)

    def normrope(x, n, g, cs):
        t = tmp.next()
        k.act(lambda e: e.activation(out=t[:n, :], in_=x[:n, :], func=AF.Square), R=[x], W=[t])
        ss = ssr.next()
        k.dve(lambda e: e.tensor_reduce(out=ss[:n, :], in_=t[:n, :].rearrange("p (s d) -> p s d", s=16), axis=AX.X, op=ALU.add),
              R=[t], W=[ss])
        rsqrt(k, ss[:n, :], ss[:n, :], [ss], [ss], 1.0 / 64, EPS)
        y = xn.next()
        y3 = y[:n, :].rearrange("p (s d) -> p s d", s=16)
        x3 = x[:n, :].rearrange("p (s d) -> p s d", s=16)
        k.dve(lambda e: e.tensor_tensor(out=y3, in0=x3, in1=bc(ss[:n, :].unsqueeze(2), [n, 16, 64]), op=ALU.mult),
              R=[x, ss], W=[y])
        k.dve(lambda e: e.tensor_tensor(out=y3, in0=y3, in1=bc(g[:n, :].unsqueeze(1), [n, 16, 64]), op=ALU.mult),
               R=[y, g], W=[y])
        rt = rtr.next()
        cosb = bc(cs[:n, 0:8].unsqueeze(1), [n, 16, 8])
        sinb = bc(cs[:n, 8:16].unsqueeze(1), [n, 16, 8])
        x1 = y3[:, :, 0:8]
        x2 = y3[:, :, 8:16]
        k.dve(lambda e: e.tensor_tensor(out=rt[:n, 0], in0=x1, in1=cosb, op=ALU.mult), R=[y, cs], W=[rt])
        k.dve(lambda e: e.tensor_tensor(out=rt[:n, 1], in0=x2, in1=sinb, op=ALU.mult), R=[y, cs], W=[rt])
        k.dve(lambda e: e.tensor_tensor(out=rt[:n, 2], in0=x2, in1=cosb, op=ALU.mult), R=[y, cs], W=[rt])
        k.dve(lambda e: e.tensor_tensor(out=rt[:n, 3], in0=x1, in1=sinb, op=ALU.mult), R=[y, cs], W=[rt])
        k.dve(lambda e: e.tensor_tensor(out=x1, in0=rt[:n, 0], in1=rt[:n, 1], op=ALU.subtract), R=[rt], W=[y])
        k.dve(lambda e: e.tensor_tensor(out=x2, in0=rt[:n, 2], in1=rt[:n, 3], op=ALU.add), R=[rt], W=[y])
        return y

    for s in P.seqs:
        b = s["b"]
        for j in range(s["past"] // 128):
            x = xr.next()
            k.load("sp", x[:, :], I["ck"][l, b, j * 128:(j + 1) * 128, :], x)
            xb = xbr.next()
            k.act(lambda e: e.activation(out=xb[:, :], in_=x[:, :], func=AF.Copy), R=[x], W=[xb])
            transpose_store(xb, 128, s["kT"][:, :, j * 128:(j + 1) * 128].rearrange("h p t -> p h t"))
            v = xr.next()
            k.load("sp", v[:, :], I["cv"][l, b, j * 128:(j + 1) * 128, :], v)
            v_store(v, 128, s["vb"][j * 128:(j + 1) * 128, :, :])
        for tl in s["tiles"]:
            n, t0, g = tl["n"], tl["t0"], tl["g"]
            cs = csr.next()
            rsrc = I["rope_p"][t0:t0 + n, :] if b is None else I["rope_s"][t0:t0 + n, :]
            k.load("sp", cs[:n, :], rsrc, cs)
            x = xr.next()
            k.load("sp", x[:n, :], P.pj(g, g + n, O_AQ, O_AQ + D), x)
            y = normrope(x, n, gq, cs)
            xb = xbr.next()
            k.act(lambda e: e.activation(out=xb[:n, :], in_=y[:n, :], func=AF.Copy), R=[y], W=[xb])
            transpose_store(xb, n, s["qT"][:, :, t0:t0 + n].rearrange("h p t -> p h t"))
            x = xr.next()
            k.load("sp", x[:n, :], P.pj(g, g + n, O_AK, O_AK + D), x)
            y = normrope(x, n, gk, cs)
            kdst = O["k_p"][l, t0:t0 + n, :] if b is None else O["k_s"][l, b, t0:t0 + n, :]
            k.store("pool", kdst, y[:n, :], y)
            xb = xbr.next()
            k.act(lambda e: e.activation(out=xb[:n, :], in_=y[:n, :], func=AF.Copy), R=[y], W=[xb])
            p0 = s["past"]
            transpose_store(xb, n, s["kT"][:, :, p0 + t0:p0 + t0 + n].rearrange("h p t -> p h t"))
            v = xr.next()
            k.load("sp", v[:n, :], P.pj(g, g + n, O_AV, O_AV + D), v)
            vdst = O["v_p"][l, t0:t0 + n, :] if b is None else O["v_s"][l, b, t0:t0 + n, :]
            k.store("pool", vdst, v[:n, :], v)
            v_store(v, n, s["vb"][p0 + t0:p0 + t0 + n, :, :])
    k.end_phase(ph)


def phase3(P, l):
    k, I, S, O = P.k, P.I, P.S, P.O
    ph = k.phase()
    lam_init = 0.8 - 0.6 * math.exp(-0.3 * l)
    lamt = bcast_row(k, ph, I["a_lambda"][l], 256, "p3lam")
    lw = k.sb(ph, [128, 2, 64], F32, "p3lw")
    l4 = lamt[:, :].rearrange("p (a b d) -> p a b d", a=2, b=2)
    k.dve(lambda e: e.tensor_tensor(out=lw[:, :, :], in0=l4[:, :, 0, :], in1=l4[:, :, 1, :], op=ALU.mult), R=[lamt], W=[lw])
    lc = k.sb(ph, [128, 4], F32, "p3lc")
    k.dve(lambda e: e.tensor_reduce(out=lc[:, 0:2], in_=lw[:, :, :], axis=AX.X, op=ALU.add), R=[lw], W=[lc])
    k.act(lambda e: e.activation(out=lc[:, 0:2], in_=lc[:, 0:2], func=AF.Exp), R=[lc], W=[lc])
    k.dve(lambda e: e.tensor_tensor(out=lc[:, 2:3], in0=lc[:, 0:1], in1=lc[:, 1:2], op=ALU.subtract), R=[lc], W=[lc])
    k.dve(lambda e: e.tensor_scalar(out=lc[:, 3:4], in0=lc[:, 2:3], scalar1=lam_init, scalar2=None, op0=ALU.add), R=[lc], W=[lc])
    gs = bcast_row(k, ph, I["a_subln_g"][l], 128, "p3gs")
    k.dve(lambda e: e.tensor_scalar(out=gs[:, :], in0=gs[:, :], scalar1=1.0 - lam_init, scalar2=None, op0=ALU.mult), R=[gs], W=[gs])

    TKmax = max(s["TK"] for s in P.seqs)
    ntkmax = (TKmax + 127) // 128
    ktr = Rot(k, ph, 2, [128, TKmax], BF16, "p3kt")
    vtr = Rot(k, ph, 2, [128, ntkmax, 129], BF16, "p3vt")
    qtr = Rot(k, ph, 2, [128, 512], BF16, "p3qt")
    psr = Rot(k, ph, 2, [128, 2, 512], F32, "p3ps", psum=True)
    acc = k.ps(ph, [128, 8, 256], F32, "p3acc")
    ptr = Rot(k, ph, 3, [128, 2, 512], BF16, "p3pt")
    accr = Rot(k, ph, 2, [128, 8, 129], F32, "p3accs")
    rrr = Rot(k, ph, 4, [128, 8], F32, "p3rr")
    tr_ = Rot(k, ph, 2, [128, 128], F32, "p3t")
    orr = Rot(k, ph, 2, [128, 128], F32, "p3o")
    ofr = Rot(k, ph, 3, [128, 128], F32, "p3of")

    for s in P.seqs:
        TK, Tq = s["TK"], s["T"]
        ntk = (TK + 127) // 128
        prompt = s["b"] is None
        qw = min(512, Tq)
        for h in range(8):
            kt = ktr.next()
            k.load("sp", kt[:, :TK], s["kT"][h, :, :], kt)
            vt = vtr.next()
            nfull = TK // 128
            k.load("sp", vt[:, :nfull, :], s["vb"][0:nfull * 128, h, :].rearrange("(j p) d -> p j d", p=128), vt)
            if TK % 128:
                rem = TK % 128
                k.load("sp", vt[:rem, nfull, :], s["vb"][nfull * 128:TK, h, :], vt)
            for q0 in range(0, Tq, qw):
                qt = qtr.next()
                k.load("sp", qt[:, :qw], s["qT"][h, :, q0:q0 + qw], qt)
                nqt = (qw + 127) // 128
                jq0 = q0 // 128
                jlast = (jq0 + nqt - 1) if prompt else (ntk - 1)
                def s_mm(j):
                    nk = min(128, TK - j * 128)
                    ps = psr.next()
                    for m in range(2):
                        k.pe(lambda e: e.matmul(ps[:nk, m, :qw], lhsT=kt[m * 64:(m + 1) * 64, j * 128:j * 128 + nk],
                                                rhs=qt[m * 64:(m + 1) * 64, :qw], start=True, stop=True),
                             R=[kt, qt], W=[ps])
                    return ps

                ps_next = s_mm(0)
                for j in range(jlast + 1):
                    nk = min(128, TK - j * 128)
                    ps = ps_next
                    if j < jlast:
                        ps_next = s_mm(j + 1)
                    pt = ptr.next()
                    k.act(lambda e: e.activation(out=pt[:nk, :, :qw], in_=ps[:nk, :, :qw], func=AF.Exp, scale=0.125),
                          R=[ps], W=[pt])
                    if prompt and j >= jq0:
                        i = j - jq0
                        k.pool(lambda e: e.memset(pt[64:128, :, i * 128:i * 128 + 64], 0.0), W=[pt])
                    for m in range(2):
                        for i in range(nqt):
                            nq = min(128, qw - i * 128)
                            last = (jq0 + i) if prompt else (ntk - 1)
                            if j > last:
                                continue
                            k.pe(lambda e: e.matmul(acc[:nq, m * 4 + i, 0:129], lhsT=pt[:nk, m, i * 128:i * 128 + nq],
                                                    rhs=vt[:nk, j, :], start=(j == 0 and i % 2 == 0), stop=(j == last),
                                                    skip_group_check=True),
                                 R=[pt, vt], W=[acc])
                nqmax = min(128, qw)
                accs = accr.next()
                for m in range(2):
                    k.dve(lambda e: e.tensor_copy(out=accs[:nqmax, m * 4:m * 4 + nqt, :], in_=acc[:nqmax, m * 4:m * 4 + nqt, 0:129]),
                          R=[acc], W=[accs])
                for i in range(nqt):
                    nq = min(128, qw - i * 128)
                    rr = rrr.next()
                    k.dve(lambda e: e.reciprocal(out=rr[:nq, 0:1], in_=accs[:nq, i, 128:129]), R=[accs], W=[rr])
                    k.dve(lambda e: e.reciprocal(out=rr[:nq, 1:2], in_=accs[:nq, 4 + i, 128:129]), R=[accs], W=[rr])
                    k.dve(lambda e: e.tensor_tensor(out=rr[:nq, 2:3], in0=rr[:nq, 1:2], in1=lc[:nq, 3:4], op=ALU.mult),
                          R=[rr, lc], W=[rr])
                    t = tr_.next()
                    k.dve(lambda e: e.tensor_scalar(out=t[:nq, :], in0=accs[:nq, 4 + i, 0:128], scalar1=rr[:nq, 2:3],
                                                    scalar2=None, op0=ALU.mult), R=[accs, rr], W=[t])
                    o = orr.next()
                    k.dve(lambda e: e.scalar_tensor_tensor(out=o[:nq, :], in0=accs[:nq, i, 0:128], scalar=rr[:nq, 0:1],
                                                           in1=t[:nq, :], op0=ALU.mult, op1=ALU.subtract),
                          R=[accs, rr, t], W=[o])
                    k.pool(lambda e: e.tensor_tensor(out=t[:nq, :], in0=o[:nq, :], in1=o[:nq, :], op=ALU.mult), R=[o], W=[t])
                    k.dve(lambda e: e.tensor_reduce(out=rr[:nq, 3:4], in_=t[:nq, :], axis=AX.X, op=ALU.add), R=[t], W=[rr])
                    k.act(lambda e: e.activation(out=rr[:nq, 4:5], in_=rr[:nq, 3:4], func=AF.Ln, scale=1.0 / 128, bias=EPS),
                          R=[rr], W=[rr])
                    k.act(lambda e: e.activation(out=rr[:nq, 5:6], in_=rr[:nq, 4:5], func=AF.Exp, scale=-0.5), R=[rr], W=[rr])
                    of = ofr.next()
                    k.dve(lambda e: e.scalar_tensor_tensor(out=of[:nq, :], in0=o[:nq, :], scalar=rr[:nq, 5:6],
                                                           in1=gs[:nq, :], op0=ALU.mult, op1=ALU.mult),
                          R=[o, rr, gs], W=[of])
                    g = s["g0"] + q0 + i * 128
                    k.store("pool", S["oa"][g:g + nq, h * 128:(h + 1) * 128], of[:nq, :], of)
    k.end_phase(ph)


def phase4(P, l):
    k, I, S, O, C = P.k, P.I, P.S, P.O, P.C
    ph = k.phase()
    lbr = k.sb(ph, [128, D], F32, "p4lb")
    oml = k.sb(ph, [128, D], F32, "p4oml")
    if l == 0:
        k.dve(lambda e: e.memset(lbr[:, :], 0.0), W=[lbr])
        k.dve(lambda e: e.memset(oml[:, :], 1.0), W=[oml])
    else:
        k.load("sp", lbr[:, :], I["b_lower"][1].partition_broadcast(128), lbr)
        k.load("sp", oml[:, :], I["b_lower"][0].partition_broadcast(128), oml)
        k.dve(lambda e: e.tensor_tensor(out=lbr[:, :], in0=lbr[:, :], in1=oml[:, :], op=ALU.subtract), R=[lbr, oml], W=[lbr])
        k.act(lambda e: e.activation(out=lbr[:, :], in_=lbr[:, :], func=AF.Sigmoid), R=[lbr], W=[lbr])
        k.dve(lambda e: e.tensor_scalar(out=oml[:, :], in0=lbr[:, :], scalar1=-1.0, scalar2=1.0, op0=ALU.mult, op1=ALU.add),
              R=[lbr], W=[oml])
    gn = bcast_row(k, ph, I["b_norm_g"][l], 128, "p4gn")
    ldr = Rot(k, ph, 6, [128, D], F32, "p4ld")
    f32r = Rot(k, ph, 8, [128, D], F32, "p4f")
    er = Rot(k, ph, 3, [128, D], F32, "p4e")
    b16r = Rot(k, ph, 10, [128, D], BF16, "p4b")
    tTr = Rot(k, ph, 4, [128, 8, 128], BF16, "p4tT")
    qpr = Rot(k, ph, 2, [128, 8, 2, 128], BF16, "p4qp")
    for t in qpr.tiles:
        k.dve(lambda e: e.memset(t[:, :, :, :], 0.0), W=[t])
    scmr = Rot(k, ph, 2, [128, 8, 128], BF16, "p4scm")
    dcr = Rot(k, ph, 2, [128, 8, 2], F32, "p4dc")
    ssr = Rot(k, ph, 2, [128, 8], F32, "p4ss")
    St = k.sb(ph, [128, 8, 128], F32, "p4S")
    Sb = k.sb(ph, [128, 8, 128], BF16, "p4Sb")
    pc = k.ps(ph, [128, D], F32, "p4pc")
    ptp = k.ps(ph, [128, 8, 128], BF16, "p4pt")
    psc = k.ps(ph, [128, 4, 128], F32, "p4psc")
    po = k.ps(ph, [128, 8, 128], F32, "p4po")
    pS = k.ps(ph, [128, 4, 128], F32, "p4pS")
    pd = k.ps(ph, [128, 8, 2], F32, "p4pd")
    ioff = P.coff["ident"]

    def cum_mm(name, logf, n):
        for hf in range(2):
            k.pe(lambda e: e.matmul(pc[:n, hf * 512:(hf + 1) * 512], lhsT=C(name)[:n, :n], rhs=logf[:n, hf * 512:(hf + 1) * 512],
                                    start=True, stop=True), R=[P.cst, logf], W=[pc])

    def transp(src, n):
        for h in range(8):
            k.pe(lambda e: e.transpose(out=ptp[:, h, :n], in_=src[:n, h * 128:(h + 1) * 128], identity=P.identb[:n, :n]),
                 R=[src, P.identb], W=[ptp])

    for s in P.seqs:
        b = s["b"]
        if b is None:
            k.dve(lambda e: e.memset(St[:, :, :], 0.0), W=[St])
        else:
            k.load("sp", St[:, :, :], I["sth"][l, b].rearrange("h k v -> k h v"), St)
        k.act(lambda e: e.activation(out=Sb[:, :, :], in_=St[:, :, :], func=AF.Copy), R=[St], W=[Sb])
        for tl in s["tiles"]:
            n, g = tl["n"], tl["g"]
            nch = n // 64
            bq, bf_, bi = ldr.next(), ldr.next(), ldr.next()
            k.load("sp", bq[:n, :], P.pj(g, g + n, O_BQ, O_BQ + D), bq)
            k.load("sp", bf_[:n, :], P.pj(g, g + n, O_BF, O_BF + D), bf_)
            k.load("sp", bi[:n, :], P.pj(g, g + n, O_BI, O_BI + D), bi)
            sg, t1, f, kin, logf, q = (f32r.next() for _ in range(6))
            k.act(lambda e: e.activation(out=sg[:n, :], in_=bf_[:n, :], func=AF.Sigmoid), R=[bf_], W=[sg])
            k.dve(lambda e: e.tensor_tensor(out=t1[:n, :], in0=sg[:n, :], in1=oml[:n, :], op=ALU.mult), R=[sg, oml], W=[t1])
            k.dve(lambda e: e.tensor_tensor(out=f[:n, :], in0=t1[:n, :], in1=lbr[:n, :], op=ALU.add), R=[t1, lbr], W=[f])
            k.dve(lambda e: e.tensor_tensor(out=kin[:n, :], in0=oml[:n, :], in1=t1[:n, :], op=ALU.subtract), R=[t1, oml], W=[kin])
            k.act(lambda e: e.activation(out=logf[:n, :], in_=f[:n, :], func=AF.Ln), R=[f], W=[logf])
            k.act(lambda e: e.activation(out=q[:n, :], in_=bq[:n, :], func=AF.Silu), R=[bq], W=[q])
            qt_, qh, kh, kt_, ib = (b16r.next() for _ in range(5))
            k.pool(lambda e: e.tensor_copy(out=ib[:n, :], in_=bi[:n, :]), R=[bi], W=[ib])
            cum_mm("h_ut", logf, n)
            e1 = er.next()
            k.act(lambda e: e.activation(out=e1[:n, :], in_=pc[:n, :], func=AF.Exp), R=[pc], W=[e1])
            k.dve(lambda e: e.tensor_tensor(out=qt_[:n, :], in0=q[:n, :], in1=e1[:n, :], op=ALU.mult), R=[q, e1], W=[qt_])
            cum_mm("h_d1", logf, n)
            e2, e3 = er.next(), er.next()
            k.act(lambda e: e.activation(out=e2[:n, :], in_=pc[:n, :], func=AF.Exp), R=[pc], W=[e2])
            k.act(lambda e: e.activation(out=e3[:n, :], in_=pc[:n, :], func=AF.Exp, scale=-1.0), R=[pc], W=[e3])
            k.dve(lambda e: e.tensor_tensor(out=qh[:n, :], in0=q[:n, :], in1=e2[:n, :], op=ALU.mult), R=[q, e2], W=[qh])
            k.dve(lambda e: e.tensor_tensor(out=kh[:n, :], in0=kin[:n, :], in1=e3[:n, :], op=ALU.mult), R=[kin, e3], W=[kh])
            cum_mm("h_d2", logf, n)
            e4 = er.next()
            k.act(lambda e: e.activation(out=e4[:n, :], in_=pc[:n, :], func=AF.Exp), R=[pc], W=[e4])
            k.dve(lambda e: e.tensor_tensor(out=kt_[:n, :], in0=kin[:n, :], in1=e4[:n, :], op=ALU.mult), R=[kin, e4], W=[kt_])
            cum_mm("h_end", logf, n)
            e5 = er.next()
            k.act(lambda e: e.activation(out=e5[:n, :], in_=pc[:n, :], func=AF.Exp), R=[pc], W=[e5])
            for h in range(8):
                k.pe(lambda e: e.matmul(pd[:, h, :nch], lhsT=e5[:n, h * 128:(h + 1) * 128],
                                        rhs=P.cst[:n, ioff:ioff + 64 * nch:64], start=True, stop=True),
                     R=[e5, P.cst], W=[pd])
            dC = dcr.next()
            k.dve(lambda e: e.tensor_copy(out=dC[:, :, :nch], in_=pd[:, :, :nch]), R=[pd], W=[dC])
            transp(qh, n)
            qhT = tTr.next()
            k.act(lambda e: e.activation(out=qhT[:, :, :n], in_=ptp[:, :, :n], func=AF.Copy), R=[ptp], W=[qhT])
            transp(kh, n)
            khT = tTr.next()
            k.dve(lambda e: e.tensor_copy(out=khT[:, :, :n], in_=ptp[:, :, :n]), R=[ptp], W=[khT])
            transp(qt_, n)
            qp = qpr.next()
            k.act(lambda e: e.activation(out=qp[:, :, 0, 0:64], in_=ptp[:, :, 0:64], func=AF.Copy), R=[ptp], W=[qp])
            if nch == 2:
                k.dve(lambda e: e.tensor_copy(out=qp[:, :, 1, 64:128], in_=ptp[:, :, 64:128]), R=[ptp], W=[qp])
            scm = scmr.next()
            k.dve(lambda e: e.memset(po[:, :, :], 0.0), W=[po])
            for hg in range(2):
                for hh in range(4):
                    h = hg * 4 + hh
                    k.pe(lambda e: e.matmul(psc[:n, hh, :n], lhsT=khT[:, h, :n], rhs=qhT[:, h, :n], start=True, stop=True),
                         R=[khT, qhT], W=[psc])
                k.dve(lambda e: e.tensor_tensor(out=scm[:n, hg * 4:hg * 4 + 4, :n], in0=psc[:n, :, :n],
                                                in1=bc(C("h_mask")[:n, :n].unsqueeze(1), [n, 4, n]), op=ALU.mult),
                      R=[psc, P.cst], W=[scm])
            for h in range(8):
                hs = slice(h * 128, (h + 1) * 128)
                k.pe(lambda e: e.matmul(po[:n, h, :], lhsT=scm[:n, h, :n], rhs=ib[:n, hs], start=False, stop=False,
                                        skip_group_check=True), R=[scm, ib], W=[po])
                k.pe(lambda e: e.matmul(po[:n, h, :], lhsT=qp[:, h, 0, :n], rhs=Sb[:, h, :], start=False, stop=(nch == 1),
                                        skip_group_check=True), R=[qp, Sb], W=[po])
            for c in range(nch):
                rows = slice(c * 64, (c + 1) * 64)
                for hg in range(2):
                    for hh in range(4):
                        h = hg * 4 + hh
                        hs = slice(h * 128, (h + 1) * 128)
                        k.pe(lambda e: e.matmul(pS[:, hh, :], lhsT=kt_[rows, hs], rhs=ib[rows, hs], start=True, stop=True),
                             R=[kt_, ib], W=[pS])
                    hsl = slice(hg * 4, hg * 4 + 4)
                    k.dve(lambda e: e.tensor_tensor(out=St[:, hsl, :], in0=St[:, hsl, :],
                                                    in1=bc(dC[:, hsl, c:c + 1], [128, 4, 128]), op=ALU.mult),
                          R=[St, dC], W=[St])
                    k.dve(lambda e: e.tensor_tensor(out=St[:, hsl, :], in0=St[:, hsl, :], in1=pS[:, :, :], op=ALU.add),
                          R=[St, pS], W=[St])
                    k.act(lambda e: e.activation(out=Sb[:, hsl, :], in_=St[:, hsl, :], func=AF.Copy), R=[St], W=[Sb])
                if c == 0 and nch == 2:
                    for h in range(8):
                        k.pe(lambda e: e.matmul(po[:n, h, :], lhsT=qp[:, h, 1, :n], rhs=Sb[:, h, :], start=False, stop=True,
                                                skip_group_check=True), R=[qp, Sb], W=[po])
            sq = f32r.next()
            k.act(lambda e: e.activation(out=sq[:n, :], in_=po[:n, :, :].rearrange("p h d -> p (h d)"), func=AF.Square),
                  R=[po], W=[sq])
            ss = ssr.next()
            k.dve(lambda e: e.tensor_reduce(out=ss[:n, :], in_=sq[:n, :].rearrange("p (h d) -> p h d", h=8), axis=AX.X, op=ALU.add),
                  R=[sq], W=[ss])
            rsqrt(k, ss[:n, :], ss[:n, :], [ss], [ss], 1.0 / 128, EPS)
            ob = f32r.next()
            ob3 = ob[:n, :].rearrange("p (h d) -> p h d", h=8)
            k.dve(lambda e: e.tensor_tensor(out=ob3, in0=po[:n, :, :], in1=bc(ss[:n, :].unsqueeze(2), [n, 8, 128]), op=ALU.mult),
                  R=[po, ss], W=[ob])
            k.dve(lambda e: e.tensor_tensor(out=ob3, in0=ob3, in1=bc(gn[:n, :].unsqueeze(1), [n, 8, 128]), op=ALU.mult),
                   R=[ob, gn], W=[ob])
            k.store("pool", S["ob"][g:g + n, :], ob[:n, :], ob)
        hdst = O["hg_p"][l] if b is None else O["hg_s"][l, b]
        k.store("pool", hdst.rearrange("h k v -> k h v"), St[:, :, :], St)
    k.end_phase(ph)


import os
P5CUT = int(os.environ.get("P5CUT", "0"))
P5NOFIN = int(os.environ.get("P5NOFIN", "0"))
P5STEPS = int(os.environ.get("P5STEPS", "-1"))
P5SUB = int(os.environ.get("P5SUB", "9"))
P5EXP = int(os.environ.get("P5EXP", "0"))


def phase5(P, l):
    k, I, S, O, C = P.k, P.I, P.S, P.O, P.C
    ph = k.phase()
    NEG_E = -math.exp(-0.5)
    mu = bcast_row(k, ph, I["c_shift_mu"][l], CW, "p5mu")
    w0 = bcast_row(k, ph, I["c_w0"][l], D, "p5w0")
    a0 = bcast_row(k, ph, I["c_a0"][l], D, "p5a0")
    kkr = bcast_row(k, ph, I["c_k_k"][l], D, "p5kk")
    kar = bcast_row(k, ph, I["c_k_a"][l], D, "p5ka")
    rkr = bcast_row(k, ph, I["c_r_k"][l], D, "p5rk")
    lnw = bcast_row(k, ph, I["c_ln_w"][l], D, "p5lnw")
    lnb = bcast_row(k, ph, I["c_ln_b"][l], D, "p5lnb")
    if l == 1:
        v0 = bcast_row(k, ph, I["c_v0"][0], D, "p5v0")
    tmpr = Rot(k, ph, 4, [128, D], F32, "p5tmp")
    stg = tmpr.tiles[0]
    w2b = k.sb(ph, [64, D], BF16, "p5w2")
    a2b = k.sb(ph, [64, D], BF16, "p5a2")
    k.load("sp", stg[:64, :], I["c_w2"][l], stg)
    k.dve(lambda e: e.tensor_copy(out=w2b[:, :], in_=stg[:64, :]), R=[stg], W=[w2b])
    k.load("sp", stg[:64, :], I["c_a2"][l], stg)
    k.dve(lambda e: e.tensor_copy(out=a2b[:, :], in_=stg[:64, :]), R=[stg], W=[a2b])
    if l == 1:
        v2b = k.sb(ph, [32, D], BF16, "p5v2w")
        k.load("sp", stg[:32, :], I["c_vres_w2"][0], stg)
        k.dve(lambda e: e.tensor_copy(out=v2b[:, :], in_=stg[:32, :]), R=[stg], W=[v2b])
    mk = k.sb(ph, [128, 4, 128], F32, "p5mk")
    for i_, nm in enumerate(["r_uts", "r_ut", "r_uts", "r_ut"]):
        k.dve(lambda e: e.tensor_copy(out=mk[:, i_, :], in_=C(nm)), R=[P.cst], W=[mk])

    mz = k.sb(ph, [128, 4, 128], F32, "p5mz")
    i4 = k.sb(ph, [128, 4, 128], BF16, "p5i4")
    for i_ in range(4):
        k.dve(lambda e: e.tensor_copy(out=mz[:, i_, :], in_=C("r_low")), R=[P.cst], W=[mz])
        k.dve(lambda e: e.tensor_copy(out=i4[:, i_, :], in_=P.identb[:, :]), R=[P.identb], W=[i4])
    cp = k.sb(ph, [128, CW], F32, "p5cp")
    cs = k.sb(ph, [128, CW], F32, "p5cs")
    hv = k.sb(ph, [128, 32], F32, "p5hv")
    vft = k.sb(ph, [128, D], F32, "p5vf")
    k2 = k.sb(ph, [128, D], F32, "p5k2")
    v2 = k.sb(ph, [128, D], F32, "p5v2")
    asig = k.sb(ph, [128, D], F32, "p5as")
    kk = k.sb(ph, [128, D], F32, "p5kkt")
    bs = k.sb(ph, [128, D], F32, "p5bs")
    ld = k.sb(ph, [128, D], F32, "p5ld")
    yt = k.sb(ph, [128, D], F32, "p5y")
    yn = k.sb(ph, [128, D], F32, "p5yn")
    s16 = Rot(k, ph, 6, [128, 16], F32, "p5s16")
    smb = k.sb(ph, [128, 3, 64], BF16, "p5smb")
    smT = k.sb(ph, [64, 3, 128], BF16, "p5smT")
    rt_, at_, bt_, kt_, bb_, kb_, vb_ = (k.sb(ph, [128, D], BF16, "p5b%d" % i_) for i_ in range(7))
    arT = k.sb(ph, [128, 8, 2, 128], BF16, "p5arT")
    bT = k.sb(ph, [128, 8, 128], BF16, "p5bT")
    kT = k.sb(ph, [128, 8, 128], BF16, "p5kT")
    AM = k.sb(ph, [128, 16, 4, 128], BF16, "p5AM")
    Qt = [k.sb(ph, [128, 4, 128], BF16, "p5Q%d" % i_) for i_ in range(4)]
    yzr = Rot(k, ph, 18, [128, 4, 128], BF16, "p5yz")
    R1 = k.sb(ph, [128, 16, 64], BF16, "p5R1")
    Ub = k.sb(ph, [128, 16, 64], BF16, "p5Ub")
    H = k.sb(ph, [128, 8, 64], F32, "p5H")
    Hb = k.sb(ph, [128, 8, 128], BF16, "p5Hb")
    k.dve(lambda e: e.memset(Hb[:, :, :], 0.0), W=[Hb])

    def refresh_hb():
        for e_ in range(2):
            rows = slice(e_ * 64, (e_ + 1) * 64)
            k.act(lambda e: e.activation(out=Hb[rows, :, e_ * 64:(e_ + 1) * 64], in_=H[rows, :, :], func=AF.Copy), R=[H], W=[Hb])
    wc = k.sb(ph, [128, 8], F32, "p5wc")
    B01 = k.ps(ph, [128, D], F32, "p5B01")
    Bt = k.ps(ph, [128, 8, 128], BF16, "p5Bt")
    gbank = Rot(k, ph, 5, [128, 512], F32, "p5g", psum=True)
    if os.environ.get("KDEBUG"):
        print("phase5 sbuf bytes remaining", P.nc.sbuf_bytes_remaining)

    def v4(bank):
        return bank[:, :].rearrange("p (a b) -> p a b", a=4)

    def v8(bank):
        return bank[:, :].rearrange("p (a b) -> p a b", a=8)

    def small_mm(col, wts, kdim, n, bias, out, func):
        for hf in range(2):
            k.pe(lambda e: e.matmul(B01[:n, hf * 512:(hf + 1) * 512], lhsT=smT[0:kdim, col, :n],
                                    rhs=wts[0:kdim, hf * 512:(hf + 1) * 512], start=True, stop=True),
                 R=[smT, wts], W=[B01])
        t = tmpr.next()
        k.dve(lambda e: e.tensor_tensor(out=t[:n, :], in0=B01[:n, :], in1=bias[:n, :], op=ALU.add), R=[B01, bias], W=[t])
        k.act(lambda e: e.activation(out=out[:n, :], in_=t[:n, :], func=func), R=[t], W=[out])

    def cum_exp(name, n, outs):
        for hf in range(2):
            k.pe(lambda e: e.matmul(B01[:n, hf * 512:(hf + 1) * 512], lhsT=C(name)[:n, :n], rhs=ld[:n, hf * 512:(hf + 1) * 512],
                                    start=True, stop=True), R=[P.cst, ld], W=[B01])
        for (t, sc) in outs:
            k.act(lambda e: e.activation(out=t[:n, :], in_=B01[:n, :], func=AF.Exp, scale=sc), R=[B01], W=[t])

    def transp8(src, n, dst_ap, dst_t, eng):
        for p in range(8):
            k.pe(lambda e: e.transpose(out=Bt[:, p, :n], in_=src[:n, p * 128:(p + 1) * 128], identity=P.identb[:n, :n]),
                 R=[src, P.identb], W=[Bt])
        if eng == "act":
            k.act(lambda e: e.activation(out=dst_ap, in_=Bt[:, :, :n], func=AF.Copy), R=[Bt], W=[dst_t])
        else:
            k.dve(lambda e: e.tensor_copy(out=dst_ap, in_=Bt[:, :, :n]), R=[Bt], W=[dst_t])

    for s in P.seqs:
        b = s["b"]
        if b is None:
            k.dve(lambda e: e.memset(H[:, :, :], 0.0), W=[H])
        else:
            Sld_t = tmpr.next()
            Sld = Sld_t[0:64, :].rearrange("p (a b) -> p a b", a=16)
            k.load("sp", Sld, I["str"][l, b].rearrange("h i j -> i h j"), Sld_t)
            for half in range(2):
                g_ = gbank.next()
                g4 = g_[:, :].rearrange("p (a b) -> p a b", a=4)
                for pp in range(4):
                    p = half * 4 + pp
                    k.pe(lambda e: e.transpose(out=g4[:, pp, 0:64], in_=Sld_t[0:64, 2 * p * 64:(2 * p + 2) * 64],
                                               identity=C("ident")[:64, :64]), R=[Sld_t, P.cst], W=[g_])
                k.dve(lambda e: e.tensor_copy(out=H[:, half * 4:half * 4 + 4, :], in_=g4[:, :, 0:64]), R=[g_], W=[H])
        refresh_hb()
        ntl = len(s["tiles"])
        for ti, tl in enumerate(s["tiles"]):
            n, g, t0 = tl["n"], tl["g"], tl["t0"]
            k.load("sp", cp[:n, :], P.pj(g, g + n, O_CP, O_CP + CW), cp)
            if t0 == 0:
                if b is None:
                    k.dve(lambda e: e.memset(cs[0:1, :], 0.0), W=[cs])
                else:
                    k.load("sp", cs[0:1, :], I["stsh"][l, b:b + 1, :], cs)
                k.load("sp", cs[1:n, :], P.pj(g, g + n - 1, O_CP, O_CP + CW), cs)
            else:
                k.load("sp", cs[:n, :], P.pj(g - 1, g + n - 1, O_CP, O_CP + CW), cs)
            if ti == ntl - 1:
                sdst = O["sh_p"][l:l + 1, :] if b is None else O["sh_s"][l, b:b + 1, :]
                k.store("pool", sdst, cp[n - 1:n, :], cp)
            k.dve(lambda e: e.tensor_tensor(out=cs[:n, :], in0=cs[:n, :], in1=cp[:n, :], op=ALU.subtract), R=[cs, cp], W=[cs])
            k.dve(lambda e: e.tensor_tensor(out=cs[:n, :], in0=cs[:n, :], in1=mu[:n, :], op=ALU.mult), R=[cs, mu], W=[cs])
            k.dve(lambda e: e.tensor_tensor(out=cs[:n, :], in0=cs[:n, :], in1=cp[:n, :], op=ALU.add), R=[cs, cp], W=[cs])
            r_ = cs[:n, C_R:C_R + D]
            kx = cs[:n, C_K:C_K + D]
            vx = cs[:n, C_V:C_V + D]
            if P5CUT and P5CUT <= 1:
                continue
            k.act(lambda e: e.activation(out=smb[:n, 0, :], in_=cs[:n, C_WLO:C_WLO + 64], func=AF.Tanh), R=[cs], W=[smb])
            k.act(lambda e: e.activation(out=smb[:n, 1, :], in_=cs[:n, C_ALO:C_ALO + 64], func=AF.Copy), R=[cs], W=[smb])
            if l == 1:
                k.load("sp", hv[:n, :], P.pj(g, g + n, O_EXT, O_EXT + 32), hv)
                k.act(lambda e: e.activation(out=smb[:n, 2, 0:32], in_=hv[:n, :], func=AF.Copy), R=[hv], W=[smb])
            for c_ in range(3 if l == 1 else 2):
                kd = 32 if c_ == 2 else 64
                k.pe(lambda e: e.transpose(out=Bt[0:kd, c_, :n], in_=smb[:n, c_, 0:kd], identity=P.identb[:n, :n]),
                     R=[smb, P.identb], W=[Bt])
            k.dve(lambda e: e.tensor_copy(out=smT[:, 0:2, :n], in_=Bt[0:64, 0:2, :n]), R=[Bt], W=[smT])
            if l == 1:
                k.dve(lambda e: e.tensor_copy(out=smT[0:32, 2, :n], in_=Bt[0:32, 2, :n]), R=[Bt], W=[smT])
            sgw = tmpr.next()
            small_mm(0, w2b, 64, n, w0, sgw, AF.Sigmoid)
            k.dve(lambda e: e.tensor_scalar(out=ld[:n, :], in0=sgw[:n, :], scalar1=NEG_E, scalar2=None, op0=ALU.mult),
                  R=[sgw], W=[ld])
            small_mm(1, a2b, 64, n, a0, asig, AF.Sigmoid)
            if l == 1:
                vmix = tmpr.next()
                small_mm(2, v2b, 32, n, v0, vmix, AF.Sigmoid)
                k.load("sp", vft[:n, :], S["vf"][g:g + n, :], vft)
                k.dve(lambda e: e.tensor_tensor(out=vft[:n, :], in0=vft[:n, :], in1=vx, op=ALU.subtract), R=[vft, cs], W=[vft])
                k.dve(lambda e: e.tensor_tensor(out=vft[:n, :], in0=vft[:n, :], in1=vmix[:n, :], op=ALU.mult), R=[vft, vmix], W=[vft])
                k.dve(lambda e: e.tensor_tensor(out=v2[:n, :], in0=vft[:n, :], in1=vx, op=ALU.add), R=[vft, cs], W=[v2])
            else:
                k.pool(lambda e: e.tensor_copy(out=v2[:n, :], in_=vx), R=[cs], W=[v2])
                k.store("pool", S["vf"][g:g + n, :], v2[:n, :], v2)
            if P5CUT and P5CUT <= 2:
                continue
            k.dve(lambda e: e.tensor_tensor(out=kk[:n, :], in0=kx, in1=kkr[:n, :], op=ALU.mult), R=[cs, kkr], W=[kk])
            t = tmpr.next()
            k.dve(lambda e: e.tensor_tensor(out=t[:n, :], in0=kk[:n, :], in1=kk[:n, :], op=ALU.mult), R=[kk], W=[t])
            sk = s16.next()
            k.dve(lambda e: e.tensor_reduce(out=sk[:n, :], in_=t[:n, :].rearrange("p (h d) -> p h d", h=16), axis=AX.X, op=ALU.add),
                  R=[t], W=[sk])
            k.act(lambda e: e.activation(out=sk[:n, :], in_=sk[:n, :], func=AF.Sqrt), R=[sk], W=[sk])
            k.dve(lambda e: e.tensor_scalar(out=sk[:n, :], in0=sk[:n, :], scalar1=1e-12, scalar2=None, op0=ALU.max), R=[sk], W=[sk])
            k.dve(lambda e: e.reciprocal(out=sk[:n, :], in_=sk[:n, :]), R=[sk], W=[sk])
            kk3 = kk[:n, :].rearrange("p (h d) -> p h d", h=16)
            k.dve(lambda e: e.tensor_tensor(out=kk3, in0=kk3, in1=bc(sk[:n, :].unsqueeze(2), [n, 16, 64]), op=ALU.mult),
                  R=[kk, sk], W=[kk])
            t = tmpr.next()
            k.dve(lambda e: e.scalar_tensor_tensor(out=t[:n, :], in0=asig[:n, :], scalar=-1.0, in1=kar[:n, :],
                                                   op0=ALU.add, op1=ALU.mult), R=[asig, kar], W=[t])
            k.dve(lambda e: e.tensor_tensor(out=t[:n, :], in0=t[:n, :], in1=kx, op=ALU.mult), R=[t, cs], W=[t])
            k.dve(lambda e: e.tensor_tensor(out=k2[:n, :], in0=t[:n, :], in1=kx, op=ALU.add), R=[t, cs], W=[k2])
            k.dve(lambda e: e.tensor_tensor(out=bs[:n, :], in0=kk[:n, :], in1=asig[:n, :], op=ALU.mult), R=[kk, asig], W=[bs])
            if P5CUT and P5CUT <= 3:
                continue
            t = tmpr.next()
            k.pool(lambda e: e.tensor_tensor(out=t[:n, :], in0=r_, in1=k2[:n, :], op=ALU.mult), R=[cs, k2], W=[t])
            k.dve(lambda e: e.tensor_tensor(out=t[:n, :], in0=t[:n, :], in1=rkr[:n, :], op=ALU.mult), R=[t, rkr], W=[t])
            s3 = s16.next()
            k.dve(lambda e: e.tensor_reduce(out=s3[:n, :], in_=t[:n, :].rearrange("p (h d) -> p h d", h=16), axis=AX.X, op=ALU.add),
                  R=[t], W=[s3])
            k.dve(lambda e: e.tensor_tensor(out=yn[:n, :].rearrange("p (h d) -> p h d", h=16),
                                            in0=v2[:n, :].rearrange("p (h d) -> p h d", h=16),
                                            in1=bc(s3[:n, :].unsqueeze(2), [n, 16, 64]), op=ALU.mult), R=[v2, s3], W=[yn])
            k.pool(lambda e: e.tensor_copy(out=vb_[:n, :], in_=v2[:n, :]), R=[v2], W=[vb_])
            ep, en = tmpr.next(), tmpr.next()
            cum_exp("r_ut", n, [(ep, 1.0), (en, -1.0)])
            k.dve(lambda e: e.tensor_tensor(out=rt_[:n, :], in0=r_, in1=ep[:n, :], op=ALU.mult), R=[cs, ep], W=[rt_])
            k.dve(lambda e: e.tensor_tensor(out=bt_[:n, :], in0=bs[:n, :], in1=en[:n, :], op=ALU.mult), R=[bs, en], W=[bt_])
            k.dve(lambda e: e.tensor_tensor(out=kt_[:n, :], in0=k2[:n, :], in1=en[:n, :], op=ALU.mult), R=[k2, en], W=[kt_])
            epa = tmpr.next()
            cum_exp("r_uts", n, [(epa, 1.0)])
            k.dve(lambda e: e.scalar_tensor_tensor(out=at_[:n, :], in0=kk[:n, :], scalar=-1.0, in1=epa[:n, :],
                                                   op0=ALU.mult, op1=ALU.mult), R=[kk, epa], W=[at_])
            eend = tmpr.next()
            cum_exp("r_low", n, [(eend, 1.0)])
            k.dve(lambda e: e.tensor_tensor(out=bb_[:n, :], in0=bs[:n, :], in1=eend[:n, :], op=ALU.mult), R=[bs, eend], W=[bb_])
            k.dve(lambda e: e.tensor_tensor(out=kb_[:n, :], in0=k2[:n, :], in1=eend[:n, :], op=ALU.mult), R=[k2, eend], W=[kb_])
            gw = gbank.next()
            for p in range(8):
                k.pe(lambda e: e.matmul(gw[:, p:p + 1], lhsT=ld[:n, p * 128:(p + 1) * 128], rhs=C("r_one")[:n, 0:1],
                                        start=True, stop=True), R=[ld, P.cst], W=[gw])
            k.act(lambda e: e.activation(out=wc[:, :], in_=gw[:, 0:8], func=AF.Exp), R=[gw], W=[wc])
            if P5CUT and P5CUT <= 4:
                continue
            transp8(at_, n, arT[:, :, 0, :n], arT, "act")
            transp8(rt_, n, arT[:, :, 1, :n], arT, "dve")
            transp8(bt_, n, bT[:, :, :n], bT, "act")
            transp8(kt_, n, kT[:, :, :n], kT, "dve")
            if P5CUT and P5CUT <= 5:
                continue
            Ycur, Zcur, ZIcur = [None] * 4, [None] * 4, [None] * 4
            for hg in range(4):
                for hh in range(4):
                    hd = hg * 4 + hh
                    p, base = hd // 2, (hd % 2) * 64
                    bs_ = slice(base, base + 64)
                    gm = gbank.next()
                    m4 = v4(gm)
                    if n == 128:
                        k.pe(lambda e: e.matmul(m4[:n, 0:2, :n], lhsT=bT[bs_, p, :n], rhs=arT[bs_, p, :, :n], start=True, stop=True),
                             R=[bT, arT], W=[gm])
                        k.pe(lambda e: e.matmul(m4[:n, 2:4, :n], lhsT=kT[bs_, p, :n], rhs=arT[bs_, p, :, :n], start=True, stop=True),
                             R=[kT, arT], W=[gm])
                    else:
                        for w_ in range(2):
                            k.pe(lambda e: e.matmul(m4[:n, w_, :n], lhsT=bT[bs_, p, :n], rhs=arT[bs_, p, w_, :n], start=True, stop=True),
                                 R=[bT, arT], W=[gm])
                            k.pe(lambda e: e.matmul(m4[:n, 2 + w_, :n], lhsT=kT[bs_, p, :n], rhs=arT[bs_, p, w_, :n], start=True, stop=True),
                                 R=[kT, arT], W=[gm])
                    k.dve(lambda e: e.tensor_tensor(out=AM[:n, hd, :, :n], in0=m4[:n, :, :n], in1=mk[:n, :, :n], op=ALU.mult),
                          R=[gm, mk], W=[AM])
            for hg in range(4):
                hsl = slice(hg * 4, hg * 4 + 4)
                for hh in range(4):
                    hd = hg * 4 + hh
                    k.pe(lambda e: e.transpose(out=Bt[:n, hg * 4 + hh - (hg // 2) * 8, :n], in_=AM[:n, hd, 0, :n], identity=P.identb[:n, :n]),
                         R=[AM, P.identb], W=[Bt])
                if hg % 2 == 1:
                    for h2 in range(2):
                        hgg = hg - 1 + h2
                        Z = yzr.next()
                        k.act(lambda e: e.activation(out=Z[:n, :, :n], in_=Bt[:n, h2 * 4:h2 * 4 + 4, :n], func=AF.Copy), R=[Bt], W=[Z])
                        Zcur[hgg] = Z
                k.dve(lambda e: e.tensor_tensor(out=Qt[hg][:n, :, :n], in0=AM[:n, hsl, 0, :n], in1=i4[:n, :, :n], op=ALU.add),
                      R=[AM, i4], W=[Qt[hg]])
            nsteps = 6 if n == 128 else 5
            for step in range(1, nsteps + 1):
                gys, gzs = [None] * 4, [None] * 4
                for hg in range(4):
                    Y, Z = Ycur[hg], Zcur[hg]
                    gy = gbank.next() if step < nsteps else None
                    gzz = gbank.next()
                    for hh in range(4):
                        hd = hg * 4 + hh
                        ysrc = AM[:n, hd, 0, :n] if Y is None else Y[:n, hh, :n]
                        ytk = AM if Y is None else Y
                        if gy is not None:
                            k.pe(lambda e: e.matmul(v4(gy)[:n, hh, :n], lhsT=Z[:n, hh, :n], rhs=ysrc, start=True, stop=True),
                                 R=[Z, ytk], W=[gy])
                        k.pe(lambda e: e.matmul(v4(gzz)[:n, hh, :n], lhsT=ysrc, rhs=Z[:n, hh, :n], start=True, stop=True),
                             R=[Z, ytk], W=[gzz])
                    Zn = yzr.next()
                    k.dve(lambda e: e.tensor_copy(out=Zn[:n, :, :n], in_=v4(gzz)[:n, :, :n]), R=[gzz], W=[Zn])
                    ZI = yzr.next()
                    k.dve(lambda e: e.tensor_tensor(out=ZI[:n, :, :n], in0=v4(gzz)[:n, :, :n], in1=i4[:n, :, :n], op=ALU.add),
                          R=[gzz, i4], W=[ZI])
                    ZIcur[hg] = ZI
                    if gy is not None:
                        Yn = yzr.next()
                        k.act(lambda e: e.activation(out=Yn[:n, :, :n], in_=v4(gy)[:n, :, :n], func=AF.Copy), R=[gy], W=[Yn])
                    else:
                        Yn = None
                    Ycur[hg], Zcur[hg] = Yn, Zn
                for hg in range(4):
                    hsl = slice(hg * 4, hg * 4 + 4)
                    Zn = Zcur[hg]
                    gq = gbank.next()
                    ZI = ZIcur[hg]
                    for hh in range(4):
                        hd = hg * 4 + hh
                        k.pe(lambda e: e.matmul(v4(gq)[:n, hh, :n], lhsT=ZI[:n, hh, :n], rhs=Qt[hg][:n, hh, :n], start=True, stop=True),
                             R=[ZI, Qt[hg]], W=[gq])
                    k.act(lambda e: e.activation(out=Qt[hg][:n, :, :n], in_=v4(gq)[:n, :, :n], func=AF.Copy), R=[gq], W=[Qt[hg]])
            if P5CUT and P5CUT <= 6:
                continue
            for half in range(2):
                g1 = gbank.next()
                for pp in range(4):
                    p = half * 4 + pp
                    k.pe(lambda e: e.matmul(v4(g1)[:n, pp, :], lhsT=arT[:, p, 0, :n], rhs=Hb[:, p, :], start=True, stop=False),
                         R=[arT, Hb], W=[g1])
                    for e_ in range(2):
                        hd = 2 * p + e_
                        k.pe(lambda e: e.matmul(v8(g1)[:n, 2 * pp + e_, :], lhsT=AM[:n, hd, 2, :n], rhs=vb_[:n, hd * 64:(hd + 1) * 64],
                                                start=False, stop=(e_ == 1)), R=[AM, vb_], W=[g1])
                k.act(lambda e: e.activation(out=R1[:n, half * 8:half * 8 + 8, :], in_=v8(g1)[:n, :, :], func=AF.Copy), R=[g1], W=[R1])
            if P5CUT == 65:
                continue
            for half in range(2):
                g2 = gbank.next()
                for h8 in range(8):
                    hd = half * 8 + h8
                    k.pe(lambda e: e.matmul(v8(g2)[:n, h8, :], lhsT=Qt[hd // 4][:n, hd % 4, :n], rhs=R1[:n, hd, :], start=True, stop=True),
                         R=[Qt[hd // 4], R1], W=[g2])
                k.dve(lambda e: e.tensor_copy(out=Ub[:n, half * 8:half * 8 + 8, :], in_=v8(g2)[:n, :, :]), R=[g2], W=[Ub])
            if P5CUT and P5CUT <= 7:
                continue
            for half in range(2):
                g3 = gbank.next()
                for pp in range(4):
                    p = half * 4 + pp
                    k.pe(lambda e: e.matmul(v4(g3)[:n, pp, :], lhsT=arT[:, p, 1, :n], rhs=Hb[:, p, :], start=True, stop=False),
                         R=[arT, Hb], W=[g3])
                    for e_ in range(2):
                        hd = 2 * p + e_
                        k.pe(lambda e: e.matmul(v8(g3)[:n, 2 * pp + e_, :], lhsT=AM[:n, hd, 1, :n], rhs=Ub[:n, hd, :], start=False, stop=False),
                             R=[AM, Ub], W=[g3])
                        k.pe(lambda e: e.matmul(v8(g3)[:n, 2 * pp + e_, :], lhsT=AM[:n, hd, 3, :n], rhs=vb_[:n, hd * 64:(hd + 1) * 64],
                                                start=False, stop=(e_ == 1)), R=[AM, vb_], W=[g3])
                k.act(lambda e: e.activation(out=yt[:n, half * 512:(half + 1) * 512], in_=g3[:n, :], func=AF.Copy), R=[g3], W=[yt])
            if P5CUT and P5CUT <= 8:
                continue
            for half in range(2):
                g4_ = gbank.next()
                for pp in range(4):
                    p = half * 4 + pp
                    ps_ = slice(p * 128, (p + 1) * 128)
                    k.pe(lambda e: e.matmul(v4(g4_)[:, pp, :], lhsT=bb_[:n, ps_], rhs=Ub[:n, 2 * p:2 * p + 2, :].rearrange("p a b -> p (a b)"),
                                            start=True, stop=False), R=[bb_, Ub], W=[g4_])
                    k.pe(lambda e: e.matmul(v4(g4_)[:, pp, :], lhsT=kb_[:n, ps_], rhs=vb_[:n, ps_], start=False, stop=True),
                         R=[kb_, vb_], W=[g4_])
                hs_ = slice(half * 4, half * 4 + 4)
                hst = tmpr.next()
                hst4 = hst[:, 0:512].rearrange("p (a b) -> p a b", a=4)
                k.act(lambda e: e.activation(out=hst[:, 0:512], in_=g4_[:, :], func=AF.Copy), R=[g4_], W=[hst])
                for e_ in range(2):
                    rows = slice(e_ * 64, (e_ + 1) * 64)
                    k.dve(lambda e: e.tensor_tensor(out=H[rows, hs_, :], in0=H[rows, hs_, :],
                                                    in1=bc(wc[rows, hs_].unsqueeze(2), [64, 4, 64]), op=ALU.mult),
                          R=[H, wc], W=[H])
                    k.dve(lambda e: e.tensor_tensor(out=H[rows, hs_, :], in0=H[rows, hs_, :],
                                                    in1=hst4[rows, :, e_ * 64:(e_ + 1) * 64], op=ALU.add),
                          R=[H, hst], W=[H])
            refresh_hb()
            if P5CUT and P5CUT <= 9:
                continue
            y3 = yt[:n, :].rearrange("p (h d) -> p h d", h=16)
            s1 = s16.next()
            k.dve(lambda e: e.tensor_reduce(out=s1[:n, :], in_=y3, axis=AX.X, op=ALU.add), R=[yt], W=[s1])
            k.dve(lambda e: e.tensor_scalar(out=s1[:n, :], in0=s1[:n, :], scalar1=1.0 / 64, scalar2=None, op0=ALU.mult), R=[s1], W=[s1])
            k.dve(lambda e: e.tensor_tensor(out=y3, in0=y3, in1=bc(s1[:n, :].unsqueeze(2), [n, 16, 64]), op=ALU.subtract),
                  R=[yt, s1], W=[yt])
            t = tmpr.next()
            k.dve(lambda e: e.tensor_tensor(out=t[:n, :], in0=yt[:n, :], in1=yt[:n, :], op=ALU.mult), R=[yt], W=[t])
            s2 = s16.next()
            k.dve(lambda e: e.tensor_reduce(out=s2[:n, :], in_=t[:n, :].rearrange("p (h d) -> p h d", h=16), axis=AX.X, op=ALU.add),
                  R=[t], W=[s2])
            rsqrt(k, s2[:n, :], s2[:n, :], [s2], [s2], 1.0 / 64, GN_EPS)
            tn = tmpr.next()
            tn3 = tn[:n, :].rearrange("p (h d) -> p h d", h=16)
            k.dve(lambda e: e.tensor_tensor(out=tn3, in0=y3, in1=bc(s2[:n, :].unsqueeze(2), [n, 16, 64]), op=ALU.mult),
                  R=[yt, s2], W=[tn])
            k.dve(lambda e: e.tensor_tensor(out=tn[:n, :], in0=tn[:n, :], in1=lnw[:n, :], op=ALU.mult), R=[tn, lnw], W=[tn])
            k.dve(lambda e: e.tensor_tensor(out=tn[:n, :], in0=tn[:n, :], in1=lnb[:n, :], op=ALU.add), R=[tn, lnb], W=[tn])
            k.dve(lambda e: e.tensor_tensor(out=yn[:n, :], in0=yn[:n, :], in1=tn[:n, :], op=ALU.add), R=[yn, tn], W=[yn])
            k.store("pool", S["oc"][g:g + n, :], yn[:n, :], yn)
        rwo_t = tmpr.next()
        rwo = rwo_t[0:64, :].rearrange("p (a b) -> p a b", a=8)
        for half in range(0 if P5NOFIN else 2):
            g_ = gbank.next()
            g4 = v4(g_)
            for pp in range(4):
                p = half * 4 + pp
                k.pe(lambda e: e.transpose(out=g4[0:64, pp, :], in_=H[:, p, :], identity=C("ident")), R=[H, P.cst], W=[g_])
            k.dve(lambda e: e.tensor_copy(out=rwo[:, half * 4:half * 4 + 4, :], in_=g4[0:64, :, :]), R=[g_], W=[rwo_t])
        rdst = O["rw_p"][l] if b is None else O["rw_s"][l, b]
        k.store("pool", rdst.rearrange("(p e) i j -> i p e j", e=2), rwo.rearrange("i p (e j) -> i p e j", e=2), rwo_t)
    k.end_phase(ph)


def phase6(P, l):
    k, I, S, O = P.k, P.I, P.S, P.O
    ph = k.phase()
    stg = Rot(k, ph, 2, [128, 2, D], F32, "p6stg")
    W = {}
    for nm in ["w_out_a", "w_out_b", "w_out_c", "w_o"]:
        wt = k.sb(ph, [128, 8, D], BF16, "p6" + nm)
        src = I[nm][l].rearrange("(k p) c -> p k c", p=128)
        for c4 in range(4):
            st = stg.next()
            k.load("sp", st[:, :, :], src[:, 2 * c4:2 * c4 + 2, :], st)
            k.dve(lambda e: e.tensor_copy(out=wt[:, 2 * c4:2 * c4 + 2, :], in_=st[:, :, :]), R=[st], W=[wt])
        W[nm] = wt
    ldr = Rot(k, ph, 12, [128, D], F32, "p6ld")
    tmpr = Rot(k, ph, 4, [128, D], F32, "p6tmp")
    mrg = Rot(k, ph, 2, [128, D], F32, "p6mrg")
    ogr = Rot(k, ph, 2, [128, D], BF16, "p6og")
    tTr = Rot(k, ph, 2, [128, 8, 128], BF16, "p6tT")
    yr = Rot(k, ph, 2, [128, D], F32, "p6y")
    pbr = Rot(k, ph, 2, [128, D], F32, "p6pb", psum=True)
    ptr = Rot(k, ph, 2, [128, 8, 128], BF16, "p6pt", psum=True)

    def proj_mm(srcb, n, wt):
        pt = ptr.next()
        for kk in range(8):
            k.pe(lambda e: e.transpose(out=pt[:, kk, :n], in_=srcb[:n, kk * 128:(kk + 1) * 128], identity=P.identb[:n, :n]),
                 R=[srcb, P.identb], W=[pt])
        tT = tTr.next()
        k.act(lambda e: e.activation(out=tT[:, :, :n], in_=pt[:, :, :n], func=AF.Copy), R=[pt], W=[tT])
        pb = pbr.next()
        for hf in range(2):
            for kk in range(8):
                k.pe(lambda e: e.matmul(pb[:n, hf * 512:(hf + 1) * 512], lhsT=tT[:, kk, :n], rhs=wt[:, kk, hf * 512:(hf + 1) * 512],
                                        start=(kk == 0), stop=(kk == 7)), R=[tT, wt], W=[pb])
        return pb

    for tl in P.tiles:
        n, g, t0 = tl["n"], tl["g"], tl["t0"]
        s = tl["seq"]
        b = s["b"]
        merged = mrg.next()
        for mi, (osrc, gcol, mcol, wn) in enumerate([("oa", O_AG, O_MA, "w_out_a"), ("ob", O_BG, O_MB, "w_out_b"),
                                                     ("oc", O_CG, O_MC, "w_out_c")]):
            o_, g_, m_ = ldr.next(), ldr.next(), ldr.next()
            k.load("sp", o_[:n, :], S[osrc][g:g + n, :], o_)
            k.load("sp", g_[:n, :], P.pj(g, g + n, gcol, gcol + D), g_)
            k.load("sp", m_[:n, :], P.pj(g, g + n, mcol, mcol + D), m_)
            sg = tmpr.next()
            k.act(lambda e: e.activation(out=sg[:n, :], in_=g_[:n, :], func=AF.Silu), R=[g_], W=[sg])
            og = ogr.next()
            k.dve(lambda e: e.tensor_tensor(out=og[:n, :], in0=o_[:n, :], in1=sg[:n, :], op=ALU.mult), R=[o_, sg], W=[og])
            pb = proj_mm(og, n, W[wn])
            sm = tmpr.next()
            k.act(lambda e: e.activation(out=sm[:n, :], in_=m_[:n, :], func=AF.Sigmoid), R=[m_], W=[sm])
            if mi == 0:
                k.dve(lambda e: e.tensor_tensor(out=merged[:n, :], in0=pb[:n, :], in1=sm[:n, :], op=ALU.mult), R=[pb, sm], W=[merged])
            else:
                k.dve(lambda e: e.tensor_tensor(out=sm[:n, :], in0=pb[:n, :], in1=sm[:n, :], op=ALU.mult), R=[pb, sm], W=[sm])
                k.pool(lambda e: e.tensor_tensor(out=merged[:n, :], in0=merged[:n, :], in1=sm[:n, :], op=ALU.add),
                       R=[merged, sm], W=[merged])
        mb = ogr.next()
        k.act(lambda e: e.activation(out=mb[:n, :], in_=merged[:n, :], func=AF.Copy), R=[merged], W=[mb])
        py = proj_mm(mb, n, W["w_o"])
        x = ldr.next()
        src, _ = P.xsrc(l, tl)
        k.load("sp", x[:n, :], src, x)
        y = yr.next()
        k.dve(lambda e: e.tensor_tensor(out=y[:n, :], in0=py[:n, :], in1=x[:n, :], op=ALU.add), R=[py, x], W=[y])
        if l == 0:
            dst = S["xmid"][g:g + n, :]
        elif b is None:
            dst = O["y_p"][t0:t0 + n, :]
        else:
            dst = O["y_s"][b, t0:t0 + n, :]
        k.store("pool", dst, y[:n, :], y)
    k.end_phase(ph)


_CACHE = {}
NCORES = 8


def _get_prog(T):
    if T not in _CACHE:
        _CACHE[T] = build(T)
    return _CACHE[T]


def kernel(x_prompt, x_sample, cache_attn_k, cache_attn_v, state_hgrn, state_rwkv, state_rwkv_shift,
           norm_g, w_in, a_qnorm_g, a_knorm_g, a_lambda, a_subln_g, b_lower, b_norm_g,
           c_shift_mu, c_w0, c_w2, c_a0, c_a2, c_k_k, c_k_a, c_r_k, c_ln_w, c_ln_b,
           c_vres_w1, c_vres_w2, c_v0, w_out_a, w_out_b, w_out_c, w_o):
    f = lambda a: np.ascontiguousarray(np.asarray(a, dtype=np.float32))
    x_prompt = f(x_prompt)
    B, T, _ = x_prompt.shape
    assert B == NCORES
    P = _get_prog(T)
    carr, _ = make_consts()
    shared = {
        "norm_g": f(norm_g), "w_in": f(w_in), "a_qnorm_g": f(a_qnorm_g), "a_knorm_g": f(a_knorm_g),
        "a_lambda": f(a_lambda).reshape(2, 256), "a_subln_g": f(a_subln_g), "b_lower": f(b_lower), "b_norm_g": f(b_norm_g),
        "c_shift_mu": f(c_shift_mu), "c_w0": f(c_w0), "c_w2": f(c_w2), "c_a0": f(c_a0), "c_a2": f(c_a2),
        "c_k_k": f(c_k_k), "c_k_a": f(c_k_a), "c_r_k": f(c_r_k).reshape(2, 1024), "c_ln_w": f(c_ln_w), "c_ln_b": f(c_ln_b),
        "c_vres_w1": f(c_vres_w1), "c_vres_w2": f(c_vres_w2), "c_v0": f(c_v0),
        "w_out_a": f(w_out_a), "w_out_b": f(w_out_b), "w_out_c": f(w_out_c), "w_o": f(w_o),
        "consts": carr,
        "rope_p": rope_tables(np.arange(T)), "rope_s": rope_tables(PAST + np.arange(TS)),
    }
    x_sample = f(x_sample)
    ck, cv = f(cache_attn_k), f(cache_attn_v)
    sth, str_, stsh = f(state_hgrn), f(state_rwkv), f(state_rwkv_shift)
    in_maps = []
    for c in range(NCORES):
        m = dict(shared)
        sl = slice(2 * c, 2 * c + 2)
        m["x_p"] = x_prompt[c]
        m["x_s"] = x_sample[sl]
        m["ck"] = np.ascontiguousarray(ck[:, sl]).reshape(2, 2, PAST, D)
        m["cv"] = np.ascontiguousarray(cv[:, sl]).reshape(2, 2, PAST, D)
        m["sth"] = np.ascontiguousarray(sth[:, sl])
        m["str"] = np.ascontiguousarray(str_[:, sl])
        m["stsh"] = np.ascontiguousarray(stsh[:, sl])
        in_maps.append(m)
    res = run_bass_kernel_spmd(P.nc, in_maps, core_ids=list(range(NCORES)))
    R = res.results
    NB = 2 * NCORES
    y_p = np.stack([R[c]["y_p"] for c in range(NCORES)], 0)
    y_s = np.concatenate([R[c]["y_s"] for c in range(NCORES)], 0)
    k_p = np.stack([R[c]["k_p"].reshape(2, T, 8, 128) for c in range(NCORES)], 1)
    v_p = np.stack([R[c]["v_p"].reshape(2, T, 8, 128) for c in range(NCORES)], 1)
    hg_p = np.stack([R[c]["hg_p"] for c in range(NCORES)], 1)
    rw_p = np.stack([R[c]["rw_p"] for c in range(NCORES)], 1)
    sh_p = np.stack([R[c]["sh_p"] for c in range(NCORES)], 1)
    k_s = np.concatenate([R[c]["k_s"].reshape(2, 2, TS, 8, 128) for c in range(NCORES)], 1)
    v_s = np.concatenate([R[c]["v_s"].reshape(2, 2, TS, 8, 128) for c in range(NCORES)], 1)
    hg_s = np.concatenate([R[c]["hg_s"] for c in range(NCORES)], 1)
    rw_s = np.concatenate([R[c]["rw_s"] for c in range(NCORES)], 1)
    sh_s = np.concatenate([R[c]["sh_s"] for c in range(NCORES)], 1)
    outs = (y_p, y_s, k_p, v_p, hg_p, rw_p, sh_p, k_s, v_s, hg_s, rw_s, sh_s)
    return tuple(np.ascontiguousarray(o, dtype=np.float32) for o in outs)
```

```python
import math
from contextlib import ExitStack

import numpy as np
import concourse.bass as bass
import concourse.mybir as mybir
from concourse.bass_utils import run_bass_kernel_spmd

F32 = mybir.dt.float32
BF16 = mybir.dt.bfloat16
AF = mybir.ActivationFunctionType
ALU = mybir.AluOpType
AX = mybir.AxisListType

D = 1024
NCOL = 15488
NCX = NCOL + 32
PAST = 1024
TS = 64
EPS = 1e-6
GN_EPS = 64e-5
ROPE_THETA = 500000.0
O_AQ, O_AK, O_AV, O_AG = 0, 1024, 2048, 3072
O_BQ, O_BF, O_BI, O_BG = 4096, 5120, 6144, 7168
O_CP = 8192
O_CG = 11392
O_MA, O_MB, O_MC = 12416, 13440, 14464
O_EXT = 15488
C_R, C_WLO, C_K, C_V, C_ALO = 0, 1024, 1088, 2112, 3136
CW = 3200


class Tk:
    def __init__(self, h, name, dram=False):
        self.h = h
        self.name = name
        self.lw = None
        self.rd = []
        self.ds = {}
        self.dram = dram
        self.tok = {}
        self.rtok = {}

    def __getitem__(self, k):
        return self.h[k]


class Ctx:
    def __init__(self, nc):
        self.nc = nc
        self.es = ExitStack()
        self.eng = {"pe": nc.tensor, "act": nc.scalar, "dve": nc.vector, "pool": nc.gpsimd, "sp": nc.sync}
        self.sem = {}
        self.cnt = {}
        self.waited = {}
        for k in self.eng:
            self.sem[k] = self.es.enter_context(nc.semaphore("es_" + k))
            self.cnt[k] = 0
            self.waited[k] = {}
        self.free_ds = {"hw": [], "sw": []}
        self.nds = 0
        self.uid = 0
        self.ninst = 0

    def get_ds(self, t, q):
        kind = "sw" if q == "pool" else "hw"
        if kind not in t.ds:
            t.ds[kind] = self.new_ds(kind)
        return t.ds[kind]

    def new_ds(self, kind):
        if self.free_ds[kind]:
            return self.free_ds[kind].pop()
        self.nds += 1
        h = self.es.enter_context(self.nc.semaphore("ds%d" % self.nds))
        return [h, 0, "ds%d" % self.nds]

    def sb(self, ph, shape, dt, name):
        self.uid += 1
        h = ph.enter_context(self.nc.sbuf_tensor("%s_%d" % (name, self.uid), list(shape), dt))
        t = Tk(h, name)
        ph.tiles.append(t)
        return t

    def ps(self, ph, shape, dt, name):
        self.uid += 1
        h = ph.enter_context(self.nc.psum_tensor("%s_%d" % (name, self.uid), list(shape), dt))
        t = Tk(h, name)
        ph.tiles.append(t)
        return t

    def phase(self):
        ph = ExitStack()
        ph.tiles = []
        return ph

    def end_phase(self, ph):
        self.barrier(ph.tiles)
        for t in ph.tiles:
            for kind, ds in t.ds.items():
                self.free_ds[kind].append(ds)
            t.ds = {}
        ph.close()

    def barrier(self, tiles):
        deps = []
        for k in self.eng:
            if k != "sp" and self.cnt[k] > 0:
                deps.append(("e", k, self.cnt[k]))
        for t in tiles:
            for ds in t.ds.values():
                if ds[1] > 0:
                    deps.append(("d", ds, ds[1]))
        for k in self.eng:
            for d in deps:
                self._wait(k, d)

    def _wait(self, ename, dep):
        if dep is None:
            return
        kind, obj, val = dep
        if kind == "e":
            if obj == ename and ename in ("pe", "sp"):
                return
            key = "e_" + obj
            semh = self.sem[obj]
        else:
            key = obj[2]
            semh = obj[0]
        w = self.waited[ename]
        if w.get(key, 0) >= val:
            return
        self.eng[ename].wait_ge(semh, val)
        w[key] = val
        self.ninst += 1

    def op(self, ename, fn, R=(), W=()):
        deps = []
        for t in R:
            deps.append(t.lw)
        for t in W:
            deps.append(t.lw)
            deps.extend(t.rd)
        for d in deps:
            self._wait(ename, d)
        ins = fn(self.eng[ename])
        self.cnt[ename] += 1
        ins.then_inc(self.sem[ename], 1)
        self.ninst += 1
        tok = ("e", ename, self.cnt[ename])
        for t in R:
            t.rd.append(tok)
        for t in W:
            t.lw = tok
            t.rd = []
        return ins

    def pe(self, fn, R=(), W=()):
        return self.op("pe", fn, R, W)

    def act(self, fn, R=(), W=()):
        return self.op("act", fn, R, W)

    def dve(self, fn, R=(), W=()):
        return self.op("dve", fn, R, W)

    def pool(self, fn, R=(), W=()):
        return self.op("pool", fn, R, W)

    def load(self, q, out_ap, in_ap, sbt, dr=None, slow=False):
        deps = [sbt.lw] + list(sbt.rd)
        for d in deps:
            self._wait(q, d)
        ds = self.get_ds(sbt, q)
        if slow:
            ins = self.eng[q].dma_start(out=out_ap, in_=in_ap, allow_slow_non_contiguous=True)
        else:
            ins = self.eng[q].dma_start(out=out_ap, in_=in_ap)
        ds[1] += 16
        ins.then_inc(ds[0], 16)
        self.ninst += 1
        sbt.lw = ("d", ds, ds[1])
        sbt.rd = []

    def store(self, q, out_ap, in_ap, sbt, dr=None):
        deps = [sbt.lw]
        for d in deps:
            self._wait(q, d)
        ds = self.get_ds(sbt, q)
        ins = self.eng[q].dma_start(out=out_ap, in_=in_ap)
        ds[1] += 16
        ins.then_inc(ds[0], 16)
        self.ninst += 1
        sbt.rd.append(("d", ds, ds[1]))


class Rot:
    def __init__(self, k, ph, n, shape, dt, name, psum=False):
        mk = k.ps if psum else k.sb
        self.tiles = [mk(ph, shape, dt, "%s%d" % (name, i)) for i in range(n)]
        self.i = 0

    def next(self):
        t = self.tiles[self.i % len(self.tiles)]
        self.i += 1
        return t


def rsqrt(k, out, in_, R, W, scale, bias):
    k.act(lambda e: e.activation(out=out, in_=in_, func=AF.Sqrt, scale=scale, bias=bias), R=R, W=W)
    k.dve(lambda e: e.reciprocal(out=out, in_=out), R=W, W=W)


def bc(ap, shape):
    return ap.to_broadcast(list(shape))


def make_consts():
    s = np.arange(128)[:, None]
    t = np.arange(128)[None, :]
    same = (s // 64) == (t // 64)
    c = {}
    c["ident"] = np.eye(128)
    ut64 = (same & (s <= t)).astype(np.float64)
    mid = 64 * (t // 64) + 31
    a_mid = (same & (s <= mid)).astype(np.float64)
    a_end = same.astype(np.float64)
    c["h_ut"] = ut64
    c["h_d1"] = ut64 - a_mid
    c["h_d2"] = a_end - ut64
    c["h_end"] = a_end
    c["h_mask"] = ut64
    c["r_ut"] = (s <= t).astype(np.float64)
    c["r_uts"] = (s < t).astype(np.float64)
    c["r_low"] = (s > t).astype(np.float64)
    c["r_one"] = np.ones((128, 128))
    names = ["ident", "h_ut", "h_d1", "h_d2", "h_end", "h_mask", "r_ut", "r_uts", "r_low", "r_one"]
    arr = np.concatenate([c[n] for n in names], axis=1).astype(np.float32)
    offs = {n: i * 128 for i, n in enumerate(names)}
    return arr, offs


def rope_tables(pos):
    half = 8
    inv_freq = (np.float32(ROPE_THETA) ** (-(np.arange(half, dtype=np.float32) * np.float32(2.0 / 16)))).astype(np.float32)
    ang = pos.astype(np.float32)[:, None] * inv_freq[None, :]
    return np.concatenate([np.cos(ang), np.sin(ang)], axis=1).astype(np.float32)


class Prog:
    pass


def build(T, nlayers=2, upto=99, debug=False):
    nc = bass.Bass("TRN2", target_bir_lowering=False)
    k = Ctx(nc)
    P = Prog()
    P.nc, P.k, P.T = nc, k, T
    Ttot = T + 2 * TS
    P.Ttot = Ttot
    carr, coff = make_consts()
    P.coff = coff

    def din(name, shape, dt=F32):
        return nc.dram_tensor(name, list(shape), dt, kind="ExternalInput").ap()

    def dout(name, shape, dt=F32):
        return Tk(nc.dram_tensor(name, list(shape), dt, kind="ExternalOutput").ap(), name, dram=True)

    def dscr(name, shape, dt=F32):
        kind = "ExternalOutput" if (debug and name in debug) else "Internal"
        return Tk(nc.dram_tensor(name, list(shape), dt, kind=kind).ap(), name, dram=True)

    I = {}
    I["x_p"] = din("x_p", [T, D])
    I["x_s"] = din("x_s", [2, TS, D])
    I["ck"] = din("ck", [2, 2, PAST, D])
    I["cv"] = din("cv", [2, 2, PAST, D])
    I["sth"] = din("sth", [2, 2, 8, 128, 128])
    I["str"] = din("str", [2, 2, 16, 64, 64])
    I["stsh"] = din("stsh", [2, 2, CW])
    I["norm_g"] = din("norm_g", [2, D])
    I["w_in"] = din("w_in", [2, D, NCOL])
    I["a_qnorm_g"] = din("a_qnorm_g", [2, 64])
    I["a_knorm_g"] = din("a_knorm_g", [2, 64])
    I["a_lambda"] = din("a_lambda", [2, 256])
    I["a_subln_g"] = din("a_subln_g", [2, 128])
    I["b_lower"] = din("b_lower", [2, 1024])
    I["b_norm_g"] = din("b_norm_g", [2, 128])
    I["c_shift_mu"] = din("c_shift_mu", [2, CW])
    I["c_w0"] = din("c_w0", [2, 1024])
    I["c_w2"] = din("c_w2", [2, 64, 1024])
    I["c_a0"] = din("c_a0", [2, 1024])
    I["c_a2"] = din("c_a2", [2, 64, 1024])
    I["c_k_k"] = din("c_k_k", [2, 1024])
    I["c_k_a"] = din("c_k_a", [2, 1024])
    I["c_r_k"] = din("c_r_k", [2, 1024])
    I["c_ln_w"] = din("c_ln_w", [2, 1024])
    I["c_ln_b"] = din("c_ln_b", [2, 1024])
    I["c_vres_w1"] = din("c_vres_w1", [1, D, 32])
    I["c_vres_w2"] = din("c_vres_w2", [1, 32, 1024])
    I["c_v0"] = din("c_v0", [1, 1024])
    I["w_out_a"] = din("w_out_a", [2, D, D])
    I["w_out_b"] = din("w_out_b", [2, D, D])
    I["w_out_c"] = din("w_out_c", [2, D, D])
    I["w_o"] = din("w_o", [2, D, D])
    I["consts"] = din("consts", list(carr.shape))
    I["rope_p"] = din("rope_p", [T, 16])
    I["rope_s"] = din("rope_s", [TS, 16])
    P.I = I

    O = {}
    O["y_p"] = dout("y_p", [T, D])
    O["y_s"] = dout("y_s", [2, TS, D])
    O["k_p"] = dout("k_p", [2, T, D])
    O["v_p"] = dout("v_p", [2, T, D])
    O["hg_p"] = dout("hg_p", [2, 8, 128, 128])
    O["rw_p"] = dout("rw_p", [2, 16, 64, 64])
    O["sh_p"] = dout("sh_p", [2, CW])
    O["k_s"] = dout("k_s", [2, 2, TS, D])
    O["v_s"] = dout("v_s", [2, 2, TS, D])
    O["hg_s"] = dout("hg_s", [2, 2, 8, 128, 128])
    O["rw_s"] = dout("rw_s", [2, 2, 16, 64, 64])
    O["sh_s"] = dout("sh_s", [2, 2, CW])
    P.O = O

    seqs = []
    seqs.append(dict(name="p", T=T, g0=0, n=128, past=0, b=None))
    seqs.append(dict(name="s0", T=TS, g0=T, n=TS, past=PAST, b=0))
    seqs.append(dict(name="s1", T=TS, g0=T + TS, n=TS, past=PAST, b=1))
    tiles = []
    for s in seqs:
        s["tiles"] = []
        for t0 in range(0, s["T"], s["n"]):
            tl = dict(seq=s, t0=t0, n=s["n"], g=s["g0"] + t0, idx=len(tiles))
            tiles.append(tl)
            s["tiles"].append(tl)
    P.seqs, P.tiles = seqs, tiles
    NTL = len(tiles)

    S = {}
    S["hT"] = dscr("hT", [NTL, 128, 8, 128], BF16)
    PJ = [(0, 4096, "pjA"), (4096, 8192, "pjB"), (8192, 12416, "pjC"), (12416, NCX, "pjM")]
    for lo, hi, nm in PJ:
        S[nm] = dscr(nm, [Ttot, hi - lo])

    def pj(r0, r1, c0, c1):
        for lo, hi, nm in PJ:
            if lo <= c0 and c1 <= hi:
                return S[nm][r0:r1, c0 - lo:c1 - lo]
        raise ValueError((c0, c1))
    P.pj = pj
    S["xmid"] = dscr("xmid", [Ttot, D])
    S["oa"] = dscr("oa", [Ttot, D])
    S["ob"] = dscr("ob", [Ttot, D])
    S["oc"] = dscr("oc", [Ttot, D])
    S["vf"] = dscr("vf", [Ttot, D])
    for s in seqs:
        TK = s["past"] + s["T"]
        s["TK"] = TK
        s["qT"] = dscr("qT_" + s["name"], [8, 128, s["T"]], BF16)
        s["kT"] = dscr("kT_" + s["name"], [8, 128, TK], BF16)
        s["vb"] = dscr("vb_" + s["name"], [TK, 8, 129], BF16)
    P.S = S

    def xsrc(l, tl):
        s = tl["seq"]
        if l == 0:
            if s["b"] is None:
                return I["x_p"][tl["t0"]:tl["t0"] + tl["n"], :], None
            return I["x_s"][s["b"], tl["t0"]:tl["t0"] + tl["n"], :], None
        return S["xmid"][tl["g"]:tl["g"] + tl["n"], :], S["xmid"]
    P.xsrc = xsrc

    gph = k.phase()
    P.gph = gph
    cst = k.sb(gph, [128, carr.shape[1]], F32, "cst")
    k.load("sp", cst[:], I["consts"][:, :], cst)
    identb = k.sb(gph, [128, 128], BF16, "identb")
    k.dve(lambda e: e.tensor_copy(out=identb[:], in_=cst[:, coff["ident"]:coff["ident"] + 128]), R=[cst], W=[identb])
    P.cst, P.identb = cst, identb

    def C(name):
        return cst[:, coff[name]:coff[name] + 128]
    P.C = C

    for l in range(nlayers):
        if upto >= 0:
            phase0(P, l)
        if upto >= 1:
            phase1(P, l)
        if upto >= 2:
            phase2(P, l)
        if upto >= 3:
            phase3(P, l)
        if upto >= 4:
            phase4(P, l)
        if upto >= 5:
            phase5(P, l)
        if upto >= 6:
            phase6(P, l)

    k.end_phase(gph)
    return P


def phase0(P, l):
    k, I, S = P.k, P.I, P.S
    ph = k.phase()
    xr = Rot(k, ph, 3, [128, D], F32, "p0x")
    jr = Rot(k, ph, 2, [128, D], BF16, "p0j")
    hr = Rot(k, ph, 2, [128, D], BF16, "p0h")
    sr = Rot(k, ph, 4, [128, 2], F32, "p0s")
    tr = Rot(k, ph, 2, [128, 8, 128], BF16, "p0t")
    pr = Rot(k, ph, 2, [128, 8, 128], BF16, "p0p", psum=True)
    for tl in P.tiles:
        n = tl["n"]
        src, dr = P.xsrc(l, tl)
        x = xr.next()
        k.load("sp", x[:n, :], src, x, dr)
        st = sr.next()
        j = jr.next()
        k.act(lambda e: e.activation(out=j[:n, :], in_=x[:n, :], func=AF.Square, accum_out=st[:n, 0:1]), R=[x], W=[j, st])
        rsqrt(k, st[:n, 1:2], st[:n, 0:1], [st], [st], 1.0 / D, EPS)
        h = hr.next()
        k.dve(lambda e: e.tensor_scalar(out=h[:n, :], in0=x[:n, :], scalar1=st[:n, 1:2], scalar2=None,
                                        op0=ALU.mult), R=[x, st], W=[h])
        pt = pr.next()
        for kk in range(8):
            k.pe(lambda e: e.transpose(out=pt[:, kk, :n], in_=h[:n, kk * 128:(kk + 1) * 128], identity=P.identb[:n, :n]),
                 R=[h, P.identb], W=[pt])
        ht = tr.next()
        k.act(lambda e: e.activation(out=ht[:, :, :n], in_=pt[:, :, :n], func=AF.Copy), R=[pt], W=[ht])
        k.store("pool", S["hT"][tl["idx"], :, :, :n], ht[:, :, :n], ht, S["hT"])
    k.end_phase(ph)


def phase1(P, l):
    k, I, S = P.k, P.I, P.S
    ph = k.phase()
    gcol = k.sb(ph, [128, 8], F32, "p1g")
    k.load("sp", gcol[:], I["norm_g"][l].rearrange("(k p) -> p k", p=128), gcol, slow=True)
    wf = Rot(k, ph, 2, [128, 8, 1024], F32, "p1wf")
    wb = Rot(k, ph, 2, [128, 8, 1024], BF16, "p1wb")
    hr = Rot(k, ph, 3, [128, 8, 128], BF16, "p1h")
    orr = Rot(k, ph, 3, [128, 1024], F32, "p1o")
    pr = Rot(k, ph, 2, [128, 1024], F32, "p1p", psum=True)
    groups = [(c0, 1024) for c0 in range(0, 11264, 1024)] + [(11264, 128)] + [(c0, 1024) for c0 in range(11392, NCOL, 1024)]
    if l == 1:
        groups.append((O_EXT, 32))
    ev = 0
    for (c0, cw) in groups:
        w32 = wf.next()
        if c0 == O_EXT:
            src = I["c_vres_w1"][0].rearrange("(k p) c -> p k c", p=128)
        else:
            src = I["w_in"][l][:, c0:c0 + cw].rearrange("(k p) c -> p k c", p=128)
        k.load("sp", w32[:, :, :cw], src, w32)
        w = wb.next()
        k.dve(lambda e: e.tensor_tensor(out=w[:, :, :cw], in0=w32[:, :, :cw],
                                        in1=bc(gcol[:, :].unsqueeze(2), [128, 8, cw]), op=ALU.mult),
              R=[w32, gcol], W=[w])
        for tl in P.tiles:
            n = tl["n"]
            h = hr.next()
            k.load("sp", h[:, :, :n], S["hT"][tl["idx"], :, :, :n], h, S["hT"])
            pt = pr.next()
            for n0 in range(0, cw, 512):
                nw = min(512, cw - n0)
                for kk in range(8):
                    k.pe(lambda e: e.matmul(pt[:n, n0:n0 + nw], lhsT=h[:, kk, :n], rhs=w[:, kk, n0:n0 + nw],
                                            start=(kk == 0), stop=(kk == 7)), R=[h, w], W=[pt])
            o = orr.next()
            if ev % 2 == 0:
                k.act(lambda e: e.activation(out=o[:n, :cw], in_=pt[:n, :cw], func=AF.Copy), R=[pt], W=[o])
            else:
                k.dve(lambda e: e.tensor_copy(out=o[:n, :cw], in_=pt[:n, :cw]), R=[pt], W=[o])
            ev += 1
            k.store("pool", P.pj(tl["g"], tl["g"] + n, c0, c0 + cw), o[:n, :cw], o)
    k.end_phase(ph)


def bcast_row(k, ph, ap1d, width, name, q="sp"):
    t = k.sb(ph, [128, width], F32, name)
    k.load(q, t[:], ap1d.partition_broadcast(128), t)
    return t


def phase2(P, l):
    k, I, S, O = P.k, P.I, P.S, P.O
    ph = k.phase()
    gq = bcast_row(k, ph, I["a_qnorm_g"][l], 64, "p2gq")
    gk = bcast_row(k, ph, I["a_knorm_g"][l], 64, "p2gk")
    xr = Rot(k, ph, 4, [128, D], F32, "p2x")
    tmp = Rot(k, ph, 2, [128, D], F32, "p2tmp")
    xn = Rot(k, ph, 3, [128, D], F32, "p2xn")
    ssr = Rot(k, ph, 4, [128, 16], F32, "p2ss")
    csr = Rot(k, ph, 2, [128, 16], F32, "p2cs")
    rtr = Rot(k, ph, 2, [128, 4, 16, 8], F32, "p2rt")
    xbr = Rot(k, ph, 3, [128, D], BF16, "p2xb")
    vbr = Rot(k, ph, 2, [128, 8, 129], BF16, "p2vb")
    tTr = Rot(k, ph, 3, [128, 8, 128], BF16, "p2tT")
    ptr = Rot(k, ph, 3, [128, 8, 128], BF16, "p2pt", psum=True)
    for vt in vbr.tiles:
        k.dve(lambda e: e.memset(vt[:, :, 128:129], 1.0), W=[vt])

    def transpose_store(xb, n, dst_ap):
        pt = ptr.next()
        for h in range(8):
            k.pe(lambda e: e.transpose(out=pt[:, h, :n], in_=xb[:n, h * 128:(h + 1) * 128], identity=P.identb[:n, :n]),
                 R=[xb, P.identb], W=[pt])
        tT = tTr.next()
        k.act(lambda e: e.activation(out=tT[:, :, :n], in_=pt[:, :, :n], func=AF.Copy), R=[pt], W=[tT])
        k.store("pool", dst_ap, tT[:, :, :n], tT)

    def v_store(v, n, dst_rows):
        vb = vbr.next()
        k.act(lambda e: e.activation(out=vb[:n, :, 0:128], in_=v[:n, :].rearrange("p (h d) -> p h d", h=8), func=AF.Copy),
              R=[v], W=[vb])
        k.store("pool", dst_rows, vb[:n, :, :], vb)

    def normrope(x, n, g, cs):
        t = tmp.next()
        k.act(lambda e: e.activation(out=t[:n, :], in_=x[:n, :], func=AF.Square), R=[x], W=[t])
        ss = ssr.next()
        k.dve(lambda e: e.tensor_reduce(out=ss[:n, :], in_=t[:n, :].rearrange("p (s d) -> p s d", s=16), axis=AX.X, op=ALU.add),
              R=[t], W=[ss])
        rsqrt(k, ss[:n, :], ss[:n, :], [ss], [ss], 1.0 / 64, EPS)
        y = xn.next()
        y3 = y[:n, :].rearrange("p (s d) -> p s d", s=16)
        x3 = x[:n, :].rearrange("p (s d) -> p s d", s=16)
        k.dve(lambda e: e.tensor_tensor(out=y3, in0=x3, in1=bc(ss[:n, :].unsqueeze(2), [n, 16, 64]), op=ALU.mult),
              R=[x, ss], W=[y])
        k.dve(lambda e: e.tensor_tensor(out=y3, in0=y3, in1=bc(g[:n, :].unsqueeze(1), [n, 16, 64]), op=ALU.mult),
               R=[y, g], W=[y])
        rt = rtr.next()
        cosb = bc(cs[:n, 0:8].unsqueeze(1), [n, 16, 8])
        sinb = bc(cs[:n, 8:16].unsqueeze(1), [n, 16, 8])
        x1 = y3[:, :, 0:8]
        x2 = y3[:, :, 8:16]
        k.dve(lambda e: e.tensor_tensor(out=rt[:n, 0], in0=x1, in1=cosb, op=ALU.mult), R=[y, cs], W=[rt])
        k.dve(lambda e: e.tensor_tensor(out=rt[:n, 1], in0=x2, in1=sinb, op=ALU.mult), R=[y, cs], W=[rt])
        k.dve(lambda e: e.tensor_tensor(out=rt[:n, 2], in0=x2, in1=cosb, op=ALU.mult), R=[y, cs], W=[rt])
        k.dve(lambda e: e.tensor_tensor(out=rt[:n, 3], in0=x1, in1=sinb, op=ALU.mult), R=[y, cs], W=[rt])
        k.dve(lambda e: e.tensor_tensor(out=x1, in0=rt[:n, 0], in1=rt[:n, 1], op=ALU.subtract), R=[rt], W=[y])
        k.dve(lambda e: e.tensor_tensor(out=x2, in0=rt[:n, 2], in1=rt[:n, 3], op=ALU.add), R=[rt], W=[y])
        return y

    for s in P.seqs:
        b = s["b"]
        for j in range(s["past"] // 128):
            x = xr.next()
            k.load("sp", x[:, :], I["ck"][l, b, j * 128:(j + 1) * 128, :], x)
            xb = xbr.next()
            k.act(lambda e: e.activation(out=xb[:, :], in_=x[:, :], func=AF.Copy), R=[x], W=[xb])
            transpose_store(xb, 128, s["kT"][:, :, j * 128:(j + 1) * 128].rearrange("h p t -> p h t"))
            v = xr.next()
            k.load("sp", v[:, :], I["cv"][l, b, j * 128:(j + 1) * 128, :], v)
            v_store(v, 128, s["vb"][j * 128:(j + 1) * 128, :, :])
        for tl in s["tiles"]:
            n, t0, g = tl["n"], tl["t0"], tl["g"]
            cs = csr.next()
            rsrc = I["rope_p"][t0:t0 + n, :] if b is None else I["rope_s"][t0:t0 + n, :]
            k.load("sp", cs[:n, :], rsrc, cs)
            x = xr.next()
            k.load("sp", x[:n, :], P.pj(g, g + n, O_AQ, O_AQ + D), x)
            y = normrope(x, n, gq, cs)
            xb = xbr.next()
            k.act(lambda e: e.activation(out=xb[:n, :], in_=y[:n, :], func=AF.Copy), R=[y], W=[xb])
            transpose_store(xb, n, s["qT"][:, :, t0:t0 + n].rearrange("h p t -> p h t"))
            x = xr.next()
            k.load("sp", x[:n, :], P.pj(g, g + n, O_AK, O_AK + D), x)
            y = normrope(x, n, gk, cs)
            kdst = O["k_p"][l, t0:t0 + n, :] if b is None else O["k_s"][l, b, t0:t0 + n, :]
            k.store("pool", kdst, y[:n, :], y)
            xb = xbr.next()
            k.act(lambda e: e.activation(out=xb[:n, :], in_=y[:n, :], func=AF.Copy), R=[y], W=[xb])
            p0 = s["past"]
            transpose_store(xb, n, s["kT"][:, :, p0 + t0:p0 + t0 + n].rearrange("h p t -> p h t"))
            v = xr.next()
            k.load("sp", v[:n, :], P.pj(g, g + n, O_AV, O_AV + D), v)
            vdst = O["v_p"][l, t0:t0 + n, :] if b is None else O["v_s"][l, b, t0:t0 + n, :]
            k.store("pool", vdst, v[:n, :], v)
            v_store(v, n, s["vb"][p0 + t0:p0 + t0 + n, :, :])
    k.end_phase(ph)


def phase3(P, l):
    k, I, S, O = P.k, P.I, P.S, P.O
    ph = k.phase()
    lam_init = 0.8 - 0.6 * math.exp(-0.3 * l)
    lamt = bcast_row(k, ph, I["a_lambda"][l], 256, "p3lam")
    lw = k.sb(ph, [128, 2, 64], F32, "p3lw")
    l4 = lamt[:, :].rearrange("p (a b d) -> p a b d", a=2, b=2)
    k.dve(lambda e: e.tensor_tensor(out=lw[:, :, :], in0=l4[:, :, 0, :], in1=l4[:, :, 1, :], op=ALU.mult), R=[lamt], W=[lw])
    lc = k.sb(ph, [128, 4], F32, "p3lc")
    k.dve(lambda e: e.tensor_reduce(out=lc[:, 0:2], in_=lw[:, :, :], axis=AX.X, op=ALU.add), R=[lw], W=[lc])
    k.act(lambda e: e.activation(out=lc[:, 0:2], in_=lc[:, 0:2], func=AF.Exp), R=[lc], W=[lc])
    k.dve(lambda e: e.tensor_tensor(out=lc[:, 2:3], in0=lc[:, 0:1], in1=lc[:, 1:2], op=ALU.subtract), R=[lc], W=[lc])
    k.dve(lambda e: e.tensor_scalar(out=lc[:, 3:4], in0=lc[:, 2:3], scalar1=lam_init, scalar2=None, op0=ALU.add), R=[lc], W=[lc])
    gs = bcast_row(k, ph, I["a_subln_g"][l], 128, "p3gs")
    k.dve(lambda e: e.tensor_scalar(out=gs[:, :], in0=gs[:, :], scalar1=1.0 - lam_init, scalar2=None, op0=ALU.mult), R=[gs], W=[gs])

    TKmax = max(s["TK"] for s in P.seqs)
    ntkmax = (TKmax + 127) // 128
    ktr = Rot(k, ph, 2, [128, TKmax], BF16, "p3kt")
    vtr = Rot(k, ph, 2, [128, ntkmax, 129], BF16, "p3vt")
    qtr = Rot(k, ph, 2, [128, 512], BF16, "p3qt")
    psr = Rot(k, ph, 2, [128, 2, 512], F32, "p3ps", psum=True)
    acc = k.ps(ph, [128, 8, 256], F32, "p3acc")
    ptr = Rot(k, ph, 3, [128, 2, 512], BF16, "p3pt")
    accr = Rot(k, ph, 2, [128, 8, 129], F32, "p3accs")
    rrr = Rot(k, ph, 4, [128, 8], F32, "p3rr")
    tr_ = Rot(k, ph, 2, [128, 128], F32, "p3t")
    orr = Rot(k, ph, 2, [128, 128], F32, "p3o")
    ofr = Rot(k, ph, 3, [128, 128], F32, "p3of")

    for s in P.seqs:
        TK, Tq = s["TK"], s["T"]
        ntk = (TK + 127) // 128
        prompt = s["b"] is None
        qw = min(512, Tq)
        for h in range(8):
            kt = ktr.next()
            k.load("sp", kt[:, :TK], s["kT"][h, :, :], kt)
            vt = vtr.next()
            nfull = TK // 128
            k.load("sp", vt[:, :nfull, :], s["vb"][0:nfull * 128, h, :].rearrange("(j p) d -> p j d", p=128), vt)
            if TK % 128:
                rem = TK % 128
                k.load("sp", vt[:rem, nfull, :], s["vb"][nfull * 128:TK, h, :], vt)
            for q0 in range(0, Tq, qw):
                qt = qtr.next()
                k.load("sp", qt[:, :qw], s["qT"][h, :, q0:q0 + qw], qt)
                nqt = (qw + 127) // 128
                jq0 = q0 // 128
                jlast = (jq0 + nqt - 1) if prompt else (ntk - 1)
                def s_mm(j):
                    nk = min(128, TK - j * 128)
                    ps = psr.next()
                    for m in range(2):
                        k.pe(lambda e: e.matmul(ps[:nk, m, :qw], lhsT=kt[m * 64:(m + 1) * 64, j * 128:j * 128 + nk],
                                                rhs=qt[m * 64:(m + 1) * 64, :qw], start=True, stop=True),
                             R=[kt, qt], W=[ps])
                    return ps

                ps_next = s_mm(0)
                for j in range(jlast + 1):
                    nk = min(128, TK - j * 128)
                    ps = ps_next
                    if j < jlast:
                        ps_next = s_mm(j + 1)
                    pt = ptr.next()
                    k.act(lambda e: e.activation(out=pt[:nk, :, :qw], in_=ps[:nk, :, :qw], func=AF.Exp, scale=0.125),
                          R=[ps], W=[pt])
                    if prompt and j >= jq0:
                        i = j - jq0
                        k.dve(lambda e: e.memset(pt[64:128, :, i * 128:i * 128 + 64], 0.0), W=[pt])
                    for m in range(2):
                        for i in range(nqt):
                            nq = min(128, qw - i * 128)
                            last = (jq0 + i) if prompt else (ntk - 1)
                            if j > last:
                                continue
                            k.pe(lambda e: e.matmul(acc[:nq, m * 4 + i, 0:129], lhsT=pt[:nk, m, i * 128:i * 128 + nq],
                                                    rhs=vt[:nk, j, :], start=(j == 0 and i % 2 == 0), stop=(j == last),
                                                    skip_group_check=True),
                                 R=[pt, vt], W=[acc])
                nqmax = min(128, qw)
                accs = accr.next()
                for m in range(2):
                    k.dve(lambda e: e.tensor_copy(out=accs[:nqmax, m * 4:m * 4 + nqt, :], in_=acc[:nqmax, m * 4:m * 4 + nqt, 0:129]),
                          R=[acc], W=[accs])
                for i in range(nqt):
                    nq = min(128, qw - i * 128)
                    rr = rrr.next()
                    k.dve(lambda e: e.reciprocal(out=rr[:nq, 0:1], in_=accs[:nq, i, 128:129]), R=[accs], W=[rr])
                    k.dve(lambda e: e.reciprocal(out=rr[:nq, 1:2], in_=accs[:nq, 4 + i, 128:129]), R=[accs], W=[rr])
                    k.dve(lambda e: e.tensor_tensor(out=rr[:nq, 2:3], in0=rr[:nq, 1:2], in1=lc[:nq, 3:4], op=ALU.mult),
                          R=[rr, lc], W=[rr])
                    t = tr_.next()
                    k.dve(lambda e: e.tensor_scalar(out=t[:nq, :], in0=accs[:nq, 4 + i, 0:128], scalar1=rr[:nq, 2:3],
                                                    scalar2=None, op0=ALU.mult), R=[accs, rr], W=[t])
                    o = orr.next()
                    k.dve(lambda e: e.scalar_tensor_tensor(out=o[:nq, :], in0=accs[:nq, i, 0:128], scalar=rr[:nq, 0:1],
                                                           in1=t[:nq, :], op0=ALU.mult, op1=ALU.subtract),
                          R=[accs, rr, t], W=[o])
                    k.dve(lambda e: e.tensor_tensor(out=t[:nq, :], in0=o[:nq, :], in1=o[:nq, :], op=ALU.mult), R=[o], W=[t])
                    k.dve(lambda e: e.tensor_reduce(out=rr[:nq, 3:4], in_=t[:nq, :], axis=AX.X, op=ALU.add), R=[t], W=[rr])
                    k.act(lambda e: e.activation(out=rr[:nq, 4:5], in_=rr[:nq, 3:4], func=AF.Ln, scale=1.0 / 128, bias=EPS),
                          R=[rr], W=[rr])
                    k.act(lambda e: e.activation(out=rr[:nq, 5:6], in_=rr[:nq, 4:5], func=AF.Exp, scale=-0.5), R=[rr], W=[rr])
                    of = ofr.next()
                    k.dve(lambda e: e.scalar_tensor_tensor(out=of[:nq, :], in0=o[:nq, :], scalar=rr[:nq, 5:6],
                                                           in1=gs[:nq, :], op0=ALU.mult, op1=ALU.mult),
                          R=[o, rr, gs], W=[of])
                    g = s["g0"] + q0 + i * 128
                    k.store("pool", S["oa"][g:g + nq, h * 128:(h + 1) * 128], of[:nq, :], of)
    k.end_phase(ph)


def phase4(P, l):
    k, I, S, O, C = P.k, P.I, P.S, P.O, P.C
    ph = k.phase()
    lbr = k.sb(ph, [128, D], F32, "p4lb")
    oml = k.sb(ph, [128, D], F32, "p4oml")
    if l == 0:
        k.dve(lambda e: e.memset(lbr[:, :], 0.0), W=[lbr])
        k.dve(lambda e: e.memset(oml[:, :], 1.0), W=[oml])
    else:
        k.load("sp", lbr[:, :], I["b_lower"][1].partition_broadcast(128), lbr)
        k.load("sp", oml[:, :], I["b_lower"][0].partition_broadcast(128), oml)
        k.dve(lambda e: e.tensor_tensor(out=lbr[:, :], in0=lbr[:, :], in1=oml[:, :], op=ALU.subtract), R=[lbr, oml], W=[lbr])
        k.act(lambda e: e.activation(out=lbr[:, :], in_=lbr[:, :], func=AF.Sigmoid), R=[lbr], W=[lbr])
        k.dve(lambda e: e.tensor_scalar(out=oml[:, :], in0=lbr[:, :], scalar1=-1.0, scalar2=1.0, op0=ALU.mult, op1=ALU.add),
              R=[lbr], W=[oml])
    gn = bcast_row(k, ph, I["b_norm_g"][l], 128, "p4gn")
    ldr = Rot(k, ph, 6, [128, D], F32, "p4ld")
    f32r = Rot(k, ph, 8, [128, D], F32, "p4f")
    er = Rot(k, ph, 3, [128, D], F32, "p4e")
    b16r = Rot(k, ph, 10, [128, D], BF16, "p4b")
    tTr = Rot(k, ph, 4, [128, 8, 128], BF16, "p4tT")
    qpr = Rot(k, ph, 2, [128, 8, 2, 128], BF16, "p4qp")
    for t in qpr.tiles:
        k.dve(lambda e: e.memset(t[:, :, :, :], 0.0), W=[t])
    scmr = Rot(k, ph, 2, [128, 8, 128], BF16, "p4scm")
    dcr = Rot(k, ph, 2, [128, 8, 2], F32, "p4dc")
    ssr = Rot(k, ph, 2, [128, 8], F32, "p4ss")
    St = k.sb(ph, [128, 8, 128], F32, "p4S")
    Sb = k.sb(ph, [128, 8, 128], BF16, "p4Sb")
    pc = k.ps(ph, [128, D], F32, "p4pc")
    ptp = k.ps(ph, [128, 8, 128], BF16, "p4pt")
    psc = k.ps(ph, [128, 4, 128], F32, "p4psc")
    po = k.ps(ph, [128, 8, 128], F32, "p4po")
    pS = k.ps(ph, [128, 4, 128], F32, "p4pS")
    pd = k.ps(ph, [128, 8, 2], F32, "p4pd")
    ioff = P.coff["ident"]

    def cum_mm(name, logf, n):
        for hf in range(2):
            k.pe(lambda e: e.matmul(pc[:n, hf * 512:(hf + 1) * 512], lhsT=C(name)[:n, :n], rhs=logf[:n, hf * 512:(hf + 1) * 512],
                                    start=True, stop=True), R=[P.cst, logf], W=[pc])

    def transp(src, n):
        for h in range(8):
            k.pe(lambda e: e.transpose(out=ptp[:, h, :n], in_=src[:n, h * 128:(h + 1) * 128], identity=P.identb[:n, :n]),
                 R=[src, P.identb], W=[ptp])

    for s in P.seqs:
        b = s["b"]
        if b is None:
            k.dve(lambda e: e.memset(St[:, :, :], 0.0), W=[St])
        else:
            k.load("sp", St[:, :, :], I["sth"][l, b].rearrange("h k v -> k h v"), St)
        k.act(lambda e: e.activation(out=Sb[:, :, :], in_=St[:, :, :], func=AF.Copy), R=[St], W=[Sb])
        for tl in s["tiles"]:
            n, g = tl["n"], tl["g"]
            nch = n // 64
            bq, bf_, bi = ldr.next(), ldr.next(), ldr.next()
            k.load("sp", bq[:n, :], P.pj(g, g + n, O_BQ, O_BQ + D), bq)
            k.load("sp", bf_[:n, :], P.pj(g, g + n, O_BF, O_BF + D), bf_)
            k.load("sp", bi[:n, :], P.pj(g, g + n, O_BI, O_BI + D), bi)
            sg, t1, f, kin, logf, q = (f32r.next() for _ in range(6))
            k.act(lambda e: e.activation(out=sg[:n, :], in_=bf_[:n, :], func=AF.Sigmoid), R=[bf_], W=[sg])
            k.dve(lambda e: e.tensor_tensor(out=t1[:n, :], in0=sg[:n, :], in1=oml[:n, :], op=ALU.mult), R=[sg, oml], W=[t1])
            k.dve(lambda e: e.tensor_tensor(out=f[:n, :], in0=t1[:n, :], in1=lbr[:n, :], op=ALU.add), R=[t1, lbr], W=[f])
            k.dve(lambda e: e.tensor_tensor(out=kin[:n, :], in0=oml[:n, :], in1=t1[:n, :], op=ALU.subtract), R=[t1, oml], W=[kin])
            k.act(lambda e: e.activation(out=logf[:n, :], in_=f[:n, :], func=AF.Ln), R=[f], W=[logf])
            k.act(lambda e: e.activation(out=q[:n, :], in_=bq[:n, :], func=AF.Silu), R=[bq], W=[q])
            qt_, qh, kh, kt_, ib = (b16r.next() for _ in range(5))
            k.dve(lambda e: e.tensor_copy(out=ib[:n, :], in_=bi[:n, :]), R=[bi], W=[ib])
            cum_mm("h_ut", logf, n)
            e1 = er.next()
            k.act(lambda e: e.activation(out=e1[:n, :], in_=pc[:n, :], func=AF.Exp), R=[pc], W=[e1])
            k.dve(lambda e: e.tensor_tensor(out=qt_[:n, :], in0=q[:n, :], in1=e1[:n, :], op=ALU.mult), R=[q, e1], W=[qt_])
            cum_mm("h_d1", logf, n)
            e2, e3 = er.next(), er.next()
            k.act(lambda e: e.activation(out=e2[:n, :], in_=pc[:n, :], func=AF.Exp), R=[pc], W=[e2])
            k.act(lambda e: e.activation(out=e3[:n, :], in_=pc[:n, :], func=AF.Exp, scale=-1.0), R=[pc], W=[e3])
            k.dve(lambda e: e.tensor_tensor(out=qh[:n, :], in0=q[:n, :], in1=e2[:n, :], op=ALU.mult), R=[q, e2], W=[qh])
            k.dve(lambda e: e.tensor_tensor(out=kh[:n, :], in0=kin[:n, :], in1=e3[:n, :], op=ALU.mult), R=[kin, e3], W=[kh])
            cum_mm("h_d2", logf, n)
            e4 = er.next()
            k.act(lambda e: e.activation(out=e4[:n, :], in_=pc[:n, :], func=AF.Exp), R=[pc], W=[e4])
            k.dve(lambda e: e.tensor_tensor(out=kt_[:n, :], in0=kin[:n, :], in1=e4[:n, :], op=ALU.mult), R=[kin, e4], W=[kt_])
            cum_mm("h_end", logf, n)
            e5 = er.next()
            k.act(lambda e: e.activation(out=e5[:n, :], in_=pc[:n, :], func=AF.Exp), R=[pc], W=[e5])
            for h in range(8):
                k.pe(lambda e: e.matmul(pd[:, h, :nch], lhsT=e5[:n, h * 128:(h + 1) * 128],
                                        rhs=P.cst[:n, ioff:ioff + 64 * nch:64], start=True, stop=True),
                     R=[e5, P.cst], W=[pd])
            dC = dcr.next()
            k.dve(lambda e: e.tensor_copy(out=dC[:, :, :nch], in_=pd[:, :, :nch]), R=[pd], W=[dC])
            transp(qh, n)
            qhT = tTr.next()
            k.act(lambda e: e.activation(out=qhT[:, :, :n], in_=ptp[:, :, :n], func=AF.Copy), R=[ptp], W=[qhT])
            transp(kh, n)
            khT = tTr.next()
            k.dve(lambda e: e.tensor_copy(out=khT[:, :, :n], in_=ptp[:, :, :n]), R=[ptp], W=[khT])
            transp(qt_, n)
            qp = qpr.next()
            k.act(lambda e: e.activation(out=qp[:, :, 0, 0:64], in_=ptp[:, :, 0:64], func=AF.Copy), R=[ptp], W=[qp])
            if nch == 2:
                k.act(lambda e: e.activation(out=qp[:, :, 1, 64:128], in_=ptp[:, :, 64:128], func=AF.Copy), R=[ptp], W=[qp])
            scm = scmr.next()
            k.dve(lambda e: e.memset(po[:, :, :], 0.0), W=[po])
            for hg in range(2):
                for hh in range(4):
                    h = hg * 4 + hh
                    k.pe(lambda e: e.matmul(psc[:n, hh, :n], lhsT=khT[:, h, :n], rhs=qhT[:, h, :n], start=True, stop=True),
                         R=[khT, qhT], W=[psc])
                k.dve(lambda e: e.tensor_tensor(out=scm[:n, hg * 4:hg * 4 + 4, :n], in0=psc[:n, :, :n],
                                                in1=bc(C("h_mask")[:n, :n].unsqueeze(1), [n, 4, n]), op=ALU.mult),
                      R=[psc, P.cst], W=[scm])
            for h in range(8):
                hs = slice(h * 128, (h + 1) * 128)
                k.pe(lambda e: e.matmul(po[:n, h, :], lhsT=scm[:n, h, :n], rhs=ib[:n, hs], start=False, stop=False,
                                        skip_group_check=True), R=[scm, ib], W=[po])
                k.pe(lambda e: e.matmul(po[:n, h, :], lhsT=qp[:, h, 0, :n], rhs=Sb[:, h, :], start=False, stop=(nch == 1),
                                        skip_group_check=True), R=[qp, Sb], W=[po])
            for c in range(nch):
                rows = slice(c * 64, (c + 1) * 64)
                for hg in range(2):
                    for hh in range(4):
                        h = hg * 4 + hh
                        hs = slice(h * 128, (h + 1) * 128)
                        k.pe(lambda e: e.matmul(pS[:, hh, :], lhsT=kt_[rows, hs], rhs=ib[rows, hs], start=True, stop=True),
                             R=[kt_, ib], W=[pS])
                    hsl = slice(hg * 4, hg * 4 + 4)
                    k.dve(lambda e: e.tensor_tensor(out=St[:, hsl, :], in0=St[:, hsl, :],
                                                    in1=bc(dC[:, hsl, c:c + 1], [128, 4, 128]), op=ALU.mult),
                          R=[St, dC], W=[St])
                    k.dve(lambda e: e.tensor_tensor(out=St[:, hsl, :], in0=St[:, hsl, :], in1=pS[:, :, :], op=ALU.add),
                          R=[St, pS], W=[St])
                    k.act(lambda e: e.activation(out=Sb[:, hsl, :], in_=St[:, hsl, :], func=AF.Copy), R=[St], W=[Sb])
                if c == 0 and nch == 2:
                    for h in range(8):
                        k.pe(lambda e: e.matmul(po[:n, h, :], lhsT=qp[:, h, 1, :n], rhs=Sb[:, h, :], start=False, stop=True,
                                                skip_group_check=True), R=[qp, Sb], W=[po])
            sq = f32r.next()
            k.act(lambda e: e.activation(out=sq[:n, :], in_=po[:n, :, :].rearrange("p h d -> p (h d)"), func=AF.Square),
                  R=[po], W=[sq])
            ss = ssr.next()
            k.dve(lambda e: e.tensor_reduce(out=ss[:n, :], in_=sq[:n, :].rearrange("p (h d) -> p h d", h=8), axis=AX.X, op=ALU.add),
                  R=[sq], W=[ss])
            rsqrt(k, ss[:n, :], ss[:n, :], [ss], [ss], 1.0 / 128, EPS)
            ob = f32r.next()
            ob3 = ob[:n, :].rearrange("p (h d) -> p h d", h=8)
            k.dve(lambda e: e.tensor_tensor(out=ob3, in0=po[:n, :, :], in1=bc(ss[:n, :].unsqueeze(2), [n, 8, 128]), op=ALU.mult),
                  R=[po, ss], W=[ob])
            k.dve(lambda e: e.tensor_tensor(out=ob3, in0=ob3, in1=bc(gn[:n, :].unsqueeze(1), [n, 8, 128]), op=ALU.mult),
                   R=[ob, gn], W=[ob])
            k.store("pool", S["ob"][g:g + n, :], ob[:n, :], ob)
        hdst = O["hg_p"][l] if b is None else O["hg_s"][l, b]
        k.store("pool", hdst.rearrange("h k v -> k h v"), St[:, :, :], St)
    k.end_phase(ph)


import os
P5CUT = int(os.environ.get("P5CUT", "0"))
P5NOFIN = int(os.environ.get("P5NOFIN", "0"))
P5STEPS = int(os.environ.get("P5STEPS", "-1"))
P5SUB = int(os.environ.get("P5SUB", "9"))
P5EXP = int(os.environ.get("P5EXP", "0"))


def phase5(P, l):
    k, I, S, O, C = P.k, P.I, P.S, P.O, P.C
    ph = k.phase()
    NEG_E = -math.exp(-0.5)
    mu = bcast_row(k, ph, I["c_shift_mu"][l], CW, "p5mu")
    w0 = bcast_row(k, ph, I["c_w0"][l], D, "p5w0")
    a0 = bcast_row(k, ph, I["c_a0"][l], D, "p5a0")
    kkr = bcast_row(k, ph, I["c_k_k"][l], D, "p5kk")
    kar = bcast_row(k, ph, I["c_k_a"][l], D, "p5ka")
    rkr = bcast_row(k, ph, I["c_r_k"][l], D, "p5rk")
    lnw = bcast_row(k, ph, I["c_ln_w"][l], D, "p5lnw")
    lnb = bcast_row(k, ph, I["c_ln_b"][l], D, "p5lnb")
    if l == 1:
        v0 = bcast_row(k, ph, I["c_v0"][0], D, "p5v0")
    tmpr = Rot(k, ph, 4, [128, D], F32, "p5tmp")
    stg = tmpr.tiles[0]
    w2b = k.sb(ph, [64, D], BF16, "p5w2")
    a2b = k.sb(ph, [64, D], BF16, "p5a2")
    k.load("sp", stg[:64, :], I["c_w2"][l], stg)
    k.dve(lambda e: e.tensor_copy(out=w2b[:, :], in_=stg[:64, :]), R=[stg], W=[w2b])
    k.load("sp", stg[:64, :], I["c_a2"][l], stg)
    k.dve(lambda e: e.tensor_copy(out=a2b[:, :], in_=stg[:64, :]), R=[stg], W=[a2b])
    if l == 1:
        v2b = k.sb(ph, [32, D], BF16, "p5v2w")
        k.load("sp", stg[:32, :], I["c_vres_w2"][0], stg)
        k.dve(lambda e: e.tensor_copy(out=v2b[:, :], in_=stg[:32, :]), R=[stg], W=[v2b])
    mk = k.sb(ph, [128, 4, 128], F32, "p5mk")
    for i_, nm in enumerate(["r_uts", "r_ut", "r_uts", "r_ut"]):
        k.dve(lambda e: e.tensor_copy(out=mk[:, i_, :], in_=C(nm)), R=[P.cst], W=[mk])

    mz = k.sb(ph, [128, 4, 128], F32, "p5mz")
    i4 = k.sb(ph, [128, 4, 128], BF16, "p5i4")
    for i_ in range(4):
        k.dve(lambda e: e.tensor_copy(out=mz[:, i_, :], in_=C("r_low")), R=[P.cst], W=[mz])
        k.dve(lambda e: e.tensor_copy(out=i4[:, i_, :], in_=P.identb[:, :]), R=[P.identb], W=[i4])
    cp = k.sb(ph, [128, CW], F32, "p5cp")
    cs = k.sb(ph, [128, CW], F32, "p5cs")
    hv = k.sb(ph, [128, 32], F32, "p5hv")
    vft = k.sb(ph, [128, D], F32, "p5vf")
    k2 = k.sb(ph, [128, D], F32, "p5k2")
    v2 = k.sb(ph, [128, D], F32, "p5v2")
    asig = k.sb(ph, [128, D], F32, "p5as")
    kk = k.sb(ph, [128, D], F32, "p5kkt")
    bs = k.sb(ph, [128, D], F32, "p5bs")
    ld = k.sb(ph, [128, D], F32, "p5ld")
    yt = k.sb(ph, [128, D], F32, "p5y")
    yn = k.sb(ph, [128, D], F32, "p5yn")
    s16 = Rot(k, ph, 6, [128, 16], F32, "p5s16")
    smb = k.sb(ph, [128, 3, 64], BF16, "p5smb")
    smT = k.sb(ph, [64, 3, 128], BF16, "p5smT")
    rt_, at_, bt_, kt_, bb_, kb_, vb_ = (k.sb(ph, [128, D], BF16, "p5b%d" % i_) for i_ in range(7))
    arT = k.sb(ph, [128, 8, 2, 128], BF16, "p5arT")
    bT = k.sb(ph, [128, 8, 128], BF16, "p5bT")
    kT = k.sb(ph, [128, 8, 128], BF16, "p5kT")
    AM = k.sb(ph, [128, 16, 4, 128], BF16, "p5AM")
    Qt = [k.sb(ph, [128, 4, 128], BF16, "p5Q%d" % i_) for i_ in range(4)]
    yzr = Rot(k, ph, 18, [128, 4, 128], BF16, "p5yz")
    R1 = k.sb(ph, [128, 16, 64], BF16, "p5R1")
    Ub = k.sb(ph, [128, 16, 64], BF16, "p5Ub")
    H = k.sb(ph, [128, 8, 64], F32, "p5H")
    Hb = k.sb(ph, [128, 8, 128], BF16, "p5Hb")
    k.dve(lambda e: e.memset(Hb[:, :, :], 0.0), W=[Hb])

    def refresh_hb():
        for e_ in range(2):
            rows = slice(e_ * 64, (e_ + 1) * 64)
            k.act(lambda e: e.activation(out=Hb[rows, :, e_ * 64:(e_ + 1) * 64], in_=H[rows, :, :], func=AF.Copy), R=[H], W=[Hb])
    wc = k.sb(ph, [128, 8], F32, "p5wc")
    B01 = k.ps(ph, [128, D], F32, "p5B01")
    Bt = k.ps(ph, [128, 8, 128], BF16, "p5Bt")
    gbank = Rot(k, ph, 5, [128, 512], F32, "p5g", psum=True)
    if os.environ.get("KDEBUG"):
        print("phase5 sbuf bytes remaining", P.nc.sbuf_bytes_remaining)

    def v4(bank):
        return bank[:, :].rearrange("p (a b) -> p a b", a=4)

    def v8(bank):
        return bank[:, :].rearrange("p (a b) -> p a b", a=8)

    def small_mm(col, wts, kdim, n, bias, out, func):
        for hf in range(2):
            k.pe(lambda e: e.matmul(B01[:n, hf * 512:(hf + 1) * 512], lhsT=smT[0:kdim, col, :n],
                                    rhs=wts[0:kdim, hf * 512:(hf + 1) * 512], start=True, stop=True),
                 R=[smT, wts], W=[B01])
        t = tmpr.next()
        k.dve(lambda e: e.tensor_tensor(out=t[:n, :], in0=B01[:n, :], in1=bias[:n, :], op=ALU.add), R=[B01, bias], W=[t])
        k.act(lambda e: e.activation(out=out[:n, :], in_=t[:n, :], func=func), R=[t], W=[out])

    def cum_exp(name, n, outs):
        for hf in range(2):
            k.pe(lambda e: e.matmul(B01[:n, hf * 512:(hf + 1) * 512], lhsT=C(name)[:n, :n], rhs=ld[:n, hf * 512:(hf + 1) * 512],
                                    start=True, stop=True), R=[P.cst, ld], W=[B01])
        for (t, sc) in outs:
            k.act(lambda e: e.activation(out=t[:n, :], in_=B01[:n, :], func=AF.Exp, scale=sc), R=[B01], W=[t])

    def transp8(src, n, dst_ap, dst_t, eng):
        for p in range(8):
            k.pe(lambda e: e.transpose(out=Bt[:, p, :n], in_=src[:n, p * 128:(p + 1) * 128], identity=P.identb[:n, :n]),
                 R=[src, P.identb], W=[Bt])
        if eng == "act":
            k.act(lambda e: e.activation(out=dst_ap, in_=Bt[:, :, :n], func=AF.Copy), R=[Bt], W=[dst_t])
        else:
            k.dve(lambda e: e.tensor_copy(out=dst_ap, in_=Bt[:, :, :n]), R=[Bt], W=[dst_t])

    for s in P.seqs:
        b = s["b"]
        if b is None:
            k.dve(lambda e: e.memset(H[:, :, :], 0.0), W=[H])
        else:
            Sld_t = tmpr.next()
            Sld = Sld_t[0:64, :].rearrange("p (a b) -> p a b", a=16)
            k.load("sp", Sld, I["str"][l, b].rearrange("h i j -> i h j"), Sld_t)
            for half in range(2):
                g_ = gbank.next()
                g4 = g_[:, :].rearrange("p (a b) -> p a b", a=4)
                for pp in range(4):
                    p = half * 4 + pp
                    k.pe(lambda e: e.transpose(out=g4[:, pp, 0:64], in_=Sld_t[0:64, 2 * p * 64:(2 * p + 2) * 64],
                                               identity=C("ident")[:64, :64]), R=[Sld_t, P.cst], W=[g_])
                k.dve(lambda e: e.tensor_copy(out=H[:, half * 4:half * 4 + 4, :], in_=g4[:, :, 0:64]), R=[g_], W=[H])
        refresh_hb()
        ntl = len(s["tiles"])
        for ti, tl in enumerate(s["tiles"]):
            n, g, t0 = tl["n"], tl["g"], tl["t0"]
            k.load("sp", cp[:n, :], P.pj(g, g + n, O_CP, O_CP + CW), cp)
            if t0 == 0:
                if b is None:
                    k.dve(lambda e: e.memset(cs[0:1, :], 0.0), W=[cs])
                else:
                    k.load("sp", cs[0:1, :], I["stsh"][l, b:b + 1, :], cs)
                k.load("sp", cs[1:n, :], P.pj(g, g + n - 1, O_CP, O_CP + CW), cs)
            else:
                k.load("sp", cs[:n, :], P.pj(g - 1, g + n - 1, O_CP, O_CP + CW), cs)
            if ti == ntl - 1:
                sdst = O["sh_p"][l:l + 1, :] if b is None else O["sh_s"][l, b:b + 1, :]
                k.store("pool", sdst, cp[n - 1:n, :], cp)
            k.dve(lambda e: e.tensor_tensor(out=cs[:n, :], in0=cs[:n, :], in1=cp[:n, :], op=ALU.subtract), R=[cs, cp], W=[cs])
            k.dve(lambda e: e.tensor_tensor(out=cs[:n, :], in0=cs[:n, :], in1=mu[:n, :], op=ALU.mult), R=[cs, mu], W=[cs])
            k.dve(lambda e: e.tensor_tensor(out=cs[:n, :], in0=cs[:n, :], in1=cp[:n, :], op=ALU.add), R=[cs, cp], W=[cs])
            r_ = cs[:n, C_R:C_R + D]
            kx = cs[:n, C_K:C_K + D]
            vx = cs[:n, C_V:C_V + D]
            if P5CUT and P5CUT <= 1:
                continue
            k.act(lambda e: e.activation(out=smb[:n, 0, :], in_=cs[:n, C_WLO:C_WLO + 64], func=AF.Tanh), R=[cs], W=[smb])
            k.act(lambda e: e.activation(out=smb[:n, 1, :], in_=cs[:n, C_ALO:C_ALO + 64], func=AF.Copy), R=[cs], W=[smb])
            if l == 1:
                k.load("sp", hv[:n, :], P.pj(g, g + n, O_EXT, O_EXT + 32), hv)
                k.act(lambda e: e.activation(out=smb[:n, 2, 0:32], in_=hv[:n, :], func=AF.Copy), R=[hv], W=[smb])
            for c_ in range(3 if l == 1 else 2):
                kd = 32 if c_ == 2 else 64
                k.pe(lambda e: e.transpose(out=Bt[0:kd, c_, :n], in_=smb[:n, c_, 0:kd], identity=P.identb[:n, :n]),
                     R=[smb, P.identb], W=[Bt])
            k.dve(lambda e: e.tensor_copy(out=smT[:, 0:2, :n], in_=Bt[0:64, 0:2, :n]), R=[Bt], W=[smT])
            if l == 1:
                k.dve(lambda e: e.tensor_copy(out=smT[0:32, 2, :n], in_=Bt[0:32, 2, :n]), R=[Bt], W=[smT])
            sgw = tmpr.next()
            small_mm(0, w2b, 64, n, w0, sgw, AF.Sigmoid)
            k.dve(lambda e: e.tensor_scalar(out=ld[:n, :], in0=sgw[:n, :], scalar1=NEG_E, scalar2=None, op0=ALU.mult),
                  R=[sgw], W=[ld])
            small_mm(1, a2b, 64, n, a0, asig, AF.Sigmoid)
            if l == 1:
                vmix = tmpr.next()
                small_mm(2, v2b, 32, n, v0, vmix, AF.Sigmoid)
                k.load("sp", vft[:n, :], S["vf"][g:g + n, :], vft)
                k.dve(lambda e: e.tensor_tensor(out=vft[:n, :], in0=vft[:n, :], in1=vx, op=ALU.subtract), R=[vft, cs], W=[vft])
                k.dve(lambda e: e.tensor_tensor(out=vft[:n, :], in0=vft[:n, :], in1=vmix[:n, :], op=ALU.mult), R=[vft, vmix], W=[vft])
                k.dve(lambda e: e.tensor_tensor(out=v2[:n, :], in0=vft[:n, :], in1=vx, op=ALU.add), R=[vft, cs], W=[v2])
            else:
                k.dve(lambda e: e.tensor_copy(out=v2[:n, :], in_=vx), R=[cs], W=[v2])
                k.store("pool", S["vf"][g:g + n, :], v2[:n, :], v2)
            if P5CUT and P5CUT <= 2:
                continue
            k.dve(lambda e: e.tensor_tensor(out=kk[:n, :], in0=kx, in1=kkr[:n, :], op=ALU.mult), R=[cs, kkr], W=[kk])
            t = tmpr.next()
            k.dve(lambda e: e.tensor_tensor(out=t[:n, :], in0=kk[:n, :], in1=kk[:n, :], op=ALU.mult), R=[kk], W=[t])
            sk = s16.next()
            k.dve(lambda e: e.tensor_reduce(out=sk[:n, :], in_=t[:n, :].rearrange("p (h d) -> p h d", h=16), axis=AX.X, op=ALU.add),
                  R=[t], W=[sk])
            k.act(lambda e: e.activation(out=sk[:n, :], in_=sk[:n, :], func=AF.Sqrt), R=[sk], W=[sk])
            k.dve(lambda e: e.tensor_scalar(out=sk[:n, :], in0=sk[:n, :], scalar1=1e-12, scalar2=None, op0=ALU.max), R=[sk], W=[sk])
            k.dve(lambda e: e.reciprocal(out=sk[:n, :], in_=sk[:n, :]), R=[sk], W=[sk])
            kk3 = kk[:n, :].rearrange("p (h d) -> p h d", h=16)
            k.dve(lambda e: e.tensor_tensor(out=kk3, in0=kk3, in1=bc(sk[:n, :].unsqueeze(2), [n, 16, 64]), op=ALU.mult),
                  R=[kk, sk], W=[kk])
            t = tmpr.next()
            k.dve(lambda e: e.scalar_tensor_tensor(out=t[:n, :], in0=asig[:n, :], scalar=-1.0, in1=kar[:n, :],
                                                   op0=ALU.add, op1=ALU.mult), R=[asig, kar], W=[t])
            k.dve(lambda e: e.tensor_tensor(out=t[:n, :], in0=t[:n, :], in1=kx, op=ALU.mult), R=[t, cs], W=[t])
            k.dve(lambda e: e.tensor_tensor(out=k2[:n, :], in0=t[:n, :], in1=kx, op=ALU.add), R=[t, cs], W=[k2])
            k.dve(lambda e: e.tensor_tensor(out=bs[:n, :], in0=kk[:n, :], in1=asig[:n, :], op=ALU.mult), R=[kk, asig], W=[bs])
            if P5CUT and P5CUT <= 3:
                continue
            t = tmpr.next()
            k.dve(lambda e: e.tensor_tensor(out=t[:n, :], in0=r_, in1=k2[:n, :], op=ALU.mult), R=[cs, k2], W=[t])
            k.dve(lambda e: e.tensor_tensor(out=t[:n, :], in0=t[:n, :], in1=rkr[:n, :], op=ALU.mult), R=[t, rkr], W=[t])
            s3 = s16.next()
            k.dve(lambda e: e.tensor_reduce(out=s3[:n, :], in_=t[:n, :].rearrange("p (h d) -> p h d", h=16), axis=AX.X, op=ALU.add),
                  R=[t], W=[s3])
            k.dve(lambda e: e.tensor_tensor(out=yn[:n, :].rearrange("p (h d) -> p h d", h=16),
                                            in0=v2[:n, :].rearrange("p (h d) -> p h d", h=16),
                                            in1=bc(s3[:n, :].unsqueeze(2), [n, 16, 64]), op=ALU.mult), R=[v2, s3], W=[yn])
            k.dve(lambda e: e.tensor_copy(out=vb_[:n, :], in_=v2[:n, :]), R=[v2], W=[vb_])
            ep, en = tmpr.next(), tmpr.next()
            cum_exp("r_ut", n, [(ep, 1.0), (en, -1.0)])
            k.dve(lambda e: e.tensor_tensor(out=rt_[:n, :], in0=r_, in1=ep[:n, :], op=ALU.mult), R=[cs, ep], W=[rt_])
            k.dve(lambda e: e.tensor_tensor(out=bt_[:n, :], in0=bs[:n, :], in1=en[:n, :], op=ALU.mult), R=[bs, en], W=[bt_])
            k.dve(lambda e: e.tensor_tensor(out=kt_[:n, :], in0=k2[:n, :], in1=en[:n, :], op=ALU.mult), R=[k2, en], W=[kt_])
            epa = tmpr.next()
            cum_exp("r_uts", n, [(epa, 1.0)])
            k.dve(lambda e: e.scalar_tensor_tensor(out=at_[:n, :], in0=kk[:n, :], scalar=-1.0, in1=epa[:n, :],
                                                   op0=ALU.mult, op1=ALU.mult), R=[kk, epa], W=[at_])
            eend = tmpr.next()
            cum_exp("r_low", n, [(eend, 1.0)])
            k.dve(lambda e: e.tensor_tensor(out=bb_[:n, :], in0=bs[:n, :], in1=eend[:n, :], op=ALU.mult), R=[bs, eend], W=[bb_])
            k.dve(lambda e: e.tensor_tensor(out=kb_[:n, :], in0=k2[:n, :], in1=eend[:n, :], op=ALU.mult), R=[k2, eend], W=[kb_])
            gw = gbank.next()
            for p in range(8):
                k.pe(lambda e: e.matmul(gw[:, p:p + 1], lhsT=ld[:n, p * 128:(p + 1) * 128], rhs=C("r_one")[:n, 0:1],
                                        start=True, stop=True), R=[ld, P.cst], W=[gw])
            k.act(lambda e: e.activation(out=wc[:, :], in_=gw[:, 0:8], func=AF.Exp), R=[gw], W=[wc])
            if P5CUT and P5CUT <= 4:
                continue
            transp8(at_, n, arT[:, :, 0, :n], arT, "act")
            transp8(rt_, n, arT[:, :, 1, :n], arT, "dve")
            transp8(bt_, n, bT[:, :, :n], bT, "act")
            transp8(kt_, n, kT[:, :, :n], kT, "dve")
            if P5CUT and P5CUT <= 5:
                continue
            Ycur, Zcur, ZIcur = [None] * 4, [None] * 4, [None] * 4
            for hg in range(4):
                for hh in range(4):
                    hd = hg * 4 + hh
                    p, base = hd // 2, (hd % 2) * 64
                    bs_ = slice(base, base + 64)
                    gm = gbank.next()
                    m4 = v4(gm)
                    if n == 128:
                        k.pe(lambda e: e.matmul(m4[:n, 0:2, :n], lhsT=bT[bs_, p, :n], rhs=arT[bs_, p, :, :n], start=True, stop=True),
                             R=[bT, arT], W=[gm])
                        k.pe(lambda e: e.matmul(m4[:n, 2:4, :n], lhsT=kT[bs_, p, :n], rhs=arT[bs_, p, :, :n], start=True, stop=True),
                             R=[kT, arT], W=[gm])
                    else:
                        for w_ in range(2):
                            k.pe(lambda e: e.matmul(m4[:n, w_, :n], lhsT=bT[bs_, p, :n], rhs=arT[bs_, p, w_, :n], start=True, stop=True),
                                 R=[bT, arT], W=[gm])
                            k.pe(lambda e: e.matmul(m4[:n, 2 + w_, :n], lhsT=kT[bs_, p, :n], rhs=arT[bs_, p, w_, :n], start=True, stop=True),
                                 R=[kT, arT], W=[gm])
                    k.dve(lambda e: e.tensor_tensor(out=AM[:n, hd, :, :n], in0=m4[:n, :, :n], in1=mk[:n, :, :n], op=ALU.mult),
                          R=[gm, mk], W=[AM])
            for hg in range(4):
                hsl = slice(hg * 4, hg * 4 + 4)
                for hh in range(4):
                    hd = hg * 4 + hh
                    k.pe(lambda e: e.transpose(out=Bt[:n, hg * 4 + hh - (hg // 2) * 8, :n], in_=AM[:n, hd, 0, :n], identity=P.identb[:n, :n]),
                         R=[AM, P.identb], W=[Bt])
                if hg % 2 == 1:
                    for h2 in range(2):
                        hgg = hg - 1 + h2
                        Z = yzr.next()
                        k.act(lambda e: e.activation(out=Z[:n, :, :n], in_=Bt[:n, h2 * 4:h2 * 4 + 4, :n], func=AF.Copy), R=[Bt], W=[Z])
                        Zcur[hgg] = Z
                k.dve(lambda e: e.tensor_tensor(out=Qt[hg][:n, :, :n], in0=AM[:n, hsl, 0, :n], in1=i4[:n, :, :n], op=ALU.add),
                      R=[AM, i4], W=[Qt[hg]])
            nsteps = 6 if n == 128 else 5
            for step in range(1, nsteps + 1):
                gys, gzs = [None] * 4, [None] * 4
                for hg in range(4):
                    Y, Z = Ycur[hg], Zcur[hg]
                    gy = gbank.next() if step < nsteps else None
                    gzz = gbank.next()
                    for hh in range(4):
                        hd = hg * 4 + hh
                        ysrc = AM[:n, hd, 0, :n] if Y is None else Y[:n, hh, :n]
                        ytk = AM if Y is None else Y
                        if gy is not None:
                            k.pe(lambda e: e.matmul(v4(gy)[:n, hh, :n], lhsT=Z[:n, hh, :n], rhs=ysrc, start=True, stop=True),
                                 R=[Z, ytk], W=[gy])
                        k.pe(lambda e: e.matmul(v4(gzz)[:n, hh, :n], lhsT=ysrc, rhs=Z[:n, hh, :n], start=True, stop=True),
                             R=[Z, ytk], W=[gzz])
                    Zn = yzr.next()
                    k.dve(lambda e: e.tensor_copy(out=Zn[:n, :, :n], in_=v4(gzz)[:n, :, :n]), R=[gzz], W=[Zn])
                    ZI = yzr.next()
                    k.dve(lambda e: e.tensor_tensor(out=ZI[:n, :, :n], in0=v4(gzz)[:n, :, :n], in1=i4[:n, :, :n], op=ALU.add),
                          R=[gzz, i4], W=[ZI])
                    ZIcur[hg] = ZI
                    if gy is not None:
                        Yn = yzr.next()
                        k.act(lambda e: e.activation(out=Yn[:n, :, :n], in_=v4(gy)[:n, :, :n], func=AF.Copy), R=[gy], W=[Yn])
                    else:
                        Yn = None
                    Ycur[hg], Zcur[hg] = Yn, Zn
                for hg in range(4):
                    hsl = slice(hg * 4, hg * 4 + 4)
                    Zn = Zcur[hg]
                    gq = gbank.next()
                    ZI = ZIcur[hg]
                    for hh in range(4):
                        hd = hg * 4 + hh
                        k.pe(lambda e: e.matmul(v4(gq)[:n, hh, :n], lhsT=ZI[:n, hh, :n], rhs=Qt[hg][:n, hh, :n], start=True, stop=True),
                             R=[ZI, Qt[hg]], W=[gq])
                    k.act(lambda e: e.activation(out=Qt[hg][:n, :, :n], in_=v4(gq)[:n, :, :n], func=AF.Copy), R=[gq], W=[Qt[hg]])
            if P5CUT and P5CUT <= 6:
                continue
            for half in range(2):
                g1 = gbank.next()
                for pp in range(4):
                    p = half * 4 + pp
                    k.pe(lambda e: e.matmul(v4(g1)[:n, pp, :], lhsT=arT[:, p, 0, :n], rhs=Hb[:, p, :], start=True, stop=False),
                         R=[arT, Hb], W=[g1])
                    for e_ in range(2):
                        hd = 2 * p + e_
                        k.pe(lambda e: e.matmul(v8(g1)[:n, 2 * pp + e_, :], lhsT=AM[:n, hd, 2, :n], rhs=vb_[:n, hd * 64:(hd + 1) * 64],
                                                start=False, stop=(e_ == 1)), R=[AM, vb_], W=[g1])
                k.act(lambda e: e.activation(out=R1[:n, half * 8:half * 8 + 8, :], in_=v8(g1)[:n, :, :], func=AF.Copy), R=[g1], W=[R1])
            if P5CUT == 65:
                continue
            for half in range(2):
                g2 = gbank.next()
                for h8 in range(8):
                    hd = half * 8 + h8
                    k.pe(lambda e: e.matmul(v8(g2)[:n, h8, :], lhsT=Qt[hd // 4][:n, hd % 4, :n], rhs=R1[:n, hd, :], start=True, stop=True),
                         R=[Qt[hd // 4], R1], W=[g2])
                k.dve(lambda e: e.tensor_copy(out=Ub[:n, half * 8:half * 8 + 8, :], in_=v8(g2)[:n, :, :]), R=[g2], W=[Ub])
            if P5CUT and P5CUT <= 7:
                continue
            for half in range(2):
                g3 = gbank.next()
                for pp in range(4):
                    p = half * 4 + pp
                    k.pe(lambda e: e.matmul(v4(g3)[:n, pp, :], lhsT=arT[:, p, 1, :n], rhs=Hb[:, p, :], start=True, stop=False),
                         R=[arT, Hb], W=[g3])
                    for e_ in range(2):
                        hd = 2 * p + e_
                        k.pe(lambda e: e.matmul(v8(g3)[:n, 2 * pp + e_, :], lhsT=AM[:n, hd, 1, :n], rhs=Ub[:n, hd, :], start=False, stop=False),
                             R=[AM, Ub], W=[g3])
                        k.pe(lambda e: e.matmul(v8(g3)[:n, 2 * pp + e_, :], lhsT=AM[:n, hd, 3, :n], rhs=vb_[:n, hd * 64:(hd + 1) * 64],
                                                start=False, stop=(e_ == 1)), R=[AM, vb_], W=[g3])
                k.act(lambda e: e.activation(out=yt[:n, half * 512:(half + 1) * 512], in_=g3[:n, :], func=AF.Copy), R=[g3], W=[yt])
            if P5CUT and P5CUT <= 8:
                continue
            for half in range(2):
                g4_ = gbank.next()
                for pp in range(4):
                    p = half * 4 + pp
                    ps_ = slice(p * 128, (p + 1) * 128)
                    k.pe(lambda e: e.matmul(v4(g4_)[:, pp, :], lhsT=bb_[:n, ps_], rhs=Ub[:n, 2 * p:2 * p + 2, :].rearrange("p a b -> p (a b)"),
                                            start=True, stop=False), R=[bb_, Ub], W=[g4_])
                    k.pe(lambda e: e.matmul(v4(g4_)[:, pp, :], lhsT=kb_[:n, ps_], rhs=vb_[:n, ps_], start=False, stop=True),
                         R=[kb_, vb_], W=[g4_])
                hs_ = slice(half * 4, half * 4 + 4)
                hst = tmpr.next()
                hst4 = hst[:, 0:512].rearrange("p (a b) -> p a b", a=4)
                k.act(lambda e: e.activation(out=hst[:, 0:512], in_=g4_[:, :], func=AF.Copy), R=[g4_], W=[hst])
                for e_ in range(2):
                    rows = slice(e_ * 64, (e_ + 1) * 64)
                    k.dve(lambda e: e.tensor_tensor(out=H[rows, hs_, :], in0=H[rows, hs_, :],
                                                    in1=bc(wc[rows, hs_].unsqueeze(2), [64, 4, 64]), op=ALU.mult),
                          R=[H, wc], W=[H])
                    k.dve(lambda e: e.tensor_tensor(out=H[rows, hs_, :], in0=H[rows, hs_, :],
                                                    in1=hst4[rows, :, e_ * 64:(e_ + 1) * 64], op=ALU.add),
                          R=[H, hst], W=[H])
            refresh_hb()
            if P5CUT and P5CUT <= 9:
                continue
            y3 = yt[:n, :].rearrange("p (h d) -> p h d", h=16)
            s1 = s16.next()
            k.dve(lambda e: e.tensor_reduce(out=s1[:n, :], in_=y3, axis=AX.X, op=ALU.add), R=[yt], W=[s1])
            k.dve(lambda e: e.tensor_scalar(out=s1[:n, :], in0=s1[:n, :], scalar1=1.0 / 64, scalar2=None, op0=ALU.mult), R=[s1], W=[s1])
            k.dve(lambda e: e.tensor_tensor(out=y3, in0=y3, in1=bc(s1[:n, :].unsqueeze(2), [n, 16, 64]), op=ALU.subtract),
                  R=[yt, s1], W=[yt])
            t = tmpr.next()
            k.dve(lambda e: e.tensor_tensor(out=t[:n, :], in0=yt[:n, :], in1=yt[:n, :], op=ALU.mult), R=[yt], W=[t])
            s2 = s16.next()
            k.dve(lambda e: e.tensor_reduce(out=s2[:n, :], in_=t[:n, :].rearrange("p (h d) -> p h d", h=16), axis=AX.X, op=ALU.add),
                  R=[t], W=[s2])
            rsqrt(k, s2[:n, :], s2[:n, :], [s2], [s2], 1.0 / 64, GN_EPS)
            tn = tmpr.next()
            tn3 = tn[:n, :].rearrange("p (h d) -> p h d", h=16)
            k.dve(lambda e: e.tensor_tensor(out=tn3, in0=y3, in1=bc(s2[:n, :].unsqueeze(2), [n, 16, 64]), op=ALU.mult),
                  R=[yt, s2], W=[tn])
            k.dve(lambda e: e.tensor_tensor(out=tn[:n, :], in0=tn[:n, :], in1=lnw[:n, :], op=ALU.mult), R=[tn, lnw], W=[tn])
            k.dve(lambda e: e.tensor_tensor(out=tn[:n, :], in0=tn[:n, :], in1=lnb[:n, :], op=ALU.add), R=[tn, lnb], W=[tn])
            k.dve(lambda e: e.tensor_tensor(out=yn[:n, :], in0=yn[:n, :], in1=tn[:n, :], op=ALU.add), R=[yn, tn], W=[yn])
            k.store("pool", S["oc"][g:g + n, :], yn[:n, :], yn)
        rwo_t = tmpr.next()
        rwo = rwo_t[0:64, :].rearrange("p (a b) -> p a b", a=8)
        for half in range(0 if P5NOFIN else 2):
            g_ = gbank.next()
            g4 = v4(g_)
            for pp in range(4):
                p = half * 4 + pp
                k.pe(lambda e: e.transpose(out=g4[0:64, pp, :], in_=H[:, p, :], identity=C("ident")), R=[H, P.cst], W=[g_])
            k.dve(lambda e: e.tensor_copy(out=rwo[:, half * 4:half * 4 + 4, :], in_=g4[0:64, :, :]), R=[g_], W=[rwo_t])
        rdst = O["rw_p"][l] if b is None else O["rw_s"][l, b]
        k.store("pool", rdst.rearrange("(p e) i j -> i p e j", e=2), rwo.rearrange("i p (e j) -> i p e j", e=2), rwo_t)
    k.end_phase(ph)


def phase6(P, l):
    k, I, S, O = P.k, P.I, P.S, P.O
    ph = k.phase()
    stg = Rot(k, ph, 2, [128, 2, D], F32, "p6stg")
    W = {}
    for nm in ["w_out_a", "w_out_b", "w_out_c", "w_o"]:
        wt = k.sb(ph, [128, 8, D], BF16, "p6" + nm)
        src = I[nm][l].rearrange("(k p) c -> p k c", p=128)
        for c4 in range(4):
            st = stg.next()
            k.load("sp", st[:, :, :], src[:, 2 * c4:2 * c4 + 2, :], st)
            k.dve(lambda e: e.tensor_copy(out=wt[:, 2 * c4:2 * c4 + 2, :], in_=st[:, :, :]), R=[st], W=[wt])
        W[nm] = wt
    ldr = Rot(k, ph, 12, [128, D], F32, "p6ld")
    tmpr = Rot(k, ph, 4, [128, D], F32, "p6tmp")
    mrg = Rot(k, ph, 2, [128, D], F32, "p6mrg")
    ogr = Rot(k, ph, 2, [128, D], BF16, "p6og")
    tTr = Rot(k, ph, 2, [128, 8, 128], BF16, "p6tT")
    yr = Rot(k, ph, 2, [128, D], F32, "p6y")
    pbr = Rot(k, ph, 2, [128, D], F32, "p6pb", psum=True)
    ptr = Rot(k, ph, 2, [128, 8, 128], BF16, "p6pt", psum=True)

    def proj_mm(srcb, n, wt):
        pt = ptr.next()
        for kk in range(8):
            k.pe(lambda e: e.transpose(out=pt[:, kk, :n], in_=srcb[:n, kk * 128:(kk + 1) * 128], identity=P.identb[:n, :n]),
                 R=[srcb, P.identb], W=[pt])
        tT = tTr.next()
        k.act(lambda e: e.activation(out=tT[:, :, :n], in_=pt[:, :, :n], func=AF.Copy), R=[pt], W=[tT])
        pb = pbr.next()
        for hf in range(2):
            for kk in range(8):
                k.pe(lambda e: e.matmul(pb[:n, hf * 512:(hf + 1) * 512], lhsT=tT[:, kk, :n], rhs=wt[:, kk, hf * 512:(hf + 1) * 512],
                                        start=(kk == 0), stop=(kk == 7)), R=[tT, wt], W=[pb])
        return pb

    for tl in P.tiles:
        n, g, t0 = tl["n"], tl["g"], tl["t0"]
        s = tl["seq"]
        b = s["b"]
        merged = mrg.next()
        for mi, (osrc, gcol, mcol, wn) in enumerate([("oa", O_AG, O_MA, "w_out_a"), ("ob", O_BG, O_MB, "w_out_b"),
                                                     ("oc", O_CG, O_MC, "w_out_c")]):
            o_, g_, m_ = ldr.next(), ldr.next(), ldr.next()
            k.load("sp", o_[:n, :], S[osrc][g:g + n, :], o_)
            k.load("sp", g_[:n, :], P.pj(g, g + n, gcol, gcol + D), g_)
            k.load("sp", m_[:n, :], P.pj(g, g + n, mcol, mcol + D), m_)
            sg = tmpr.next()
            k.act(lambda e: e.activation(out=sg[:n, :], in_=g_[:n, :], func=AF.Silu), R=[g_], W=[sg])
            og = ogr.next()
            k.dve(lambda e: e.tensor_tensor(out=og[:n, :], in0=o_[:n, :], in1=sg[:n, :], op=ALU.mult), R=[o_, sg], W=[og])
            pb = proj_mm(og, n, W[wn])
            sm = tmpr.next()
            k.act(lambda e: e.activation(out=sm[:n, :], in_=m_[:n, :], func=AF.Sigmoid), R=[m_], W=[sm])
            if mi == 0:
                k.dve(lambda e: e.tensor_tensor(out=merged[:n, :], in0=pb[:n, :], in1=sm[:n, :], op=ALU.mult), R=[pb, sm], W=[merged])
            else:
                k.dve(lambda e: e.tensor_tensor(out=sm[:n, :], in0=pb[:n, :], in1=sm[:n, :], op=ALU.mult), R=[pb, sm], W=[sm])
                k.dve(lambda e: e.tensor_tensor(out=merged[:n, :], in0=merged[:n, :], in1=sm[:n, :], op=ALU.add),
                       R=[merged, sm], W=[merged])
        mb = ogr.next()
        k.act(lambda e: e.activation(out=mb[:n, :], in_=merged[:n, :], func=AF.Copy), R=[merged], W=[mb])
        py = proj_mm(mb, n, W["w_o"])
        x = ldr.next()
        src, _ = P.xsrc(l, tl)
        k.load("sp", x[:n, :], src, x)
        y = yr.next()
        k.dve(lambda e: e.tensor_tensor(out=y[:n, :], in0=py[:n, :], in1=x[:n, :], op=ALU.add), R=[py, x], W=[y])
        if l == 0:
            dst = S["xmid"][g:g + n, :]
        elif b is None:
            dst = O["y_p"][t0:t0 + n, :]
        else:
            dst = O["y_s"][b, t0:t0 + n, :]
        k.store("pool", dst, y[:n, :], y)
    k.end_phase(ph)


_CACHE = {}
NCORES = 8


def _get_prog(T):
    if T not in _CACHE:
        _CACHE[T] = build(T)
    return _CACHE[T]


def kernel(x_prompt, x_sample, cache_attn_k, cache_attn_v, state_hgrn, state_rwkv, state_rwkv_shift,
           norm_g, w_in, a_qnorm_g, a_knorm_g, a_lambda, a_subln_g, b_lower, b_norm_g,
           c_shift_mu, c_w0, c_w2, c_a0, c_a2, c_k_k, c_k_a, c_r_k, c_ln_w, c_ln_b,
           c_vres_w1, c_vres_w2, c_v0, w_out_a, w_out_b, w_out_c, w_o):
    f = lambda a: np.ascontiguousarray(np.asarray(a, dtype=np.float32))
    x_prompt = f(x_prompt)
    B, T, _ = x_prompt.shape
    assert B == NCORES
    P = _get_prog(T)
    carr, _ = make_consts()
    shared = {
        "norm_g": f(norm_g), "w_in": f(w_in), "a_qnorm_g": f(a_qnorm_g), "a_knorm_g": f(a_knorm_g),
        "a_lambda": f(a_lambda).reshape(2, 256), "a_subln_g": f(a_subln_g), "b_lower": f(b_lower), "b_norm_g": f(b_norm_g),
        "c_shift_mu": f(c_shift_mu), "c_w0": f(c_w0), "c_w2": f(c_w2), "c_a0": f(c_a0), "c_a2": f(c_a2),
        "c_k_k": f(c_k_k), "c_k_a": f(c_k_a), "c_r_k": f(c_r_k).reshape(2, 1024), "c_ln_w": f(c_ln_w), "c_ln_b": f(c_ln_b),
        "c_vres_w1": f(c_vres_w1), "c_vres_w2": f(c_vres_w2), "c_v0": f(c_v0),
        "w_out_a": f(w_out_a), "w_out_b": f(w_out_b), "w_out_c": f(w_out_c), "w_o": f(w_o),
        "consts": carr,
        "rope_p": rope_tables(np.arange(T)), "rope_s": rope_tables(PAST + np.arange(TS)),
    }
    x_sample = f(x_sample)
    ck, cv = f(cache_attn_k), f(cache_attn_v)
    sth, str_, stsh = f(state_hgrn), f(state_rwkv), f(state_rwkv_shift)
    in_maps = []
    for c in range(NCORES):
        m = dict(shared)
        sl = slice(2 * c, 2 * c + 2)
        m["x_p"] = x_prompt[c]
        m["x_s"] = x_sample[sl]
        m["ck"] = np.ascontiguousarray(ck[:, sl]).reshape(2, 2, PAST, D)
        m["cv"] = np.ascontiguousarray(cv[:, sl]).reshape(2, 2, PAST, D)
        m["sth"] = np.ascontiguousarray(sth[:, sl])
        m["str"] = np.ascontiguousarray(str_[:, sl])
        m["stsh"] = np.ascontiguousarray(stsh[:, sl])
        in_maps.append(m)
    res = run_bass_kernel_spmd(P.nc, in_maps, core_ids=list(range(NCORES)))
    R = res.results
    NB = 2 * NCORES
    y_p = np.stack([R[c]["y_p"] for c in range(NCORES)], 0)
    y_s = np.concatenate([R[c]["y_s"] for c in range(NCORES)], 0)
    k_p = np.stack([R[c]["k_p"].reshape(2, T, 8, 128) for c in range(NCORES)], 1)
    v_p = np.stack([R[c]["v_p"].reshape(2, T, 8, 128) for c in range(NCORES)], 1)
    hg_p = np.stack([R[c]["hg_p"] for c in range(NCORES)], 1)
    rw_p = np.stack([R[c]["rw_p"] for c in range(NCORES)], 1)
    sh_p = np.stack([R[c]["sh_p"] for c in range(NCORES)], 1)
    k_s = np.concatenate([R[c]["k_s"].reshape(2, 2, TS, 8, 128) for c in range(NCORES)], 1)
    v_s = np.concatenate([R[c]["v_s"].reshape(2, 2, TS, 8, 128) for c in range(NCORES)], 1)
    hg_s = np.concatenate([R[c]["hg_s"] for c in range(NCORES)], 1)
    rw_s = np.concatenate([R[c]["rw_s"] for c in range(NCORES)], 1)
    sh_s = np.concatenate([R[c]["sh_s"] for c in range(NCORES)], 1)
    outs = (y_p, y_s, k_p, v_p, hg_p, rw_p, sh_p, k_s, v_s, hg_s, rw_s, sh_s)
    return tuple(np.ascontiguousarray(o, dtype=np.float32) for o in outs)
```
